# Optimizing a Trainium2 kernel written in Bass

```python
import jax, jax.numpy as jnp
from jax import lax
import numpy as np

D_MODEL = 1024
BATCH = 2
SEQ = 8192
DEPTH = 2

N_META = 16
CHUNK = 64
SUB = 16
RET_HEADS = 4
RET_DK = 128
RET_DV = 128
GLA_HEADS = 4
GLA_DK = 64
GLA_DV = 128
GLA_GATE_RANK = 16
GLA_TAU = 16.0
D_FF = 2816
CONV_W = 3
ROPE_BASE = 10000.0
EPS = 1e-6

RET_QK = RET_HEADS * RET_DK
RET_V = RET_HEADS * RET_DV
GLA_QK = GLA_HEADS * GLA_DK
GLA_V = GLA_HEADS * GLA_DV
D_MIX = RET_V + GLA_V
IN_SPLITS = (RET_QK, RET_QK, RET_V, RET_V, GLA_QK, GLA_QK, GLA_V, GLA_V, GLA_GATE_RANK)
IN_WIDTH = 2 * RET_QK + 2 * RET_V + 2 * GLA_QK + 2 * GLA_V + GLA_GATE_RANK

kernel_name = "hybrid_retention_gla_convffn"


def _rmsnorm(x, w):
    xf = x.astype(jnp.float32)
    y = xf * lax.rsqrt(jnp.mean(xf * xf, axis=-1, keepdims=True) + EPS)
    return (y * w.astype(jnp.float32)).astype(x.dtype)


def _rope(t, pos):
    half = t.shape[-1] // 2
    inv = ROPE_BASE ** (-jnp.arange(half, dtype=jnp.float32) / half)
    ang = pos[:, None] * inv[None, :]
    c = jnp.cos(ang)[None, :, None, :]
    s = jnp.sin(ang)[None, :, None, :]
    t = t.astype(jnp.float32)
    t1, t2 = t[..., :half], t[..., half:]
    return jnp.concatenate([t1 * c - t2 * s, t1 * s + t2 * c], axis=-1)


def _to_chunks(t):
    t = jnp.pad(t.astype(jnp.float32), ((0, 0), (CHUNK - N_META, 0), (0, 0), (0, 0)))
    b, lp, h, d = t.shape
    return t.reshape(b, lp // CHUNK, CHUNK, h, d).transpose(0, 3, 1, 2, 4)


def _from_chunks(o):
    b, h, n, c, d = o.shape
    o = o.transpose(0, 2, 3, 1, 4).reshape(b, n * c, h, d)
    return o[:, CHUNK - N_META:]


def _scan_states(decay, kv):
    def step(state, inp):
        dec_n, kv_n = inp
        return dec_n * state + kv_n, state
    init = jnp.zeros(kv.shape[:2] + kv.shape[3:], kv.dtype)
    _, prev = lax.scan(step, init, (jnp.moveaxis(decay, 2, 0), jnp.moveaxis(kv, 2, 0)))
    return jnp.moveaxis(prev, 0, 2)


def _retention(q, k, v):
    b, h, n, c, _ = q.shape
    log_g = jnp.log(1.0 - 2.0 ** (-5.0 - jnp.arange(h, dtype=jnp.float32)))
    idx = jnp.arange(c, dtype=jnp.float32)
    diff = idx[:, None] - idx[None, :]
    dmat = jnp.where(diff >= 0, jnp.exp(log_g[:, None, None] * jnp.maximum(diff, 0.0)), 0.0)
    k = k * (RET_DK ** -0.5)
    scores = jnp.einsum('bhncd,bhnsd->bhncs', q, k) * dmat[None, :, None]
    o_intra = jnp.einsum('bhncs,bhnsv->bhncv', scores, v)
    zeta = jnp.exp(log_g[:, None] * (c - 1.0 - idx)[None, :])
    kv = jnp.einsum('bhncd,hc,bhncv->bhndv', k, zeta, v)
    chunk_decay = jnp.broadcast_to(jnp.exp(log_g * c)[None, :, None, None, None], (b, h, n, 1, 1))
    prev = _scan_states(chunk_decay, kv)
    xi = jnp.exp(log_g[:, None] * (idx + 1.0)[None, :])
    o_inter = jnp.einsum('bhncd,bhndv->bhncv', q, prev) * xi[None, :, None, :, None]
    return o_intra + o_inter


def _gla(q, k, v, log_a):
    b, h, n, c, dk = q.shape
    dv = v.shape[-1]
    ns = c // SUB
    q = q * (GLA_DK ** -0.5)
    cum = jnp.cumsum(log_a, axis=3)
    last = cum[:, :, :, -1:, :]
    kv = jnp.einsum('bhncd,bhncv->bhndv', k * jnp.exp(last - cum), v)
    prev = _scan_states(jnp.exp(last[:, :, :, 0, :])[..., None], kv)
    o_inter = jnp.einsum('bhncd,bhndv->bhncv', q * jnp.exp(cum), prev)
    qs = q.reshape(b, h, n, ns, SUB, dk)
    ks = k.reshape(b, h, n, ns, SUB, dk)
    vs = v.reshape(b, h, n, ns, SUB, dv)
    cs = cum.reshape(b, h, n, ns, SUB, dk)
    ref = jnp.concatenate([jnp.zeros((b, h, n, 1, dk), cum.dtype), cum[:, :, :, SUB - 1:c - 1:SUB, :]], axis=3)
    q_hat = qs * jnp.exp(cs - ref[:, :, :, :, None, :])
    k_hat = k[:, :, :, None] * jnp.exp(jnp.minimum(ref[:, :, :, :, None, :] - cum[:, :, :, None], 0.0))
    off = jnp.einsum('bhnasd,bhnacd->bhnasc', q_hat, k_hat)
    off_mask = jnp.arange(c)[None, :] < (jnp.arange(ns) * SUB)[:, None]
    off = jnp.where(off_mask[:, None, :], off, 0.0)
    o_off = jnp.einsum('bhnasc,bhncv->bhnasv', off, v)
    causal = jnp.tril(jnp.ones((SUB, SUB), dtype=bool))
    ddiff = cs[..., :, None, :] - cs[..., None, :, :]
    dec = jnp.exp(jnp.where(causal[:, :, None], ddiff, -jnp.inf))
    diag = jnp.einsum('bhnasd,bhnatd,bhnastd->bhnast', qs, ks, dec)
    o_diag = jnp.einsum('bhnast,bhnatv->bhnasv', diag, vs)
    return o_inter + (o_off + o_diag).reshape(b, h, n, c, dv)


def _head_group_norm(o, w):
    mu = jnp.mean(o, axis=-1, keepdims=True)
    var = jnp.mean(jnp.square(o - mu), axis=-1, keepdims=True)
    y = (o - mu) * lax.rsqrt(var + EPS)
    return y.reshape(o.shape[0], o.shape[1], -1) * w.astype(jnp.float32)


def _head_rms_norm(o, w):
    y = o * lax.rsqrt(jnp.mean(o * o, axis=-1, keepdims=True) + EPS)
    return y.reshape(o.shape[0], o.shape[1], -1) * w.astype(jnp.float32)


def _mixer(h, pos, w_in, gla_gate_w2, gla_gate_b, ret_norm_w, gla_norm_w, w_out):
    bsz, length, _ = h.shape
    proj = h @ w_in
    offs = np.cumsum(np.array(IN_SPLITS))[:-1].tolist()
    rq, rk, rv, rg, gq, gk, gv, gr, ga = jnp.split(proj, offs, axis=-1)
    rq = _rope(rq.reshape(bsz, length, RET_HEADS, RET_DK), pos)
    rk = _rope(rk.reshape(bsz, length, RET_HEADS, RET_DK), pos)
    rv = rv.reshape(bsz, length, RET_HEADS, RET_DV)
    o_ret = _from_chunks(_retention(_to_chunks(rq), _to_chunks(rk), _to_chunks(rv)))
    o_ret = _head_group_norm(o_ret, ret_norm_w) * jax.nn.silu(rg.astype(jnp.float32))
    z = (ga @ gla_gate_w2 + gla_gate_b).astype(jnp.float32)
    log_a = (jax.nn.log_sigmoid(z) / GLA_TAU).reshape(bsz, length, GLA_HEADS, GLA_DK)
    gq = gq.reshape(bsz, length, GLA_HEADS, GLA_DK)
    gk = gk.reshape(bsz, length, GLA_HEADS, GLA_DK)
    gv = gv.reshape(bsz, length, GLA_HEADS, GLA_DV)
    o_gla = _from_chunks(_gla(_to_chunks(gq), _to_chunks(gk), _to_chunks(gv), _to_chunks(log_a)))
    o_gla = _head_rms_norm(o_gla, gla_norm_w) * jax.nn.silu(gr.astype(jnp.float32))
    merged = jnp.concatenate([o_ret, o_gla], axis=-1).astype(h.dtype)
    return merged @ w_out


def _conv_ffn(h, ffn_up, ffn_conv_w, ffn_conv_b, ffn_down):
    length = h.shape[1]
    u = h @ ffn_up
    up = jnp.pad(u, ((0, 0), (CONV_W - 1, 0), (0, 0)))
    conv = ffn_conv_b + sum(up[:, i:i + length] * ffn_conv_w[i] for i in range(CONV_W))
    a, g = jnp.split(conv, 2, axis=-1)
    return (jax.nn.gelu(a, approximate=True) * g) @ ffn_down


def setup_inputs(seed: int = 0) -> dict:
    key = jax.random.key(seed)
    ks = jax.random.split(key, 18)
    nrm = lambda k, shape, s: jax.random.normal(k, shape, jnp.float32) * s
    gain = lambda k, shape: 1.0 + 0.02 * jax.random.normal(k, shape, jnp.float32)
    return {
        "x": nrm(ks[0], (BATCH, SEQ, D_MODEL), 1.0),
        "meta_tokens": nrm(ks[1], (N_META, D_MODEL), 1.0),
        "pre_mix_norm": gain(ks[2], (DEPTH, D_MODEL)),
        "w_in": nrm(ks[3], (DEPTH, D_MODEL, IN_WIDTH), D_MODEL ** -0.5),
        "gla_gate_w2": nrm(ks[4], (DEPTH, GLA_GATE_RANK, GLA_QK), GLA_GATE_RANK ** -0.5),
        "gla_gate_b": nrm(ks[5], (DEPTH, GLA_QK), 0.1),
        "ret_norm_w": gain(ks[6], (DEPTH, RET_V)),
        "gla_norm_w": gain(ks[7], (DEPTH, GLA_V)),
        "w_out": nrm(ks[8], (DEPTH, D_MIX, D_MODEL), D_MIX ** -0.5),
        "post_mix_norm": gain(ks[9], (DEPTH, D_MODEL)),
        "pre_ffn_norm": gain(ks[10], (DEPTH, D_MODEL)),
        "ffn_up": nrm(ks[11], (DEPTH, D_MODEL, 2 * D_FF), D_MODEL ** -0.5),
        "ffn_conv_w": nrm(ks[12], (DEPTH, CONV_W, 2 * D_FF), CONV_W ** -0.5),
        "ffn_conv_b": nrm(ks[13], (DEPTH, 2 * D_FF), 0.02),
        "ffn_down": nrm(ks[14], (DEPTH, D_FF, D_MODEL), D_FF ** -0.5),
        "post_ffn_norm": gain(ks[15], (DEPTH, D_MODEL)),
    }


def reference(x, meta_tokens, pre_mix_norm, w_in, gla_gate_w2, gla_gate_b, ret_norm_w, gla_norm_w,
              w_out, post_mix_norm, pre_ffn_norm, ffn_up, ffn_conv_w, ffn_conv_b, ffn_down, post_ffn_norm):
    bsz = x.shape[0]
    meta = jnp.broadcast_to(meta_tokens.astype(x.dtype)[None], (bsz, N_META, x.shape[-1]))
    h = jnp.concatenate([meta, x], axis=1)
    pos = jnp.arange(h.shape[1], dtype=jnp.float32)
    for i in range(DEPTH):
        m = _mixer(_rmsnorm(h, pre_mix_norm[i]), pos, w_in[i], gla_gate_w2[i], gla_gate_b[i],
                   ret_norm_w[i], gla_norm_w[i], w_out[i])
        h = h + _rmsnorm(m, post_mix_norm[i])
        f = _conv_ffn(_rmsnorm(h, pre_ffn_norm[i]), ffn_up[i], ffn_conv_w[i], ffn_conv_b[i], ffn_down[i])
        h = h + _rmsnorm(f, post_ffn_norm[i])
    return h[:, N_META:]
```

```python
import numpy as np
import concourse.bass as bass
import concourse.mybir as mybir
from concourse.bass_utils import run_bass_kernel_spmd

F32 = mybir.dt.float32
BF16 = mybir.dt.bfloat16
AF = mybir.ActivationFunctionType
ALU = mybir.AluOpType
AX = mybir.AxisListType

P = 128
D = 1024
KC = 8
NPRE = 16
NSEQ = 2048
NLOC = NPRE + NSEQ
TT = 128
CH = 128
INW = 3600
DFF = 2816
NFC = 22
EPS = 1e-6
GLA_TAU = 16.0
EPOCH = 6000

C_RQ, C_RK, C_RV, C_RG, C_GQ, C_GK, C_GV, C_GG, C_GA = 0, 512, 1024, 1536, 2048, 2304, 2560, 3072, 3584


class Buf:
    __slots__ = ("name", "lw", "rd", "rd_dma")

    def __init__(self, name):
        self.name = name
        self.lw = None
        self.rd = {}
        self.rd_dma = []


class Ins:
    __slots__ = ("eng", "fn", "deps", "sig", "cnt", "is_dma", "key", "val", "inc")

    def __init__(self, eng, fn):
        self.eng = eng
        self.fn = fn
        self.deps = []
        self.sig = False
        self.cnt = 0
        self.is_dma = False
        self.key = None
        self.val = 0
        self.inc = 16


class Prog:
    ENGS = ("pe", "act", "dve", "pool", "sp")

    def __init__(self):
        self.streams = {e: [] for e in self.ENGS}
        self.key_cnt = {}
        self.key_last = {}
        self.bar = {}

    def barrier(self, exclude=()):
        deps = []
        for e in self.ENGS:
            for ins in reversed(self.streams[e]):
                if not ins.is_dma:
                    deps.append(ins)
                    break
        deps += [v for k, v in self.key_last.items() if k not in exclude]
        for e in self.ENGS:
            self.bar[e] = list(deps) + self.bar.get(e, [])

    def emit(self, eng, fn, reads=(), writes=(), dma_key=None, serialize=True, inc=16):
        ins = Ins(eng, fn)
        raw = set()
        oth = set()
        for b in reads:
            if b.lw is not None:
                raw.add(b.lw)
        for b in writes:
            if b.lw is not None:
                oth.add(b.lw)
            for r in b.rd.values():
                oth.add(r)
            for r in b.rd_dma:
                oth.add(r)
        if dma_key is not None:
            ins.is_dma = True
            ins.key = dma_key
            ins.inc = inc
            self.key_cnt[dma_key] = self.key_cnt.get(dma_key, 0) + inc
            ins.val = self.key_cnt[dma_key]
            if serialize and dma_key in self.key_last:
                oth.add(self.key_last[dma_key])
            self.key_last[dma_key] = ins
        deps = []
        for d in self.bar.pop(eng, []):
            if d.is_dma or d.eng != eng:
                if d not in raw and d not in oth:
                    deps.append(d)
        for d in raw | oth:
            if d is ins:
                continue
            if d.is_dma or ins.is_dma:
                deps.append(d)
            elif d.eng != eng:
                deps.append(d)
            else:
                if eng != "pe":
                    deps.append(d)
        for d in deps:
            if not d.is_dma:
                d.sig = True
        ins.deps = deps
        for b in reads:
            if ins.is_dma:
                b.rd_dma.append(ins)
            else:
                b.rd[eng] = ins
        for b in writes:
            b.lw = ins
            b.rd = {}
            b.rd_dma = []
        self.streams[eng].append(ins)
        return ins

    def finalize(self):
        for e in self.ENGS:
            c = 0
            for ins in self.streams[e]:
                if ins.is_dma:
                    continue
                if ins.sig:
                    c += 1
                    ins.cnt = c
        return {e: sum(1 for i in self.streams[e] if i.sig and not i.is_dma) for e in self.ENGS}

    def replay(self, eng, eobj, eng_sems, dma_sems):
        seen_cnt = {e: 0 for e in self.ENGS}
        seen_dma = {}
        for ins in self.streams[eng]:
            for d in ins.deps:
                if d.is_dma:
                    if seen_dma.get(d.key, 0) >= d.val:
                        continue
                    eobj.wait_ge(dma_sems[d.key], d.val)
                    seen_dma[d.key] = d.val
                else:
                    if seen_cnt[d.eng] >= d.cnt:
                        continue
                    ep = (d.cnt - 1) // EPOCH
                    eobj.wait_ge(eng_sems[d.eng][ep], d.cnt - ep * EPOCH)
                    seen_cnt[d.eng] = d.cnt
            bi = ins.fn(eobj)
            if ins.is_dma:
                bi.then_inc(dma_sems[ins.key], ins.inc)
            elif ins.sig:
                ep = (ins.cnt - 1) // EPOCH
                bi.then_inc(eng_sems[eng][ep], 1)


class SBAlloc:
    BASE = 16512
    LIMIT = 229376 - 64

    def __init__(self, nc):
        self.nc = nc
        self.off = self.BASE
        self.peak = self.off

    def alloc(self, name, shape, dtype):
        isz = 2 if dtype == BF16 else 4
        size = int(np.prod(shape[1:])) * isz
        off = (self.off + 63) // 64 * 64
        assert off + size <= self.LIMIT, f"SBUF overflow allocating {name}: {off + size}"
        t = self.nc.alloc_sbuf_tensor_at(name, list(shape), dtype, offset=off)
        self.off = off + size
        self.peak = max(self.peak, self.off)
        return t

    def mark(self):
        return self.off

    def release(self, m):
        self.off = m


class PsTile:
    def __init__(self, mgr, bank, gen):
        self.mgr = mgr
        self.bank = bank
        self.gen = gen

    @property
    def buf(self):
        assert self.mgr.gen[self.bank] == self.gen, "PSUM tile used after its bank was re-allocated"
        return self.mgr.bufs[self.bank]

    @property
    def t(self):
        return self.mgr.tens[self.bank]

    def f32(self, cols=512):
        return self.t[:, 0:cols]

    def v3(self, a, b):
        return self.t[:, 0:a * b].rearrange("p (a b) -> p a b", a=a)

    def bf(self):
        return self.t[:, :].bitcast(BF16)

    def bf3(self, a, b):
        return self.bf()[:, 0:a * b].rearrange("p (a b) -> p a b", a=a)


class PsMgr:
    def __init__(self, nc=None, parent=None, banks=None):
        if parent is None:
            self.tens = [nc.alloc_psum_tensor(f"psb{i}", [P, 512], F32) for i in range(8)]
            self.bufs = [Buf(f"psb{i}") for i in range(8)]
            self.gen = [0] * 8
        else:
            self.tens, self.bufs, self.gen = parent.tens, parent.bufs, parent.gen
        self.banks = list(banks) if banks is not None else list(range(8))
        self.nxt = 0
        self.reserved = set()

    def alloc(self):
        for _ in range(len(self.banks)):
            b = self.banks[self.nxt]
            self.nxt = (self.nxt + 1) % len(self.banks)
            if b not in self.reserved:
                self.gen[b] += 1
                return PsTile(self, b, self.gen[b])
        raise RuntimeError("no PSUM bank")

    def reserve(self):
        t = self.alloc()
        self.reserved.add(t.bank)
        return t

    def unreserve(self, t):
        self.reserved.discard(t.bank)


def cst_layout(nl):
    lay = {}
    off = 0

    def add(name, n):
        nonlocal off
        lay[name] = (off, n)
        off += n

    add("eps", 1)
    add("one", 1)
    add("flag", 1)
    add("acoef", 4)
    add("bcoef", 4)
    add("gr128", 4)
    add("gr16", 4)
    add("drt", 16)
    for l in range(nl):
        add(f"pmn{l}", 8)
        add(f"pon{l}", 8)
        add(f"pfn{l}", 8)
        add(f"pofn{l}", 8)
        add(f"retnw{l}", 4)
        add(f"glanw{l}", 4)
        add(f"convw{l}", 44 * 3)
        add(f"convb{l}", 44)
    add("xiq", 512)
    add("kinv", 512)
    add("mt", 128)
    return lay, off


class Builder:
    def __init__(self, nl, dbg=None, final_layer=True):
        self.nl = nl
        self.final_layer = final_layer
        self.dbg = dbg
        self.nc = nc = bass.Bass("TRN2", target_bir_lowering=False)
        self.pg = Prog()
        self.sb = SBAlloc(nc)
        self.ps = PsMgr(nc)
        self.psA = PsMgr(parent=self.ps, banks=[0, 1])
        self.psB = PsMgr(parent=self.ps, banks=[2, 3])
        self.psO = [PsMgr(parent=self.ps, banks=[4, 5]), PsMgr(parent=self.ps, banks=[6, 7])]
        self.lay, self.ncst = cst_layout(nl)
        self.d_x = nc.dram_tensor("xT", [P, KC * NLOC], F32, kind="ExternalInput").ap()
        self.d_cst = nc.dram_tensor("cst", [P, self.ncst], F32, kind="ExternalInput").ap()
        self.d_w2b = nc.dram_tensor("w2b", [17, nl * 256], F32, kind="ExternalInput").ap()
        self.d_cb = nc.dram_tensor("cb", [P, 256], F32, kind="ExternalInput").ap()
        self.d_rope = nc.dram_tensor("rope", [P, 2 * NLOC], F32, kind="ExternalInput").ap()
        self.d_win = nc.dram_tensor("win", [nl, P, KC * INW], F32, kind="ExternalInput").ap()
        self.d_wout = nc.dram_tensor("wout", [nl, P, KC * D], F32, kind="ExternalInput").ap()
        self.d_wup = nc.dram_tensor("wup", [nl * 11, P, KC * 512], F32, kind="ExternalInput").ap()
        self.d_wdn = nc.dram_tensor("wdn", [nl * 8, P, NFC * 128], F32, kind="ExternalInput").ap()
        self.d_y = nc.dram_tensor("y", [P, KC * NLOC], F32, kind="ExternalOutput").ap()
        self.cc1_src = [nc.dram_tensor(f"cc1s{l}", [P, 776], F32) for l in range(nl)]
        self.cc1_dst = [nc.dram_tensor(f"cc1d{l}", [4 * P, 776], F32) for l in range(nl)]
        self.cc2_src = [nc.dram_tensor(f"cc2s{l}", [P, 16], F32) for l in range(nl)]
        self.cc2_dst = [nc.dram_tensor(f"cc2d{l}", [4 * P, 16], F32) for l in range(nl)]
        if dbg:
            self.d_dbg = nc.dram_tensor("dbg", [P, dbg], F32, kind="ExternalOutput").ap()

    def pe(self, fn, R, W):
        return self.pg.emit("pe", fn, R, W)

    def act(self, fn, R, W):
        return self.pg.emit("act", fn, R, W)

    def dve(self, fn, R, W):
        return self.pg.emit("dve", fn, R, W)

    def pool(self, fn, R, W):
        return self.pg.emit("pool", fn, R, W)

    def dma(self, q, fn, R, W, key, serialize=True, inc=16):
        return self.pg.emit(q, fn, R, W, dma_key=key, serialize=serialize, inc=inc)

    def c(self, name, i=0, n=1):
        o, _ = self.lay[name]
        return self.cst[:, o + i:o + i + n]

    def build(self):
        nc, sb = self.nc, self.sb
        nl = self.nl
        self.xT = sb.alloc("xT", [P, KC, NLOC], F32)
        self.mt_tiles = [(0, NPRE)] + [(NPRE + TT * i, TT) for i in range(NSEQ // TT)]
        self.XB = [Buf(f"X{i}") for i in range(len(self.mt_tiles))]
        self.cst = sb.alloc("cst", [P, self.ncst], F32)
        self.B_cst = Buf("cst")
        self.w2b = sb.alloc("w2b", [32, nl * 256], F32)
        self.cb = sb.alloc("cb", [P, 256], BF16)
        self.ident = self.cb[:, 0:128]
        self.ones = self.cb[:, 128:256]
        self.pay = sb.alloc("pay", [P, 776], F32)
        self.S_r = self.pay[:, 0:512].rearrange("p (h v) -> p h v", h=4)
        self.S_g = self.pay[:, 512:768].rearrange("p (h v) -> p h v", h=2)
        self.Dacc = self.pay[:, 768:770]
        self.B_Sr, self.B_Sg, self.B_D = Buf("S_r"), Buf("S_g"), Buf("Dacc")
        self.Sb_r = sb.alloc("Sb_r", [P, 4, 128], BF16)
        self.Sb_g = sb.alloc("Sb_g", [P, 2, 128], BF16)
        self.B_Sbr, self.B_Sbg = Buf("Sb_r"), Buf("Sb_g")
        self.hal = sb.alloc("hal", [P, 16], F32)
        self.B_hal = Buf("hal")
        self.gh = sb.alloc("gh", [P, 4, 16], F32)
        self.B_gh = Buf("gh")
        self.rt = sb.alloc("rt", [P, 512], F32)
        self.rstd = sb.alloc("rstd", [P, 512], F32)
        self.B_rt, self.B_rstd = Buf("rt"), Buf("rstd")
        self.tmpx, self.B_tmpx = self.rt, self.B_rt
        self.sc0 = (self.rt, self.B_rt, self.rstd, self.B_rstd)
        base_mark = sb.mark()

        self.dma("sp", lambda e: e.dma_start(out=self.cst[:, :], in_=self.d_cst[:, :]), [], [self.B_cst], "c0")
        self.dma("sp", lambda e: e.dma_start(out=self.w2b[0:17, :], in_=self.d_w2b[:, :]), [], [self.B_cst], "c1")
        self.dma("pool", lambda e: e.dma_start(out=self.cb[:, :], in_=self.d_cb[:, :]), [], [self.B_cst], "c2")
        xflat = self.xT[:, :, :].rearrange("p k t -> p (k t)")
        self.dma("sp", lambda e: e.dma_start(out=xflat, in_=self.d_x[:, :]), [], self.XB, "x")

        import os
        for l in range(nl):
            if int(os.environ.get("KSTAGE", "99")) >= 1:
                self.layer(l, base_mark)

        self.dma("sp", lambda e: e.dma_start(out=self.d_y[:, :], in_=xflat), self.XB, [], "y")
        self.B_fin = Buf("fin")
        fin_reads = []
        b = Buf("finy")
        b.lw = self.pg.key_last["y"]
        fin_reads.append(b)
        if self.dbg:
            b = Buf("findbg")
            if "dbg" in self.pg.key_last:
                b.lw = self.pg.key_last["dbg"]
                fin_reads.append(b)
        self.pg.emit("sp", lambda e: e.nop(), fin_reads, [self.B_fin])
        return self.finish()

    def finish(self):
        nc, pg = self.nc, self.pg
        counts = pg.finalize()
        self.counts = counts
        n_ep = {e: max(1, (counts[e] + EPOCH - 1) // EPOCH) for e in pg.ENGS}
        keys = sorted(pg.key_cnt.keys())
        import contextlib
        with contextlib.ExitStack() as st:
            eng_sems = {e: [st.enter_context(nc.semaphore(f"s_{e}{i}")) for i in range(n_ep[e])] for e in pg.ENGS}
            dma_sems = {k: st.enter_context(nc.semaphore(f"d_{k}")) for k in keys}
            block = st.enter_context(nc.Block())

            @block.tensor
            def _(e):
                pg.replay("pe", e, eng_sems, dma_sems)

            @block.scalar
            def _(e):
                pg.replay("act", e, eng_sems, dma_sems)

            @block.vector
            def _(e):
                pg.replay("dve", e, eng_sems, dma_sems)

            @block.gpsimd
            def _(e):
                pg.replay("pool", e, eng_sems, dma_sems)

            @block.sync
            def _(e):
                pg.replay("sp", e, eng_sems, dma_sems)
        return nc

    def dbg_dump(self, ap2d, bufs, col0, ncols, parts=P):
        if not self.dbg:
            return
        self.dma("sp", lambda e: e.dma_start(out=self.d_dbg[0:parts, col0:col0 + ncols], in_=ap2d), bufs, [], "dbg")

    def rmsnorm(self, src3, src_bufs, wname, dst3, dst_bufs, n, sq3, B_sq, ps=None, sc=None):
        ps = ps or self.ps
        sc = sc or self.sc0
        self.act(lambda e: e.activation(out=sq3[:, :, 0:n], in_=src3, func=AF.Square), src_bufs, [B_sq])
        pt = ps.alloc()
        for kc in range(KC):
            self.pe(lambda e, kc=kc, pt=pt: e.matmul(pt.f32(n), lhsT=self.ones, rhs=sq3[:, kc, 0:n],
                                                     start=(kc == 0), stop=(kc == KC - 1)),
                    [B_sq, self.B_cst], [pt.buf])
        self.rstd_from(pt, n, 1.0 / D, sc)
        rstd, B_rstd = sc[2], sc[3]
        if wname is None:
            rstd_bc = rstd[:, 0:n].unsqueeze(1).broadcast_to([P, KC, n])
            self.dve(lambda e: e.tensor_tensor(out=dst3[:, :, 0:n], in0=src3, in1=rstd_bc, op=ALU.mult),
                     src_bufs + [B_rstd], dst_bufs)
            return
        for kc in range(KC):
            self.dve(lambda e, kc=kc: e.scalar_tensor_tensor(out=dst3[:, kc, 0:n], in0=src3[:, kc, :],
                                                             scalar=self.c(wname, kc), in1=rstd[:, 0:n],
                                                             op0=ALU.mult, op1=ALU.mult),
                     src_bufs + [B_rstd, self.B_cst], dst_bufs)

    def rstd_from(self, pt, n, scale, sc=None):
        rt, B_rt, rstd, B_rstd = sc or self.sc0
        self.act(lambda e, pt=pt: e.activation(out=rstd[:, 0:n], in_=pt.f32(n), func=AF.Ln,
                                               scale=scale, bias=self.c("eps")),
                 [pt.buf, self.B_cst], [B_rstd])
        self.act(lambda e: e.activation(out=rstd[:, 0:n], in_=rstd[:, 0:n], func=AF.Exp, scale=-0.5), [B_rstd], [B_rstd])

    def post_norm_residual(self, m_sb3, B_m, sq3, B_sq, wname, t0, n, xbufs):
        pt = self.ps.alloc()
        for kc in range(KC):
            self.pe(lambda e, kc=kc, pt=pt: e.matmul(pt.f32(n), lhsT=self.ones, rhs=sq3[:, kc, 0:n],
                                                     start=(kc == 0), stop=(kc == KC - 1)),
                    [B_sq, self.B_cst], [pt.buf])
        self.rstd_from(pt, n, 1.0 / D)
        for kc in range(KC):
            self.dve(lambda e, kc=kc: e.scalar_tensor_tensor(out=self.tmpx[:, 0:n], in0=m_sb3[:, kc, 0:n],
                                                             scalar=self.c(wname, kc), in1=self.rstd[:, 0:n],
                                                             op0=ALU.mult, op1=ALU.mult),
                     [B_m, self.B_rstd, self.B_cst], [self.B_tmpx])
            self.dve(lambda e, kc=kc: e.tensor_tensor(out=self.xT[:, kc, t0:t0 + n], in0=self.xT[:, kc, t0:t0 + n],
                                                      in1=self.tmpx[:, 0:n], op=ALU.add),
                     xbufs + [self.B_tmpx], xbufs)

    def layer(self, l, base_mark):
        sb = self.sb
        sb.release(base_mark)
        if l == 0:
            self.win = sb.alloc("win", [P, KC, INW], BF16)
            self.wout = sb.alloc("wout", [P, KC, D], BF16)
            self.B_win = [Buf(f"win{k}") for k in range(KC)]
            self.B_winq = [Buf(f"winq{k}") for k in range(KC)]
            self.B_win2 = [Buf(f"win2{k}") for k in range(KC)]
            self.B_winq2 = [Buf(f"winq2{k}") for k in range(KC)]
            self.B_wout = [Buf(f"wout{k}") for k in range(KC)]
            self.alloc_mixer()
        def wdma(kc, c0, c1, bw, key):
            self.dma("pool", lambda e: e.dma_start(out=self.win[:, kc, c0:c1], in_=self.d_win[l, :, kc * INW + c0:kc * INW + c1]),
                     [], [bw], key)
        for kc in range(KC):
            wdma(kc, C_RK, C_RG, self.B_win[kc], f"winA{kc}")
            wdma(kc, C_GK, INW, self.B_win2[kc], f"winC{kc}")
        later = []
        for kc in range(KC):
            later.append(lambda kc=kc: wdma(kc, C_RQ, C_RK, self.B_winq[kc], f"winB{kc}"))
            later.append(lambda kc=kc: wdma(kc, C_RG, C_GK, self.B_winq2[kc], f"winD{kc}"))
        for kc in range(KC):
            later.append(lambda kc=kc: self.dma("pool", lambda e: e.dma_start(out=self.wout[:, kc, :], in_=self.d_wout[l, :, kc * D:(kc + 1) * D]),
                                                [], [self.B_wout[kc]], f"wout{kc}"))
        def fold(kc, c0, c1):
            bw = self.win_buf(kc, c0)
            self.dve(lambda e: e.tensor_scalar(out=self.win[:, kc, c0:c1], in0=self.win[:, kc, c0:c1], scalar1=self.c(f"pmn{l}", kc),
                                               scalar2=None, op0=ALU.mult), [bw, self.B_cst], [bw])
        for kc in range(KC):
            fold(kc, C_RK, C_RG)
            fold(kc, C_GK, INW)
        self.deferred_folds = later + [(lambda kc=kc, c0=c0, c1=c1: fold(kc, c0, c1)) for kc in range(KC) for (c0, c1) in ((C_RQ, C_RK), (C_RG, C_GK))]
        self.dve(lambda e: e.memset(self.pay[:, :], 0.0), [], [self.B_Sr, self.B_Sg, self.B_D])
        self.dve(lambda e: e.memset(self.Dacc, 1.0), [], [self.B_D])
        for S in self.sets:
            self.dve(lambda e, S=S: e.memset(S.gaT[:, :], 1.0), [], [S.B_ga])
        self.dve(lambda e: e.memset(self.kz[:, :, :], 0.0), [], [self.B_kz])
        import os
        STAGE = int(os.environ.get("KSTAGE", "99"))
        if STAGE < 3:
            return
        self.mixer_pass(l, False)
        while self.deferred_folds:
            self.deferred_folds.pop(0)()
        wno, _ = self.lay[f"retnw{l}"]
        wn_bc = self.cst[:, wno:wno + 8].unsqueeze(2).broadcast_to([P, KC, D])
        self.dve(lambda e: e.tensor_tensor(out=self.wout[:, :, :], in0=self.wout[:, :, :], in1=wn_bc, op=ALU.mult),
                 self.B_wout + [self.B_cst], self.B_wout)
        if STAGE < 4:
            return

        def mid():
            self.exchange_state(l)
            self.pg.barrier()
            self.dve(lambda e: e.memset(self.qz[:, :, :], 0.0), [], [self.B_qz])
        self.mixer_pass(l, True, pre=2, mid_hook=mid)
        if STAGE < 6:
            return
        halo_keys = self.halo_send(l)
        self.pg.barrier(exclude=halo_keys)
        sb.release(base_mark)
        self.ffn(l)
        self.dve(lambda e: e.tensor_scalar(out=self.xT[:, :, 0:NPRE], in0=self.xT[:, :, 0:NPRE], scalar1=self.c("flag"),
                                           scalar2=None, op0=ALU.mult), [self.XB[0], self.B_cst], [self.XB[0]])
        self.pg.barrier()

    def mixer_pass(self, l, with_out, pre=0, mid_hook=None):
        import os
        NT = len(self.mt_tiles)
        REP = [int(v) for v in os.environ.get("KREP", "1,1,1").split(",")]
        K1 = int(os.environ.get("KK1", "6"))
        a_gen, a_idx, a_cnt = None, -1, 0
        b1_gen, b1_idx = None, -1
        b2_gen, b2_idx = None, -1
        a_ready = [False] * NT
        b1_done = [False] * NT
        b2_done = [False] * NT

        def done2(i):
            return i < 0 or b2_done[i]
        for i in range(pre):
            for _ in self.gen_A(l, i, with_out, self.sets[i % 2]):
                pass
            a_ready[i] = True
            a_idx = i
        if mid_hook is not None:
            mid_hook()
        while True:
            if a_gen is None and a_idx + 1 < NT and done2(a_idx + 1 - 2):
                a_idx, a_cnt = a_idx + 1, 0
                a_gen = self.gen_A(l, a_idx, with_out, self.sets[a_idx % 2])
            if b1_gen is None and b1_idx + 1 < NT and a_ready[b1_idx + 1] and done2(b1_idx + 1 - 2):
                b1_idx += 1
                b1_gen = self.gen_B(l, b1_idx, with_out, self.sets[b1_idx % 2])
            if b2_gen is None and b2_idx + 1 < NT and b1_done[b2_idx + 1]:
                b2_idx += 1
                b2_gen = self.gen_B2(l, b2_idx, with_out, self.sets[b2_idx % 2])
            if a_gen is None and b1_gen is None and b2_gen is None:
                if b2_idx + 1 >= NT:
                    break
                raise RuntimeError("mixer pipeline stalled")
            for _rep in range(REP[0]):
                if a_gen is not None:
                    try:
                        next(a_gen)
                        if not with_out and getattr(self, "deferred_folds", None):
                            self.deferred_folds.pop(0)()
                        a_cnt += 1
                        if a_cnt >= K1:
                            a_ready[a_idx] = True
                    except StopIteration:
                        a_ready[a_idx] = True
                        a_gen = None
            for _rep in range(REP[1]):
                if b1_gen is not None:
                    try:
                        next(b1_gen)
                    except StopIteration:
                        b1_done[b1_idx] = True
                        b1_gen = None
            for _rep in range(REP[2]):
                if b2_gen is not None:
                    try:
                        next(b2_gen)
                    except StopIteration:
                        b2_done[b2_idx] = True
                        b2_gen = None

    def alloc_mixer(self):
        sb = self.sb
        A = sb.alloc

        class NS:
            pass
        self.sets = []
        for i in range(2):
            S = NS()
            S.ropeT = A(f"ropeT{i}", [P, 2, TT], F32)
            S.hT = A(f"hT{i}", [P, KC, TT], BF16)
            S.kr = A(f"kr{i}", [P, 4, TT], BF16)
            S.kg = A(f"kg{i}", [P, 2, TT], F32)
            S.gaT = A(f"gaT{i}", [32, TT], F32)
            S.vt = A(f"vt{i}", [P, 1024], BF16)
            for nm in ("rope", "hT", "kr", "qr", "kg", "qg", "gr", "ga", "vt", "mT"):
                setattr(S, "B_" + nm, Buf(f"{nm}{i}"))
            self.sets.append(S)
        self.sqn = A("sqn", [P, KC, TT], BF16)
        self.B_sqn = Buf("sqn")
        self.rp1 = A("rp1", [P, 4, TT], F32)
        self.rp2 = A("rp2", [P, 4, TT], F32)
        self.B_rp1, self.B_rp2 = Buf("rp1"), Buf("rp2")
        rstdA = A("rstdA", [P, TT], F32)
        B_rstdA = Buf("rstdA")
        self.scA = (rstdA, B_rstdA, rstdA, B_rstdA)
        self.ez = A("ez", [P, 256], F32)
        self.lsp = A("lsp", [P, 256], F32)
        self.B_ez, self.B_lsp = Buf("ez"), Buf("lsp")
        self.E1 = A("E1", [P, 2, CH], F32)
        self.E2 = A("E2", [P, 2, CH], F32)
        self.B_E1, self.B_E2 = Buf("E1"), Buf("E2")
        self.kz = A("kz", [P, 4, CH], BF16)
        self.B_qz, self.B_kz = Buf("qz"), Buf("kz")
        self.st = A("st", [P, 64], F32)
        self.B_st = Buf("st")
        self.dprime = A("dprime", [P, 8], F32)
        self.B_dprime = Buf("dprime")
        rstdB = A("rstdB", [P, TT], F32)
        B_rstdB = Buf("rstdB")
        self.scB = (rstdB, B_rstdB, rstdB, B_rstdB)
        for i, S in enumerate(self.sets):
            S.qr = A(f"qr{i}", [P, 4, TT], BF16)
            S.qg = A(f"qg{i}", [P, 2, TT], F32)
            S.gr = A(f"gr{i}", [P, 8, TT], BF16)
        m = sb.mark()
        self.gath = A("gath", [P, 4, 776], F32)
        self.B_gath = Buf("gath")
        self.ubuf = A("ubuf", [P, 768], F32)
        self.B_ubuf = Buf("ubuf")
        e1 = sb.mark()
        sb.release(m)
        self.ktok = A("ktok", [P, 8, 128], BF16)
        self.B_ktok = Buf("ktok")
        self.tmpS = A("tmpS", [P, 4, 128], F32)
        self.B_tmpS = Buf("tmpS")
        for i, S in enumerate(self.sets):
            S.mT = A(f"mT{i}", [P, KC, TT], BF16)
        self.qz = A("qz", [P, 4, CH], BF16)
        self.A_r = A("A_r", [P, 4, CH], BF16)
        self.A_g = A("A_g", [P, 4, CH], BF16)
        self.B_Ar, self.B_Ag = Buf("A_r"), Buf("A_g")
        self.sqo = A("sqo", [P, 8, 128], BF16)
        self.B_sqo = Buf("sqo")
        self.tmp4 = self.sqo[:, :, :].rearrange("p a b -> p (a b)").bitcast(F32).rearrange("p (a b) -> p a b", a=4)
        self.on = A("on", [P, 8, 128], BF16)
        self.B_on = Buf("on")
        self.sqm = A("sqm", [P, KC, TT], BF16)
        self.B_sqm = Buf("sqm")
        sb.release(max(sb.mark(), e1))

    def win_buf(self, kc, col):
        if col < C_RK:
            return self.B_winq[kc]
        if col < C_RG:
            return self.B_win[kc]
        if col < C_GK:
            return self.B_winq2[kc]
        return self.B_win2[kc]

    def fm_proj(self, S, cols, m, n):
        pt = self.psA.alloc()
        v = pt.v3(4, TT)
        for j, col0 in enumerate(cols):
            for kc in range(KC):
                bw = self.win_buf(kc, col0)
                self.pe(lambda e, kc=kc, j=j, col0=col0: e.matmul(v[0:m, j, 0:n], lhsT=self.win[:, kc, col0:col0 + m],
                                                                  rhs=S.hT[:, kc, 0:n], start=(kc == 0), stop=(kc == KC - 1)),
                        [bw, S.B_hT], [pt.buf])
        return pt, v

    def rope_evac(self, S, pt, v, n, dst3, B_dst, dec_name):
        c_bc = S.ropeT[:, 0, 0:n].unsqueeze(1).broadcast_to([P, 4, n])
        self.dve(lambda e: e.tensor_tensor(out=self.rp1[:, :, 0:n], in0=v[:, :, 0:n], in1=c_bc, op=ALU.mult),
                 [pt.buf, S.B_rope], [self.B_rp1])
        for lo, hi in ((0, 64), (64, 0)):
            s_bc = S.ropeT[lo:lo + 64, 1, 0:n].unsqueeze(1).broadcast_to([64, 4, n])
            self.dve(lambda e, lo=lo, hi=hi, s_bc=s_bc: e.tensor_tensor(out=self.rp2[lo:lo + 64, :, 0:n], in0=v[hi:hi + 64, :, 0:n],
                                                                       in1=s_bc, op=ALU.mult),
                     [pt.buf, S.B_rope], [self.B_rp2])
        self.dve(lambda e: e.tensor_tensor(out=self.rp1[:, :, 0:n], in0=self.rp1[:, :, 0:n], in1=self.rp2[:, :, 0:n], op=ALU.add),
                 [self.B_rp1, self.B_rp2], [self.B_rp1])
        o, _ = self.lay[dec_name]
        dec = self.cst[:, o:o + 512].rearrange("p (h t) -> p h t", h=4)[:, :, 0:n]
        self.dve(lambda e: e.tensor_tensor(out=dst3, in0=self.rp1[:, :, 0:n], in1=dec, op=ALU.mult),
                 [self.B_rp1, self.B_cst], [B_dst])

    def gen_A(self, l, ti, with_out, S):
        t0, n = self.mt_tiles[ti]
        XB = [self.XB[ti]]
        self.dma("sp", lambda e: e.dma_start(out=S.ropeT[:, 0, 0:n], in_=self.d_rope[:, t0:t0 + n]),
                 [], [S.B_rope], "rope0")
        self.dma("sp", lambda e: e.dma_start(out=S.ropeT[:, 1, 0:n], in_=self.d_rope[:, NLOC + t0:NLOC + t0 + n]),
                 [], [S.B_rope], "rope1")
        self.rmsnorm(self.xT[:, :, t0:t0 + n], XB, None, S.hT, [S.B_hT], n, self.sqn, self.B_sqn, ps=self.psA, sc=self.scA)
        yield
        pt, v = self.fm_proj(S, [C_GA], 16, n)
        self.act(lambda e, v=v: e.activation(out=S.gaT[0:16, 0:n], in_=v[0:16, 0, 0:n], func=AF.Copy),
                 [pt.buf], [S.B_ga])
        yield
        pt, v = self.fm_proj(S, [C_GK, C_GK + 128], 128, n)
        self.act(lambda e, v=v: e.activation(out=S.kg[:, :, 0:n], in_=v[:, 0:2, 0:n], func=AF.Copy), [pt.buf], [S.B_kg])
        yield
        pt, v = self.fm_proj(S, [C_RK + h * 128 for h in range(4)], 128, n)
        self.rope_evac(S, pt, v, n, S.kr[:, :, 0:n], S.B_kr, "kinv")
        yield
        for half, col in ((0, C_RV), (1, C_GV)):
            pt = self.psA.alloc()
            for kc in range(KC):
                self.pe(lambda e, kc=kc, pt=pt, col=col: e.matmul(
                    pt.t[0:n, 0:512], lhsT=S.hT[:, kc, 0:n], rhs=self.win[:, kc, col:col + 512],
                    start=(kc == 0), stop=(kc == KC - 1)), [S.B_hT, self.win_buf(kc, col)], [pt.buf])
            self.act(lambda e, pt=pt, half=half: e.activation(
                out=S.vt[0:n, half * 512:(half + 1) * 512], in_=pt.t[0:n, 0:512], func=AF.Copy),
                [pt.buf], [S.B_vt])
            yield
        if with_out:
            pt, v = self.fm_proj(S, [C_RQ + h * 128 for h in range(4)], 128, n)
            self.rope_evac(S, pt, v, n, S.qr[:, :, 0:n], S.B_qr, "xiq")
            yield
            pt, v = self.fm_proj(S, [C_GQ, C_GQ + 128], 128, n)
            self.act(lambda e, v=v: e.mul(out=S.qg[:, :, 0:n], in_=v[:, 0:2, 0:n], mul=0.125), [pt.buf], [S.B_qg])
            yield
            for g4 in range(2):
                base = C_RG if g4 == 0 else C_GG
                pt, v = self.fm_proj(S, [base + h * 128 for h in range(4)], 128, n)
                self.act(lambda e, v=v, g4=g4: e.activation(out=S.gr[:, 4 * g4:4 * g4 + 4, 0:n], in_=v[:, :, 0:n], func=AF.Silu),
                         [pt.buf], [S.B_gr])
            yield

    def gen_B(self, l, ti, with_out, S):
        import os
        SUB = int(os.environ.get("KM2SUB", "99")) if with_out else 99
        if SUB < 2:
            return
        t0, cn = self.mt_tiles[ti]
        is_pre = (ti == 0)
        ps = self.psB
        Bv = S.B_vt
        pz = ps.alloc()
        self.pe(lambda e: e.matmul(pz.t[0:cn, 0:256], lhsT=S.gaT[0:17, 0:cn], rhs=self.w2b[0:17, l * 256:(l + 1) * 256],
                                   start=True, stop=True), [S.B_ga, self.B_cst], [pz.buf])
        self.act(lambda e: e.activation(out=self.ez[0:cn, :], in_=pz.t[0:cn, 0:256], func=AF.Exp, scale=-1.0),
                 [pz.buf], [self.B_ez])
        self.act(lambda e: e.activation(out=self.lsp[0:cn, :], in_=self.ez[0:cn, :], func=AF.Ln, bias=self.c("one")[0:cn, :]),
                 [self.B_ez, self.B_cst], [self.B_lsp])
        if is_pre:
            self.dve(lambda e: e.tensor_scalar(out=self.lsp[0:cn, :], in0=self.lsp[0:cn, :], scalar1=self.c("flag")[0:cn, :],
                                               scalar2=None, op0=ALU.mult), [self.B_lsp, self.B_cst], [self.B_lsp])
        yield
        mt = self.c("mt", 0, 128)
        pc3 = pz.t[:, 256:512].rearrange("p (a b) -> p a b", a=2)
        for hp in range(2):
            self.pe(lambda e, hp=hp: e.matmul(pc3[:, hp, 0:cn], lhsT=self.lsp[0:cn, hp * 128:(hp + 1) * 128], rhs=mt[0:cn, 0:cn],
                                              start=True, stop=True), [self.B_lsp, self.B_cst], [pz.buf])
        self.act(lambda e: e.activation(out=self.E2[:, :, 0:cn], in_=pc3[:, :, 0:cn], func=AF.Exp, scale=1.0 / GLA_TAU),
                 [pz.buf], [self.B_E2])
        self.act(lambda e: e.activation(out=self.E1[:, :, 0:cn], in_=pc3[:, :, 0:cn], func=AF.Exp, scale=-1.0 / GLA_TAU),
                 [pz.buf], [self.B_E1])
        yield
        kz4 = self.kz[:, :, :].rearrange("p (a b) t -> p a b t", b=2)
        for half in range(2):
            lo = 64 * half
            self.dve(lambda e, lo=lo, half=half: e.tensor_tensor(out=kz4[lo:lo + 64, :, half, 0:cn], in0=S.kg[lo:lo + 64, :, 0:cn],
                                                                 in1=self.E2[lo:lo + 64, :, 0:cn], op=ALU.mult),
                     [S.B_kg, self.B_E2], [self.B_kz])
        if with_out:
            qz4 = self.qz[:, :, :].rearrange("p (a b) t -> p a b t", b=2)
            for half in range(2):
                lo = 64 * half
                self.dve(lambda e, lo=lo, half=half: e.tensor_tensor(out=qz4[lo:lo + 64, :, half, 0:cn], in0=S.qg[lo:lo + 64, :, 0:cn],
                                                                     in1=self.E1[lo:lo + 64, :, 0:cn], op=ALU.mult),
                         [S.B_qg, self.B_E1], [self.B_qz])
        yield
        if with_out:
            pa = ps.alloc()
            pa3 = pa.v3(4, CH)
            for h in range(4):
                self.pe(lambda e, h=h: e.matmul(pa3[0:cn, h, 0:cn], lhsT=S.kr[:, h, 0:cn], rhs=S.qr[:, h, 0:cn],
                                                start=True, stop=True), [S.B_kr, S.B_qr], [pa.buf])
            mask = mt[0:cn, 0:cn].unsqueeze(1).broadcast_to([cn, 4, cn])
            self.dve(lambda e: e.tensor_tensor(out=self.A_r[0:cn, :, 0:cn], in0=pa3[0:cn, :, 0:cn], in1=mask, op=ALU.mult),
                     [pa.buf, self.B_cst], [self.B_Ar])
            yield
            pg_ = ps.alloc()
            pg3 = pg_.v3(4, CH)
            for h in range(4):
                self.pe(lambda e, h=h: e.matmul(pg3[0:cn, h, 0:cn], lhsT=self.kz[:, h, 0:cn], rhs=self.qz[:, h, 0:cn],
                                                start=True, stop=True), [self.B_kz, self.B_qz], [pg_.buf])
            self.dve(lambda e: e.tensor_tensor(out=self.A_g[0:cn, :, 0:cn], in0=pg3[0:cn, :, 0:cn], in1=mask, op=ALU.mult),
                     [pg_.buf, self.B_cst], [self.B_Ag])
            yield
            po_r = self.psO[ti % 2].alloc()
            por3 = po_r.v3(4, 128)
            for h in range(4):
                self.pe(lambda e, h=h: e.matmul(por3[0:cn, h, :], lhsT=self.A_r[0:cn, h, 0:cn], rhs=S.vt[0:cn, h * 128:(h + 1) * 128],
                                                start=True, stop=False), [self.B_Ar, Bv], [po_r.buf])
                self.pe(lambda e, h=h: e.matmul(por3[0:cn, h, :], lhsT=S.qr[:, h, 0:cn], rhs=self.Sb_r[:, h, :],
                                                start=False, stop=True), [S.B_qr, self.B_Sbr], [po_r.buf])
            yield
            po_g = self.psO[ti % 2].alloc()
            pog3 = po_g.v3(4, 128)
            S.po_r, S.po_g = po_r, po_g
            for h in range(4):
                self.pe(lambda e, h=h: e.matmul(pog3[0:cn, h, :], lhsT=self.A_g[0:cn, h, 0:cn],
                                                rhs=S.vt[0:cn, 512 + h * 128:512 + (h + 1) * 128],
                                                start=True, stop=False), [self.B_Ag, Bv], [po_g.buf])
                self.pe(lambda e, h=h: e.matmul(pog3[0:cn, h, :], lhsT=self.qz[:, h, 0:cn],
                                                rhs=self.Sb_g[:, h // 2, :], start=False, stop=True),
                        [self.B_qz, self.B_Sbg], [po_g.buf])
            yield
        pk = ps.alloc()
        pk3 = pk.bf3(8, 128)
        for h in range(4):
            self.pe(lambda e, h=h: e.transpose(pk3[0:cn, h, :], S.kr[:, h, 0:cn], self.ident), [S.B_kr, self.B_cst], [pk.buf])
        for h in range(4):
            self.pe(lambda e, h=h: e.transpose(pk3[0:cn, 4 + h, :], self.kz[:, h, 0:cn], self.ident), [self.B_kz, self.B_cst], [pk.buf])
        self.act(lambda e: e.activation(out=self.ktok[0:cn, :, :], in_=pk3[0:cn, :, :], func=AF.Copy), [pk.buf], [self.B_ktok])
        yield
        pkv = ps.alloc()
        pkv3 = pkv.v3(4, 128)
        for h in range(4):
            self.pe(lambda e, h=h: e.matmul(pkv3[:, h, :], lhsT=self.ktok[0:cn, h, :], rhs=S.vt[0:cn, h * 128:(h + 1) * 128],
                                            start=True, stop=True), [self.B_ktok, Bv], [pkv.buf])
        gname = "gr16" if is_pre else "gr128"
        go, _ = self.lay[gname]
        gbc = self.cst[:, go:go + 4].unsqueeze(2).broadcast_to([P, 4, 128])
        self.dve(lambda e: e.tensor_tensor(out=self.tmpS[:, 0:4, :], in0=self.S_r, in1=pkv3[:, :, :], op=ALU.add),
                 [self.B_Sr, pkv.buf], [self.B_tmpS])
        self.dve(lambda e: e.tensor_tensor(out=self.S_r, in0=self.tmpS[:, 0:4, :], in1=gbc, op=ALU.mult),
                 [self.B_tmpS, self.B_cst], [self.B_Sr])
        if with_out:
            self.act(lambda e: e.activation(out=self.Sb_r[:, :, :], in_=self.S_r, func=AF.Copy), [self.B_Sr], [self.B_Sbr])
        yield
        pkg = ps.alloc()
        pkg3 = pkg.v3(2, 128)
        for h in range(4):
            self.pe(lambda e, h=h: e.matmul(pkg3[:, h // 2, :], lhsT=self.ktok[0:cn, 4 + h, :],
                                            rhs=S.vt[0:cn, 512 + h * 128:512 + (h + 1) * 128],
                                            start=(h % 2 == 0), stop=(h % 2 == 1)), [self.B_ktok, Bv], [pkg.buf])
        self.dve(lambda e: e.tensor_tensor(out=self.tmpS[:, 0:2, :], in0=self.S_g, in1=pkg3[:, :, :], op=ALU.add),
                 [self.B_Sg, pkg.buf], [self.B_tmpS])
        for hp in range(2):
            self.act(lambda e, hp=hp: e.mul(out=self.S_g[:, hp, :], in_=self.tmpS[:, hp, :], mul=self.E1[:, hp, cn - 1:cn]),
                     [self.B_tmpS, self.B_E1], [self.B_Sg])
        if with_out:
            self.act(lambda e: e.activation(out=self.Sb_g[:, :, :], in_=self.S_g, func=AF.Copy), [self.B_Sg], [self.B_Sbg])
        else:
            self.dve(lambda e: e.tensor_tensor(out=self.Dacc, in0=self.Dacc, in1=self.E1[:, :, cn - 1], op=ALU.mult),
                     [self.B_D, self.B_E1], [self.B_D])
        yield
        return

    def gen_B2(self, l, ti, with_out, S):
        if not with_out:
            return
        SUB = 99
        t0, cn = self.mt_tiles[ti]
        ps = self.psA
        po_r, po_g = S.po_r, S.po_g
        por3, pog3 = po_r.v3(4, 128), po_g.v3(4, 128)
        st = self.st
        s1 = st[0:cn, 0:4]
        s2 = st[0:cn, 4:12]
        mean = st[0:cn, 12:16]
        msq = st[0:cn, 16:20]
        var = st[0:cn, 20:28]
        rtv = st[0:cn, 28:36]
        rsd = st[0:cn, 36:44]
        nmr = st[0:cn, 44:48]
        self.dve(lambda e: e.reduce_sum(out=s1, in_=por3[0:cn, :, :], axis=AX.X), [po_r.buf], [self.B_st])
        self.act(lambda e: e.activation(out=self.sqo[0:cn, 0:4, :], in_=por3[0:cn, :, :], func=AF.Square), [po_r.buf], [self.B_sqo])
        self.act(lambda e: e.activation(out=self.sqo[0:cn, 4:8, :], in_=pog3[0:cn, :, :], func=AF.Square), [po_g.buf], [self.B_sqo])
        yield
        self.dve(lambda e: e.reduce_sum(out=s2, in_=self.sqo[0:cn, :, :], axis=AX.X), [self.B_sqo], [self.B_st])
        self.dve(lambda e: e.tensor_tensor(out=msq, in0=s1, in1=s1, op=ALU.mult), [self.B_st], [self.B_st])
        self.dve(lambda e: e.scalar_tensor_tensor(out=s2[:, 0:4], in0=msq, scalar=-1.0 / 128, in1=s2[:, 0:4], op0=ALU.mult, op1=ALU.add),
                 [self.B_st], [self.B_st])
        yield
        self.act(lambda e: e.activation(out=rsd, in_=s2, func=AF.Ln, scale=1.0 / 128, bias=self.c("eps")[0:cn, :]), [self.B_st, self.B_cst], [self.B_st])
        self.act(lambda e: e.activation(out=rsd, in_=rsd, func=AF.Exp, scale=-0.5), [self.B_st], [self.B_st])
        self.dve(lambda e: e.tensor_scalar(out=mean, in0=s1, scalar1=1.0 / 128, scalar2=None, op0=ALU.mult), [self.B_st], [self.B_st])
        yield
        mean_bc = mean.unsqueeze(2).broadcast_to([cn, 4, 128])
        rsdr_bc = rsd[:, 0:4].unsqueeze(2).broadcast_to([cn, 4, 128])
        rsdg_bc = rsd[:, 4:8].unsqueeze(2).broadcast_to([cn, 4, 128])
        self.dve(lambda e: e.tensor_tensor(out=self.tmp4[0:cn, :, :], in0=por3[0:cn, :, :], in1=mean_bc, op=ALU.subtract),
                 [po_r.buf, self.B_st], [self.B_sqo])
        self.dve(lambda e: e.tensor_tensor(out=self.on[0:cn, 0:4, :], in0=self.tmp4[0:cn, :, :], in1=rsdr_bc, op=ALU.mult),
                 [self.B_sqo, self.B_st], [self.B_on])
        for h in range(4):
            self.act(lambda e, h=h: e.mul(out=self.on[0:cn, 4 + h, :], in_=pog3[0:cn, h, :], mul=rsd[:, 4 + h:5 + h]),
                     [po_g.buf, self.B_st], [self.B_on])
        yield
        if SUB < 4:
            return
        pT = ps.alloc()
        pT3 = pT.bf3(8, CH)
        for h in range(8):
            self.pe(lambda e, h=h: e.transpose(pT3[:, h, 0:cn], self.on[0:cn, h, :], self.ident[0:cn, 0:cn]),
                    [self.B_on, self.B_cst], [pT.buf])
        self.dve(lambda e: e.tensor_tensor(out=S.mT[:, :, 0:cn], in0=pT3[:, :, 0:cn], in1=S.gr[:, :, 0:cn], op=ALU.mult),
                 [pT.buf, S.B_gr], [S.B_mT])
        yield
        if SUB < 5:
            return
        n = cn
        pts = [po_r, po_g]
        for oc in range(KC):
            pt = pts[oc // 4]
            v = pt.v3(4, TT)[:, oc % 4, 0:n]
            for kc in range(KC):
                self.pe(lambda e, kc=kc, v=v, oc=oc: e.matmul(v, lhsT=self.wout[:, kc, oc * 128:(oc + 1) * 128],
                                                              rhs=S.mT[:, kc, 0:n], start=(kc == 0), stop=(kc == KC - 1)),
                        [self.B_wout[kc], S.B_mT], [pt.buf])
            if oc % 4 == 3:
                self.act(lambda e, pt=pt, oc=oc: e.activation(out=self.sqm[:, oc - 3:oc + 1, 0:n], in_=pt.v3(4, TT)[:, :, 0:n], func=AF.Square),
                         [pt.buf], [self.B_sqm])
                yield
        pss = ps.alloc()
        for kc in range(KC):
            self.pe(lambda e, kc=kc: e.matmul(pss.f32(n), lhsT=self.ones, rhs=self.sqm[:, kc, 0:n],
                                              start=(kc == 0), stop=(kc == KC - 1)), [self.B_sqm, self.B_cst], [pss.buf])
        self.rstd_from(pss, n, 1.0 / D, self.scB)
        rstd, B_rstd = self.scB[2], self.scB[3]
        yield
        if SUB < 6:
            return
        xb = [self.XB[ti]]
        pno, _ = self.lay[f"pon{l}"]
        rstd_bc = rstd[:, 0:n].unsqueeze(1).broadcast_to([P, 4, n])
        for b4 in range(2):
            pt = pts[b4]
            pw_bc = self.cst[:, pno + 4 * b4:pno + 4 * b4 + 4].unsqueeze(2).broadcast_to([P, 4, n])
            self.dve(lambda e, pt=pt: e.tensor_tensor(out=self.tmp4[:, :, 0:n], in0=pt.v3(4, TT)[:, :, 0:n], in1=rstd_bc, op=ALU.mult),
                     [pt.buf, B_rstd], [self.B_sqo])
            self.dve(lambda e, pw_bc=pw_bc: e.tensor_tensor(out=self.tmp4[:, :, 0:n], in0=self.tmp4[:, :, 0:n], in1=pw_bc, op=ALU.mult),
                     [self.B_sqo, self.B_cst], [self.B_sqo])
            self.dve(lambda e, b4=b4: e.tensor_tensor(out=self.xT[:, 4 * b4:4 * b4 + 4, t0:t0 + n], in0=self.xT[:, 4 * b4:4 * b4 + 4, t0:t0 + n],
                                                      in1=self.tmp4[:, :, 0:n], op=ALU.add), xb + [self.B_sqo], xb)
            yield

    def exchange_state(self, l):
        nc = self.nc
        src, dst = self.cc1_src[l], self.cc1_dst[l]
        B_src, B_dst = Buf("cc1src"), Buf("cc1dst")
        self.dma("pool", lambda e: e.dma_start(out=src.ap()[:, :], in_=self.pay[:, :]), [self.B_Sr, self.B_Sg, self.B_D], [B_src], f"cc1a{l}")
        self.dma("pool", lambda e: e.collective_compute("AllGather", ALU.bypass, replica_groups=[[0, 1, 2, 3], [4, 5, 6, 7]],
                                                        ins=[src.ap().opt()], outs=[dst.ap().opt()]),
                 [B_src], [B_dst], f"cc1b{l}", inc=1)
        self.dma("pool", lambda e: e.dma_start(out=self.gath[:, :, :], in_=dst.ap().rearrange("(r p) f -> p r f", p=P)),
                 [B_dst], [self.B_gath], f"cc1c{l}")
        self.dve(lambda e: e.memset(self.pay[:, :], 0.0), [], [self.B_Sr, self.B_Sg, self.B_D])
        dro, _ = self.lay["drt"]
        for i in range(3):
            a_i = self.c("acoef", i)
            self.dve(lambda e, i=i, a_i=a_i: e.tensor_scalar(out=self.dprime[:, 0:4], in0=self.cst[:, dro + 4 * i:dro + 4 * i + 4],
                                                              scalar1=-1.0, scalar2=a_i, op0=ALU.add, op1=ALU.mult),
                     [self.B_cst], [self.B_dprime])
            self.dve(lambda e, i=i, a_i=a_i: e.tensor_scalar(out=self.dprime[:, 4:6], in0=self.gath[:, i, 768:770],
                                                              scalar1=-1.0, scalar2=a_i, op0=ALU.add, op1=ALU.mult),
                     [self.B_gath, self.B_cst], [self.B_dprime])
            self.dve(lambda e: e.tensor_scalar(out=self.dprime[:, 0:6], in0=self.dprime[:, 0:6], scalar1=1.0, scalar2=None, op0=ALU.add),
                     [self.B_dprime], [self.B_dprime])
            self.dve(lambda e, i=i, a_i=a_i: e.tensor_scalar(out=self.ubuf[:, :], in0=self.gath[:, i, 0:768], scalar1=a_i, scalar2=None,
                                                              op0=ALU.mult), [self.B_gath, self.B_cst], [self.B_ubuf])
            dr_bc = self.dprime[:, 0:4].unsqueeze(2).broadcast_to([P, 4, 128])
            dg_bc = self.dprime[:, 4:6].unsqueeze(2).broadcast_to([P, 2, 128])
            self.dve(lambda e, dr_bc=dr_bc: e.tensor_tensor(out=self.S_r, in0=self.S_r, in1=dr_bc, op=ALU.mult),
                     [self.B_Sr, self.B_dprime], [self.B_Sr])
            self.dve(lambda e, dg_bc=dg_bc: e.tensor_tensor(out=self.S_g, in0=self.S_g, in1=dg_bc, op=ALU.mult),
                     [self.B_Sg, self.B_dprime], [self.B_Sg])
            self.dve(lambda e: e.tensor_tensor(out=self.pay[:, 0:768], in0=self.pay[:, 0:768], in1=self.ubuf[:, :], op=ALU.add),
                     [self.B_Sr, self.B_Sg, self.B_ubuf], [self.B_Sr, self.B_Sg])
        self.act(lambda e: e.activation(out=self.Sb_r[:, :, :], in_=self.S_r, func=AF.Copy), [self.B_Sr], [self.B_Sbr])
        self.act(lambda e: e.activation(out=self.Sb_g[:, :, :], in_=self.S_g, func=AF.Copy), [self.B_Sg], [self.B_Sbg])

    def halo_send(self, l):
        src, dst = self.cc2_src[l], self.cc2_dst[l]
        B_src, B_dst = Buf("cc2src"), Buf("cc2dst")
        last = self.XB[-1]
        self.dve(lambda e: e.tensor_copy(out=self.hal[:, :].rearrange("p (k t) -> p k t", k=KC), in_=self.xT[:, :, NLOC - 2:NLOC]),
                 [last], [self.B_hal])
        self.dma("sp", lambda e: e.dma_start(out=src.ap()[:, :], in_=self.hal[:, :]), [self.B_hal], [B_src], f"cc2a{l}")
        self.dma("pool", lambda e: e.collective_compute("AllGather", ALU.bypass, replica_groups=[[0, 1, 2, 3], [4, 5, 6, 7]],
                                                        ins=[src.ap().opt()], outs=[dst.ap().opt()]),
                 [B_src], [B_dst], f"cc2b{l}", inc=1)
        self.dma("sp", lambda e: e.dma_start(out=self.gh[:, :, :], in_=dst.ap().rearrange("(r p) f -> p r f", p=P)),
                 [B_dst], [self.B_gh], f"cc2c{l}")
        return {f"cc2a{l}", f"cc2b{l}", f"cc2c{l}"}

    def halo_recv(self, l):
        self.dve(lambda e: e.tensor_scalar(out=self.hal[:, :], in0=self.gh[:, 0, :], scalar1=self.c("bcoef", 0), scalar2=None, op0=ALU.mult),
                 [self.B_gh, self.B_cst], [self.B_hal])
        for i in range(1, 4):
            self.dve(lambda e, i=i: e.scalar_tensor_tensor(out=self.hal[:, :], in0=self.gh[:, i, :], scalar=self.c("bcoef", i),
                                                           in1=self.hal[:, :], op0=ALU.mult, op1=ALU.add),
                     [self.B_gh, self.B_cst, self.B_hal], [self.B_hal])
        self.dve(lambda e: e.tensor_tensor(out=self.xT[:, :, NPRE - 2:NPRE], in0=self.xT[:, :, NPRE - 2:NPRE],
                                           in1=self.hal[:, :].rearrange("p (k t) -> p k t", k=KC), op=ALU.add),
                 [self.XB[0], self.B_hal], [self.XB[0]])

    def ffn(self, l):
        sb = self.sb
        A = sb.alloc
        NH = 1042
        act_ = A("act", [P, NFC, 1040], BF16)
        B_act = [Buf(f"act{i}") for i in range(NFC)]
        wsl = [A(f"wsl{i}", [P, 4096], BF16) for i in range(3)]
        B_wsl = [Buf(f"wsl{i}") for i in range(3)]
        uhalo = A("uhalo", [P, 44, 2], F32)
        B_uhalo = Buf("uhalo")
        m_c = sb.mark()
        gl0_ = A("gl0", [P, 512], F32)
        gl = [gl0_, gl0_]
        B_gl0_ = Buf("gl0")
        B_gl = [B_gl0_, B_gl0_]
        ca = [A(f"ca{i}", [P, 512], F32) for i in range(2)]
        cg = [A(f"cg{i}", [P, 512], F32) for i in range(2)]
        B_ca, B_cg = [Buf("ca0"), Buf("ca1")], [Buf("cg0"), Buf("cg1")]
        m_c2 = sb.mark()
        sb.release(m_c)
        cB = A("cB", [P, 44, 16], F32)
        tB = A("tB", [P, 44, 16], F32)
        glB = A("glB", [P, NFC, 16], F32)
        assert sb.mark() <= m_c2
        sb.release(m_c2)
        G_c = [B_gl0_, B_ca[0], B_ca[1], B_cg[0], B_cg[1]]
        m_u = sb.mark()
        sqf = A("sqf", [P, 2, 512], BF16)
        B_sqf = [Buf("sqf0"), Buf("sqf1")]
        tmp2 = [self.tmpx, A("tmpx2", [P, 512], F32)]
        B_tmp2 = [self.B_tmpx, Buf("tmpx2")]
        m_u2 = sb.mark()
        sb.release(m_u)
        upre = A("upre", [P, 44, 16], F32)
        assert sb.mark() <= m_u2
        sb.release(m_u2)
        G_u = [B_sqf[0], B_sqf[1], B_tmp2[1]]
        m1 = sb.mark()
        h2T = A("h2T", [P, KC, NH], BF16)
        sqn = A("sqn2", [P, KC, 512], BF16)
        ua = [A(f"ua{i}", [P, NH], F32) for i in range(2)]
        ug = [A(f"ug{i}", [P, NH], F32) for i in range(2)]
        sb.release(m1)
        f_sb = A("f_sb", [P, KC, 1040], F32)
        wo, _ = self.lay[f"convw{l}"]
        bo, _ = self.lay[f"convb{l}"]

        halves = [
            [(2, 16, [0], 0), (18, 512, [1, 2, 3, 4], 16), (530, 512, [5, 6, 7, 8], 528)],
            [(2, 512, [9, 10, 11, 12], 1040), (514, 512, [13, 14, 15, 16], 1552)],
        ]
        wcnt = [0]

        def next_slot():
            i = wcnt[0] % 3
            wcnt[0] += 1
            return i

        B_h2T, B_sqn = Buf("h2T"), Buf("sqn2")
        B_ua, B_ug = [Buf("ua0"), Buf("ua1")], [Buf("ug0"), Buf("ug1")]
        G_fsb = [B_h2T, B_sqn] + B_ua + B_ug
        last_layer = (l == self.nl - 1) and self.final_layer
        for hi, tiles in enumerate(halves):
            order = [t for t in tiles if t[2] != [0]] + [t for t in tiles if t[2] == [0]]
            for (co, n, xt, t0) in order:
                if xt == [0]:
                    self.halo_recv(l)
                self.rmsnorm(self.xT[:, :, t0:t0 + n], [self.XB[i] for i in xt], f"pfn{l}", h2T[:, :, co:co + n], [B_h2T], n, sqn, B_sqn)
            if hi == 0:
                for k in range(2):
                    self.dve(lambda e, k=k: e.memset(ua[k][:, 0:2], 0.0), [], [B_ua[k]])
                    self.dve(lambda e, k=k: e.memset(ug[k][:, 0:2], 0.0), [], [B_ug[k]])
            for j in range(11):
                si = next_slot()
                w = wsl[si]
                self.dma("pool", lambda e, j=j, w=w: e.dma_start(out=w[:, :], in_=self.d_wup[l * 11 + j, :, :]), [], [B_wsl[si]], f"wsl{si}")
                w3 = w[:, :].rearrange("p (k c) -> p k c", k=KC)
                for q in range(2):
                    i = 2 * j + q
                    k = i % 2
                    uab, ugb = ua[k], ug[k]
                    if hi == 1:
                        self.dve(lambda e, i=i, uab=uab: e.tensor_copy(out=uab[:, 0:2], in_=uhalo[:, i, :]), [B_uhalo], [B_ua[k]])
                        self.dve(lambda e, i=i, ugb=ugb: e.tensor_copy(out=ugb[:, 0:2], in_=uhalo[:, NFC + i, :]), [B_uhalo], [B_ug[k]])
                    for ti_, (co, n, xt, t0) in enumerate(tiles):
                        pa = self.ps.alloc()
                        pgt = self.ps.alloc()
                        for kc in range(KC):
                            self.pe(lambda e, kc=kc, pa=pa, co=co, n=n, q=q, w3=w3: e.matmul(
                                pa.f32(n), lhsT=w3[:, kc, q * 128:(q + 1) * 128], rhs=h2T[:, kc, co:co + n],
                                start=(kc == 0), stop=(kc == KC - 1)), [B_wsl[si], B_h2T], [pa.buf])
                        for kc in range(KC):
                            self.pe(lambda e, kc=kc, pgt=pgt, co=co, n=n, q=q, w3=w3: e.matmul(
                                pgt.f32(n), lhsT=w3[:, kc, 256 + q * 128:256 + (q + 1) * 128], rhs=h2T[:, kc, co:co + n],
                                start=(kc == 0), stop=(kc == KC - 1)), [B_wsl[si], B_h2T], [pgt.buf])
                        if xt == [0] and not last_layer:
                            self.act(lambda e, pa=pa, co=co, n=n, uab=uab: e.activation(out=uab[:, co:co + n], in_=pa.f32(n), func=AF.Copy), [pa.buf], [B_ua[k]])
                            self.act(lambda e, pgt=pgt, co=co, n=n, ugb=ugb: e.activation(out=ugb[:, co:co + n], in_=pgt.f32(n), func=AF.Copy), [pgt.buf], [B_ug[k]])
                            self.act(lambda e, pa=pa, i=i, n=n: e.activation(out=upre[:, i, 0:n], in_=pa.f32(n), func=AF.Copy), [pa.buf], G_u)
                            self.act(lambda e, pgt=pgt, i=i, n=n: e.activation(out=upre[:, NFC + i, 0:n], in_=pgt.f32(n), func=AF.Copy), [pgt.buf], G_u)
                            continue
                        if last_layer and xt == [0]:
                            self.act(lambda e, pa=pa, co=co, n=n, uab=uab: e.activation(out=uab[:, co:co + n], in_=pa.f32(n), func=AF.Copy), [pa.buf], [B_ua[k]])
                            self.act(lambda e, pgt=pgt, co=co, n=n, ugb=ugb: e.activation(out=ugb[:, co:co + n], in_=pgt.f32(n), func=AF.Copy), [pgt.buf], [B_ug[k]])
                            continue
                        gi = (i * len(tiles) + ti_) % 2
                        cab, cgb = ca[gi], cg[gi]
                        for (pt_, u, B_u, cdst, B_c, ch, second) in ((pa, uab, B_ua[k], cab, B_ca[gi], i, "dve"),
                                                                     (pgt, ugb, B_ug[k], cgb, B_cg[gi], NFC + i, "dve")):
                            w0 = self.cst[:, wo + ch * 3 + 0:wo + ch * 3 + 1]
                            w1 = self.cst[:, wo + ch * 3 + 1:wo + ch * 3 + 2]
                            w2 = self.cst[:, wo + ch * 3 + 2:wo + ch * 3 + 3]
                            bb = self.cst[:, bo + ch:bo + ch + 1]
                            self.act(lambda e, pt_=pt_, u=u, co=co, n=n: e.activation(out=u[:, co:co + n], in_=pt_.f32(n), func=AF.Copy),
                                     [pt_.buf], [B_u])
                            self.act(lambda e, pt_=pt_, cdst=cdst, n=n, w2=w2, bb=bb: e.activation(
                                out=cdst[:, 0:n], in_=pt_.f32(n), func=AF.Identity, scale=w2, bias=bb), [pt_.buf, self.B_cst], [B_c])
                            self.dve(lambda e, u=u, cdst=cdst, co=co, n=n, w1=w1: e.scalar_tensor_tensor(
                                out=cdst[:, 0:n], in0=u[:, co - 1:co - 1 + n], scalar=w1, in1=cdst[:, 0:n], op0=ALU.mult, op1=ALU.add),
                                [B_u, self.B_cst, B_c], [B_c])
                            self.pg.emit(second, lambda e, u=u, cdst=cdst, co=co, n=n, w0=w0: e.scalar_tensor_tensor(
                                out=cdst[:, 0:n], in0=u[:, co - 2:co - 2 + n], scalar=w0, in1=cdst[:, 0:n], op0=ALU.mult, op1=ALU.add),
                                [B_u, self.B_cst, B_c], [B_c])
                        self.act(lambda e, n=n, gi=gi, cab=cab: e.activation(out=gl[gi][:, 0:n], in_=cab[:, 0:n], func=AF.Gelu_apprx_tanh),
                                 [B_ca[gi]], [B_gl[gi]])
                        ac0 = co - 2
                        import os
                        self.pg.emit(os.environ.get("KGATE", "dve"), lambda e, i=i, ac0=ac0, n=n, gi=gi, cgb=cgb: e.tensor_tensor(
                            out=act_[:, i, ac0:ac0 + n], in0=gl[gi][:, 0:n], in1=cgb[:, 0:n], op=ALU.mult),
                            [B_gl[gi], B_cg[gi]], [B_act[i]])
                    if hi == 0:
                        self.dve(lambda e, i=i, uab=uab: e.tensor_copy(out=uhalo[:, i, :], in_=uab[:, NH - 2:NH]), [B_ua[k]], [B_uhalo])
                        self.dve(lambda e, i=i, ugb=ugb: e.tensor_copy(out=uhalo[:, NFC + i, :], in_=ugb[:, NH - 2:NH]), [B_ug[k]], [B_uhalo])
            if hi == 0 and not last_layer:
                wv = self.cst[:, wo:wo + 132].rearrange("p (c k) -> p c k", k=3)

                def wbc(tap, ncol):
                    return wv[:, :, tap].unsqueeze(2).broadcast_to([P, 44, ncol])
                b_bc = self.cst[:, bo:bo + 44].unsqueeze(2).broadcast_to([P, 44, 16])
                self.dve(lambda e: e.tensor_tensor(out=cB[:, :, :], in0=upre[:, :, :], in1=wbc(2, 16), op=ALU.mult), G_u + [self.B_cst], G_c)
                self.dve(lambda e: e.tensor_tensor(out=tB[:, :, 1:16], in0=upre[:, :, 0:15], in1=wbc(1, 15), op=ALU.mult), G_u + [self.B_cst], G_c)
                self.dve(lambda e: e.tensor_tensor(out=cB[:, :, 1:16], in0=cB[:, :, 1:16], in1=tB[:, :, 1:16], op=ALU.add), G_c, G_c)
                self.dve(lambda e: e.tensor_tensor(out=tB[:, :, 2:16], in0=upre[:, :, 0:14], in1=wbc(0, 14), op=ALU.mult), G_u + [self.B_cst], G_c)
                self.dve(lambda e: e.tensor_tensor(out=cB[:, :, 2:16], in0=cB[:, :, 2:16], in1=tB[:, :, 2:16], op=ALU.add), G_c, G_c)
                self.dve(lambda e: e.tensor_tensor(out=cB[:, :, :], in0=cB[:, :, :], in1=b_bc, op=ALU.add), G_c + [self.B_cst], G_c)
                self.act(lambda e: e.activation(out=glB[:, :, :], in_=cB[:, 0:NFC, :], func=AF.Gelu_apprx_tanh), G_c, G_c)
                self.dve(lambda e: e.tensor_tensor(out=act_[:, :, 0:16], in0=glB[:, :, :], in1=cB[:, NFC:2 * NFC, :], op=ALU.mult), G_c, B_act)
            if last_layer:
                tiles = [t for t in tiles if t[2] != [0]]
            sst = [self.ps.reserve() for _ in tiles]
            pend = None
            for oc in range(KC):
                si = next_slot()
                w = wsl[si]
                self.dma("pool", lambda e, oc=oc, w=w: e.dma_start(out=w[:, 0:NFC * 128], in_=self.d_wdn[l * 8 + oc, :, :]), [], [B_wsl[si]], f"wsl{si}")
                w3 = w[:, 0:NFC * 128].rearrange("p (k c) -> p k c", k=NFC)
                for ri, (co, n, xt, t0) in enumerate(tiles):
                    ac0 = co - 2
                    pt = self.ps.alloc()
                    for kc in range(NFC):
                        self.pe(lambda e, kc=kc, pt=pt, ac0=ac0, n=n, w3=w3: e.matmul(
                            pt.f32(n), lhsT=w3[:, kc, :], rhs=act_[:, kc, ac0:ac0 + n], start=(kc == 0), stop=(kc == NFC - 1)),
                            [B_wsl[si], B_act[kc]], [pt.buf])
                    sq_i = (oc * len(tiles) + ri) % 2
                    self.act(lambda e, pt=pt, oc=oc, ac0=ac0, n=n: e.mul(out=f_sb[:, oc, ac0:ac0 + n], in_=pt.f32(n), mul=self.c(f"pofn{l}", oc)),
                             [pt.buf, self.B_cst], G_fsb)
                    self.act(lambda e, pt=pt, sq_i=sq_i, n=n: e.activation(out=sqf[:, sq_i, 0:n], in_=pt.f32(n), func=AF.Square),
                             [pt.buf], [B_sqf[sq_i]])
                    if pend is not None:
                        self.pe(*pend)
                    pend = (lambda e, st_=sst[ri], sq_i=sq_i, n=n, oc=oc: e.matmul(st_.f32(n), lhsT=self.ones, rhs=sqf[:, sq_i, 0:n],
                                                                                    start=(oc == 0), stop=(oc == KC - 1)),
                            [B_sqf[sq_i], self.B_cst], [sst[ri].buf])
            if pend is not None:
                self.pe(*pend)
            for ri, (co, n, xt, t0) in enumerate(tiles):
                ac0 = co - 2
                self.rstd_from(sst[ri], n, 1.0 / D)
                xb = [self.XB[i] for i in xt]
                for kc in range(KC):
                    tk = kc % 2
                    self.dve(lambda e, kc=kc, ac0=ac0, n=n, tk=tk: e.tensor_tensor(
                        out=tmp2[tk][:, 0:n], in0=f_sb[:, kc, ac0:ac0 + n], in1=self.rstd[:, 0:n], op=ALU.mult),
                        G_fsb + [self.B_rstd], [B_tmp2[tk]])
                    self.pg.emit("pool" if kc % 2 == 0 else "dve",
                                 lambda e, kc=kc, t0=t0, n=n, tk=tk: e.tensor_tensor(out=self.xT[:, kc, t0:t0 + n], in0=self.xT[:, kc, t0:t0 + n],
                                                                                     in1=tmp2[tk][:, 0:n], op=ALU.add), xb + [B_tmp2[tk]], xb)
            for t in sst:
                self.ps.unreserve(t)


def _img(w):
    K, N = w.shape
    return np.ascontiguousarray(w.reshape(K // P, P, N).transpose(1, 0, 2).reshape(P, (K // P) * N))


def _host_consts(r, nl, layers, prm):
    lay, ncst = cst_layout(nl)
    cst = np.zeros((P, ncst), np.float32)

    def put(name, arr):
        o, n = lay[name]
        arr = np.asarray(arr, np.float32)
        cst[:, o:o + n] = arr.reshape(-1, n) if arr.ndim > 1 else arr[None, :]

    put("eps", [EPS])
    put("one", [1.0])
    put("flag", [1.0 if r == 0 else 0.0])
    put("acoef", [1.0 if i < r else 0.0 for i in range(4)])
    put("bcoef", [1.0 if i == r - 1 else 0.0 for i in range(4)])
    hh = np.arange(4, dtype=np.float64)
    log_g = np.log(1.0 - 2.0 ** (-5.0 - hh))
    put("gr128", np.exp(log_g * 128))
    put("gr16", np.exp(log_g * 16) if r == 0 else np.ones(4))
    drt = np.zeros((4, 4))
    for i in range(4):
        drt[i] = np.exp(log_g * (NLOC if i == 0 else NSEQ))
    put("drt", drt.reshape(-1))
    tt = np.arange(128, dtype=np.float64)
    put("xiq", np.exp(log_g[:, None] * (tt[None, :] + 1.0)).reshape(-1))
    put("kinv", (np.exp(-log_g[:, None] * (tt[None, :] + 1.0)) * (128.0 ** -0.5)).reshape(-1))
    mt = (np.arange(128)[None, :] >= np.arange(128)[:, None]).astype(np.float32)
    o, n = lay["mt"]
    cst[:, o:o + n] = mt
    for li, l in enumerate(layers):
        def fm(v):
            return np.asarray(v, np.float32).reshape(KC, P).T
        for nm, key in (("pmn", "pre_mix_norm"), ("pon", "post_mix_norm"), ("pfn", "pre_ffn_norm"), ("pofn", "post_ffn_norm")):
            o, n = lay[f"{nm}{li}"]
            cst[:, o:o + n] = fm(prm[key][l])
        o, n = lay[f"retnw{li}"]
        cst[:, o:o + n] = np.asarray(prm["ret_norm_w"][l], np.float32).reshape(4, P).T
        o, n = lay[f"glanw{li}"]
        cst[:, o:o + n] = np.asarray(prm["gla_norm_w"][l], np.float32).reshape(4, P).T
        cw = np.asarray(prm["ffn_conv_w"][l], np.float32)
        o, n = lay[f"convw{li}"]
        cst[:, o:o + n] = cw.reshape(3, 44, P).transpose(2, 1, 0).reshape(P, 44 * 3)
        cbv = np.asarray(prm["ffn_conv_b"][l], np.float32)
        o, n = lay[f"convb{li}"]
        cst[:, o:o + n] = cbv.reshape(44, P).T
    return cst


def _host_weights(layers, prm):
    nl = len(layers)
    win = np.stack([_img(np.asarray(prm["w_in"][l], np.float32)) for l in layers])
    wout = np.stack([_img(np.asarray(prm["w_out"][l], np.float32)) for l in layers])
    wup = np.zeros((nl * 11, P, KC * 512), np.float32)
    wdn = np.zeros((nl * 8, P, NFC * 128), np.float32)
    for li, l in enumerate(layers):
        up = np.asarray(prm["ffn_up"][l], np.float32)
        dn = np.asarray(prm["ffn_down"][l], np.float32)
        for j in range(11):
            cols = np.concatenate([np.arange(128 * (2 * j), 128 * (2 * j + 2)), DFF + np.arange(128 * (2 * j), 128 * (2 * j + 2))])
            wup[li * 11 + j] = _img(up[:, cols])
        for oc in range(8):
            wdn[li * 8 + oc] = _img(dn[:, oc * 128:(oc + 1) * 128])
    w2b = np.zeros((17, nl * 256), np.float32)
    for li, l in enumerate(layers):
        w2b[0:16, li * 256:(li + 1) * 256] = np.asarray(prm["gla_gate_w2"][l], np.float32)
        w2b[16, li * 256:(li + 1) * 256] = np.asarray(prm["gla_gate_b"][l], np.float32)
    return win, wout, wup, wdn, w2b


def _host_rope(r):
    half = 64
    inv = 10000.0 ** (-np.arange(half, dtype=np.float64) / half)
    pos = float(NSEQ * r) + np.arange(NLOC, dtype=np.float64)
    ang = pos[:, None] * inv[None, :]
    c = np.cos(ang).astype(np.float32).T
    s = np.sin(ang).astype(np.float32).T
    cosT = np.concatenate([c, c], axis=0)
    sinT = np.concatenate([-s, s], axis=0)
    return np.ascontiguousarray(np.concatenate([cosT, sinT], axis=1))


_CACHE = {}


def _get_program(nl, dbg=None, final=True):
    key = (nl, dbg, final)
    if key not in _CACHE:
        b = Builder(nl, dbg, final_layer=final)
        _CACHE[key] = (b.build(), b)
    return _CACHE[key]


def _run(xT_imgs, layers, prm, dbg=None):
    nl = len(layers)
    nc, b = _get_program(nl, dbg, final=(layers[-1] == prm["w_in"].shape[0] - 1))
    win, wout, wup, wdn, w2b = _host_weights(layers, prm)
    cbm = np.concatenate([np.eye(P, dtype=np.float32), np.ones((P, P), np.float32)], axis=1)
    in_maps = []
    for j in range(8):
        r = j % 4
        in_maps.append({
            "xT": xT_imgs[j], "cst": _host_consts(r, nl, layers, prm), "w2b": w2b, "cb": cbm, "rope": _host_rope(r),
            "win": win, "wout": wout, "wup": wup, "wdn": wdn,
        })
    res = run_bass_kernel_spmd(nc, in_maps, core_ids=list(range(8)))
    return res


LAUNCH_SPLIT = False


def kernel(x, meta_tokens, pre_mix_norm, w_in, gla_gate_w2, gla_gate_b, ret_norm_w, gla_norm_w,
           w_out, post_mix_norm, pre_ffn_norm, ffn_up, ffn_conv_w, ffn_conv_b, ffn_down, post_ffn_norm):
    prm = dict(pre_mix_norm=pre_mix_norm, w_in=w_in, gla_gate_w2=gla_gate_w2, gla_gate_b=gla_gate_b,
               ret_norm_w=ret_norm_w, gla_norm_w=gla_norm_w, w_out=w_out, post_mix_norm=post_mix_norm,
               pre_ffn_norm=pre_ffn_norm, ffn_up=ffn_up, ffn_conv_w=ffn_conv_w, ffn_conv_b=ffn_conv_b,
               ffn_down=ffn_down, post_ffn_norm=post_ffn_norm)
    prm = {k: np.asarray(v) for k, v in prm.items()}
    x = np.asarray(x, np.float32)
    meta = np.asarray(meta_tokens, np.float32)
    imgs = []
    for j in range(8):
        b, r = j // 4, j % 4
        xin = np.zeros((NLOC, D), np.float32)
        if r == 0:
            xin[0:NPRE] = meta
        xin[NPRE:] = x[b, NSEQ * r:NSEQ * (r + 1)]
        imgs.append(np.ascontiguousarray(xin.T.reshape(KC, P, NLOC).transpose(1, 0, 2).reshape(P, KC * NLOC)))
    nl_total = prm["w_in"].shape[0]
    if LAUNCH_SPLIT:
        for l in range(nl_total):
            res = _run(imgs, [l], prm)
            imgs = [np.ascontiguousarray(res.results[j]["y"]) for j in range(8)]
        outs = imgs
    else:
        res = _run(imgs, list(range(nl_total)), prm)
        outs = [res.results[j]["y"] for j in range(8)]
    out = np.zeros((2, 4 * NSEQ, D), np.float32)
    for j in range(8):
        b, r = j // 4, j % 4
        y = np.asarray(outs[j]).reshape(P, KC, NLOC)[:, :, NPRE:]
        out[b, NSEQ * r:NSEQ * (r + 1)] = y.transpose(2, 1, 0).reshape(NSEQ, D)
    return out
```

```python
import numpy as np
import concourse.bass as bass
import concourse.mybir as mybir
from concourse.bass_utils import run_bass_kernel_spmd

F32 = mybir.dt.float32
BF16 = mybir.dt.bfloat16
AF = mybir.ActivationFunctionType
ALU = mybir.AluOpType
AX = mybir.AxisListType

P = 128
D = 1024
KC = 8
NPRE = 16
NSEQ = 2048
NLOC = NPRE + NSEQ
TT = 128
CH = 128
INW = 3600
DFF = 2816
NFC = 22
EPS = 1e-6
GLA_TAU = 16.0
EPOCH = 6000
import os as _os
CHAIN_ON_DVE = _os.environ.get("KCHAIN", "dve") == "dve"

C_RQ, C_RK, C_RV, C_RG, C_GQ, C_GK, C_GV, C_GG, C_GA = 0, 512, 1024, 1536, 2048, 2304, 2560, 3072, 3584


class Buf:
    __slots__ = ("name", "lw", "rd", "rd_dma")

    def __init__(self, name):
        self.name = name
        self.lw = None
        self.rd = {}
        self.rd_dma = []


class Ins:
    __slots__ = ("eng", "fn", "deps", "sig", "cnt", "is_dma", "key", "val", "inc")

    def __init__(self, eng, fn):
        self.eng = eng
        self.fn = fn
        self.deps = []
        self.sig = False
        self.cnt = 0
        self.is_dma = False
        self.key = None
        self.val = 0
        self.inc = 16


class Prog:
    ENGS = ("pe", "act", "dve", "pool", "sp")

    def __init__(self):
        self.streams = {e: [] for e in self.ENGS}
        self.key_cnt = {}
        self.key_last = {}
        self.bar = {}

    def barrier(self, exclude=()):
        deps = []
        for e in self.ENGS:
            for ins in reversed(self.streams[e]):
                if not ins.is_dma:
                    deps.append(ins)
                    break
        deps += [v for k, v in self.key_last.items() if k not in exclude]
        for e in self.ENGS:
            self.bar[e] = list(deps) + self.bar.get(e, [])

    def emit(self, eng, fn, reads=(), writes=(), dma_key=None, serialize=True, inc=16):
        ins = Ins(eng, fn)
        raw = set()
        oth = set()
        for b in reads:
            if b.lw is not None:
                raw.add(b.lw)
        for b in writes:
            if b.lw is not None:
                oth.add(b.lw)
            for r in b.rd.values():
                oth.add(r)
            for r in b.rd_dma:
                oth.add(r)
        if dma_key is not None:
            ins.is_dma = True
            ins.key = dma_key
            ins.inc = inc
            self.key_cnt[dma_key] = self.key_cnt.get(dma_key, 0) + inc
            ins.val = self.key_cnt[dma_key]
            if serialize and dma_key in self.key_last:
                oth.add(self.key_last[dma_key])
            self.key_last[dma_key] = ins
        deps = []
        for d in self.bar.pop(eng, []):
            if d.is_dma or d.eng != eng:
                if d not in raw and d not in oth:
                    deps.append(d)
        for d in raw | oth:
            if d is ins:
                continue
            if d.is_dma or ins.is_dma:
                deps.append(d)
            elif d.eng != eng:
                deps.append(d)
            else:
                if eng != "pe":
                    deps.append(d)
        for d in deps:
            if not d.is_dma:
                d.sig = True
        ins.deps = deps
        for b in reads:
            if ins.is_dma:
                b.rd_dma.append(ins)
            else:
                b.rd[eng] = ins
        for b in writes:
            b.lw = ins
            b.rd = {}
            b.rd_dma = []
        self.streams[eng].append(ins)
        return ins

    def finalize(self):
        for e in self.ENGS:
            c = 0
            for ins in self.streams[e]:
                if ins.is_dma:
                    continue
                if ins.sig:
                    c += 1
                    ins.cnt = c
        return {e: sum(1 for i in self.streams[e] if i.sig and not i.is_dma) for e in self.ENGS}

    def replay(self, eng, eobj, eng_sems, dma_sems):
        seen_cnt = {e: 0 for e in self.ENGS}
        seen_dma = {}
        for ins in self.streams[eng]:
            for d in ins.deps:
                if d.is_dma:
                    if seen_dma.get(d.key, 0) >= d.val:
                        continue
                    eobj.wait_ge(dma_sems[d.key], d.val)
                    seen_dma[d.key] = d.val
                else:
                    if seen_cnt[d.eng] >= d.cnt:
                        continue
                    ep = (d.cnt - 1) // EPOCH
                    eobj.wait_ge(eng_sems[d.eng][ep], d.cnt - ep * EPOCH)
                    seen_cnt[d.eng] = d.cnt
            bi = ins.fn(eobj)
            if ins.is_dma:
                bi.then_inc(dma_sems[ins.key], ins.inc)
            elif ins.sig:
                ep = (ins.cnt - 1) // EPOCH
                bi.then_inc(eng_sems[eng][ep], 1)


class SBAlloc:
    BASE = 16512
    LIMIT = 229376 - 64

    def __init__(self, nc):
        self.nc = nc
        self.off = self.BASE
        self.peak = self.off

    def alloc(self, name, shape, dtype):
        isz = 2 if dtype == BF16 else 4
        size = int(np.prod(shape[1:])) * isz
        off = (self.off + 63) // 64 * 64
        assert off + size <= self.LIMIT, f"SBUF overflow allocating {name}: {off + size}"
        t = self.nc.alloc_sbuf_tensor_at(name, list(shape), dtype, offset=off)
        self.off = off + size
        self.peak = max(self.peak, self.off)
        return t

    def mark(self):
        return self.off

    def release(self, m):
        self.off = m


class PsTile:
    def __init__(self, mgr, bank, gen):
        self.mgr = mgr
        self.bank = bank
        self.gen = gen

    @property
    def buf(self):
        assert self.mgr.gen[self.bank] == self.gen, "PSUM tile used after its bank was re-allocated"
        return self.mgr.bufs[self.bank]

    @property
    def t(self):
        return self.mgr.tens[self.bank]

    def f32(self, cols=512):
        return self.t[:, 0:cols]

    def v3(self, a, b):
        return self.t[:, 0:a * b].rearrange("p (a b) -> p a b", a=a)

    def bf(self):
        return self.t[:, :].bitcast(BF16)

    def bf3(self, a, b):
        return self.bf()[:, 0:a * b].rearrange("p (a b) -> p a b", a=a)


class PsMgr:
    def __init__(self, nc=None, parent=None, banks=None):
        if parent is None:
            self.tens = [nc.alloc_psum_tensor(f"psb{i}", [P, 512], F32) for i in range(8)]
            self.bufs = [Buf(f"psb{i}") for i in range(8)]
            self.gen = [0] * 8
        else:
            self.tens, self.bufs, self.gen = parent.tens, parent.bufs, parent.gen
        self.banks = list(banks) if banks is not None else list(range(8))
        self.nxt = 0
        self.reserved = set()

    def alloc(self):
        for _ in range(len(self.banks)):
            b = self.banks[self.nxt]
            self.nxt = (self.nxt + 1) % len(self.banks)
            if b not in self.reserved:
                self.gen[b] += 1
                return PsTile(self, b, self.gen[b])
        raise RuntimeError("no PSUM bank")

    def reserve(self):
        t = self.alloc()
        self.reserved.add(t.bank)
        return t

    def unreserve(self, t):
        self.reserved.discard(t.bank)


def cst_layout(nl):
    lay = {}
    off = 0

    def add(name, n):
        nonlocal off
        lay[name] = (off, n)
        off += n

    add("eps", 1)
    add("one", 1)
    add("flag", 1)
    add("acoef", 4)
    add("bcoef", 4)
    add("gr128", 4)
    add("gr16", 4)
    add("drt", 16)
    for l in range(nl):
        add(f"pmn{l}", 8)
        add(f"pon{l}", 8)
        add(f"pfn{l}", 8)
        add(f"pofn{l}", 8)
        add(f"retnw{l}", 4)
        add(f"glanw{l}", 4)
        add(f"convw{l}", 44 * 3)
        add(f"convb{l}", 44)
    add("xiq", 512)
    add("kinv", 512)
    add("mt", 128)
    return lay, off


class Builder:
    def __init__(self, nl, dbg=None, final_layer=True):
        self.nl = nl
        self.final_layer = final_layer
        self.dbg = dbg
        self.nc = nc = bass.Bass("TRN2", target_bir_lowering=False)
        self.pg = Prog()
        self.sb = SBAlloc(nc)
        self.ps = PsMgr(nc)
        self.psA = PsMgr(parent=self.ps, banks=[0, 1])
        self.psB = PsMgr(parent=self.ps, banks=[2, 3])
        self.psO = [PsMgr(parent=self.ps, banks=[4, 5]), PsMgr(parent=self.ps, banks=[6, 7])]
        self.lay, self.ncst = cst_layout(nl)
        self.d_x = nc.dram_tensor("xT", [P, KC * NLOC], F32, kind="ExternalInput").ap()
        self.d_cst = nc.dram_tensor("cst", [P, self.ncst], F32, kind="ExternalInput").ap()
        self.d_w2b = nc.dram_tensor("w2b", [17, nl * 256], F32, kind="ExternalInput").ap()
        self.d_cb = nc.dram_tensor("cb", [P, 256], F32, kind="ExternalInput").ap()
        self.d_rope = nc.dram_tensor("rope", [P, 2 * NLOC], F32, kind="ExternalInput").ap()
        self.d_win = nc.dram_tensor("win", [nl, P, KC * INW], F32, kind="ExternalInput").ap()
        self.d_wout = nc.dram_tensor("wout", [nl, P, KC * D], F32, kind="ExternalInput").ap()
        self.d_wup = nc.dram_tensor("wup", [nl * 11, P, KC * 512], F32, kind="ExternalInput").ap()
        self.d_wdn = nc.dram_tensor("wdn", [nl * 8, P, NFC * 128], F32, kind="ExternalInput").ap()
        self.d_y = nc.dram_tensor("y", [P, KC * NLOC], F32, kind="ExternalOutput").ap()
        self.cc1_src = [nc.dram_tensor(f"cc1s{l}", [P, 776], F32) for l in range(nl)]
        self.cc1_dst = [nc.dram_tensor(f"cc1d{l}", [4 * P, 776], F32) for l in range(nl)]
        self.cc2_src = [nc.dram_tensor(f"cc2s{l}", [P, 16], F32) for l in range(nl)]
        self.cc2_dst = [nc.dram_tensor(f"cc2d{l}", [4 * P, 16], F32) for l in range(nl)]
        if dbg:
            self.d_dbg = nc.dram_tensor("dbg", [P, dbg], F32, kind="ExternalOutput").ap()

    def pe(self, fn, R, W):
        return self.pg.emit("pe", fn, R, W)

    def act(self, fn, R, W):
        return self.pg.emit("act", fn, R, W)

    def dve(self, fn, R, W):
        return self.pg.emit("dve", fn, R, W)

    def pool(self, fn, R, W):
        return self.pg.emit("pool", fn, R, W)

    def dma(self, q, fn, R, W, key, serialize=True, inc=16):
        return self.pg.emit(q, fn, R, W, dma_key=key, serialize=serialize, inc=inc)

    def c(self, name, i=0, n=1):
        o, _ = self.lay[name]
        return self.cst[:, o + i:o + i + n]

    def build(self):
        nc, sb = self.nc, self.sb
        nl = self.nl
        self.xT = sb.alloc("xT", [P, KC, NLOC], F32)
        self.mt_tiles = [(0, NPRE)] + [(NPRE + TT * i, TT) for i in range(NSEQ // TT)]
        self.XB = [Buf(f"X{i}") for i in range(len(self.mt_tiles))]
        self.cst = sb.alloc("cst", [P, self.ncst], F32)
        self.B_cst = Buf("cst")
        self.w2b = sb.alloc("w2b", [32, nl * 256], F32)
        self.cb = sb.alloc("cb", [P, 256], BF16)
        self.ident = self.cb[:, 0:128]
        self.ones = self.cb[:, 128:256]
        self.pay = sb.alloc("pay", [P, 776], F32)
        self.S_r = self.pay[:, 0:512].rearrange("p (h v) -> p h v", h=4)
        self.S_g = self.pay[:, 512:768].rearrange("p (h v) -> p h v", h=2)
        self.Dacc = self.pay[:, 768:770]
        self.B_Sr, self.B_Sg, self.B_D = Buf("S_r"), Buf("S_g"), Buf("Dacc")
        self.Sb_r = sb.alloc("Sb_r", [P, 4, 128], BF16)
        self.Sb_g = sb.alloc("Sb_g", [P, 2, 128], BF16)
        self.B_Sbr, self.B_Sbg = Buf("Sb_r"), Buf("Sb_g")
        self.hal = sb.alloc("hal", [P, 16], F32)
        self.B_hal = Buf("hal")
        self.gh = sb.alloc("gh", [P, 4, 16], F32)
        self.B_gh = Buf("gh")
        self.rt = sb.alloc("rt", [P, 512], F32)
        self.rstd = sb.alloc("rstd", [P, 512], F32)
        self.B_rt, self.B_rstd = Buf("rt"), Buf("rstd")
        self.tmpx, self.B_tmpx = self.rt, self.B_rt
        self.sc0 = (self.rt, self.B_rt, self.rstd, self.B_rstd)
        base_mark = sb.mark()

        self.dma("sp", lambda e: e.dma_start(out=self.cst[:, :], in_=self.d_cst[:, :]), [], [self.B_cst], "c0")
        self.dma("sp", lambda e: e.dma_start(out=self.w2b[0:17, :], in_=self.d_w2b[:, :]), [], [self.B_cst], "c1")
        self.dma("pool", lambda e: e.dma_start(out=self.cb[:, :], in_=self.d_cb[:, :]), [], [self.B_cst], "c2")
        xflat = self.xT[:, :, :].rearrange("p k t -> p (k t)")
        self.dma("sp", lambda e: e.dma_start(out=xflat, in_=self.d_x[:, :]), [], self.XB, "x")

        import os
        for l in range(nl):
            if int(os.environ.get("KSTAGE", "99")) >= 1:
                self.layer(l, base_mark)

        self.dma("sp", lambda e: e.dma_start(out=self.d_y[:, :], in_=xflat), self.XB, [], "y")
        self.B_fin = Buf("fin")
        fin_reads = []
        b = Buf("finy")
        b.lw = self.pg.key_last["y"]
        fin_reads.append(b)
        if self.dbg:
            b = Buf("findbg")
            if "dbg" in self.pg.key_last:
                b.lw = self.pg.key_last["dbg"]
                fin_reads.append(b)
        self.pg.emit("sp", lambda e: e.nop(), fin_reads, [self.B_fin])
        return self.finish()

    def finish(self):
        nc, pg = self.nc, self.pg
        counts = pg.finalize()
        self.counts = counts
        n_ep = {e: max(1, (counts[e] + EPOCH - 1) // EPOCH) for e in pg.ENGS}
        keys = sorted(pg.key_cnt.keys())
        import contextlib
        with contextlib.ExitStack() as st:
            eng_sems = {e: [st.enter_context(nc.semaphore(f"s_{e}{i}")) for i in range(n_ep[e])] for e in pg.ENGS}
            dma_sems = {k: st.enter_context(nc.semaphore(f"d_{k}")) for k in keys}
            block = st.enter_context(nc.Block())

            @block.tensor
            def _(e):
                pg.replay("pe", e, eng_sems, dma_sems)

            @block.scalar
            def _(e):
                pg.replay("act", e, eng_sems, dma_sems)

            @block.vector
            def _(e):
                pg.replay("dve", e, eng_sems, dma_sems)

            @block.gpsimd
            def _(e):
                pg.replay("pool", e, eng_sems, dma_sems)

            @block.sync
            def _(e):
                pg.replay("sp", e, eng_sems, dma_sems)
        return nc

    def dbg_dump(self, ap2d, bufs, col0, ncols, parts=P):
        if not self.dbg:
            return
        self.dma("sp", lambda e: e.dma_start(out=self.d_dbg[0:parts, col0:col0 + ncols], in_=ap2d), bufs, [], "dbg")

    def rmsnorm(self, src3, src_bufs, wname, dst3, dst_bufs, n, sq3, B_sq, ps=None, sc=None):
        ps = ps or self.ps
        sc = sc or self.sc0
        self.act(lambda e: e.activation(out=sq3[:, :, 0:n], in_=src3, func=AF.Square), src_bufs, [B_sq])
        pt = ps.alloc()
        for kc in range(KC):
            self.pe(lambda e, kc=kc, pt=pt: e.matmul(pt.f32(n), lhsT=self.ones, rhs=sq3[:, kc, 0:n],
                                                     start=(kc == 0), stop=(kc == KC - 1)),
                    [B_sq, self.B_cst], [pt.buf])
        self.rstd_from(pt, n, 1.0 / D, sc)
        rstd, B_rstd = sc[2], sc[3]
        if wname is None:
            rstd_bc = rstd[:, 0:n].unsqueeze(1).broadcast_to([P, KC, n])
            self.dve(lambda e: e.tensor_tensor(out=dst3[:, :, 0:n], in0=src3, in1=rstd_bc, op=ALU.mult),
                     src_bufs + [B_rstd], dst_bufs)
            return
        for kc in range(KC):
            self.dve(lambda e, kc=kc: e.scalar_tensor_tensor(out=dst3[:, kc, 0:n], in0=src3[:, kc, :],
                                                             scalar=self.c(wname, kc), in1=rstd[:, 0:n],
                                                             op0=ALU.mult, op1=ALU.mult),
                     src_bufs + [B_rstd, self.B_cst], dst_bufs)

    def rstd_from(self, pt, n, scale, sc=None):
        rt, B_rt, rstd, B_rstd = sc or self.sc0
        self.act(lambda e, pt=pt: e.activation(out=rstd[:, 0:n], in_=pt.f32(n), func=AF.Ln,
                                               scale=scale, bias=self.c("eps")),
                 [pt.buf, self.B_cst], [B_rstd])
        self.act(lambda e: e.activation(out=rstd[:, 0:n], in_=rstd[:, 0:n], func=AF.Exp, scale=-0.5), [B_rstd], [B_rstd])

    def post_norm_residual(self, m_sb3, B_m, sq3, B_sq, wname, t0, n, xbufs):
        pt = self.ps.alloc()
        for kc in range(KC):
            self.pe(lambda e, kc=kc, pt=pt: e.matmul(pt.f32(n), lhsT=self.ones, rhs=sq3[:, kc, 0:n],
                                                     start=(kc == 0), stop=(kc == KC - 1)),
                    [B_sq, self.B_cst], [pt.buf])
        self.rstd_from(pt, n, 1.0 / D)
        for kc in range(KC):
            self.dve(lambda e, kc=kc: e.scalar_tensor_tensor(out=self.tmpx[:, 0:n], in0=m_sb3[:, kc, 0:n],
                                                             scalar=self.c(wname, kc), in1=self.rstd[:, 0:n],
                                                             op0=ALU.mult, op1=ALU.mult),
                     [B_m, self.B_rstd, self.B_cst], [self.B_tmpx])
            self.dve(lambda e, kc=kc: e.tensor_tensor(out=self.xT[:, kc, t0:t0 + n], in0=self.xT[:, kc, t0:t0 + n],
                                                      in1=self.tmpx[:, 0:n], op=ALU.add),
                     xbufs + [self.B_tmpx], xbufs)

    def layer(self, l, base_mark):
        sb = self.sb
        sb.release(base_mark)
        if l == 0:
            self.win = sb.alloc("win", [P, KC, INW], BF16)
            self.wout = sb.alloc("wout", [P, KC, D], BF16)
            self.B_win = [Buf(f"win{k}") for k in range(KC)]
            self.B_winq = [Buf(f"winq{k}") for k in range(KC)]
            self.B_win2 = [Buf(f"win2{k}") for k in range(KC)]
            self.B_winq2 = [Buf(f"winq2{k}") for k in range(KC)]
            self.B_wout = [Buf(f"wout{k}") for k in range(KC)]
            self.alloc_mixer()
        def wdma(kc, c0, c1, bw, key):
            self.dma("pool", lambda e: e.dma_start(out=self.win[:, kc, c0:c1], in_=self.d_win[l, :, kc * INW + c0:kc * INW + c1]),
                     [], [bw], key)
        for kc in range(KC):
            wdma(kc, C_RK, C_RG, self.B_win[kc], f"winA{kc}")
            wdma(kc, C_GK, INW, self.B_win2[kc], f"winC{kc}")
        later = []
        for kc in range(KC):
            later.append(lambda kc=kc: wdma(kc, C_RQ, C_RK, self.B_winq[kc], f"winB{kc}"))
            later.append(lambda kc=kc: wdma(kc, C_RG, C_GK, self.B_winq2[kc], f"winD{kc}"))
        for kc in range(KC):
            later.append(lambda kc=kc: self.dma("pool", lambda e: e.dma_start(out=self.wout[:, kc, :], in_=self.d_wout[l, :, kc * D:(kc + 1) * D]),
                                                [], [self.B_wout[kc]], f"wout{kc}"))
        def fold(kc, c0, c1):
            bw = self.win_buf(kc, c0)
            self.dve(lambda e: e.tensor_scalar(out=self.win[:, kc, c0:c1], in0=self.win[:, kc, c0:c1], scalar1=self.c(f"pmn{l}", kc),
                                               scalar2=None, op0=ALU.mult), [bw, self.B_cst], [bw])
        for kc in range(KC):
            fold(kc, C_RK, C_RG)
            fold(kc, C_GK, INW)
        self.deferred_folds = later + [(lambda kc=kc, c0=c0, c1=c1: fold(kc, c0, c1)) for kc in range(KC) for (c0, c1) in ((C_RQ, C_RK), (C_RG, C_GK))]
        self.dve(lambda e: e.memset(self.pay[:, :], 0.0), [], [self.B_Sr, self.B_Sg, self.B_D])
        self.dve(lambda e: e.memset(self.Dacc, 1.0), [], [self.B_D])
        for S in self.sets:
            self.dve(lambda e, S=S: e.memset(S.gaT[:, :], 1.0), [], [S.B_ga])
        self.dve(lambda e: e.memset(self.kz[:, :, :], 0.0), [], [self.B_kz])
        import os
        STAGE = int(os.environ.get("KSTAGE", "99"))
        if STAGE < 3:
            return
        self.mixer_pass(l, False)
        while self.deferred_folds:
            self.deferred_folds.pop(0)()
        wno, _ = self.lay[f"retnw{l}"]
        wn_bc = self.cst[:, wno:wno + 8].unsqueeze(2).broadcast_to([P, KC, D])
        self.dve(lambda e: e.tensor_tensor(out=self.wout[:, :, :], in0=self.wout[:, :, :], in1=wn_bc, op=ALU.mult),
                 self.B_wout + [self.B_cst], self.B_wout)
        if STAGE < 4:
            return

        def mid():
            self.exchange_state(l)
            self.pg.barrier()
            self.dve(lambda e: e.memset(self.qz[:, :, :], 0.0), [], [self.B_qz])
        self.mixer_pass(l, True, pre=2, mid_hook=mid)
        if STAGE < 6:
            return
        halo_keys = self.halo_send(l)
        self.pg.barrier(exclude=halo_keys)
        sb.release(base_mark)
        self.ffn(l)
        self.dve(lambda e: e.tensor_scalar(out=self.xT[:, :, 0:NPRE], in0=self.xT[:, :, 0:NPRE], scalar1=self.c("flag"),
                                           scalar2=None, op0=ALU.mult), [self.XB[0], self.B_cst], [self.XB[0]])
        self.pg.barrier()

    def mixer_pass(self, l, with_out, pre=0, mid_hook=None):
        import os
        NT = len(self.mt_tiles)
        REP = [int(v) for v in os.environ.get("KREP", "1,1,1").split(",")]
        K1 = int(os.environ.get("KK1", "6"))
        a_gen, a_idx, a_cnt = None, -1, 0
        b1_gen, b1_idx = None, -1
        b2_gen, b2_idx = None, -1
        a_ready = [False] * NT
        b1_done = [False] * NT
        b2_done = [False] * NT

        def done2(i):
            return i < 0 or b2_done[i]
        for i in range(pre):
            for _ in self.gen_A(l, i, with_out, self.sets[i % 2]):
                pass
            a_ready[i] = True
            a_idx = i
        if mid_hook is not None:
            mid_hook()
        while True:
            if a_gen is None and a_idx + 1 < NT and done2(a_idx + 1 - 2):
                a_idx, a_cnt = a_idx + 1, 0
                a_gen = self.gen_A(l, a_idx, with_out, self.sets[a_idx % 2])
            if b1_gen is None and b1_idx + 1 < NT and a_ready[b1_idx + 1] and done2(b1_idx + 1 - 2):
                b1_idx += 1
                b1_gen = self.gen_B(l, b1_idx, with_out, self.sets[b1_idx % 2])
            if b2_gen is None and b2_idx + 1 < NT and b1_done[b2_idx + 1]:
                b2_idx += 1
                b2_gen = self.gen_B2(l, b2_idx, with_out, self.sets[b2_idx % 2])
            if a_gen is None and b1_gen is None and b2_gen is None:
                if b2_idx + 1 >= NT:
                    break
                raise RuntimeError("mixer pipeline stalled")
            for _rep in range(REP[0]):
                if a_gen is not None:
                    try:
                        next(a_gen)
                        if not with_out and getattr(self, "deferred_folds", None):
                            self.deferred_folds.pop(0)()
                        a_cnt += 1
                        if a_cnt >= K1:
                            a_ready[a_idx] = True
                    except StopIteration:
                        a_ready[a_idx] = True
                        a_gen = None
            for _rep in range(REP[1]):
                if b1_gen is not None:
                    try:
                        next(b1_gen)
                    except StopIteration:
                        b1_done[b1_idx] = True
                        b1_gen = None
            for _rep in range(REP[2]):
                if b2_gen is not None:
                    try:
                        next(b2_gen)
                    except StopIteration:
                        b2_done[b2_idx] = True
                        b2_gen = None

    def alloc_mixer(self):
        sb = self.sb
        A = sb.alloc

        class NS:
            pass
        self.sets = []
        for i in range(2):
            S = NS()
            S.ropeT = A(f"ropeT{i}", [P, 2, TT], F32)
            S.hT = A(f"hT{i}", [P, KC, TT], BF16)
            S.kr = A(f"kr{i}", [P, 4, TT], BF16)
            S.kg = A(f"kg{i}", [P, 2, TT], F32)
            S.gaT = A(f"gaT{i}", [32, TT], F32)
            S.vt = A(f"vt{i}", [P, 1024], BF16)
            for nm in ("rope", "hT", "kr", "qr", "kg", "qg", "gr", "ga", "vt", "mT"):
                setattr(S, "B_" + nm, Buf(f"{nm}{i}"))
            self.sets.append(S)
        self.sqn = A("sqn", [P, KC, TT], BF16)
        self.B_sqn = Buf("sqn")
        self.rp1 = A("rp1", [P, 4, TT], F32)
        self.rp2 = A("rp2", [P, 4, TT], F32)
        self.B_rp1, self.B_rp2 = Buf("rp1"), Buf("rp2")
        rstdA = A("rstdA", [P, TT], F32)
        B_rstdA = Buf("rstdA")
        self.scA = (rstdA, B_rstdA, rstdA, B_rstdA)
        self.ez = A("ez", [P, 256], F32)
        self.lsp = A("lsp", [P, 256], F32)
        self.B_ez, self.B_lsp = Buf("ez"), Buf("lsp")
        self.E1 = A("E1", [P, 2, CH], F32)
        self.E2 = A("E2", [P, 2, CH], F32)
        self.B_E1, self.B_E2 = Buf("E1"), Buf("E2")
        self.kz = A("kz", [P, 4, CH], BF16)
        self.B_qz, self.B_kz = Buf("qz"), Buf("kz")
        self.st = A("st", [P, 64], F32)
        self.B_st = Buf("st")
        self.dprime = A("dprime", [P, 8], F32)
        self.B_dprime = Buf("dprime")
        rstdB = A("rstdB", [P, TT], F32)
        B_rstdB = Buf("rstdB")
        self.scB = (rstdB, B_rstdB, rstdB, B_rstdB)
        for i, S in enumerate(self.sets):
            S.qr = A(f"qr{i}", [P, 4, TT], BF16)
            S.qg = A(f"qg{i}", [P, 2, TT], F32)
            S.gr = A(f"gr{i}", [P, 8, TT], BF16)
        m = sb.mark()
        self.gath = A("gath", [P, 4, 776], F32)
        self.B_gath = Buf("gath")
        self.ubuf = A("ubuf", [P, 768], F32)
        self.B_ubuf = Buf("ubuf")
        e1 = sb.mark()
        sb.release(m)
        self.ktok = A("ktok", [P, 8, 128], BF16)
        self.B_ktok = Buf("ktok")
        self.tmpS = A("tmpS", [P, 4, 128], F32)
        self.B_tmpS = Buf("tmpS")
        for i, S in enumerate(self.sets):
            S.mT = A(f"mT{i}", [P, KC, TT], BF16)
        self.qz = A("qz", [P, 4, CH], BF16)
        self.A_r = A("A_r", [P, 4, CH], BF16)
        self.A_g = A("A_g", [P, 4, CH], BF16)
        self.B_Ar, self.B_Ag = Buf("A_r"), Buf("A_g")
        self.sqo = A("sqo", [P, 8, 128], BF16)
        self.B_sqo = Buf("sqo")
        self.tmp4 = self.sqo[:, :, :].rearrange("p a b -> p (a b)").bitcast(F32).rearrange("p (a b) -> p a b", a=4)
        self.on = A("on", [P, 8, 128], BF16)
        self.B_on = Buf("on")
        self.sqm = A("sqm", [P, KC, TT], BF16)
        self.B_sqm = Buf("sqm")
        sb.release(max(sb.mark(), e1))

    def win_buf(self, kc, col):
        if col < C_RK:
            return self.B_winq[kc]
        if col < C_RG:
            return self.B_win[kc]
        if col < C_GK:
            return self.B_winq2[kc]
        return self.B_win2[kc]

    def fm_proj(self, S, cols, m, n):
        pt = self.psA.alloc()
        v = pt.v3(4, TT)
        for j, col0 in enumerate(cols):
            for kc in range(KC):
                bw = self.win_buf(kc, col0)
                self.pe(lambda e, kc=kc, j=j, col0=col0: e.matmul(v[0:m, j, 0:n], lhsT=self.win[:, kc, col0:col0 + m],
                                                                  rhs=S.hT[:, kc, 0:n], start=(kc == 0), stop=(kc == KC - 1)),
                        [bw, S.B_hT], [pt.buf])
        return pt, v

    def rope_evac(self, S, pt, v, n, dst3, B_dst, dec_name):
        c_bc = S.ropeT[:, 0, 0:n].unsqueeze(1).broadcast_to([P, 4, n])
        self.dve(lambda e: e.tensor_tensor(out=self.rp1[:, :, 0:n], in0=v[:, :, 0:n], in1=c_bc, op=ALU.mult),
                 [pt.buf, S.B_rope], [self.B_rp1])
        for lo, hi in ((0, 64), (64, 0)):
            s_bc = S.ropeT[lo:lo + 64, 1, 0:n].unsqueeze(1).broadcast_to([64, 4, n])
            self.dve(lambda e, lo=lo, hi=hi, s_bc=s_bc: e.tensor_tensor(out=self.rp2[lo:lo + 64, :, 0:n], in0=v[hi:hi + 64, :, 0:n],
                                                                       in1=s_bc, op=ALU.mult),
                     [pt.buf, S.B_rope], [self.B_rp2])
        self.dve(lambda e: e.tensor_tensor(out=self.rp1[:, :, 0:n], in0=self.rp1[:, :, 0:n], in1=self.rp2[:, :, 0:n], op=ALU.add),
                 [self.B_rp1, self.B_rp2], [self.B_rp1])
        o, _ = self.lay[dec_name]
        dec = self.cst[:, o:o + 512].rearrange("p (h t) -> p h t", h=4)[:, :, 0:n]
        self.dve(lambda e: e.tensor_tensor(out=dst3, in0=self.rp1[:, :, 0:n], in1=dec, op=ALU.mult),
                 [self.B_rp1, self.B_cst], [B_dst])

    def gen_A(self, l, ti, with_out, S):
        t0, n = self.mt_tiles[ti]
        XB = [self.XB[ti]]
        self.dma("sp", lambda e: e.dma_start(out=S.ropeT[:, 0, 0:n], in_=self.d_rope[:, t0:t0 + n]),
                 [], [S.B_rope], "rope0")
        self.dma("sp", lambda e: e.dma_start(out=S.ropeT[:, 1, 0:n], in_=self.d_rope[:, NLOC + t0:NLOC + t0 + n]),
                 [], [S.B_rope], "rope1")
        self.rmsnorm(self.xT[:, :, t0:t0 + n], XB, None, S.hT, [S.B_hT], n, self.sqn, self.B_sqn, ps=self.psA, sc=self.scA)
        yield
        pt, v = self.fm_proj(S, [C_GA], 16, n)
        self.act(lambda e, v=v: e.activation(out=S.gaT[0:16, 0:n], in_=v[0:16, 0, 0:n], func=AF.Copy),
                 [pt.buf], [S.B_ga])
        yield
        pt, v = self.fm_proj(S, [C_GK, C_GK + 128], 128, n)
        self.act(lambda e, v=v: e.activation(out=S.kg[:, :, 0:n], in_=v[:, 0:2, 0:n], func=AF.Copy), [pt.buf], [S.B_kg])
        yield
        pt, v = self.fm_proj(S, [C_RK + h * 128 for h in range(4)], 128, n)
        self.rope_evac(S, pt, v, n, S.kr[:, :, 0:n], S.B_kr, "kinv")
        yield
        for half, col in ((0, C_RV), (1, C_GV)):
            pt = self.psA.alloc()
            for kc in range(KC):
                self.pe(lambda e, kc=kc, pt=pt, col=col: e.matmul(
                    pt.t[0:n, 0:512], lhsT=S.hT[:, kc, 0:n], rhs=self.win[:, kc, col:col + 512],
                    start=(kc == 0), stop=(kc == KC - 1)), [S.B_hT, self.win_buf(kc, col)], [pt.buf])
            self.act(lambda e, pt=pt, half=half: e.activation(
                out=S.vt[0:n, half * 512:(half + 1) * 512], in_=pt.t[0:n, 0:512], func=AF.Copy),
                [pt.buf], [S.B_vt])
            yield
        if with_out:
            pt, v = self.fm_proj(S, [C_RQ + h * 128 for h in range(4)], 128, n)
            self.rope_evac(S, pt, v, n, S.qr[:, :, 0:n], S.B_qr, "xiq")
            yield
            pt, v = self.fm_proj(S, [C_GQ, C_GQ + 128], 128, n)
            self.act(lambda e, v=v: e.mul(out=S.qg[:, :, 0:n], in_=v[:, 0:2, 0:n], mul=0.125), [pt.buf], [S.B_qg])
            yield
            for g4 in range(2):
                base = C_RG if g4 == 0 else C_GG
                pt, v = self.fm_proj(S, [base + h * 128 for h in range(4)], 128, n)
                self.act(lambda e, v=v, g4=g4: e.activation(out=S.gr[:, 4 * g4:4 * g4 + 4, 0:n], in_=v[:, :, 0:n], func=AF.Silu),
                         [pt.buf], [S.B_gr])
            yield

    def gen_B(self, l, ti, with_out, S):
        import os
        SUB = int(os.environ.get("KM2SUB", "99")) if with_out else 99
        if SUB < 2:
            return
        t0, cn = self.mt_tiles[ti]
        is_pre = (ti == 0)
        ps = self.psB
        Bv = S.B_vt
        pz = ps.alloc()
        self.pe(lambda e: e.matmul(pz.t[0:cn, 0:256], lhsT=S.gaT[0:17, 0:cn], rhs=self.w2b[0:17, l * 256:(l + 1) * 256],
                                   start=True, stop=True), [S.B_ga, self.B_cst], [pz.buf])
        self.act(lambda e: e.activation(out=self.ez[0:cn, :], in_=pz.t[0:cn, 0:256], func=AF.Exp, scale=-1.0),
                 [pz.buf], [self.B_ez])
        self.act(lambda e: e.activation(out=self.lsp[0:cn, :], in_=self.ez[0:cn, :], func=AF.Ln, bias=self.c("one")[0:cn, :]),
                 [self.B_ez, self.B_cst], [self.B_lsp])
        if is_pre:
            self.dve(lambda e: e.tensor_scalar(out=self.lsp[0:cn, :], in0=self.lsp[0:cn, :], scalar1=self.c("flag")[0:cn, :],
                                               scalar2=None, op0=ALU.mult), [self.B_lsp, self.B_cst], [self.B_lsp])
        yield
        mt = self.c("mt", 0, 128)
        pc3 = pz.t[:, 256:512].rearrange("p (a b) -> p a b", a=2)
        for hp in range(2):
            self.pe(lambda e, hp=hp: e.matmul(pc3[:, hp, 0:cn], lhsT=self.lsp[0:cn, hp * 128:(hp + 1) * 128], rhs=mt[0:cn, 0:cn],
                                              start=True, stop=True), [self.B_lsp, self.B_cst], [pz.buf])
        self.act(lambda e: e.activation(out=self.E2[:, :, 0:cn], in_=pc3[:, :, 0:cn], func=AF.Exp, scale=1.0 / GLA_TAU),
                 [pz.buf], [self.B_E2])
        self.act(lambda e: e.activation(out=self.E1[:, :, 0:cn], in_=pc3[:, :, 0:cn], func=AF.Exp, scale=-1.0 / GLA_TAU),
                 [pz.buf], [self.B_E1])
        yield
        kz4 = self.kz[:, :, :].rearrange("p (a b) t -> p a b t", b=2)
        for half in range(2):
            lo = 64 * half
            self.dve(lambda e, lo=lo, half=half: e.tensor_tensor(out=kz4[lo:lo + 64, :, half, 0:cn], in0=S.kg[lo:lo + 64, :, 0:cn],
                                                                 in1=self.E2[lo:lo + 64, :, 0:cn], op=ALU.mult),
                     [S.B_kg, self.B_E2], [self.B_kz])
        if with_out:
            qz4 = self.qz[:, :, :].rearrange("p (a b) t -> p a b t", b=2)
            for half in range(2):
                lo = 64 * half
                self.dve(lambda e, lo=lo, half=half: e.tensor_tensor(out=qz4[lo:lo + 64, :, half, 0:cn], in0=S.qg[lo:lo + 64, :, 0:cn],
                                                                     in1=self.E1[lo:lo + 64, :, 0:cn], op=ALU.mult),
                         [S.B_qg, self.B_E1], [self.B_qz])
        yield
        if with_out:
            pa = ps.alloc()
            pa3 = pa.v3(4, CH)
            for h in range(4):
                self.pe(lambda e, h=h: e.matmul(pa3[0:cn, h, 0:cn], lhsT=S.kr[:, h, 0:cn], rhs=S.qr[:, h, 0:cn],
                                                start=True, stop=True), [S.B_kr, S.B_qr], [pa.buf])
            mask = mt[0:cn, 0:cn].unsqueeze(1).broadcast_to([cn, 4, cn])
            self.dve(lambda e: e.tensor_tensor(out=self.A_r[0:cn, :, 0:cn], in0=pa3[0:cn, :, 0:cn], in1=mask, op=ALU.mult),
                     [pa.buf, self.B_cst], [self.B_Ar])
            yield
            pg_ = ps.alloc()
            pg3 = pg_.v3(4, CH)
            for h in range(4):
                self.pe(lambda e, h=h: e.matmul(pg3[0:cn, h, 0:cn], lhsT=self.kz[:, h, 0:cn], rhs=self.qz[:, h, 0:cn],
                                                start=True, stop=True), [self.B_kz, self.B_qz], [pg_.buf])
            self.dve(lambda e: e.tensor_tensor(out=self.A_g[0:cn, :, 0:cn], in0=pg3[0:cn, :, 0:cn], in1=mask, op=ALU.mult),
                     [pg_.buf, self.B_cst], [self.B_Ag])
            yield
            po_r = self.psO[ti % 2].alloc()
            por3 = po_r.v3(4, 128)
            for h in range(4):
                self.pe(lambda e, h=h: e.matmul(por3[0:cn, h, :], lhsT=self.A_r[0:cn, h, 0:cn], rhs=S.vt[0:cn, h * 128:(h + 1) * 128],
                                                start=True, stop=False), [self.B_Ar, Bv], [po_r.buf])
                self.pe(lambda e, h=h: e.matmul(por3[0:cn, h, :], lhsT=S.qr[:, h, 0:cn], rhs=self.Sb_r[:, h, :],
                                                start=False, stop=True), [S.B_qr, self.B_Sbr], [po_r.buf])
            yield
            po_g = self.psO[ti % 2].alloc()
            pog3 = po_g.v3(4, 128)
            S.po_r, S.po_g = po_r, po_g
            for h in range(4):
                self.pe(lambda e, h=h: e.matmul(pog3[0:cn, h, :], lhsT=self.A_g[0:cn, h, 0:cn],
                                                rhs=S.vt[0:cn, 512 + h * 128:512 + (h + 1) * 128],
                                                start=True, stop=False), [self.B_Ag, Bv], [po_g.buf])
                self.pe(lambda e, h=h: e.matmul(pog3[0:cn, h, :], lhsT=self.qz[:, h, 0:cn],
                                                rhs=self.Sb_g[:, h // 2, :], start=False, stop=True),
                        [self.B_qz, self.B_Sbg], [po_g.buf])
            yield
        pk = ps.alloc()
        pk3 = pk.bf3(8, 128)
        for h in range(4):
            self.pe(lambda e, h=h: e.transpose(pk3[0:cn, h, :], S.kr[:, h, 0:cn], self.ident), [S.B_kr, self.B_cst], [pk.buf])
        for h in range(4):
            self.pe(lambda e, h=h: e.transpose(pk3[0:cn, 4 + h, :], self.kz[:, h, 0:cn], self.ident), [self.B_kz, self.B_cst], [pk.buf])
        if CHAIN_ON_DVE:
            self.dve(lambda e: e.tensor_copy(out=self.ktok[0:cn, :, :], in_=pk3[0:cn, :, :]), [pk.buf], [self.B_ktok])
        else:
            self.act(lambda e: e.activation(out=self.ktok[0:cn, :, :], in_=pk3[0:cn, :, :], func=AF.Copy), [pk.buf], [self.B_ktok])
        yield
        pkv = ps.alloc()
        pkv3 = pkv.v3(4, 128)
        for h in range(4):
            self.pe(lambda e, h=h: e.matmul(pkv3[:, h, :], lhsT=self.ktok[0:cn, h, :], rhs=S.vt[0:cn, h * 128:(h + 1) * 128],
                                            start=True, stop=True), [self.B_ktok, Bv], [pkv.buf])
        gname = "gr16" if is_pre else "gr128"
        go, _ = self.lay[gname]
        gbc = self.cst[:, go:go + 4].unsqueeze(2).broadcast_to([P, 4, 128])
        self.dve(lambda e: e.tensor_tensor(out=self.tmpS[:, 0:4, :], in0=self.S_r, in1=pkv3[:, :, :], op=ALU.add),
                 [self.B_Sr, pkv.buf], [self.B_tmpS])
        self.dve(lambda e: e.tensor_tensor(out=self.S_r, in0=self.tmpS[:, 0:4, :], in1=gbc, op=ALU.mult),
                 [self.B_tmpS, self.B_cst], [self.B_Sr])
        if with_out:
            if CHAIN_ON_DVE:
                self.dve(lambda e: e.tensor_copy(out=self.Sb_r[:, :, :], in_=self.S_r), [self.B_Sr], [self.B_Sbr])
            else:
                self.act(lambda e: e.activation(out=self.Sb_r[:, :, :], in_=self.S_r, func=AF.Copy), [self.B_Sr], [self.B_Sbr])
        yield
        pkg = ps.alloc()
        pkg3 = pkg.v3(2, 128)
        for h in range(4):
            self.pe(lambda e, h=h: e.matmul(pkg3[:, h // 2, :], lhsT=self.ktok[0:cn, 4 + h, :],
                                            rhs=S.vt[0:cn, 512 + h * 128:512 + (h + 1) * 128],
                                            start=(h % 2 == 0), stop=(h % 2 == 1)), [self.B_ktok, Bv], [pkg.buf])
        self.dve(lambda e: e.tensor_tensor(out=self.tmpS[:, 0:2, :], in0=self.S_g, in1=pkg3[:, :, :], op=ALU.add),
                 [self.B_Sg, pkg.buf], [self.B_tmpS])
        for hp in range(2):
            self.dve(lambda e, hp=hp: e.tensor_scalar(out=self.S_g[:, hp, :], in0=self.tmpS[:, hp, :],
                                                      scalar1=self.E1[:, hp, cn - 1:cn], scalar2=None, op0=ALU.mult),
                     [self.B_tmpS, self.B_E1], [self.B_Sg])
        if with_out:
            if CHAIN_ON_DVE:
                self.dve(lambda e: e.tensor_copy(out=self.Sb_g[:, :, :], in_=self.S_g), [self.B_Sg], [self.B_Sbg])
            else:
                self.act(lambda e: e.activation(out=self.Sb_g[:, :, :], in_=self.S_g, func=AF.Copy), [self.B_Sg], [self.B_Sbg])
        else:
            self.dve(lambda e: e.tensor_tensor(out=self.Dacc, in0=self.Dacc, in1=self.E1[:, :, cn - 1], op=ALU.mult),
                     [self.B_D, self.B_E1], [self.B_D])
        yield
        return

    def gen_B2(self, l, ti, with_out, S):
        if not with_out:
            return
        SUB = 99
        t0, cn = self.mt_tiles[ti]
        ps = self.psA
        po_r, po_g = S.po_r, S.po_g
        por3, pog3 = po_r.v3(4, 128), po_g.v3(4, 128)
        st = self.st
        s1 = st[0:cn, 0:4]
        s2 = st[0:cn, 4:12]
        mean = st[0:cn, 12:16]
        msq = st[0:cn, 16:20]
        var = st[0:cn, 20:28]
        rtv = st[0:cn, 28:36]
        rsd = st[0:cn, 36:44]
        nmr = st[0:cn, 44:48]
        self.dve(lambda e: e.reduce_sum(out=s1, in_=por3[0:cn, :, :], axis=AX.X), [po_r.buf], [self.B_st])
        self.act(lambda e: e.activation(out=self.sqo[0:cn, 0:4, :], in_=por3[0:cn, :, :], func=AF.Square), [po_r.buf], [self.B_sqo])
        self.act(lambda e: e.activation(out=self.sqo[0:cn, 4:8, :], in_=pog3[0:cn, :, :], func=AF.Square), [po_g.buf], [self.B_sqo])
        yield
        self.dve(lambda e: e.reduce_sum(out=s2, in_=self.sqo[0:cn, :, :], axis=AX.X), [self.B_sqo], [self.B_st])
        self.dve(lambda e: e.tensor_tensor(out=msq, in0=s1, in1=s1, op=ALU.mult), [self.B_st], [self.B_st])
        self.dve(lambda e: e.scalar_tensor_tensor(out=s2[:, 0:4], in0=msq, scalar=-1.0 / 128, in1=s2[:, 0:4], op0=ALU.mult, op1=ALU.add),
                 [self.B_st], [self.B_st])
        yield
        self.act(lambda e: e.activation(out=rsd, in_=s2, func=AF.Ln, scale=1.0 / 128, bias=self.c("eps")[0:cn, :]), [self.B_st, self.B_cst], [self.B_st])
        self.act(lambda e: e.activation(out=rsd, in_=rsd, func=AF.Exp, scale=-0.5), [self.B_st], [self.B_st])
        self.dve(lambda e: e.tensor_scalar(out=mean, in0=s1, scalar1=1.0 / 128, scalar2=None, op0=ALU.mult), [self.B_st], [self.B_st])
        yield
        mean_bc = mean.unsqueeze(2).broadcast_to([cn, 4, 128])
        rsdr_bc = rsd[:, 0:4].unsqueeze(2).broadcast_to([cn, 4, 128])
        rsdg_bc = rsd[:, 4:8].unsqueeze(2).broadcast_to([cn, 4, 128])
        self.dve(lambda e: e.tensor_tensor(out=self.tmp4[0:cn, :, :], in0=por3[0:cn, :, :], in1=mean_bc, op=ALU.subtract),
                 [po_r.buf, self.B_st], [self.B_sqo])
        self.dve(lambda e: e.tensor_tensor(out=self.on[0:cn, 0:4, :], in0=self.tmp4[0:cn, :, :], in1=rsdr_bc, op=ALU.mult),
                 [self.B_sqo, self.B_st], [self.B_on])
        self.dve(lambda e: e.tensor_tensor(out=self.on[0:cn, 4:8, :], in0=pog3[0:cn, :, :], in1=rsdg_bc, op=ALU.mult),
                 [po_g.buf, self.B_st], [self.B_on])
        yield
        if SUB < 4:
            return
        pT = ps.alloc()
        pT3 = pT.bf3(8, CH)
        for h in range(8):
            self.pe(lambda e, h=h: e.transpose(pT3[:, h, 0:cn], self.on[0:cn, h, :], self.ident[0:cn, 0:cn]),
                    [self.B_on, self.B_cst], [pT.buf])
        self.dve(lambda e: e.tensor_tensor(out=S.mT[:, :, 0:cn], in0=pT3[:, :, 0:cn], in1=S.gr[:, :, 0:cn], op=ALU.mult),
                 [pT.buf, S.B_gr], [S.B_mT])
        yield
        if SUB < 5:
            return
        n = cn
        pts = [po_r, po_g]
        for oc in range(KC):
            pt = pts[oc // 4]
            v = pt.v3(4, TT)[:, oc % 4, 0:n]
            for kc in range(KC):
                self.pe(lambda e, kc=kc, v=v, oc=oc: e.matmul(v, lhsT=self.wout[:, kc, oc * 128:(oc + 1) * 128],
                                                              rhs=S.mT[:, kc, 0:n], start=(kc == 0), stop=(kc == KC - 1)),
                        [self.B_wout[kc], S.B_mT], [pt.buf])
            if oc % 4 == 3:
                self.act(lambda e, pt=pt, oc=oc: e.activation(out=self.sqm[:, oc - 3:oc + 1, 0:n], in_=pt.v3(4, TT)[:, :, 0:n], func=AF.Square),
                         [pt.buf], [self.B_sqm])
                yield
        pss = ps.alloc()
        for kc in range(KC):
            self.pe(lambda e, kc=kc: e.matmul(pss.f32(n), lhsT=self.ones, rhs=self.sqm[:, kc, 0:n],
                                              start=(kc == 0), stop=(kc == KC - 1)), [self.B_sqm, self.B_cst], [pss.buf])
        self.rstd_from(pss, n, 1.0 / D, self.scB)
        rstd, B_rstd = self.scB[2], self.scB[3]
        yield
        if SUB < 6:
            return
        xb = [self.XB[ti]]
        pno, _ = self.lay[f"pon{l}"]
        rstd_bc = rstd[:, 0:n].unsqueeze(1).broadcast_to([P, 4, n])
        for b4 in range(2):
            pt = pts[b4]
            pw_bc = self.cst[:, pno + 4 * b4:pno + 4 * b4 + 4].unsqueeze(2).broadcast_to([P, 4, n])
            self.dve(lambda e, pt=pt: e.tensor_tensor(out=self.tmp4[:, :, 0:n], in0=pt.v3(4, TT)[:, :, 0:n], in1=rstd_bc, op=ALU.mult),
                     [pt.buf, B_rstd], [self.B_sqo])
            self.dve(lambda e, pw_bc=pw_bc: e.tensor_tensor(out=self.tmp4[:, :, 0:n], in0=self.tmp4[:, :, 0:n], in1=pw_bc, op=ALU.mult),
                     [self.B_sqo, self.B_cst], [self.B_sqo])
            self.dve(lambda e, b4=b4: e.tensor_tensor(out=self.xT[:, 4 * b4:4 * b4 + 4, t0:t0 + n], in0=self.xT[:, 4 * b4:4 * b4 + 4, t0:t0 + n],
                                                      in1=self.tmp4[:, :, 0:n], op=ALU.add), xb + [self.B_sqo], xb)
            yield

    def exchange_state(self, l):
        nc = self.nc
        src, dst = self.cc1_src[l], self.cc1_dst[l]
        B_src, B_dst = Buf("cc1src"), Buf("cc1dst")
        self.dma("pool", lambda e: e.dma_start(out=src.ap()[:, :], in_=self.pay[:, :]), [self.B_Sr, self.B_Sg, self.B_D], [B_src], f"cc1a{l}")
        self.dma("pool", lambda e: e.collective_compute("AllGather", ALU.bypass, replica_groups=[[0, 1, 2, 3], [4, 5, 6, 7]],
                                                        ins=[src.ap().opt()], outs=[dst.ap().opt()]),
                 [B_src], [B_dst], f"cc1b{l}", inc=1)
        self.dma("pool", lambda e: e.dma_start(out=self.gath[:, :, :], in_=dst.ap().rearrange("(r p) f -> p r f", p=P)),
                 [B_dst], [self.B_gath], f"cc1c{l}")
        self.dve(lambda e: e.memset(self.pay[:, :], 0.0), [], [self.B_Sr, self.B_Sg, self.B_D])
        dro, _ = self.lay["drt"]
        for i in range(3):
            a_i = self.c("acoef", i)
            self.dve(lambda e, i=i, a_i=a_i: e.tensor_scalar(out=self.dprime[:, 0:4], in0=self.cst[:, dro + 4 * i:dro + 4 * i + 4],
                                                              scalar1=-1.0, scalar2=a_i, op0=ALU.add, op1=ALU.mult),
                     [self.B_cst], [self.B_dprime])
            self.dve(lambda e, i=i, a_i=a_i: e.tensor_scalar(out=self.dprime[:, 4:6], in0=self.gath[:, i, 768:770],
                                                              scalar1=-1.0, scalar2=a_i, op0=ALU.add, op1=ALU.mult),
                     [self.B_gath, self.B_cst], [self.B_dprime])
            self.dve(lambda e: e.tensor_scalar(out=self.dprime[:, 0:6], in0=self.dprime[:, 0:6], scalar1=1.0, scalar2=None, op0=ALU.add),
                     [self.B_dprime], [self.B_dprime])
            self.dve(lambda e, i=i, a_i=a_i: e.tensor_scalar(out=self.ubuf[:, :], in0=self.gath[:, i, 0:768], scalar1=a_i, scalar2=None,
                                                              op0=ALU.mult), [self.B_gath, self.B_cst], [self.B_ubuf])
            dr_bc = self.dprime[:, 0:4].unsqueeze(2).broadcast_to([P, 4, 128])
            dg_bc = self.dprime[:, 4:6].unsqueeze(2).broadcast_to([P, 2, 128])
            self.dve(lambda e, dr_bc=dr_bc: e.tensor_tensor(out=self.S_r, in0=self.S_r, in1=dr_bc, op=ALU.mult),
                     [self.B_Sr, self.B_dprime], [self.B_Sr])
            self.dve(lambda e, dg_bc=dg_bc: e.tensor_tensor(out=self.S_g, in0=self.S_g, in1=dg_bc, op=ALU.mult),
                     [self.B_Sg, self.B_dprime], [self.B_Sg])
            self.dve(lambda e: e.tensor_tensor(out=self.pay[:, 0:768], in0=self.pay[:, 0:768], in1=self.ubuf[:, :], op=ALU.add),
                     [self.B_Sr, self.B_Sg, self.B_ubuf], [self.B_Sr, self.B_Sg])
        self.act(lambda e: e.activation(out=self.Sb_r[:, :, :], in_=self.S_r, func=AF.Copy), [self.B_Sr], [self.B_Sbr])
        self.act(lambda e: e.activation(out=self.Sb_g[:, :, :], in_=self.S_g, func=AF.Copy), [self.B_Sg], [self.B_Sbg])

    def halo_send(self, l):
        src, dst = self.cc2_src[l], self.cc2_dst[l]
        B_src, B_dst = Buf("cc2src"), Buf("cc2dst")
        last = self.XB[-1]
        self.dve(lambda e: e.tensor_copy(out=self.hal[:, :].rearrange("p (k t) -> p k t", k=KC), in_=self.xT[:, :, NLOC - 2:NLOC]),
                 [last], [self.B_hal])
        self.dma("sp", lambda e: e.dma_start(out=src.ap()[:, :], in_=self.hal[:, :]), [self.B_hal], [B_src], f"cc2a{l}")
        self.dma("pool", lambda e: e.collective_compute("AllGather", ALU.bypass, replica_groups=[[0, 1, 2, 3], [4, 5, 6, 7]],
                                                        ins=[src.ap().opt()], outs=[dst.ap().opt()]),
                 [B_src], [B_dst], f"cc2b{l}", inc=1)
        self.dma("sp", lambda e: e.dma_start(out=self.gh[:, :, :], in_=dst.ap().rearrange("(r p) f -> p r f", p=P)),
                 [B_dst], [self.B_gh], f"cc2c{l}")
        return {f"cc2a{l}", f"cc2b{l}", f"cc2c{l}"}

    def halo_recv(self, l):
        self.dve(lambda e: e.tensor_scalar(out=self.hal[:, :], in0=self.gh[:, 0, :], scalar1=self.c("bcoef", 0), scalar2=None, op0=ALU.mult),
                 [self.B_gh, self.B_cst], [self.B_hal])
        for i in range(1, 4):
            self.dve(lambda e, i=i: e.scalar_tensor_tensor(out=self.hal[:, :], in0=self.gh[:, i, :], scalar=self.c("bcoef", i),
                                                           in1=self.hal[:, :], op0=ALU.mult, op1=ALU.add),
                     [self.B_gh, self.B_cst, self.B_hal], [self.B_hal])
        self.dve(lambda e: e.tensor_tensor(out=self.xT[:, :, NPRE - 2:NPRE], in0=self.xT[:, :, NPRE - 2:NPRE],
                                           in1=self.hal[:, :].rearrange("p (k t) -> p k t", k=KC), op=ALU.add),
                 [self.XB[0], self.B_hal], [self.XB[0]])

    def ffn(self, l):
        sb = self.sb
        A = sb.alloc
        NH = 1042
        act_ = A("act", [P, NFC, 1040], BF16)
        B_act = [Buf(f"act{i}") for i in range(NFC)]
        wsl = [A(f"wsl{i}", [P, 4096], BF16) for i in range(3)]
        B_wsl = [Buf(f"wsl{i}") for i in range(3)]
        uhalo = A("uhalo", [P, 44, 2], F32)
        B_uhalo = Buf("uhalo")
        m_c = sb.mark()
        gl0_ = A("gl0", [P, 512], F32)
        gl = [gl0_, gl0_]
        B_gl0_ = Buf("gl0")
        B_gl = [B_gl0_, B_gl0_]
        ca = [A(f"ca{i}", [P, 512], F32) for i in range(2)]
        cg = [A(f"cg{i}", [P, 512], F32) for i in range(2)]
        B_ca, B_cg = [Buf("ca0"), Buf("ca1")], [Buf("cg0"), Buf("cg1")]
        m_c2 = sb.mark()
        sb.release(m_c)
        cB = A("cB", [P, 44, 16], F32)
        tB = A("tB", [P, 44, 16], F32)
        glB = A("glB", [P, NFC, 16], F32)
        assert sb.mark() <= m_c2
        sb.release(m_c2)
        G_c = [B_gl0_, B_ca[0], B_ca[1], B_cg[0], B_cg[1]]
        m_u = sb.mark()
        sqf = A("sqf", [P, 2, 512], BF16)
        B_sqf = [Buf("sqf0"), Buf("sqf1")]
        tmp2 = [self.tmpx, A("tmpx2", [P, 512], F32)]
        B_tmp2 = [self.B_tmpx, Buf("tmpx2")]
        m_u2 = sb.mark()
        sb.release(m_u)
        upre = A("upre", [P, 44, 16], F32)
        assert sb.mark() <= m_u2
        sb.release(m_u2)
        G_u = [B_sqf[0], B_sqf[1], B_tmp2[1]]
        m1 = sb.mark()
        h2T = A("h2T", [P, KC, NH], BF16)
        sqn = A("sqn2", [P, KC, 512], BF16)
        ua = [A(f"ua{i}", [P, NH], F32) for i in range(2)]
        ug = [A(f"ug{i}", [P, NH], F32) for i in range(2)]
        sb.release(m1)
        f_sb = A("f_sb", [P, KC, 1040], F32)
        wo, _ = self.lay[f"convw{l}"]
        bo, _ = self.lay[f"convb{l}"]

        halves = [
            [(2, 16, [0], 0), (18, 512, [1, 2, 3, 4], 16), (530, 512, [5, 6, 7, 8], 528)],
            [(2, 512, [9, 10, 11, 12], 1040), (514, 512, [13, 14, 15, 16], 1552)],
        ]
        wcnt = [0]

        def next_slot():
            i = wcnt[0] % 3
            wcnt[0] += 1
            return i

        B_h2T, B_sqn = Buf("h2T"), Buf("sqn2")
        B_ua, B_ug = [Buf("ua0"), Buf("ua1")], [Buf("ug0"), Buf("ug1")]
        G_fsb = [B_h2T, B_sqn] + B_ua + B_ug
        last_layer = (l == self.nl - 1) and self.final_layer
        for hi, tiles in enumerate(halves):
            order = [t for t in tiles if t[2] != [0]] + [t for t in tiles if t[2] == [0]]
            for (co, n, xt, t0) in order:
                if xt == [0]:
                    self.halo_recv(l)
                self.rmsnorm(self.xT[:, :, t0:t0 + n], [self.XB[i] for i in xt], f"pfn{l}", h2T[:, :, co:co + n], [B_h2T], n, sqn, B_sqn)
            if hi == 0:
                for k in range(2):
                    self.dve(lambda e, k=k: e.memset(ua[k][:, 0:2], 0.0), [], [B_ua[k]])
                    self.dve(lambda e, k=k: e.memset(ug[k][:, 0:2], 0.0), [], [B_ug[k]])
            for j in range(11):
                si = next_slot()
                w = wsl[si]
                self.dma("pool", lambda e, j=j, w=w: e.dma_start(out=w[:, :], in_=self.d_wup[l * 11 + j, :, :]), [], [B_wsl[si]], f"wsl{si}")
                w3 = w[:, :].rearrange("p (k c) -> p k c", k=KC)
                for q in range(2):
                    i = 2 * j + q
                    k = i % 2
                    uab, ugb = ua[k], ug[k]
                    if hi == 1:
                        self.dve(lambda e, i=i, uab=uab: e.tensor_copy(out=uab[:, 0:2], in_=uhalo[:, i, :]), [B_uhalo], [B_ua[k]])
                        self.dve(lambda e, i=i, ugb=ugb: e.tensor_copy(out=ugb[:, 0:2], in_=uhalo[:, NFC + i, :]), [B_uhalo], [B_ug[k]])
                    for ti_, (co, n, xt, t0) in enumerate(tiles):
                        pa = self.ps.alloc()
                        pgt = self.ps.alloc()
                        for kc in range(KC):
                            self.pe(lambda e, kc=kc, pa=pa, co=co, n=n, q=q, w3=w3: e.matmul(
                                pa.f32(n), lhsT=w3[:, kc, q * 128:(q + 1) * 128], rhs=h2T[:, kc, co:co + n],
                                start=(kc == 0), stop=(kc == KC - 1)), [B_wsl[si], B_h2T], [pa.buf])
                        for kc in range(KC):
                            self.pe(lambda e, kc=kc, pgt=pgt, co=co, n=n, q=q, w3=w3: e.matmul(
                                pgt.f32(n), lhsT=w3[:, kc, 256 + q * 128:256 + (q + 1) * 128], rhs=h2T[:, kc, co:co + n],
                                start=(kc == 0), stop=(kc == KC - 1)), [B_wsl[si], B_h2T], [pgt.buf])
                        if xt == [0] and not last_layer:
                            self.act(lambda e, pa=pa, co=co, n=n, uab=uab: e.activation(out=uab[:, co:co + n], in_=pa.f32(n), func=AF.Copy), [pa.buf], [B_ua[k]])
                            self.act(lambda e, pgt=pgt, co=co, n=n, ugb=ugb: e.activation(out=ugb[:, co:co + n], in_=pgt.f32(n), func=AF.Copy), [pgt.buf], [B_ug[k]])
                            self.act(lambda e, pa=pa, i=i, n=n: e.activation(out=upre[:, i, 0:n], in_=pa.f32(n), func=AF.Copy), [pa.buf], G_u)
                            self.act(lambda e, pgt=pgt, i=i, n=n: e.activation(out=upre[:, NFC + i, 0:n], in_=pgt.f32(n), func=AF.Copy), [pgt.buf], G_u)
                            continue
                        if last_layer and xt == [0]:
                            self.act(lambda e, pa=pa, co=co, n=n, uab=uab: e.activation(out=uab[:, co:co + n], in_=pa.f32(n), func=AF.Copy), [pa.buf], [B_ua[k]])
                            self.act(lambda e, pgt=pgt, co=co, n=n, ugb=ugb: e.activation(out=ugb[:, co:co + n], in_=pgt.f32(n), func=AF.Copy), [pgt.buf], [B_ug[k]])
                            continue
                        gi = (i * len(tiles) + ti_) % 2
                        cab, cgb = ca[gi], cg[gi]
                        for (pt_, u, B_u, cdst, B_c, ch, second) in ((pa, uab, B_ua[k], cab, B_ca[gi], i, "dve"),
                                                                     (pgt, ugb, B_ug[k], cgb, B_cg[gi], NFC + i, "dve")):
                            w0 = self.cst[:, wo + ch * 3 + 0:wo + ch * 3 + 1]
                            w1 = self.cst[:, wo + ch * 3 + 1:wo + ch * 3 + 2]
                            w2 = self.cst[:, wo + ch * 3 + 2:wo + ch * 3 + 3]
                            bb = self.cst[:, bo + ch:bo + ch + 1]
                            self.act(lambda e, pt_=pt_, u=u, co=co, n=n: e.activation(out=u[:, co:co + n], in_=pt_.f32(n), func=AF.Copy),
                                     [pt_.buf], [B_u])
                            self.act(lambda e, pt_=pt_, cdst=cdst, n=n, w2=w2, bb=bb: e.activation(
                                out=cdst[:, 0:n], in_=pt_.f32(n), func=AF.Identity, scale=w2, bias=bb), [pt_.buf, self.B_cst], [B_c])
                            self.dve(lambda e, u=u, cdst=cdst, co=co, n=n, w1=w1: e.scalar_tensor_tensor(
                                out=cdst[:, 0:n], in0=u[:, co - 1:co - 1 + n], scalar=w1, in1=cdst[:, 0:n], op0=ALU.mult, op1=ALU.add),
                                [B_u, self.B_cst, B_c], [B_c])
                            self.pg.emit(second, lambda e, u=u, cdst=cdst, co=co, n=n, w0=w0: e.scalar_tensor_tensor(
                                out=cdst[:, 0:n], in0=u[:, co - 2:co - 2 + n], scalar=w0, in1=cdst[:, 0:n], op0=ALU.mult, op1=ALU.add),
                                [B_u, self.B_cst, B_c], [B_c])
                        self.act(lambda e, n=n, gi=gi, cab=cab: e.activation(out=gl[gi][:, 0:n], in_=cab[:, 0:n], func=AF.Gelu_apprx_tanh),
                                 [B_ca[gi]], [B_gl[gi]])
                        ac0 = co - 2
                        import os
                        self.pg.emit(os.environ.get("KGATE", "dve"), lambda e, i=i, ac0=ac0, n=n, gi=gi, cgb=cgb: e.tensor_tensor(
                            out=act_[:, i, ac0:ac0 + n], in0=gl[gi][:, 0:n], in1=cgb[:, 0:n], op=ALU.mult),
                            [B_gl[gi], B_cg[gi]], [B_act[i]])
                    if hi == 0:
                        self.dve(lambda e, i=i, uab=uab: e.tensor_copy(out=uhalo[:, i, :], in_=uab[:, NH - 2:NH]), [B_ua[k]], [B_uhalo])
                        self.dve(lambda e, i=i, ugb=ugb: e.tensor_copy(out=uhalo[:, NFC + i, :], in_=ugb[:, NH - 2:NH]), [B_ug[k]], [B_uhalo])
            if hi == 0 and not last_layer:
                wv = self.cst[:, wo:wo + 132].rearrange("p (c k) -> p c k", k=3)

                def wbc(tap, ncol):
                    return wv[:, :, tap].unsqueeze(2).broadcast_to([P, 44, ncol])
                b_bc = self.cst[:, bo:bo + 44].unsqueeze(2).broadcast_to([P, 44, 16])
                self.dve(lambda e: e.tensor_tensor(out=cB[:, :, :], in0=upre[:, :, :], in1=wbc(2, 16), op=ALU.mult), G_u + [self.B_cst], G_c)
                self.dve(lambda e: e.tensor_tensor(out=tB[:, :, 1:16], in0=upre[:, :, 0:15], in1=wbc(1, 15), op=ALU.mult), G_u + [self.B_cst], G_c)
                self.dve(lambda e: e.tensor_tensor(out=cB[:, :, 1:16], in0=cB[:, :, 1:16], in1=tB[:, :, 1:16], op=ALU.add), G_c, G_c)
                self.dve(lambda e: e.tensor_tensor(out=tB[:, :, 2:16], in0=upre[:, :, 0:14], in1=wbc(0, 14), op=ALU.mult), G_u + [self.B_cst], G_c)
                self.dve(lambda e: e.tensor_tensor(out=cB[:, :, 2:16], in0=cB[:, :, 2:16], in1=tB[:, :, 2:16], op=ALU.add), G_c, G_c)
                self.dve(lambda e: e.tensor_tensor(out=cB[:, :, :], in0=cB[:, :, :], in1=b_bc, op=ALU.add), G_c + [self.B_cst], G_c)
                self.act(lambda e: e.activation(out=glB[:, :, :], in_=cB[:, 0:NFC, :], func=AF.Gelu_apprx_tanh), G_c, G_c)
                self.dve(lambda e: e.tensor_tensor(out=act_[:, :, 0:16], in0=glB[:, :, :], in1=cB[:, NFC:2 * NFC, :], op=ALU.mult), G_c, B_act)
            if last_layer:
                tiles = [t for t in tiles if t[2] != [0]]
            sst = [self.ps.reserve() for _ in tiles]
            pend = None
            for oc in range(KC):
                si = next_slot()
                w = wsl[si]
                self.dma("pool", lambda e, oc=oc, w=w: e.dma_start(out=w[:, 0:NFC * 128], in_=self.d_wdn[l * 8 + oc, :, :]), [], [B_wsl[si]], f"wsl{si}")
                w3 = w[:, 0:NFC * 128].rearrange("p (k c) -> p k c", k=NFC)
                for ri, (co, n, xt, t0) in enumerate(tiles):
                    ac0 = co - 2
                    pt = self.ps.alloc()
                    for kc in range(NFC):
                        self.pe(lambda e, kc=kc, pt=pt, ac0=ac0, n=n, w3=w3: e.matmul(
                            pt.f32(n), lhsT=w3[:, kc, :], rhs=act_[:, kc, ac0:ac0 + n], start=(kc == 0), stop=(kc == NFC - 1)),
                            [B_wsl[si], B_act[kc]], [pt.buf])
                    sq_i = (oc * len(tiles) + ri) % 2
                    self.act(lambda e, pt=pt, oc=oc, ac0=ac0, n=n: e.mul(out=f_sb[:, oc, ac0:ac0 + n], in_=pt.f32(n), mul=self.c(f"pofn{l}", oc)),
                             [pt.buf, self.B_cst], G_fsb)
                    self.act(lambda e, pt=pt, sq_i=sq_i, n=n: e.activation(out=sqf[:, sq_i, 0:n], in_=pt.f32(n), func=AF.Square),
                             [pt.buf], [B_sqf[sq_i]])
                    if pend is not None:
                        self.pe(*pend)
                    pend = (lambda e, st_=sst[ri], sq_i=sq_i, n=n, oc=oc: e.matmul(st_.f32(n), lhsT=self.ones, rhs=sqf[:, sq_i, 0:n],
                                                                                    start=(oc == 0), stop=(oc == KC - 1)),
                            [B_sqf[sq_i], self.B_cst], [sst[ri].buf])
            if pend is not None:
                self.pe(*pend)
            for ri, (co, n, xt, t0) in enumerate(tiles):
                ac0 = co - 2
                self.rstd_from(sst[ri], n, 1.0 / D)
                xb = [self.XB[i] for i in xt]
                for kc in range(KC):
                    tk = kc % 2
                    self.dve(lambda e, kc=kc, ac0=ac0, n=n, tk=tk: e.tensor_tensor(
                        out=tmp2[tk][:, 0:n], in0=f_sb[:, kc, ac0:ac0 + n], in1=self.rstd[:, 0:n], op=ALU.mult),
                        G_fsb + [self.B_rstd], [B_tmp2[tk]])
                    self.pg.emit("pool" if kc % 2 == 0 else "dve",
                                 lambda e, kc=kc, t0=t0, n=n, tk=tk: e.tensor_tensor(out=self.xT[:, kc, t0:t0 + n], in0=self.xT[:, kc, t0:t0 + n],
                                                                                     in1=tmp2[tk][:, 0:n], op=ALU.add), xb + [B_tmp2[tk]], xb)
            for t in sst:
                self.ps.unreserve(t)


def _img(w):
    K, N = w.shape
    return np.ascontiguousarray(w.reshape(K // P, P, N).transpose(1, 0, 2).reshape(P, (K // P) * N))


def _host_consts(r, nl, layers, prm):
    lay, ncst = cst_layout(nl)
    cst = np.zeros((P, ncst), np.float32)

    def put(name, arr):
        o, n = lay[name]
        arr = np.asarray(arr, np.float32)
        cst[:, o:o + n] = arr.reshape(-1, n) if arr.ndim > 1 else arr[None, :]

    put("eps", [EPS])
    put("one", [1.0])
    put("flag", [1.0 if r == 0 else 0.0])
    put("acoef", [1.0 if i < r else 0.0 for i in range(4)])
    put("bcoef", [1.0 if i == r - 1 else 0.0 for i in range(4)])
    hh = np.arange(4, dtype=np.float64)
    log_g = np.log(1.0 - 2.0 ** (-5.0 - hh))
    put("gr128", np.exp(log_g * 128))
    put("gr16", np.exp(log_g * 16) if r == 0 else np.ones(4))
    drt = np.zeros((4, 4))
    for i in range(4):
        drt[i] = np.exp(log_g * (NLOC if i == 0 else NSEQ))
    put("drt", drt.reshape(-1))
    tt = np.arange(128, dtype=np.float64)
    put("xiq", np.exp(log_g[:, None] * (tt[None, :] + 1.0)).reshape(-1))
    put("kinv", (np.exp(-log_g[:, None] * (tt[None, :] + 1.0)) * (128.0 ** -0.5)).reshape(-1))
    mt = (np.arange(128)[None, :] >= np.arange(128)[:, None]).astype(np.float32)
    o, n = lay["mt"]
    cst[:, o:o + n] = mt
    for li, l in enumerate(layers):
        def fm(v):
            return np.asarray(v, np.float32).reshape(KC, P).T
        for nm, key in (("pmn", "pre_mix_norm"), ("pon", "post_mix_norm"), ("pfn", "pre_ffn_norm"), ("pofn", "post_ffn_norm")):
            o, n = lay[f"{nm}{li}"]
            cst[:, o:o + n] = fm(prm[key][l])
        o, n = lay[f"retnw{li}"]
        cst[:, o:o + n] = np.asarray(prm["ret_norm_w"][l], np.float32).reshape(4, P).T
        o, n = lay[f"glanw{li}"]
        cst[:, o:o + n] = np.asarray(prm["gla_norm_w"][l], np.float32).reshape(4, P).T
        cw = np.asarray(prm["ffn_conv_w"][l], np.float32)
        o, n = lay[f"convw{li}"]
        cst[:, o:o + n] = cw.reshape(3, 44, P).transpose(2, 1, 0).reshape(P, 44 * 3)
        cbv = np.asarray(prm["ffn_conv_b"][l], np.float32)
        o, n = lay[f"convb{li}"]
        cst[:, o:o + n] = cbv.reshape(44, P).T
    return cst


def _host_weights(layers, prm):
    nl = len(layers)
    win = np.stack([_img(np.asarray(prm["w_in"][l], np.float32)) for l in layers])
    wout = np.stack([_img(np.asarray(prm["w_out"][l], np.float32)) for l in layers])
    wup = np.zeros((nl * 11, P, KC * 512), np.float32)
    wdn = np.zeros((nl * 8, P, NFC * 128), np.float32)
    for li, l in enumerate(layers):
        up = np.asarray(prm["ffn_up"][l], np.float32)
        dn = np.asarray(prm["ffn_down"][l], np.float32)
        for j in range(11):
            cols = np.concatenate([np.arange(128 * (2 * j), 128 * (2 * j + 2)), DFF + np.arange(128 * (2 * j), 128 * (2 * j + 2))])
            wup[li * 11 + j] = _img(up[:, cols])
        for oc in range(8):
            wdn[li * 8 + oc] = _img(dn[:, oc * 128:(oc + 1) * 128])
    w2b = np.zeros((17, nl * 256), np.float32)
    for li, l in enumerate(layers):
        w2b[0:16, li * 256:(li + 1) * 256] = np.asarray(prm["gla_gate_w2"][l], np.float32)
        w2b[16, li * 256:(li + 1) * 256] = np.asarray(prm["gla_gate_b"][l], np.float32)
    return win, wout, wup, wdn, w2b


def _host_rope(r):
    half = 64
    inv = 10000.0 ** (-np.arange(half, dtype=np.float64) / half)
    pos = float(NSEQ * r) + np.arange(NLOC, dtype=np.float64)
    ang = pos[:, None] * inv[None, :]
    c = np.cos(ang).astype(np.float32).T
    s = np.sin(ang).astype(np.float32).T
    cosT = np.concatenate([c, c], axis=0)
    sinT = np.concatenate([-s, s], axis=0)
    return np.ascontiguousarray(np.concatenate([cosT, sinT], axis=1))


_CACHE = {}


def _get_program(nl, dbg=None, final=True):
    key = (nl, dbg, final)
    if key not in _CACHE:
        b = Builder(nl, dbg, final_layer=final)
        _CACHE[key] = (b.build(), b)
    return _CACHE[key]


def _run(xT_imgs, layers, prm, dbg=None):
    nl = len(layers)
    nc, b = _get_program(nl, dbg, final=(layers[-1] == prm["w_in"].shape[0] - 1))
    win, wout, wup, wdn, w2b = _host_weights(layers, prm)
    cbm = np.concatenate([np.eye(P, dtype=np.float32), np.ones((P, P), np.float32)], axis=1)
    in_maps = []
    for j in range(8):
        r = j % 4
        in_maps.append({
            "xT": xT_imgs[j], "cst": _host_consts(r, nl, layers, prm), "w2b": w2b, "cb": cbm, "rope": _host_rope(r),
            "win": win, "wout": wout, "wup": wup, "wdn": wdn,
        })
    res = run_bass_kernel_spmd(nc, in_maps, core_ids=list(range(8)))
    return res


LAUNCH_SPLIT = False


def kernel(x, meta_tokens, pre_mix_norm, w_in, gla_gate_w2, gla_gate_b, ret_norm_w, gla_norm_w,
           w_out, post_mix_norm, pre_ffn_norm, ffn_up, ffn_conv_w, ffn_conv_b, ffn_down, post_ffn_norm):
    prm = dict(pre_mix_norm=pre_mix_norm, w_in=w_in, gla_gate_w2=gla_gate_w2, gla_gate_b=gla_gate_b,
               ret_norm_w=ret_norm_w, gla_norm_w=gla_norm_w, w_out=w_out, post_mix_norm=post_mix_norm,
               pre_ffn_norm=pre_ffn_norm, ffn_up=ffn_up, ffn_conv_w=ffn_conv_w, ffn_conv_b=ffn_conv_b,
               ffn_down=ffn_down, post_ffn_norm=post_ffn_norm)
    prm = {k: np.asarray(v) for k, v in prm.items()}
    x = np.asarray(x, np.float32)
    meta = np.asarray(meta_tokens, np.float32)
    imgs = []
    for j in range(8):
        b, r = j // 4, j % 4
        xin = np.zeros((NLOC, D), np.float32)
        if r == 0:
            xin[0:NPRE] = meta
        xin[NPRE:] = x[b, NSEQ * r:NSEQ * (r + 1)]
        imgs.append(np.ascontiguousarray(xin.T.reshape(KC, P, NLOC).transpose(1, 0, 2).reshape(P, KC * NLOC)))
    nl_total = prm["w_in"].shape[0]
    if LAUNCH_SPLIT:
        for l in range(nl_total):
            res = _run(imgs, [l], prm)
            imgs = [np.ascontiguousarray(res.results[j]["y"]) for j in range(8)]
        outs = imgs
    else:
        res = _run(imgs, list(range(nl_total)), prm)
        outs = [res.results[j]["y"] for j in range(8)]
    out = np.zeros((2, 4 * NSEQ, D), np.float32)
    for j in range(8):
        b, r = j // 4, j % 4
        y = np.asarray(outs[j]).reshape(P, KC, NLOC)[:, :, NPRE:]
        out[b, NSEQ * r:NSEQ * (r + 1)] = y.transpose(2, 1, 0).reshape(NSEQ, D)
    return out
```

```python
import numpy as np
import concourse.bass as bass
import concourse.mybir as mybir
from concourse.bass_utils import run_bass_kernel_spmd

F32 = mybir.dt.float32
BF16 = mybir.dt.bfloat16
AF = mybir.ActivationFunctionType
ALU = mybir.AluOpType
AX = mybir.AxisListType

P = 128
D = 1024
KC = 8
NPRE = 16
NSEQ = 2048
NLOC = NPRE + NSEQ
TT = 128
CH = 128
INW = 3600
DFF = 2816
NFC = 22
EPS = 1e-6
GLA_TAU = 16.0
EPOCH = 6000
import os as _os
GATE_IN_ON_DVE = _os.environ.get("KGA", "dve") == "dve"
V_ON_DVE = _os.environ.get("KV", "act") == "dve"
CHAIN_ON_DVE = _os.environ.get("KCHAIN", "dve") == "dve"

C_RQ, C_RK, C_RV, C_RG, C_GQ, C_GK, C_GV, C_GG, C_GA = 0, 512, 1024, 1536, 2048, 2304, 2560, 3072, 3584


class Buf:
    __slots__ = ("name", "lw", "rd", "rd_dma")

    def __init__(self, name):
        self.name = name
        self.lw = None
        self.rd = {}
        self.rd_dma = []


class Ins:
    __slots__ = ("eng", "fn", "deps", "sig", "cnt", "is_dma", "key", "val", "inc")

    def __init__(self, eng, fn):
        self.eng = eng
        self.fn = fn
        self.deps = []
        self.sig = False
        self.cnt = 0
        self.is_dma = False
        self.key = None
        self.val = 0
        self.inc = 16


class Prog:
    ENGS = ("pe", "act", "dve", "pool", "sp")

    def __init__(self):
        self.streams = {e: [] for e in self.ENGS}
        self.key_cnt = {}
        self.key_last = {}
        self.bar = {}

    def barrier(self, exclude=()):
        deps = []
        for e in self.ENGS:
            for ins in reversed(self.streams[e]):
                if not ins.is_dma:
                    deps.append(ins)
                    break
        deps += [v for k, v in self.key_last.items() if k not in exclude]
        for e in self.ENGS:
            self.bar[e] = list(deps) + self.bar.get(e, [])

    def emit(self, eng, fn, reads=(), writes=(), dma_key=None, serialize=True, inc=16):
        ins = Ins(eng, fn)
        raw = set()
        oth = set()
        for b in reads:
            if b.lw is not None:
                raw.add(b.lw)
        for b in writes:
            if b.lw is not None:
                oth.add(b.lw)
            for r in b.rd.values():
                oth.add(r)
            for r in b.rd_dma:
                oth.add(r)
        if dma_key is not None:
            ins.is_dma = True
            ins.key = dma_key
            ins.inc = inc
            self.key_cnt[dma_key] = self.key_cnt.get(dma_key, 0) + inc
            ins.val = self.key_cnt[dma_key]
            if serialize and dma_key in self.key_last:
                oth.add(self.key_last[dma_key])
            self.key_last[dma_key] = ins
        deps = []
        for d in self.bar.pop(eng, []):
            if d.is_dma or d.eng != eng:
                if d not in raw and d not in oth:
                    deps.append(d)
        for d in raw | oth:
            if d is ins:
                continue
            if d.is_dma or ins.is_dma:
                deps.append(d)
            elif d.eng != eng:
                deps.append(d)
            else:
                if eng != "pe":
                    deps.append(d)
        for d in deps:
            if not d.is_dma:
                d.sig = True
        ins.deps = deps
        for b in reads:
            if ins.is_dma:
                b.rd_dma.append(ins)
            else:
                b.rd[eng] = ins
        for b in writes:
            b.lw = ins
            b.rd = {}
            b.rd_dma = []
        self.streams[eng].append(ins)
        return ins

    def finalize(self):
        for e in self.ENGS:
            c = 0
            for ins in self.streams[e]:
                if ins.is_dma:
                    continue
                if ins.sig:
                    c += 1
                    ins.cnt = c
        return {e: sum(1 for i in self.streams[e] if i.sig and not i.is_dma) for e in self.ENGS}

    def replay(self, eng, eobj, eng_sems, dma_sems):
        seen_cnt = {e: 0 for e in self.ENGS}
        seen_dma = {}
        for ins in self.streams[eng]:
            for d in ins.deps:
                if d.is_dma:
                    if seen_dma.get(d.key, 0) >= d.val:
                        continue
                    eobj.wait_ge(dma_sems[d.key], d.val)
                    seen_dma[d.key] = d.val
                else:
                    if seen_cnt[d.eng] >= d.cnt:
                        continue
                    ep = (d.cnt - 1) // EPOCH
                    eobj.wait_ge(eng_sems[d.eng][ep], d.cnt - ep * EPOCH)
                    seen_cnt[d.eng] = d.cnt
            bi = ins.fn(eobj)
            if ins.is_dma:
                bi.then_inc(dma_sems[ins.key], ins.inc)
            elif ins.sig:
                ep = (ins.cnt - 1) // EPOCH
                bi.then_inc(eng_sems[eng][ep], 1)


class SBAlloc:
    BASE = 16512
    LIMIT = 229376 - 64

    def __init__(self, nc):
        self.nc = nc
        self.off = self.BASE
        self.peak = self.off

    def alloc(self, name, shape, dtype):
        isz = 2 if dtype == BF16 else 4
        size = int(np.prod(shape[1:])) * isz
        off = (self.off + 63) // 64 * 64
        assert off + size <= self.LIMIT, f"SBUF overflow allocating {name}: {off + size}"
        t = self.nc.alloc_sbuf_tensor_at(name, list(shape), dtype, offset=off)
        self.off = off + size
        self.peak = max(self.peak, self.off)
        return t

    def mark(self):
        return self.off

    def release(self, m):
        self.off = m


class PsTile:
    def __init__(self, mgr, bank, gen):
        self.mgr = mgr
        self.bank = bank
        self.gen = gen

    @property
    def buf(self):
        assert self.mgr.gen[self.bank] == self.gen, "PSUM tile used after its bank was re-allocated"
        return self.mgr.bufs[self.bank]

    @property
    def t(self):
        return self.mgr.tens[self.bank]

    def f32(self, cols=512):
        return self.t[:, 0:cols]

    def v3(self, a, b):
        return self.t[:, 0:a * b].rearrange("p (a b) -> p a b", a=a)

    def bf(self):
        return self.t[:, :].bitcast(BF16)

    def bf3(self, a, b):
        return self.bf()[:, 0:a * b].rearrange("p (a b) -> p a b", a=a)


class PsMgr:
    def __init__(self, nc=None, parent=None, banks=None):
        if parent is None:
            self.tens = [nc.alloc_psum_tensor(f"psb{i}", [P, 512], F32) for i in range(8)]
            self.bufs = [Buf(f"psb{i}") for i in range(8)]
            self.gen = [0] * 8
        else:
            self.tens, self.bufs, self.gen = parent.tens, parent.bufs, parent.gen
        self.banks = list(banks) if banks is not None else list(range(8))
        self.nxt = 0
        self.reserved = set()

    def alloc(self):
        for _ in range(len(self.banks)):
            b = self.banks[self.nxt]
            self.nxt = (self.nxt + 1) % len(self.banks)
            if b not in self.reserved:
                self.gen[b] += 1
                return PsTile(self, b, self.gen[b])
        raise RuntimeError("no PSUM bank")

    def reserve(self):
        t = self.alloc()
        self.reserved.add(t.bank)
        return t

    def unreserve(self, t):
        self.reserved.discard(t.bank)


def cst_layout(nl):
    lay = {}
    off = 0

    def add(name, n):
        nonlocal off
        lay[name] = (off, n)
        off += n

    add("eps", 1)
    add("one", 1)
    add("flag", 1)
    add("acoef", 4)
    add("bcoef", 4)
    add("gr128", 4)
    add("gr16", 4)
    add("drt", 16)
    for l in range(nl):
        add(f"pmn{l}", 8)
        add(f"pon{l}", 8)
        add(f"pfn{l}", 8)
        add(f"pofn{l}", 8)
        add(f"retnw{l}", 4)
        add(f"glanw{l}", 4)
        add(f"convw{l}", 44 * 3)
        add(f"convb{l}", 44)
    add("xiq", 512)
    add("kinv", 512)
    add("mt", 128)
    return lay, off


class Builder:
    def __init__(self, nl, dbg=None, final_layer=True):
        self.nl = nl
        self.final_layer = final_layer
        self.dbg = dbg
        self.nc = nc = bass.Bass("TRN2", target_bir_lowering=False)
        self.pg = Prog()
        self.sb = SBAlloc(nc)
        self.ps = PsMgr(nc)
        self.psA = PsMgr(parent=self.ps, banks=[0, 1])
        self.psB = PsMgr(parent=self.ps, banks=[2, 3])
        self.psO = [PsMgr(parent=self.ps, banks=[4, 5]), PsMgr(parent=self.ps, banks=[6, 7])]
        self.lay, self.ncst = cst_layout(nl)
        self.d_x = nc.dram_tensor("xT", [P, KC * NLOC], F32, kind="ExternalInput").ap()
        self.d_cst = nc.dram_tensor("cst", [P, self.ncst], F32, kind="ExternalInput").ap()
        self.d_w2b = nc.dram_tensor("w2b", [17, nl * 256], F32, kind="ExternalInput").ap()
        self.d_cb = nc.dram_tensor("cb", [P, 256], F32, kind="ExternalInput").ap()
        self.d_rope = nc.dram_tensor("rope", [P, 2 * NLOC], F32, kind="ExternalInput").ap()
        self.d_win = nc.dram_tensor("win", [nl, P, KC * INW], F32, kind="ExternalInput").ap()
        self.d_wout = nc.dram_tensor("wout", [nl, P, KC * D], F32, kind="ExternalInput").ap()
        self.d_wup = nc.dram_tensor("wup", [nl * 11, P, KC * 512], F32, kind="ExternalInput").ap()
        self.d_wdn = nc.dram_tensor("wdn", [nl * 8, P, NFC * 128], F32, kind="ExternalInput").ap()
        self.d_y = nc.dram_tensor("y", [P, KC * NLOC], F32, kind="ExternalOutput").ap()
        self.cc1_src = [nc.dram_tensor(f"cc1s{l}", [P, 776], F32) for l in range(nl)]
        self.cc1_dst = [nc.dram_tensor(f"cc1d{l}", [4 * P, 776], F32) for l in range(nl)]
        self.cc2_src = [nc.dram_tensor(f"cc2s{l}", [P, 16], F32) for l in range(nl)]
        self.cc2_dst = [nc.dram_tensor(f"cc2d{l}", [4 * P, 16], F32) for l in range(nl)]
        if dbg:
            self.d_dbg = nc.dram_tensor("dbg", [P, dbg], F32, kind="ExternalOutput").ap()

    def pe(self, fn, R, W):
        return self.pg.emit("pe", fn, R, W)

    def act(self, fn, R, W):
        return self.pg.emit("act", fn, R, W)

    def dve(self, fn, R, W):
        return self.pg.emit("dve", fn, R, W)

    def pool(self, fn, R, W):
        return self.pg.emit("pool", fn, R, W)

    def dma(self, q, fn, R, W, key, serialize=True, inc=16):
        return self.pg.emit(q, fn, R, W, dma_key=key, serialize=serialize, inc=inc)

    def c(self, name, i=0, n=1):
        o, _ = self.lay[name]
        return self.cst[:, o + i:o + i + n]

    def build(self):
        nc, sb = self.nc, self.sb
        nl = self.nl
        self.xT = sb.alloc("xT", [P, KC, NLOC], F32)
        self.mt_tiles = [(0, NPRE)] + [(NPRE + TT * i, TT) for i in range(NSEQ // TT)]
        self.XB = [Buf(f"X{i}") for i in range(len(self.mt_tiles))]
        self.cst = sb.alloc("cst", [P, self.ncst], F32)
        self.B_cst = Buf("cst")
        self.w2b = sb.alloc("w2b", [32, nl * 256], F32)
        self.cb = sb.alloc("cb", [P, 256], BF16)
        self.ident = self.cb[:, 0:128]
        self.ones = self.cb[:, 128:256]
        self.pay = sb.alloc("pay", [P, 776], F32)
        self.S_r = self.pay[:, 0:512].rearrange("p (h v) -> p h v", h=4)
        self.S_g = self.pay[:, 512:768].rearrange("p (h v) -> p h v", h=2)
        self.Dacc = self.pay[:, 768:770]
        self.B_Sr, self.B_Sg, self.B_D = Buf("S_r"), Buf("S_g"), Buf("Dacc")
        self.Sb_r = sb.alloc("Sb_r", [P, 4, 128], BF16)
        self.Sb_g = sb.alloc("Sb_g", [P, 2, 128], BF16)
        self.B_Sbr, self.B_Sbg = Buf("Sb_r"), Buf("Sb_g")
        self.hal = sb.alloc("hal", [P, 16], F32)
        self.B_hal = Buf("hal")
        self.gh = sb.alloc("gh", [P, 4, 16], F32)
        self.B_gh = Buf("gh")
        self.rt = sb.alloc("rt", [P, 512], F32)
        self.rstd = sb.alloc("rstd", [P, 512], F32)
        self.B_rt, self.B_rstd = Buf("rt"), Buf("rstd")
        self.tmpx, self.B_tmpx = self.rt, self.B_rt
        self.sc0 = (self.rt, self.B_rt, self.rstd, self.B_rstd)
        base_mark = sb.mark()

        self.dma("sp", lambda e: e.dma_start(out=self.cst[:, :], in_=self.d_cst[:, :]), [], [self.B_cst], "c0")
        self.dma("sp", lambda e: e.dma_start(out=self.w2b[0:17, :], in_=self.d_w2b[:, :]), [], [self.B_cst], "c1")
        self.dma("pool", lambda e: e.dma_start(out=self.cb[:, :], in_=self.d_cb[:, :]), [], [self.B_cst], "c2")
        xflat = self.xT[:, :, :].rearrange("p k t -> p (k t)")
        self.dma("sp", lambda e: e.dma_start(out=xflat, in_=self.d_x[:, :]), [], self.XB, "x")

        import os
        for l in range(nl):
            if int(os.environ.get("KSTAGE", "99")) >= 1:
                self.layer(l, base_mark)

        self.dma("sp", lambda e: e.dma_start(out=self.d_y[:, :], in_=xflat), self.XB, [], "y")
        self.B_fin = Buf("fin")
        fin_reads = []
        b = Buf("finy")
        b.lw = self.pg.key_last["y"]
        fin_reads.append(b)
        if self.dbg:
            b = Buf("findbg")
            if "dbg" in self.pg.key_last:
                b.lw = self.pg.key_last["dbg"]
                fin_reads.append(b)
        self.pg.emit("sp", lambda e: e.nop(), fin_reads, [self.B_fin])
        return self.finish()

    def finish(self):
        nc, pg = self.nc, self.pg
        counts = pg.finalize()
        self.counts = counts
        n_ep = {e: max(1, (counts[e] + EPOCH - 1) // EPOCH) for e in pg.ENGS}
        keys = sorted(pg.key_cnt.keys())
        import contextlib
        with contextlib.ExitStack() as st:
            eng_sems = {e: [st.enter_context(nc.semaphore(f"s_{e}{i}")) for i in range(n_ep[e])] for e in pg.ENGS}
            dma_sems = {k: st.enter_context(nc.semaphore(f"d_{k}")) for k in keys}
            block = st.enter_context(nc.Block())

            @block.tensor
            def _(e):
                pg.replay("pe", e, eng_sems, dma_sems)

            @block.scalar
            def _(e):
                pg.replay("act", e, eng_sems, dma_sems)

            @block.vector
            def _(e):
                pg.replay("dve", e, eng_sems, dma_sems)

            @block.gpsimd
            def _(e):
                pg.replay("pool", e, eng_sems, dma_sems)

            @block.sync
            def _(e):
                pg.replay("sp", e, eng_sems, dma_sems)
        return nc

    def dbg_dump(self, ap2d, bufs, col0, ncols, parts=P):
        if not self.dbg:
            return
        self.dma("sp", lambda e: e.dma_start(out=self.d_dbg[0:parts, col0:col0 + ncols], in_=ap2d), bufs, [], "dbg")

    def rmsnorm(self, src3, src_bufs, wname, dst3, dst_bufs, n, sq3, B_sq, ps=None, sc=None):
        ps = ps or self.ps
        sc = sc or self.sc0
        self.act(lambda e: e.activation(out=sq3[:, :, 0:n], in_=src3, func=AF.Square), src_bufs, [B_sq])
        pt = ps.alloc()
        for kc in range(KC):
            self.pe(lambda e, kc=kc, pt=pt: e.matmul(pt.f32(n), lhsT=self.ones, rhs=sq3[:, kc, 0:n],
                                                     start=(kc == 0), stop=(kc == KC - 1)),
                    [B_sq, self.B_cst], [pt.buf])
        self.rstd_from(pt, n, 1.0 / D, sc)
        rstd, B_rstd = sc[2], sc[3]
        if wname is None:
            rstd_bc = rstd[:, 0:n].unsqueeze(1).broadcast_to([P, KC, n])
            self.dve(lambda e: e.tensor_tensor(out=dst3[:, :, 0:n], in0=src3, in1=rstd_bc, op=ALU.mult),
                     src_bufs + [B_rstd], dst_bufs)
            return
        for kc in range(KC):
            self.dve(lambda e, kc=kc: e.scalar_tensor_tensor(out=dst3[:, kc, 0:n], in0=src3[:, kc, :],
                                                             scalar=self.c(wname, kc), in1=rstd[:, 0:n],
                                                             op0=ALU.mult, op1=ALU.mult),
                     src_bufs + [B_rstd, self.B_cst], dst_bufs)

    def rstd_from(self, pt, n, scale, sc=None):
        rt, B_rt, rstd, B_rstd = sc or self.sc0
        self.act(lambda e, pt=pt: e.activation(out=rstd[:, 0:n], in_=pt.f32(n), func=AF.Ln,
                                               scale=scale, bias=self.c("eps")),
                 [pt.buf, self.B_cst], [B_rstd])
        self.act(lambda e: e.activation(out=rstd[:, 0:n], in_=rstd[:, 0:n], func=AF.Exp, scale=-0.5), [B_rstd], [B_rstd])

    def post_norm_residual(self, m_sb3, B_m, sq3, B_sq, wname, t0, n, xbufs):
        pt = self.ps.alloc()
        for kc in range(KC):
            self.pe(lambda e, kc=kc, pt=pt: e.matmul(pt.f32(n), lhsT=self.ones, rhs=sq3[:, kc, 0:n],
                                                     start=(kc == 0), stop=(kc == KC - 1)),
                    [B_sq, self.B_cst], [pt.buf])
        self.rstd_from(pt, n, 1.0 / D)
        for kc in range(KC):
            self.dve(lambda e, kc=kc: e.scalar_tensor_tensor(out=self.tmpx[:, 0:n], in0=m_sb3[:, kc, 0:n],
                                                             scalar=self.c(wname, kc), in1=self.rstd[:, 0:n],
                                                             op0=ALU.mult, op1=ALU.mult),
                     [B_m, self.B_rstd, self.B_cst], [self.B_tmpx])
            self.dve(lambda e, kc=kc: e.tensor_tensor(out=self.xT[:, kc, t0:t0 + n], in0=self.xT[:, kc, t0:t0 + n],
                                                      in1=self.tmpx[:, 0:n], op=ALU.add),
                     xbufs + [self.B_tmpx], xbufs)

    def layer(self, l, base_mark):
        sb = self.sb
        sb.release(base_mark)
        if l == 0:
            self.win = sb.alloc("win", [P, KC, INW], BF16)
            self.wout = sb.alloc("wout", [P, KC, D], BF16)
            self.B_win = [Buf(f"win{k}") for k in range(KC)]
            self.B_winq = [Buf(f"winq{k}") for k in range(KC)]
            self.B_win2 = [Buf(f"win2{k}") for k in range(KC)]
            self.B_winq2 = [Buf(f"winq2{k}") for k in range(KC)]
            self.B_wout = [Buf(f"wout{k}") for k in range(KC)]
            self.alloc_mixer()
        def wdma(kc, c0, c1, bw, key):
            self.dma("pool", lambda e: e.dma_start(out=self.win[:, kc, c0:c1], in_=self.d_win[l, :, kc * INW + c0:kc * INW + c1]),
                     [], [bw], key)
        for kc in range(KC):
            wdma(kc, C_RK, C_RG, self.B_win[kc], f"winA{kc}")
            wdma(kc, C_GK, INW, self.B_win2[kc], f"winC{kc}")
        later = []
        for kc in range(KC):
            later.append(lambda kc=kc: wdma(kc, C_RQ, C_RK, self.B_winq[kc], f"winB{kc}"))
            later.append(lambda kc=kc: wdma(kc, C_RG, C_GK, self.B_winq2[kc], f"winD{kc}"))
        for kc in range(KC):
            later.append(lambda kc=kc: self.dma("pool", lambda e: e.dma_start(out=self.wout[:, kc, :], in_=self.d_wout[l, :, kc * D:(kc + 1) * D]),
                                                [], [self.B_wout[kc]], f"wout{kc}"))
        def fold(kc, c0, c1):
            bw = self.win_buf(kc, c0)
            self.dve(lambda e: e.tensor_scalar(out=self.win[:, kc, c0:c1], in0=self.win[:, kc, c0:c1], scalar1=self.c(f"pmn{l}", kc),
                                               scalar2=None, op0=ALU.mult), [bw, self.B_cst], [bw])
        for kc in range(KC):
            fold(kc, C_RK, C_RG)
            fold(kc, C_GK, INW)
        self.deferred_folds = later + [(lambda kc=kc, c0=c0, c1=c1: fold(kc, c0, c1)) for kc in range(KC) for (c0, c1) in ((C_RQ, C_RK), (C_RG, C_GK))]
        self.dve(lambda e: e.memset(self.pay[:, :], 0.0), [], [self.B_Sr, self.B_Sg, self.B_D])
        self.dve(lambda e: e.memset(self.Dacc, 1.0), [], [self.B_D])
        for S in self.sets:
            self.dve(lambda e, S=S: e.memset(S.gaT[:, :], 1.0), [], [S.B_ga])
        self.dve(lambda e: e.memset(self.kz[:, :, :], 0.0), [], [self.B_kz])
        import os
        STAGE = int(os.environ.get("KSTAGE", "99"))
        if STAGE < 3:
            return
        self.mixer_pass(l, False)
        while self.deferred_folds:
            self.deferred_folds.pop(0)()
        wno, _ = self.lay[f"retnw{l}"]
        wn_bc = self.cst[:, wno:wno + 8].unsqueeze(2).broadcast_to([P, KC, D])
        self.dve(lambda e: e.tensor_tensor(out=self.wout[:, :, :], in0=self.wout[:, :, :], in1=wn_bc, op=ALU.mult),
                 self.B_wout + [self.B_cst], self.B_wout)
        if STAGE < 4:
            return

        def mid():
            self.exchange_state(l)
            self.pg.barrier()
            self.dve(lambda e: e.memset(self.qz[:, :, :], 0.0), [], [self.B_qz])
        self.mixer_pass(l, True, pre=2, mid_hook=mid)
        if STAGE < 6:
            return
        halo_keys = self.halo_send(l)
        self.pg.barrier(exclude=halo_keys)
        sb.release(base_mark)
        self.ffn(l)
        self.dve(lambda e: e.tensor_scalar(out=self.xT[:, :, 0:NPRE], in0=self.xT[:, :, 0:NPRE], scalar1=self.c("flag"),
                                           scalar2=None, op0=ALU.mult), [self.XB[0], self.B_cst], [self.XB[0]])
        self.pg.barrier()

    def mixer_pass(self, l, with_out, pre=0, mid_hook=None):
        import os
        NT = len(self.mt_tiles)
        ORDER = os.environ.get("KORD", "a,b1,b2").split(",")
        K1 = int(os.environ.get("KK1", "6"))
        a_gen, a_idx, a_cnt = None, -1, 0
        b1_gen, b1_idx = None, -1
        b2_gen, b2_idx = None, -1
        a_ready = [False] * NT
        b1_done = [False] * NT
        b2_done = [False] * NT

        def done2(i):
            return i < 0 or b2_done[i]
        for i in range(pre):
            for _ in self.gen_A(l, i, with_out, self.sets[i % 2]):
                pass
            a_ready[i] = True
            a_idx = i
        if mid_hook is not None:
            mid_hook()
        while True:
            if a_gen is None and a_idx + 1 < NT and done2(a_idx + 1 - 2):
                a_idx, a_cnt = a_idx + 1, 0
                a_gen = self.gen_A(l, a_idx, with_out, self.sets[a_idx % 2])
            if b1_gen is None and b1_idx + 1 < NT and a_ready[b1_idx + 1] and done2(b1_idx + 1 - 2):
                b1_idx += 1
                b1_gen = self.gen_B(l, b1_idx, with_out, self.sets[b1_idx % 2])
            if b2_gen is None and b2_idx + 1 < NT and b1_done[b2_idx + 1]:
                b2_idx += 1
                b2_gen = self.gen_B2(l, b2_idx, with_out, self.sets[b2_idx % 2])
            if a_gen is None and b1_gen is None and b2_gen is None:
                if b2_idx + 1 >= NT:
                    break
                raise RuntimeError("mixer pipeline stalled")
            def step_a():
                nonlocal a_gen, a_cnt
                if a_gen is not None:
                    try:
                        next(a_gen)
                        if not with_out and getattr(self, "deferred_folds", None):
                            self.deferred_folds.pop(0)()
                        a_cnt += 1
                        if a_cnt >= K1:
                            a_ready[a_idx] = True
                    except StopIteration:
                        a_ready[a_idx] = True
                        a_gen = None

            def step_b1():
                nonlocal b1_gen
                if b1_gen is not None:
                    try:
                        next(b1_gen)
                    except StopIteration:
                        b1_done[b1_idx] = True
                        b1_gen = None

            def step_b2():
                nonlocal b2_gen
                if b2_gen is not None:
                    try:
                        next(b2_gen)
                    except StopIteration:
                        b2_done[b2_idx] = True
                        b2_gen = None
            for nm in ORDER:
                {"a": step_a, "b1": step_b1, "b2": step_b2}[nm]()

    def alloc_mixer(self):
        sb = self.sb
        A = sb.alloc

        class NS:
            pass
        self.sets = []
        for i in range(2):
            S = NS()
            S.ropeT = A(f"ropeT{i}", [P, 2, TT], F32)
            S.hT = A(f"hT{i}", [P, KC, TT], BF16)
            S.kr = A(f"kr{i}", [P, 4, TT], BF16)
            S.kg = A(f"kg{i}", [P, 2, TT], F32)
            S.gaT = A(f"gaT{i}", [32, TT], F32)
            S.vt = A(f"vt{i}", [P, 1024], BF16)
            for nm in ("rope", "hT", "kr", "qr", "kg", "qg", "gr", "ga", "vt", "mT"):
                setattr(S, "B_" + nm, Buf(f"{nm}{i}"))
            self.sets.append(S)
        self.sqn = A("sqn", [P, KC, TT], BF16)
        self.B_sqn = Buf("sqn")
        self.rp1 = A("rp1", [P, 4, TT], F32)
        self.rp2 = A("rp2", [P, 4, TT], F32)
        self.B_rp1, self.B_rp2 = Buf("rp1"), Buf("rp2")
        rstdA = A("rstdA", [P, TT], F32)
        B_rstdA = Buf("rstdA")
        self.scA = (rstdA, B_rstdA, rstdA, B_rstdA)
        self.ez = A("ez", [P, 256], F32)
        self.lsp = A("lsp", [P, 256], F32)
        self.B_ez, self.B_lsp = Buf("ez"), Buf("lsp")
        self.E1 = A("E1", [P, 2, CH], F32)
        self.E2 = A("E2", [P, 2, CH], F32)
        self.B_E1, self.B_E2 = Buf("E1"), Buf("E2")
        self.kz = A("kz", [P, 4, CH], BF16)
        self.B_qz, self.B_kz = Buf("qz"), Buf("kz")
        self.st = A("st", [P, 64], F32)
        self.B_st = Buf("st")
        self.dprime = A("dprime", [P, 8], F32)
        self.B_dprime = Buf("dprime")
        rstdB = A("rstdB", [P, TT], F32)
        B_rstdB = Buf("rstdB")
        self.scB = (rstdB, B_rstdB, rstdB, B_rstdB)
        for i, S in enumerate(self.sets):
            S.qr = A(f"qr{i}", [P, 4, TT], BF16)
            S.qg = A(f"qg{i}", [P, 2, TT], F32)
            S.gr = A(f"gr{i}", [P, 8, TT], BF16)
        m = sb.mark()
        self.gath = A("gath", [P, 4, 776], F32)
        self.B_gath = Buf("gath")
        self.ubuf = A("ubuf", [P, 768], F32)
        self.B_ubuf = Buf("ubuf")
        e1 = sb.mark()
        sb.release(m)
        self.ktok = A("ktok", [P, 8, 128], BF16)
        self.B_ktok = Buf("ktok")
        self.tmpS = A("tmpS", [P, 4, 128], F32)
        self.B_tmpS = Buf("tmpS")
        for i, S in enumerate(self.sets):
            S.mT = A(f"mT{i}", [P, KC, TT], BF16)
        self.qz = A("qz", [P, 4, CH], BF16)
        self.A_r = A("A_r", [P, 4, CH], BF16)
        self.A_g = A("A_g", [P, 4, CH], BF16)
        self.B_Ar, self.B_Ag = Buf("A_r"), Buf("A_g")
        self.sqo = A("sqo", [P, 8, 128], BF16)
        self.B_sqo = Buf("sqo")
        self.tmp4 = self.sqo[:, :, :].rearrange("p a b -> p (a b)").bitcast(F32).rearrange("p (a b) -> p a b", a=4)
        self.on = A("on", [P, 8, 128], BF16)
        self.B_on = Buf("on")
        self.sqm = A("sqm", [P, KC, TT], BF16)
        self.B_sqm = Buf("sqm")
        sb.release(max(sb.mark(), e1))

    def win_buf(self, kc, col):
        if col < C_RK:
            return self.B_winq[kc]
        if col < C_RG:
            return self.B_win[kc]
        if col < C_GK:
            return self.B_winq2[kc]
        return self.B_win2[kc]

    def fm_proj(self, S, cols, m, n):
        pt = self.psA.alloc()
        v = pt.v3(4, TT)
        for j, col0 in enumerate(cols):
            for kc in range(KC):
                bw = self.win_buf(kc, col0)
                self.pe(lambda e, kc=kc, j=j, col0=col0: e.matmul(v[0:m, j, 0:n], lhsT=self.win[:, kc, col0:col0 + m],
                                                                  rhs=S.hT[:, kc, 0:n], start=(kc == 0), stop=(kc == KC - 1)),
                        [bw, S.B_hT], [pt.buf])
        return pt, v

    def rope_evac(self, S, pt, v, n, dst3, B_dst, dec_name):
        c_bc = S.ropeT[:, 0, 0:n].unsqueeze(1).broadcast_to([P, 4, n])
        self.dve(lambda e: e.tensor_tensor(out=self.rp1[:, :, 0:n], in0=v[:, :, 0:n], in1=c_bc, op=ALU.mult),
                 [pt.buf, S.B_rope], [self.B_rp1])
        for lo, hi in ((0, 64), (64, 0)):
            s_bc = S.ropeT[lo:lo + 64, 1, 0:n].unsqueeze(1).broadcast_to([64, 4, n])
            self.dve(lambda e, lo=lo, hi=hi, s_bc=s_bc: e.tensor_tensor(out=self.rp2[lo:lo + 64, :, 0:n], in0=v[hi:hi + 64, :, 0:n],
                                                                       in1=s_bc, op=ALU.mult),
                     [pt.buf, S.B_rope], [self.B_rp2])
        self.dve(lambda e: e.tensor_tensor(out=self.rp1[:, :, 0:n], in0=self.rp1[:, :, 0:n], in1=self.rp2[:, :, 0:n], op=ALU.add),
                 [self.B_rp1, self.B_rp2], [self.B_rp1])
        o, _ = self.lay[dec_name]
        dec = self.cst[:, o:o + 512].rearrange("p (h t) -> p h t", h=4)[:, :, 0:n]
        self.dve(lambda e: e.tensor_tensor(out=dst3, in0=self.rp1[:, :, 0:n], in1=dec, op=ALU.mult),
                 [self.B_rp1, self.B_cst], [B_dst])

    def gen_A(self, l, ti, with_out, S):
        t0, n = self.mt_tiles[ti]
        XB = [self.XB[ti]]
        self.dma("sp", lambda e: e.dma_start(out=S.ropeT[:, 0, 0:n], in_=self.d_rope[:, t0:t0 + n]),
                 [], [S.B_rope], "rope0")
        self.dma("sp", lambda e: e.dma_start(out=S.ropeT[:, 1, 0:n], in_=self.d_rope[:, NLOC + t0:NLOC + t0 + n]),
                 [], [S.B_rope], "rope1")
        self.rmsnorm(self.xT[:, :, t0:t0 + n], XB, None, S.hT, [S.B_hT], n, self.sqn, self.B_sqn, ps=self.psA, sc=self.scA)
        yield
        pt, v = self.fm_proj(S, [C_GA], 16, n)
        if GATE_IN_ON_DVE:
            self.dve(lambda e, v=v: e.tensor_copy(out=S.gaT[0:16, 0:n], in_=v[0:16, 0, 0:n]), [pt.buf], [S.B_ga])
        else:
            self.act(lambda e, v=v: e.activation(out=S.gaT[0:16, 0:n], in_=v[0:16, 0, 0:n], func=AF.Copy),
                     [pt.buf], [S.B_ga])
        yield
        pt, v = self.fm_proj(S, [C_GK, C_GK + 128], 128, n)
        if GATE_IN_ON_DVE:
            self.dve(lambda e, v=v: e.tensor_copy(out=S.kg[:, :, 0:n], in_=v[:, 0:2, 0:n]), [pt.buf], [S.B_kg])
        else:
            self.act(lambda e, v=v: e.activation(out=S.kg[:, :, 0:n], in_=v[:, 0:2, 0:n], func=AF.Copy), [pt.buf], [S.B_kg])
        yield
        pt, v = self.fm_proj(S, [C_RK + h * 128 for h in range(4)], 128, n)
        self.rope_evac(S, pt, v, n, S.kr[:, :, 0:n], S.B_kr, "kinv")
        yield
        for half, col in ((0, C_RV), (1, C_GV)):
            pt = self.psA.alloc()
            for kc in range(KC):
                self.pe(lambda e, kc=kc, pt=pt, col=col: e.matmul(
                    pt.t[0:n, 0:512], lhsT=S.hT[:, kc, 0:n], rhs=self.win[:, kc, col:col + 512],
                    start=(kc == 0), stop=(kc == KC - 1)), [S.B_hT, self.win_buf(kc, col)], [pt.buf])
            if V_ON_DVE:
                self.dve(lambda e, pt=pt, half=half: e.tensor_copy(out=S.vt[0:n, half * 512:(half + 1) * 512], in_=pt.t[0:n, 0:512]),
                         [pt.buf], [S.B_vt])
            else:
                self.act(lambda e, pt=pt, half=half: e.activation(
                    out=S.vt[0:n, half * 512:(half + 1) * 512], in_=pt.t[0:n, 0:512], func=AF.Copy),
                    [pt.buf], [S.B_vt])
            yield
        if with_out:
            pt, v = self.fm_proj(S, [C_RQ + h * 128 for h in range(4)], 128, n)
            self.rope_evac(S, pt, v, n, S.qr[:, :, 0:n], S.B_qr, "xiq")
            yield
            pt, v = self.fm_proj(S, [C_GQ, C_GQ + 128], 128, n)
            self.act(lambda e, v=v: e.mul(out=S.qg[:, :, 0:n], in_=v[:, 0:2, 0:n], mul=0.125), [pt.buf], [S.B_qg])
            yield
            for g4 in range(2):
                base = C_RG if g4 == 0 else C_GG
                pt, v = self.fm_proj(S, [base + h * 128 for h in range(4)], 128, n)
                self.act(lambda e, v=v, g4=g4: e.activation(out=S.gr[:, 4 * g4:4 * g4 + 4, 0:n], in_=v[:, :, 0:n], func=AF.Silu),
                         [pt.buf], [S.B_gr])
            yield

    def gen_B(self, l, ti, with_out, S):
        import os
        SUB = int(os.environ.get("KM2SUB", "99")) if with_out else 99
        if SUB < 2:
            return
        t0, cn = self.mt_tiles[ti]
        is_pre = (ti == 0)
        ps = self.psB
        Bv = S.B_vt
        pz = ps.alloc()
        self.pe(lambda e: e.matmul(pz.t[0:cn, 0:256], lhsT=S.gaT[0:17, 0:cn], rhs=self.w2b[0:17, l * 256:(l + 1) * 256],
                                   start=True, stop=True), [S.B_ga, self.B_cst], [pz.buf])
        self.act(lambda e: e.activation(out=self.ez[0:cn, :], in_=pz.t[0:cn, 0:256], func=AF.Exp, scale=-1.0),
                 [pz.buf], [self.B_ez])
        self.act(lambda e: e.activation(out=self.lsp[0:cn, :], in_=self.ez[0:cn, :], func=AF.Ln, bias=self.c("one")[0:cn, :]),
                 [self.B_ez, self.B_cst], [self.B_lsp])
        if is_pre:
            self.dve(lambda e: e.tensor_scalar(out=self.lsp[0:cn, :], in0=self.lsp[0:cn, :], scalar1=self.c("flag")[0:cn, :],
                                               scalar2=None, op0=ALU.mult), [self.B_lsp, self.B_cst], [self.B_lsp])
        yield
        mt = self.c("mt", 0, 128)
        pc3 = pz.t[:, 256:512].rearrange("p (a b) -> p a b", a=2)
        for hp in range(2):
            self.pe(lambda e, hp=hp: e.matmul(pc3[:, hp, 0:cn], lhsT=self.lsp[0:cn, hp * 128:(hp + 1) * 128], rhs=mt[0:cn, 0:cn],
                                              start=True, stop=True), [self.B_lsp, self.B_cst], [pz.buf])
        self.act(lambda e: e.activation(out=self.E2[:, :, 0:cn], in_=pc3[:, :, 0:cn], func=AF.Exp, scale=1.0 / GLA_TAU),
                 [pz.buf], [self.B_E2])
        self.act(lambda e: e.activation(out=self.E1[:, :, 0:cn], in_=pc3[:, :, 0:cn], func=AF.Exp, scale=-1.0 / GLA_TAU),
                 [pz.buf], [self.B_E1])
        yield
        kz4 = self.kz[:, :, :].rearrange("p (a b) t -> p a b t", b=2)
        for half in range(2):
            lo = 64 * half
            self.dve(lambda e, lo=lo, half=half: e.tensor_tensor(out=kz4[lo:lo + 64, :, half, 0:cn], in0=S.kg[lo:lo + 64, :, 0:cn],
                                                                 in1=self.E2[lo:lo + 64, :, 0:cn], op=ALU.mult),
                     [S.B_kg, self.B_E2], [self.B_kz])
        if with_out:
            qz4 = self.qz[:, :, :].rearrange("p (a b) t -> p a b t", b=2)
            for half in range(2):
                lo = 64 * half
                self.dve(lambda e, lo=lo, half=half: e.tensor_tensor(out=qz4[lo:lo + 64, :, half, 0:cn], in0=S.qg[lo:lo + 64, :, 0:cn],
                                                                     in1=self.E1[lo:lo + 64, :, 0:cn], op=ALU.mult),
                         [S.B_qg, self.B_E1], [self.B_qz])
        yield
        if with_out:
            pa = ps.alloc()
            pa3 = pa.v3(4, CH)
            for h in range(4):
                self.pe(lambda e, h=h: e.matmul(pa3[0:cn, h, 0:cn], lhsT=S.kr[:, h, 0:cn], rhs=S.qr[:, h, 0:cn],
                                                start=True, stop=True), [S.B_kr, S.B_qr], [pa.buf])
            mask = mt[0:cn, 0:cn].unsqueeze(1).broadcast_to([cn, 4, cn])
            self.dve(lambda e: e.tensor_tensor(out=self.A_r[0:cn, :, 0:cn], in0=pa3[0:cn, :, 0:cn], in1=mask, op=ALU.mult),
                     [pa.buf, self.B_cst], [self.B_Ar])
            yield
            pg_ = ps.alloc()
            pg3 = pg_.v3(4, CH)
            for h in range(4):
                self.pe(lambda e, h=h: e.matmul(pg3[0:cn, h, 0:cn], lhsT=self.kz[:, h, 0:cn], rhs=self.qz[:, h, 0:cn],
                                                start=True, stop=True), [self.B_kz, self.B_qz], [pg_.buf])
            self.dve(lambda e: e.tensor_tensor(out=self.A_g[0:cn, :, 0:cn], in0=pg3[0:cn, :, 0:cn], in1=mask, op=ALU.mult),
                     [pg_.buf, self.B_cst], [self.B_Ag])
            yield
            po_r = self.psO[ti % 2].alloc()
            por3 = po_r.v3(4, 128)
            for h in range(4):
                self.pe(lambda e, h=h: e.matmul(por3[0:cn, h, :], lhsT=self.A_r[0:cn, h, 0:cn], rhs=S.vt[0:cn, h * 128:(h + 1) * 128],
                                                start=True, stop=False), [self.B_Ar, Bv], [po_r.buf])
                self.pe(lambda e, h=h: e.matmul(por3[0:cn, h, :], lhsT=S.qr[:, h, 0:cn], rhs=self.Sb_r[:, h, :],
                                                start=False, stop=True), [S.B_qr, self.B_Sbr], [po_r.buf])
            yield
            po_g = self.psO[ti % 2].alloc()
            pog3 = po_g.v3(4, 128)
            S.po_r, S.po_g = po_r, po_g
            for h in range(4):
                self.pe(lambda e, h=h: e.matmul(pog3[0:cn, h, :], lhsT=self.A_g[0:cn, h, 0:cn],
                                                rhs=S.vt[0:cn, 512 + h * 128:512 + (h + 1) * 128],
                                                start=True, stop=False), [self.B_Ag, Bv], [po_g.buf])
                self.pe(lambda e, h=h: e.matmul(pog3[0:cn, h, :], lhsT=self.qz[:, h, 0:cn],
                                                rhs=self.Sb_g[:, h // 2, :], start=False, stop=True),
                        [self.B_qz, self.B_Sbg], [po_g.buf])
            yield
        pk = ps.alloc()
        pk3 = pk.bf3(8, 128)
        for h in range(4):
            self.pe(lambda e, h=h: e.transpose(pk3[0:cn, h, :], S.kr[:, h, 0:cn], self.ident), [S.B_kr, self.B_cst], [pk.buf])
        for h in range(4):
            self.pe(lambda e, h=h: e.transpose(pk3[0:cn, 4 + h, :], self.kz[:, h, 0:cn], self.ident), [self.B_kz, self.B_cst], [pk.buf])
        if CHAIN_ON_DVE:
            self.dve(lambda e: e.tensor_copy(out=self.ktok[0:cn, :, :], in_=pk3[0:cn, :, :]), [pk.buf], [self.B_ktok])
        else:
            self.act(lambda e: e.activation(out=self.ktok[0:cn, :, :], in_=pk3[0:cn, :, :], func=AF.Copy), [pk.buf], [self.B_ktok])
        yield
        pkv = ps.alloc()
        pkv3 = pkv.v3(4, 128)
        for h in range(4):
            self.pe(lambda e, h=h: e.matmul(pkv3[:, h, :], lhsT=self.ktok[0:cn, h, :], rhs=S.vt[0:cn, h * 128:(h + 1) * 128],
                                            start=True, stop=True), [self.B_ktok, Bv], [pkv.buf])
        gname = "gr16" if is_pre else "gr128"
        go, _ = self.lay[gname]
        gbc = self.cst[:, go:go + 4].unsqueeze(2).broadcast_to([P, 4, 128])
        self.dve(lambda e: e.tensor_tensor(out=self.tmpS[:, 0:4, :], in0=self.S_r, in1=pkv3[:, :, :], op=ALU.add),
                 [self.B_Sr, pkv.buf], [self.B_tmpS])
        self.dve(lambda e: e.tensor_tensor(out=self.S_r, in0=self.tmpS[:, 0:4, :], in1=gbc, op=ALU.mult),
                 [self.B_tmpS, self.B_cst], [self.B_Sr])
        if with_out:
            if CHAIN_ON_DVE:
                self.dve(lambda e: e.tensor_copy(out=self.Sb_r[:, :, :], in_=self.S_r), [self.B_Sr], [self.B_Sbr])
            else:
                self.act(lambda e: e.activation(out=self.Sb_r[:, :, :], in_=self.S_r, func=AF.Copy), [self.B_Sr], [self.B_Sbr])
        yield
        pkg = ps.alloc()
        pkg3 = pkg.v3(2, 128)
        for h in range(4):
            self.pe(lambda e, h=h: e.matmul(pkg3[:, h // 2, :], lhsT=self.ktok[0:cn, 4 + h, :],
                                            rhs=S.vt[0:cn, 512 + h * 128:512 + (h + 1) * 128],
                                            start=(h % 2 == 0), stop=(h % 2 == 1)), [self.B_ktok, Bv], [pkg.buf])
        self.dve(lambda e: e.tensor_tensor(out=self.tmpS[:, 0:2, :], in0=self.S_g, in1=pkg3[:, :, :], op=ALU.add),
                 [self.B_Sg, pkg.buf], [self.B_tmpS])
        for hp in range(2):
            self.dve(lambda e, hp=hp: e.tensor_scalar(out=self.S_g[:, hp, :], in0=self.tmpS[:, hp, :],
                                                      scalar1=self.E1[:, hp, cn - 1:cn], scalar2=None, op0=ALU.mult),
                     [self.B_tmpS, self.B_E1], [self.B_Sg])
        if with_out:
            if CHAIN_ON_DVE:
                self.dve(lambda e: e.tensor_copy(out=self.Sb_g[:, :, :], in_=self.S_g), [self.B_Sg], [self.B_Sbg])
            else:
                self.act(lambda e: e.activation(out=self.Sb_g[:, :, :], in_=self.S_g, func=AF.Copy), [self.B_Sg], [self.B_Sbg])
        else:
            self.dve(lambda e: e.tensor_tensor(out=self.Dacc, in0=self.Dacc, in1=self.E1[:, :, cn - 1], op=ALU.mult),
                     [self.B_D, self.B_E1], [self.B_D])
        yield
        return

    def gen_B2(self, l, ti, with_out, S):
        if not with_out:
            return
        SUB = 99
        t0, cn = self.mt_tiles[ti]
        ps = self.psA
        po_r, po_g = S.po_r, S.po_g
        por3, pog3 = po_r.v3(4, 128), po_g.v3(4, 128)
        st = self.st
        s1 = st[0:cn, 0:4]
        s2 = st[0:cn, 4:12]
        mean = st[0:cn, 12:16]
        msq = st[0:cn, 16:20]
        var = st[0:cn, 20:28]
        rtv = st[0:cn, 28:36]
        rsd = st[0:cn, 36:44]
        nmr = st[0:cn, 44:48]
        self.dve(lambda e: e.reduce_sum(out=s1, in_=por3[0:cn, :, :], axis=AX.X), [po_r.buf], [self.B_st])
        self.act(lambda e: e.activation(out=self.sqo[0:cn, 0:4, :], in_=por3[0:cn, :, :], func=AF.Square), [po_r.buf], [self.B_sqo])
        self.act(lambda e: e.activation(out=self.sqo[0:cn, 4:8, :], in_=pog3[0:cn, :, :], func=AF.Square), [po_g.buf], [self.B_sqo])
        yield
        self.dve(lambda e: e.reduce_sum(out=s2, in_=self.sqo[0:cn, :, :], axis=AX.X), [self.B_sqo], [self.B_st])
        self.dve(lambda e: e.tensor_tensor(out=msq, in0=s1, in1=s1, op=ALU.mult), [self.B_st], [self.B_st])
        self.dve(lambda e: e.scalar_tensor_tensor(out=s2[:, 0:4], in0=msq, scalar=-1.0 / 128, in1=s2[:, 0:4], op0=ALU.mult, op1=ALU.add),
                 [self.B_st], [self.B_st])
        yield
        self.act(lambda e: e.activation(out=rsd, in_=s2, func=AF.Ln, scale=1.0 / 128, bias=self.c("eps")[0:cn, :]), [self.B_st, self.B_cst], [self.B_st])
        self.act(lambda e: e.activation(out=rsd, in_=rsd, func=AF.Exp, scale=-0.5), [self.B_st], [self.B_st])
        self.dve(lambda e: e.tensor_scalar(out=mean, in0=s1, scalar1=1.0 / 128, scalar2=None, op0=ALU.mult), [self.B_st], [self.B_st])
        yield
        mean_bc = mean.unsqueeze(2).broadcast_to([cn, 4, 128])
        rsdr_bc = rsd[:, 0:4].unsqueeze(2).broadcast_to([cn, 4, 128])
        rsdg_bc = rsd[:, 4:8].unsqueeze(2).broadcast_to([cn, 4, 128])
        self.dve(lambda e: e.tensor_tensor(out=self.tmp4[0:cn, :, :], in0=por3[0:cn, :, :], in1=mean_bc, op=ALU.subtract),
                 [po_r.buf, self.B_st], [self.B_sqo])
        self.dve(lambda e: e.tensor_tensor(out=self.on[0:cn, 0:4, :], in0=self.tmp4[0:cn, :, :], in1=rsdr_bc, op=ALU.mult),
                 [self.B_sqo, self.B_st], [self.B_on])
        self.dve(lambda e: e.tensor_tensor(out=self.on[0:cn, 4:8, :], in0=pog3[0:cn, :, :], in1=rsdg_bc, op=ALU.mult),
                 [po_g.buf, self.B_st], [self.B_on])
        yield
        if SUB < 4:
            return
        pT = ps.alloc()
        pT3 = pT.bf3(8, CH)
        for h in range(8):
            self.pe(lambda e, h=h: e.transpose(pT3[:, h, 0:cn], self.on[0:cn, h, :], self.ident[0:cn, 0:cn]),
                    [self.B_on, self.B_cst], [pT.buf])
        self.dve(lambda e: e.tensor_tensor(out=S.mT[:, :, 0:cn], in0=pT3[:, :, 0:cn], in1=S.gr[:, :, 0:cn], op=ALU.mult),
                 [pT.buf, S.B_gr], [S.B_mT])
        yield
        if SUB < 5:
            return
        n = cn
        pts = [po_r, po_g]
        for oc in range(KC):
            pt = pts[oc // 4]
            v = pt.v3(4, TT)[:, oc % 4, 0:n]
            for kc in range(KC):
                self.pe(lambda e, kc=kc, v=v, oc=oc: e.matmul(v, lhsT=self.wout[:, kc, oc * 128:(oc + 1) * 128],
                                                              rhs=S.mT[:, kc, 0:n], start=(kc == 0), stop=(kc == KC - 1)),
                        [self.B_wout[kc], S.B_mT], [pt.buf])
            if oc % 4 == 3:
                self.act(lambda e, pt=pt, oc=oc: e.activation(out=self.sqm[:, oc - 3:oc + 1, 0:n], in_=pt.v3(4, TT)[:, :, 0:n], func=AF.Square),
                         [pt.buf], [self.B_sqm])
                yield
        pss = ps.alloc()
        for kc in range(KC):
            self.pe(lambda e, kc=kc: e.matmul(pss.f32(n), lhsT=self.ones, rhs=self.sqm[:, kc, 0:n],
                                              start=(kc == 0), stop=(kc == KC - 1)), [self.B_sqm, self.B_cst], [pss.buf])
        self.rstd_from(pss, n, 1.0 / D, self.scB)
        rstd, B_rstd = self.scB[2], self.scB[3]
        yield
        if SUB < 6:
            return
        xb = [self.XB[ti]]
        pno, _ = self.lay[f"pon{l}"]
        rstd_bc = rstd[:, 0:n].unsqueeze(1).broadcast_to([P, 4, n])
        for b4 in range(2):
            pt = pts[b4]
            pw_bc = self.cst[:, pno + 4 * b4:pno + 4 * b4 + 4].unsqueeze(2).broadcast_to([P, 4, n])
            self.dve(lambda e, pt=pt: e.tensor_tensor(out=self.tmp4[:, :, 0:n], in0=pt.v3(4, TT)[:, :, 0:n], in1=rstd_bc, op=ALU.mult),
                     [pt.buf, B_rstd], [self.B_sqo])
            self.dve(lambda e, pw_bc=pw_bc: e.tensor_tensor(out=self.tmp4[:, :, 0:n], in0=self.tmp4[:, :, 0:n], in1=pw_bc, op=ALU.mult),
                     [self.B_sqo, self.B_cst], [self.B_sqo])
            self.dve(lambda e, b4=b4: e.tensor_tensor(out=self.xT[:, 4 * b4:4 * b4 + 4, t0:t0 + n], in0=self.xT[:, 4 * b4:4 * b4 + 4, t0:t0 + n],
                                                      in1=self.tmp4[:, :, 0:n], op=ALU.add), xb + [self.B_sqo], xb)
            yield

    def exchange_state(self, l):
        nc = self.nc
        src, dst = self.cc1_src[l], self.cc1_dst[l]
        B_src, B_dst = Buf("cc1src"), Buf("cc1dst")
        self.dma("pool", lambda e: e.dma_start(out=src.ap()[:, :], in_=self.pay[:, :]), [self.B_Sr, self.B_Sg, self.B_D], [B_src], f"cc1a{l}")
        self.dma("pool", lambda e: e.collective_compute("AllGather", ALU.bypass, replica_groups=[[0, 1, 2, 3], [4, 5, 6, 7]],
                                                        ins=[src.ap().opt()], outs=[dst.ap().opt()]),
                 [B_src], [B_dst], f"cc1b{l}", inc=1)
        self.dma("pool", lambda e: e.dma_start(out=self.gath[:, :, :], in_=dst.ap().rearrange("(r p) f -> p r f", p=P)),
                 [B_dst], [self.B_gath], f"cc1c{l}")
        self.dve(lambda e: e.memset(self.pay[:, :], 0.0), [], [self.B_Sr, self.B_Sg, self.B_D])
        dro, _ = self.lay["drt"]
        for i in range(3):
            a_i = self.c("acoef", i)
            self.dve(lambda e, i=i, a_i=a_i: e.tensor_scalar(out=self.dprime[:, 0:4], in0=self.cst[:, dro + 4 * i:dro + 4 * i + 4],
                                                              scalar1=-1.0, scalar2=a_i, op0=ALU.add, op1=ALU.mult),
                     [self.B_cst], [self.B_dprime])
            self.dve(lambda e, i=i, a_i=a_i: e.tensor_scalar(out=self.dprime[:, 4:6], in0=self.gath[:, i, 768:770],
                                                              scalar1=-1.0, scalar2=a_i, op0=ALU.add, op1=ALU.mult),
                     [self.B_gath, self.B_cst], [self.B_dprime])
            self.dve(lambda e: e.tensor_scalar(out=self.dprime[:, 0:6], in0=self.dprime[:, 0:6], scalar1=1.0, scalar2=None, op0=ALU.add),
                     [self.B_dprime], [self.B_dprime])
            self.dve(lambda e, i=i, a_i=a_i: e.tensor_scalar(out=self.ubuf[:, :], in0=self.gath[:, i, 0:768], scalar1=a_i, scalar2=None,
                                                              op0=ALU.mult), [self.B_gath, self.B_cst], [self.B_ubuf])
            dr_bc = self.dprime[:, 0:4].unsqueeze(2).broadcast_to([P, 4, 128])
            dg_bc = self.dprime[:, 4:6].unsqueeze(2).broadcast_to([P, 2, 128])
            self.dve(lambda e, dr_bc=dr_bc: e.tensor_tensor(out=self.S_r, in0=self.S_r, in1=dr_bc, op=ALU.mult),
                     [self.B_Sr, self.B_dprime], [self.B_Sr])
            self.dve(lambda e, dg_bc=dg_bc: e.tensor_tensor(out=self.S_g, in0=self.S_g, in1=dg_bc, op=ALU.mult),
                     [self.B_Sg, self.B_dprime], [self.B_Sg])
            self.dve(lambda e: e.tensor_tensor(out=self.pay[:, 0:768], in0=self.pay[:, 0:768], in1=self.ubuf[:, :], op=ALU.add),
                     [self.B_Sr, self.B_Sg, self.B_ubuf], [self.B_Sr, self.B_Sg])
        self.act(lambda e: e.activation(out=self.Sb_r[:, :, :], in_=self.S_r, func=AF.Copy), [self.B_Sr], [self.B_Sbr])
        self.act(lambda e: e.activation(out=self.Sb_g[:, :, :], in_=self.S_g, func=AF.Copy), [self.B_Sg], [self.B_Sbg])

    def halo_send(self, l):
        src, dst = self.cc2_src[l], self.cc2_dst[l]
        B_src, B_dst = Buf("cc2src"), Buf("cc2dst")
        last = self.XB[-1]
        self.dve(lambda e: e.tensor_copy(out=self.hal[:, :].rearrange("p (k t) -> p k t", k=KC), in_=self.xT[:, :, NLOC - 2:NLOC]),
                 [last], [self.B_hal])
        self.dma("sp", lambda e: e.dma_start(out=src.ap()[:, :], in_=self.hal[:, :]), [self.B_hal], [B_src], f"cc2a{l}")
        self.dma("pool", lambda e: e.collective_compute("AllGather", ALU.bypass, replica_groups=[[0, 1, 2, 3], [4, 5, 6, 7]],
                                                        ins=[src.ap().opt()], outs=[dst.ap().opt()]),
                 [B_src], [B_dst], f"cc2b{l}", inc=1)
        self.dma("sp", lambda e: e.dma_start(out=self.gh[:, :, :], in_=dst.ap().rearrange("(r p) f -> p r f", p=P)),
                 [B_dst], [self.B_gh], f"cc2c{l}")
        return {f"cc2a{l}", f"cc2b{l}", f"cc2c{l}"}

    def halo_recv(self, l):
        self.dve(lambda e: e.tensor_scalar(out=self.hal[:, :], in0=self.gh[:, 0, :], scalar1=self.c("bcoef", 0), scalar2=None, op0=ALU.mult),
                 [self.B_gh, self.B_cst], [self.B_hal])
        for i in range(1, 4):
            self.dve(lambda e, i=i: e.scalar_tensor_tensor(out=self.hal[:, :], in0=self.gh[:, i, :], scalar=self.c("bcoef", i),
                                                           in1=self.hal[:, :], op0=ALU.mult, op1=ALU.add),
                     [self.B_gh, self.B_cst, self.B_hal], [self.B_hal])
        self.dve(lambda e: e.tensor_tensor(out=self.xT[:, :, NPRE - 2:NPRE], in0=self.xT[:, :, NPRE - 2:NPRE],
                                           in1=self.hal[:, :].rearrange("p (k t) -> p k t", k=KC), op=ALU.add),
                 [self.XB[0], self.B_hal], [self.XB[0]])

    def ffn(self, l):
        sb = self.sb
        A = sb.alloc
        NH = 1042
        act_ = A("act", [P, NFC, 1040], BF16)
        B_act = [Buf(f"act{i}") for i in range(NFC)]
        wsl = [A(f"wsl{i}", [P, 4096], BF16) for i in range(3)]
        B_wsl = [Buf(f"wsl{i}") for i in range(3)]
        uhalo = A("uhalo", [P, 44, 2], F32)
        B_uhalo = Buf("uhalo")
        m_c = sb.mark()
        gl0_ = A("gl0", [P, 512], F32)
        gl = [gl0_, gl0_]
        B_gl0_ = Buf("gl0")
        B_gl = [B_gl0_, B_gl0_]
        ca = [A(f"ca{i}", [P, 512], F32) for i in range(2)]
        cg = [A(f"cg{i}", [P, 512], F32) for i in range(2)]
        B_ca, B_cg = [Buf("ca0"), Buf("ca1")], [Buf("cg0"), Buf("cg1")]
        m_c2 = sb.mark()
        sb.release(m_c)
        cB = A("cB", [P, 44, 16], F32)
        tB = A("tB", [P, 44, 16], F32)
        glB = A("glB", [P, NFC, 16], F32)
        assert sb.mark() <= m_c2
        sb.release(m_c2)
        G_c = [B_gl0_, B_ca[0], B_ca[1], B_cg[0], B_cg[1]]
        m_u = sb.mark()
        sqf = A("sqf", [P, 2, 512], BF16)
        B_sqf = [Buf("sqf0"), Buf("sqf1")]
        tmp2 = [self.tmpx, A("tmpx2", [P, 512], F32)]
        B_tmp2 = [self.B_tmpx, Buf("tmpx2")]
        m_u2 = sb.mark()
        sb.release(m_u)
        upre = A("upre", [P, 44, 16], F32)
        assert sb.mark() <= m_u2
        sb.release(m_u2)
        G_u = [B_sqf[0], B_sqf[1], B_tmp2[1]]
        m1 = sb.mark()
        h2T = A("h2T", [P, KC, NH], BF16)
        sqn = A("sqn2", [P, KC, 512], BF16)
        ua = [A(f"ua{i}", [P, NH], F32) for i in range(2)]
        ug = [A(f"ug{i}", [P, NH], F32) for i in range(2)]
        sb.release(m1)
        f_sb = A("f_sb", [P, KC, 1040], F32)
        wo, _ = self.lay[f"convw{l}"]
        bo, _ = self.lay[f"convb{l}"]

        halves = [
            [(2, 16, [0], 0), (18, 512, [1, 2, 3, 4], 16), (530, 512, [5, 6, 7, 8], 528)],
            [(2, 512, [9, 10, 11, 12], 1040), (514, 512, [13, 14, 15, 16], 1552)],
        ]
        wcnt = [0]

        def next_slot():
            i = wcnt[0] % 3
            wcnt[0] += 1
            return i

        B_h2T, B_sqn = Buf("h2T"), Buf("sqn2")
        B_ua, B_ug = [Buf("ua0"), Buf("ua1")], [Buf("ug0"), Buf("ug1")]
        G_fsb = [B_h2T, B_sqn] + B_ua + B_ug
        last_layer = (l == self.nl - 1) and self.final_layer
        for hi, tiles in enumerate(halves):
            order = [t for t in tiles if t[2] != [0]] + [t for t in tiles if t[2] == [0]]
            for (co, n, xt, t0) in order:
                if xt == [0]:
                    self.halo_recv(l)
                self.rmsnorm(self.xT[:, :, t0:t0 + n], [self.XB[i] for i in xt], f"pfn{l}", h2T[:, :, co:co + n], [B_h2T], n, sqn, B_sqn)
            if hi == 0:
                for k in range(2):
                    self.dve(lambda e, k=k: e.memset(ua[k][:, 0:2], 0.0), [], [B_ua[k]])
                    self.dve(lambda e, k=k: e.memset(ug[k][:, 0:2], 0.0), [], [B_ug[k]])
            for j in range(11):
                si = next_slot()
                w = wsl[si]
                self.dma("pool", lambda e, j=j, w=w: e.dma_start(out=w[:, :], in_=self.d_wup[l * 11 + j, :, :]), [], [B_wsl[si]], f"wsl{si}")
                w3 = w[:, :].rearrange("p (k c) -> p k c", k=KC)
                for q in range(2):
                    i = 2 * j + q
                    k = i % 2
                    uab, ugb = ua[k], ug[k]
                    if hi == 1:
                        self.dve(lambda e, i=i, uab=uab: e.tensor_copy(out=uab[:, 0:2], in_=uhalo[:, i, :]), [B_uhalo], [B_ua[k]])
                        self.dve(lambda e, i=i, ugb=ugb: e.tensor_copy(out=ugb[:, 0:2], in_=uhalo[:, NFC + i, :]), [B_uhalo], [B_ug[k]])
                    for ti_, (co, n, xt, t0) in enumerate(tiles):
                        pa = self.ps.alloc()
                        pgt = self.ps.alloc()
                        for kc in range(KC):
                            self.pe(lambda e, kc=kc, pa=pa, co=co, n=n, q=q, w3=w3: e.matmul(
                                pa.f32(n), lhsT=w3[:, kc, q * 128:(q + 1) * 128], rhs=h2T[:, kc, co:co + n],
                                start=(kc == 0), stop=(kc == KC - 1)), [B_wsl[si], B_h2T], [pa.buf])
                        for kc in range(KC):
                            self.pe(lambda e, kc=kc, pgt=pgt, co=co, n=n, q=q, w3=w3: e.matmul(
                                pgt.f32(n), lhsT=w3[:, kc, 256 + q * 128:256 + (q + 1) * 128], rhs=h2T[:, kc, co:co + n],
                                start=(kc == 0), stop=(kc == KC - 1)), [B_wsl[si], B_h2T], [pgt.buf])
                        if xt == [0] and not last_layer:
                            self.act(lambda e, pa=pa, co=co, n=n, uab=uab: e.activation(out=uab[:, co:co + n], in_=pa.f32(n), func=AF.Copy), [pa.buf], [B_ua[k]])
                            self.act(lambda e, pgt=pgt, co=co, n=n, ugb=ugb: e.activation(out=ugb[:, co:co + n], in_=pgt.f32(n), func=AF.Copy), [pgt.buf], [B_ug[k]])
                            self.act(lambda e, pa=pa, i=i, n=n: e.activation(out=upre[:, i, 0:n], in_=pa.f32(n), func=AF.Copy), [pa.buf], G_u)
                            self.act(lambda e, pgt=pgt, i=i, n=n: e.activation(out=upre[:, NFC + i, 0:n], in_=pgt.f32(n), func=AF.Copy), [pgt.buf], G_u)
                            continue
                        if last_layer and xt == [0]:
                            self.act(lambda e, pa=pa, co=co, n=n, uab=uab: e.activation(out=uab[:, co:co + n], in_=pa.f32(n), func=AF.Copy), [pa.buf], [B_ua[k]])
                            self.act(lambda e, pgt=pgt, co=co, n=n, ugb=ugb: e.activation(out=ugb[:, co:co + n], in_=pgt.f32(n), func=AF.Copy), [pgt.buf], [B_ug[k]])
                            continue
                        gi = (i * len(tiles) + ti_) % 2
                        cab, cgb = ca[gi], cg[gi]
                        for (pt_, u, B_u, cdst, B_c, ch, second) in ((pa, uab, B_ua[k], cab, B_ca[gi], i, "dve"),
                                                                     (pgt, ugb, B_ug[k], cgb, B_cg[gi], NFC + i, "dve")):
                            w0 = self.cst[:, wo + ch * 3 + 0:wo + ch * 3 + 1]
                            w1 = self.cst[:, wo + ch * 3 + 1:wo + ch * 3 + 2]
                            w2 = self.cst[:, wo + ch * 3 + 2:wo + ch * 3 + 3]
                            bb = self.cst[:, bo + ch:bo + ch + 1]
                            self.act(lambda e, pt_=pt_, u=u, co=co, n=n: e.activation(out=u[:, co:co + n], in_=pt_.f32(n), func=AF.Copy),
                                     [pt_.buf], [B_u])
                            self.act(lambda e, pt_=pt_, cdst=cdst, n=n, w2=w2, bb=bb: e.activation(
                                out=cdst[:, 0:n], in_=pt_.f32(n), func=AF.Identity, scale=w2, bias=bb), [pt_.buf, self.B_cst], [B_c])
                            self.dve(lambda e, u=u, cdst=cdst, co=co, n=n, w1=w1: e.scalar_tensor_tensor(
                                out=cdst[:, 0:n], in0=u[:, co - 1:co - 1 + n], scalar=w1, in1=cdst[:, 0:n], op0=ALU.mult, op1=ALU.add),
                                [B_u, self.B_cst, B_c], [B_c])
                            self.pg.emit(second, lambda e, u=u, cdst=cdst, co=co, n=n, w0=w0: e.scalar_tensor_tensor(
                                out=cdst[:, 0:n], in0=u[:, co - 2:co - 2 + n], scalar=w0, in1=cdst[:, 0:n], op0=ALU.mult, op1=ALU.add),
                                [B_u, self.B_cst, B_c], [B_c])
                        self.act(lambda e, n=n, gi=gi, cab=cab: e.activation(out=gl[gi][:, 0:n], in_=cab[:, 0:n], func=AF.Gelu_apprx_tanh),
                                 [B_ca[gi]], [B_gl[gi]])
                        ac0 = co - 2
                        import os
                        self.pg.emit(os.environ.get("KGATE", "dve"), lambda e, i=i, ac0=ac0, n=n, gi=gi, cgb=cgb: e.tensor_tensor(
                            out=act_[:, i, ac0:ac0 + n], in0=gl[gi][:, 0:n], in1=cgb[:, 0:n], op=ALU.mult),
                            [B_gl[gi], B_cg[gi]], [B_act[i]])
                    if hi == 0:
                        self.dve(lambda e, i=i, uab=uab: e.tensor_copy(out=uhalo[:, i, :], in_=uab[:, NH - 2:NH]), [B_ua[k]], [B_uhalo])
                        self.dve(lambda e, i=i, ugb=ugb: e.tensor_copy(out=uhalo[:, NFC + i, :], in_=ugb[:, NH - 2:NH]), [B_ug[k]], [B_uhalo])
            if hi == 0 and not last_layer:
                wv = self.cst[:, wo:wo + 132].rearrange("p (c k) -> p c k", k=3)

                def wbc(tap, ncol):
                    return wv[:, :, tap].unsqueeze(2).broadcast_to([P, 44, ncol])
                b_bc = self.cst[:, bo:bo + 44].unsqueeze(2).broadcast_to([P, 44, 16])
                self.dve(lambda e: e.tensor_tensor(out=cB[:, :, :], in0=upre[:, :, :], in1=wbc(2, 16), op=ALU.mult), G_u + [self.B_cst], G_c)
                self.dve(lambda e: e.tensor_tensor(out=tB[:, :, 1:16], in0=upre[:, :, 0:15], in1=wbc(1, 15), op=ALU.mult), G_u + [self.B_cst], G_c)
                self.dve(lambda e: e.tensor_tensor(out=cB[:, :, 1:16], in0=cB[:, :, 1:16], in1=tB[:, :, 1:16], op=ALU.add), G_c, G_c)
                self.dve(lambda e: e.tensor_tensor(out=tB[:, :, 2:16], in0=upre[:, :, 0:14], in1=wbc(0, 14), op=ALU.mult), G_u + [self.B_cst], G_c)
                self.dve(lambda e: e.tensor_tensor(out=cB[:, :, 2:16], in0=cB[:, :, 2:16], in1=tB[:, :, 2:16], op=ALU.add), G_c, G_c)
                self.dve(lambda e: e.tensor_tensor(out=cB[:, :, :], in0=cB[:, :, :], in1=b_bc, op=ALU.add), G_c + [self.B_cst], G_c)
                self.act(lambda e: e.activation(out=glB[:, :, :], in_=cB[:, 0:NFC, :], func=AF.Gelu_apprx_tanh), G_c, G_c)
                self.dve(lambda e: e.tensor_tensor(out=act_[:, :, 0:16], in0=glB[:, :, :], in1=cB[:, NFC:2 * NFC, :], op=ALU.mult), G_c, B_act)
            if last_layer:
                tiles = [t for t in tiles if t[2] != [0]]
            sst = [self.ps.reserve() for _ in tiles]
            pend = None
            for oc in range(KC):
                si = next_slot()
                w = wsl[si]
                self.dma("pool", lambda e, oc=oc, w=w: e.dma_start(out=w[:, 0:NFC * 128], in_=self.d_wdn[l * 8 + oc, :, :]), [], [B_wsl[si]], f"wsl{si}")
                w3 = w[:, 0:NFC * 128].rearrange("p (k c) -> p k c", k=NFC)
                for ri, (co, n, xt, t0) in enumerate(tiles):
                    ac0 = co - 2
                    pt = self.ps.alloc()
                    for kc in range(NFC):
                        self.pe(lambda e, kc=kc, pt=pt, ac0=ac0, n=n, w3=w3: e.matmul(
                            pt.f32(n), lhsT=w3[:, kc, :], rhs=act_[:, kc, ac0:ac0 + n], start=(kc == 0), stop=(kc == NFC - 1)),
                            [B_wsl[si], B_act[kc]], [pt.buf])
                    sq_i = (oc * len(tiles) + ri) % 2
                    self.act(lambda e, pt=pt, oc=oc, ac0=ac0, n=n: e.mul(out=f_sb[:, oc, ac0:ac0 + n], in_=pt.f32(n), mul=self.c(f"pofn{l}", oc)),
                             [pt.buf, self.B_cst], G_fsb)
                    self.act(lambda e, pt=pt, sq_i=sq_i, n=n: e.activation(out=sqf[:, sq_i, 0:n], in_=pt.f32(n), func=AF.Square),
                             [pt.buf], [B_sqf[sq_i]])
                    if pend is not None:
                        self.pe(*pend)
                    pend = (lambda e, st_=sst[ri], sq_i=sq_i, n=n, oc=oc: e.matmul(st_.f32(n), lhsT=self.ones, rhs=sqf[:, sq_i, 0:n],
                                                                                    start=(oc == 0), stop=(oc == KC - 1)),
                            [B_sqf[sq_i], self.B_cst], [sst[ri].buf])
            if pend is not None:
                self.pe(*pend)
            for ri, (co, n, xt, t0) in enumerate(tiles):
                ac0 = co - 2
                self.rstd_from(sst[ri], n, 1.0 / D)
                xb = [self.XB[i] for i in xt]
                for kc in range(KC):
                    tk = kc % 2
                    self.dve(lambda e, kc=kc, ac0=ac0, n=n, tk=tk: e.tensor_tensor(
                        out=tmp2[tk][:, 0:n], in0=f_sb[:, kc, ac0:ac0 + n], in1=self.rstd[:, 0:n], op=ALU.mult),
                        G_fsb + [self.B_rstd], [B_tmp2[tk]])
                    self.pg.emit("pool" if kc % 2 == 0 else "dve",
                                 lambda e, kc=kc, t0=t0, n=n, tk=tk: e.tensor_tensor(out=self.xT[:, kc, t0:t0 + n], in0=self.xT[:, kc, t0:t0 + n],
                                                                                     in1=tmp2[tk][:, 0:n], op=ALU.add), xb + [B_tmp2[tk]], xb)
            for t in sst:
                self.ps.unreserve(t)


def _img(w):
    K, N = w.shape
    return np.ascontiguousarray(w.reshape(K // P, P, N).transpose(1, 0, 2).reshape(P, (K // P) * N))


def _host_consts(r, nl, layers, prm):
    lay, ncst = cst_layout(nl)
    cst = np.zeros((P, ncst), np.float32)

    def put(name, arr):
        o, n = lay[name]
        arr = np.asarray(arr, np.float32)
        cst[:, o:o + n] = arr.reshape(-1, n) if arr.ndim > 1 else arr[None, :]

    put("eps", [EPS])
    put("one", [1.0])
    put("flag", [1.0 if r == 0 else 0.0])
    put("acoef", [1.0 if i < r else 0.0 for i in range(4)])
    put("bcoef", [1.0 if i == r - 1 else 0.0 for i in range(4)])
    hh = np.arange(4, dtype=np.float64)
    log_g = np.log(1.0 - 2.0 ** (-5.0 - hh))
    put("gr128", np.exp(log_g * 128))
    put("gr16", np.exp(log_g * 16) if r == 0 else np.ones(4))
    drt = np.zeros((4, 4))
    for i in range(4):
        drt[i] = np.exp(log_g * (NLOC if i == 0 else NSEQ))
    put("drt", drt.reshape(-1))
    tt = np.arange(128, dtype=np.float64)
    put("xiq", np.exp(log_g[:, None] * (tt[None, :] + 1.0)).reshape(-1))
    put("kinv", (np.exp(-log_g[:, None] * (tt[None, :] + 1.0)) * (128.0 ** -0.5)).reshape(-1))
    mt = (np.arange(128)[None, :] >= np.arange(128)[:, None]).astype(np.float32)
    o, n = lay["mt"]
    cst[:, o:o + n] = mt
    for li, l in enumerate(layers):
        def fm(v):
            return np.asarray(v, np.float32).reshape(KC, P).T
        for nm, key in (("pmn", "pre_mix_norm"), ("pon", "post_mix_norm"), ("pfn", "pre_ffn_norm"), ("pofn", "post_ffn_norm")):
            o, n = lay[f"{nm}{li}"]
            cst[:, o:o + n] = fm(prm[key][l])
        o, n = lay[f"retnw{li}"]
        cst[:, o:o + n] = np.asarray(prm["ret_norm_w"][l], np.float32).reshape(4, P).T
        o, n = lay[f"glanw{li}"]
        cst[:, o:o + n] = np.asarray(prm["gla_norm_w"][l], np.float32).reshape(4, P).T
        cw = np.asarray(prm["ffn_conv_w"][l], np.float32)
        o, n = lay[f"convw{li}"]
        cst[:, o:o + n] = cw.reshape(3, 44, P).transpose(2, 1, 0).reshape(P, 44 * 3)
        cbv = np.asarray(prm["ffn_conv_b"][l], np.float32)
        o, n = lay[f"convb{li}"]
        cst[:, o:o + n] = cbv.reshape(44, P).T
    return cst


def _host_weights(layers, prm):
    nl = len(layers)
    win = np.stack([_img(np.asarray(prm["w_in"][l], np.float32)) for l in layers])
    wout = np.stack([_img(np.asarray(prm["w_out"][l], np.float32)) for l in layers])
    wup = np.zeros((nl * 11, P, KC * 512), np.float32)
    wdn = np.zeros((nl * 8, P, NFC * 128), np.float32)
    for li, l in enumerate(layers):
        up = np.asarray(prm["ffn_up"][l], np.float32)
        dn = np.asarray(prm["ffn_down"][l], np.float32)
        for j in range(11):
            cols = np.concatenate([np.arange(128 * (2 * j), 128 * (2 * j + 2)), DFF + np.arange(128 * (2 * j), 128 * (2 * j + 2))])
            wup[li * 11 + j] = _img(up[:, cols])
        for oc in range(8):
            wdn[li * 8 + oc] = _img(dn[:, oc * 128:(oc + 1) * 128])
    w2b = np.zeros((17, nl * 256), np.float32)
    for li, l in enumerate(layers):
        w2b[0:16, li * 256:(li + 1) * 256] = np.asarray(prm["gla_gate_w2"][l], np.float32)
        w2b[16, li * 256:(li + 1) * 256] = np.asarray(prm["gla_gate_b"][l], np.float32)
    return win, wout, wup, wdn, w2b


def _host_rope(r):
    half = 64
    inv = 10000.0 ** (-np.arange(half, dtype=np.float64) / half)
    pos = float(NSEQ * r) + np.arange(NLOC, dtype=np.float64)
    ang = pos[:, None] * inv[None, :]
    c = np.cos(ang).astype(np.float32).T
    s = np.sin(ang).astype(np.float32).T
    cosT = np.concatenate([c, c], axis=0)
    sinT = np.concatenate([-s, s], axis=0)
    return np.ascontiguousarray(np.concatenate([cosT, sinT], axis=1))


_CACHE = {}


def _get_program(nl, dbg=None, final=True):
    key = (nl, dbg, final)
    if key not in _CACHE:
        b = Builder(nl, dbg, final_layer=final)
        _CACHE[key] = (b.build(), b)
    return _CACHE[key]


def _run(xT_imgs, layers, prm, dbg=None):
    nl = len(layers)
    nc, b = _get_program(nl, dbg, final=(layers[-1] == prm["w_in"].shape[0] - 1))
    win, wout, wup, wdn, w2b = _host_weights(layers, prm)
    cbm = np.concatenate([np.eye(P, dtype=np.float32), np.ones((P, P), np.float32)], axis=1)
    in_maps = []
    for j in range(8):
        r = j % 4
        in_maps.append({
            "xT": xT_imgs[j], "cst": _host_consts(r, nl, layers, prm), "w2b": w2b, "cb": cbm, "rope": _host_rope(r),
            "win": win, "wout": wout, "wup": wup, "wdn": wdn,
        })
    res = run_bass_kernel_spmd(nc, in_maps, core_ids=list(range(8)))
    return res


LAUNCH_SPLIT = False


def kernel(x, meta_tokens, pre_mix_norm, w_in, gla_gate_w2, gla_gate_b, ret_norm_w, gla_norm_w,
           w_out, post_mix_norm, pre_ffn_norm, ffn_up, ffn_conv_w, ffn_conv_b, ffn_down, post_ffn_norm):
    prm = dict(pre_mix_norm=pre_mix_norm, w_in=w_in, gla_gate_w2=gla_gate_w2, gla_gate_b=gla_gate_b,
               ret_norm_w=ret_norm_w, gla_norm_w=gla_norm_w, w_out=w_out, post_mix_norm=post_mix_norm,
               pre_ffn_norm=pre_ffn_norm, ffn_up=ffn_up, ffn_conv_w=ffn_conv_w, ffn_conv_b=ffn_conv_b,
               ffn_down=ffn_down, post_ffn_norm=post_ffn_norm)
    prm = {k: np.asarray(v) for k, v in prm.items()}
    x = np.asarray(x, np.float32)
    meta = np.asarray(meta_tokens, np.float32)
    imgs = []
    for j in range(8):
        b, r = j // 4, j % 4
        xin = np.zeros((NLOC, D), np.float32)
        if r == 0:
            xin[0:NPRE] = meta
        xin[NPRE:] = x[b, NSEQ * r:NSEQ * (r + 1)]
        imgs.append(np.ascontiguousarray(xin.T.reshape(KC, P, NLOC).transpose(1, 0, 2).reshape(P, KC * NLOC)))
    nl_total = prm["w_in"].shape[0]
    if LAUNCH_SPLIT:
        for l in range(nl_total):
            res = _run(imgs, [l], prm)
            imgs = [np.ascontiguousarray(res.results[j]["y"]) for j in range(8)]
        outs = imgs
    else:
        res = _run(imgs, list(range(nl_total)), prm)
        outs = [res.results[j]["y"] for j in range(8)]
    out = np.zeros((2, 4 * NSEQ, D), np.float32)
    for j in range(8):
        b, r = j // 4, j % 4
        y = np.asarray(outs[j]).reshape(P, KC, NLOC)[:, :, NPRE:]
        out[b, NSEQ * r:NSEQ * (r + 1)] = y.transpose(2, 1, 0).reshape(NSEQ, D)
    return out
```

```python
import numpy as np
import concourse.bass as bass
import concourse.mybir as mybir
from concourse.bass_utils import run_bass_kernel_spmd

F32 = mybir.dt.float32
BF16 = mybir.dt.bfloat16
AF = mybir.ActivationFunctionType
ALU = mybir.AluOpType
AX = mybir.AxisListType

P = 128
D = 1024
KC = 8
NPRE = 16
NSEQ = 2048
NLOC = NPRE + NSEQ
TT = 128
CH = 128
INW = 3600
DFF = 2816
NFC = 22
EPS = 1e-6
GLA_TAU = 16.0
EPOCH = 6000
import os as _os
GATE_IN_ON_DVE = _os.environ.get("KGA", "dve") == "dve"
V_ON_DVE = _os.environ.get("KV", "act") == "dve"
CHAIN_ON_DVE = _os.environ.get("KCHAIN", "dve") == "dve"

C_RQ, C_RK, C_RV, C_RG, C_GQ, C_GK, C_GV, C_GG, C_GA = 0, 512, 1024, 1536, 2048, 2304, 2560, 3072, 3584


class Buf:
    __slots__ = ("name", "lw", "rd", "rd_dma")

    def __init__(self, name):
        self.name = name
        self.lw = None
        self.rd = {}
        self.rd_dma = []


class Ins:
    __slots__ = ("eng", "fn", "deps", "sig", "cnt", "is_dma", "key", "val", "inc")

    def __init__(self, eng, fn):
        self.eng = eng
        self.fn = fn
        self.deps = []
        self.sig = False
        self.cnt = 0
        self.is_dma = False
        self.key = None
        self.val = 0
        self.inc = 16


class Prog:
    ENGS = ("pe", "act", "dve", "pool", "sp")

    def __init__(self):
        self.streams = {e: [] for e in self.ENGS}
        self.key_cnt = {}
        self.key_last = {}
        self.bar = {}

    def barrier(self, exclude=()):
        deps = []
        for e in self.ENGS:
            for ins in reversed(self.streams[e]):
                if not ins.is_dma:
                    deps.append(ins)
                    break
        deps += [v for k, v in self.key_last.items() if k not in exclude]
        for e in self.ENGS:
            self.bar[e] = list(deps) + self.bar.get(e, [])

    def emit(self, eng, fn, reads=(), writes=(), dma_key=None, serialize=True, inc=16):
        ins = Ins(eng, fn)
        raw = set()
        oth = set()
        for b in reads:
            if b.lw is not None:
                raw.add(b.lw)
        for b in writes:
            if b.lw is not None:
                oth.add(b.lw)
            for r in b.rd.values():
                oth.add(r)
            for r in b.rd_dma:
                oth.add(r)
        if dma_key is not None:
            ins.is_dma = True
            ins.key = dma_key
            ins.inc = inc
            self.key_cnt[dma_key] = self.key_cnt.get(dma_key, 0) + inc
            ins.val = self.key_cnt[dma_key]
            if serialize and dma_key in self.key_last:
                oth.add(self.key_last[dma_key])
            self.key_last[dma_key] = ins
        deps = []
        for d in self.bar.pop(eng, []):
            if d.is_dma or d.eng != eng:
                if d not in raw and d not in oth:
                    deps.append(d)
        for d in raw | oth:
            if d is ins:
                continue
            if d.is_dma or ins.is_dma:
                deps.append(d)
            elif d.eng != eng:
                deps.append(d)
            else:
                if eng != "pe":
                    deps.append(d)
        for d in deps:
            if not d.is_dma:
                d.sig = True
        ins.deps = deps
        for b in reads:
            if ins.is_dma:
                b.rd_dma.append(ins)
            else:
                b.rd[eng] = ins
        for b in writes:
            b.lw = ins
            b.rd = {}
            b.rd_dma = []
        self.streams[eng].append(ins)
        return ins

    def finalize(self):
        for e in self.ENGS:
            c = 0
            for ins in self.streams[e]:
                if ins.is_dma:
                    continue
                if ins.sig:
                    c += 1
                    ins.cnt = c
        return {e: sum(1 for i in self.streams[e] if i.sig and not i.is_dma) for e in self.ENGS}

    def replay(self, eng, eobj, eng_sems, dma_sems):
        seen_cnt = {e: 0 for e in self.ENGS}
        seen_dma = {}
        for ins in self.streams[eng]:
            for d in ins.deps:
                if d.is_dma:
                    if seen_dma.get(d.key, 0) >= d.val:
                        continue
                    eobj.wait_ge(dma_sems[d.key], d.val)
                    seen_dma[d.key] = d.val
                else:
                    if seen_cnt[d.eng] >= d.cnt:
                        continue
                    ep = (d.cnt - 1) // EPOCH
                    eobj.wait_ge(eng_sems[d.eng][ep], d.cnt - ep * EPOCH)
                    seen_cnt[d.eng] = d.cnt
            bi = ins.fn(eobj)
            if ins.is_dma:
                bi.then_inc(dma_sems[ins.key], ins.inc)
            elif ins.sig:
                ep = (ins.cnt - 1) // EPOCH
                bi.then_inc(eng_sems[eng][ep], 1)


class SBAlloc:
    BASE = 16512
    LIMIT = 229376 - 64

    def __init__(self, nc):
        self.nc = nc
        self.off = self.BASE
        self.peak = self.off

    def alloc(self, name, shape, dtype):
        isz = 2 if dtype == BF16 else 4
        size = int(np.prod(shape[1:])) * isz
        off = (self.off + 63) // 64 * 64
        assert off + size <= self.LIMIT, f"SBUF overflow allocating {name}: {off + size}"
        t = self.nc.alloc_sbuf_tensor_at(name, list(shape), dtype, offset=off)
        self.off = off + size
        self.peak = max(self.peak, self.off)
        return t

    def mark(self):
        return self.off

    def release(self, m):
        self.off = m


class PsTile:
    def __init__(self, mgr, bank, gen):
        self.mgr = mgr
        self.bank = bank
        self.gen = gen

    @property
    def buf(self):
        assert self.mgr.gen[self.bank] == self.gen, "PSUM tile used after its bank was re-allocated"
        return self.mgr.bufs[self.bank]

    @property
    def t(self):
        return self.mgr.tens[self.bank]

    def f32(self, cols=512):
        return self.t[:, 0:cols]

    def v3(self, a, b):
        return self.t[:, 0:a * b].rearrange("p (a b) -> p a b", a=a)

    def bf(self):
        return self.t[:, :].bitcast(BF16)

    def bf3(self, a, b):
        return self.bf()[:, 0:a * b].rearrange("p (a b) -> p a b", a=a)


class PsMgr:
    def __init__(self, nc=None, parent=None, banks=None):
        if parent is None:
            self.tens = [nc.alloc_psum_tensor(f"psb{i}", [P, 512], F32) for i in range(8)]
            self.bufs = [Buf(f"psb{i}") for i in range(8)]
            self.gen = [0] * 8
        else:
            self.tens, self.bufs, self.gen = parent.tens, parent.bufs, parent.gen
        self.banks = list(banks) if banks is not None else list(range(8))
        self.nxt = 0
        self.reserved = set()

    def alloc(self):
        for _ in range(len(self.banks)):
            b = self.banks[self.nxt]
            self.nxt = (self.nxt + 1) % len(self.banks)
            if b not in self.reserved:
                self.gen[b] += 1
                return PsTile(self, b, self.gen[b])
        raise RuntimeError("no PSUM bank")

    def reserve(self):
        t = self.alloc()
        self.reserved.add(t.bank)
        return t

    def unreserve(self, t):
        self.reserved.discard(t.bank)


def cst_layout(nl):
    lay = {}
    off = 0

    def add(name, n):
        nonlocal off
        lay[name] = (off, n)
        off += n

    add("eps", 1)
    add("one", 1)
    add("flag", 1)
    add("acoef", 4)
    add("bcoef", 4)
    add("gr128", 4)
    add("gr16", 4)
    add("drt", 16)
    for l in range(nl):
        add(f"pmn{l}", 8)
        add(f"pon{l}", 8)
        add(f"pfn{l}", 8)
        add(f"pofn{l}", 8)
        add(f"retnw{l}", 4)
        add(f"glanw{l}", 4)
        add(f"convw{l}", 44 * 3)
        add(f"convb{l}", 44)
    add("xiq", 512)
    add("kinv", 512)
    add("mt", 128)
    return lay, off


class Builder:
    def __init__(self, nl, dbg=None, final_layer=True):
        self.nl = nl
        self.final_layer = final_layer
        self.dbg = dbg
        self.nc = nc = bass.Bass("TRN2", target_bir_lowering=False)
        self.pg = Prog()
        self.sb = SBAlloc(nc)
        self.ps = PsMgr(nc)
        self.psA = PsMgr(parent=self.ps, banks=[0, 1])
        self.psB = PsMgr(parent=self.ps, banks=[2, 3])
        self.psO = [PsMgr(parent=self.ps, banks=[4, 5]), PsMgr(parent=self.ps, banks=[6, 7])]
        self.lay, self.ncst = cst_layout(nl)
        self.d_x = nc.dram_tensor("xT", [P, KC * NLOC], F32, kind="ExternalInput").ap()
        self.d_cst = nc.dram_tensor("cst", [P, self.ncst], F32, kind="ExternalInput").ap()
        self.d_w2b = nc.dram_tensor("w2b", [17, nl * 256], F32, kind="ExternalInput").ap()
        self.d_cb = nc.dram_tensor("cb", [P, 256], F32, kind="ExternalInput").ap()
        self.d_rope = nc.dram_tensor("rope", [P, 2 * NLOC], F32, kind="ExternalInput").ap()
        self.d_win = nc.dram_tensor("win", [nl, P, KC * INW], F32, kind="ExternalInput").ap()
        self.d_wout = nc.dram_tensor("wout", [nl, P, KC * D], F32, kind="ExternalInput").ap()
        self.d_wup = nc.dram_tensor("wup", [nl * 11, P, KC * 512], F32, kind="ExternalInput").ap()
        self.d_wdn = nc.dram_tensor("wdn", [nl * 8, P, NFC * 128], F32, kind="ExternalInput").ap()
        self.d_y = nc.dram_tensor("y", [P, KC * NLOC], F32, kind="ExternalOutput").ap()
        self.cc1_src = [nc.dram_tensor(f"cc1s{l}", [P, 776], F32) for l in range(nl)]
        self.cc1_dst = [nc.dram_tensor(f"cc1d{l}", [4 * P, 776], F32) for l in range(nl)]
        self.cc2_src = [nc.dram_tensor(f"cc2s{l}", [P, 16], F32) for l in range(nl)]
        self.cc2_dst = [nc.dram_tensor(f"cc2d{l}", [4 * P, 16], F32) for l in range(nl)]
        if dbg:
            self.d_dbg = nc.dram_tensor("dbg", [P, dbg], F32, kind="ExternalOutput").ap()

    def pe(self, fn, R, W):
        return self.pg.emit("pe", fn, R, W)

    def act(self, fn, R, W):
        return self.pg.emit("act", fn, R, W)

    def dve(self, fn, R, W):
        return self.pg.emit("dve", fn, R, W)

    def pool(self, fn, R, W):
        return self.pg.emit("pool", fn, R, W)

    def dma(self, q, fn, R, W, key, serialize=True, inc=16):
        return self.pg.emit(q, fn, R, W, dma_key=key, serialize=serialize, inc=inc)

    def c(self, name, i=0, n=1):
        o, _ = self.lay[name]
        return self.cst[:, o + i:o + i + n]

    def build(self):
        nc, sb = self.nc, self.sb
        nl = self.nl
        self.xT = sb.alloc("xT", [P, KC, NLOC], F32)
        self.mt_tiles = [(0, NPRE)] + [(NPRE + TT * i, TT) for i in range(NSEQ // TT)]
        self.XB = [Buf(f"X{i}") for i in range(len(self.mt_tiles))]
        self.cst = sb.alloc("cst", [P, self.ncst], F32)
        self.B_cst = Buf("cst")
        self.w2b = sb.alloc("w2b", [32, nl * 256], F32)
        self.cb = sb.alloc("cb", [P, 256], BF16)
        self.ident = self.cb[:, 0:128]
        self.ones = self.cb[:, 128:256]
        self.pay = sb.alloc("pay", [P, 776], F32)
        self.S_r = self.pay[:, 0:512].rearrange("p (h v) -> p h v", h=4)
        self.S_g = self.pay[:, 512:768].rearrange("p (h v) -> p h v", h=2)
        self.Dacc = self.pay[:, 768:770]
        self.B_Sr, self.B_Sg, self.B_D = Buf("S_r"), Buf("S_g"), Buf("Dacc")
        self.Sb_r = sb.alloc("Sb_r", [P, 4, 128], BF16)
        self.Sb_g = sb.alloc("Sb_g", [P, 2, 128], BF16)
        self.B_Sbr, self.B_Sbg = Buf("Sb_r"), Buf("Sb_g")
        self.hal = sb.alloc("hal", [P, 16], F32)
        self.B_hal = Buf("hal")
        self.gh = sb.alloc("gh", [P, 4, 16], F32)
        self.B_gh = Buf("gh")
        self.rt = sb.alloc("rt", [P, 512], F32)
        self.rstd = sb.alloc("rstd", [P, 512], F32)
        self.B_rt, self.B_rstd = Buf("rt"), Buf("rstd")
        self.tmpx, self.B_tmpx = self.rt, self.B_rt
        self.sc0 = (self.rt, self.B_rt, self.rstd, self.B_rstd)
        base_mark = sb.mark()

        self.dma("sp", lambda e: e.dma_start(out=self.cst[:, :], in_=self.d_cst[:, :]), [], [self.B_cst], "c0")
        self.dma("sp", lambda e: e.dma_start(out=self.w2b[0:17, :], in_=self.d_w2b[:, :]), [], [self.B_cst], "c1")
        self.dma("pool", lambda e: e.dma_start(out=self.cb[:, :], in_=self.d_cb[:, :]), [], [self.B_cst], "c2")
        xflat = self.xT[:, :, :].rearrange("p k t -> p (k t)")
        self.dma("sp", lambda e: e.dma_start(out=xflat, in_=self.d_x[:, :]), [], self.XB, "x")

        import os
        for l in range(nl):
            if int(os.environ.get("KSTAGE", "99")) >= 1:
                self.layer(l, base_mark)

        self.dma("sp", lambda e: e.dma_start(out=self.d_y[:, :], in_=xflat), self.XB, [], "y")
        self.B_fin = Buf("fin")
        fin_reads = []
        b = Buf("finy")
        b.lw = self.pg.key_last["y"]
        fin_reads.append(b)
        if self.dbg:
            b = Buf("findbg")
            if "dbg" in self.pg.key_last:
                b.lw = self.pg.key_last["dbg"]
                fin_reads.append(b)
        self.pg.emit("sp", lambda e: e.nop(), fin_reads, [self.B_fin])
        return self.finish()

    def finish(self):
        nc, pg = self.nc, self.pg
        counts = pg.finalize()
        self.counts = counts
        n_ep = {e: max(1, (counts[e] + EPOCH - 1) // EPOCH) for e in pg.ENGS}
        keys = sorted(pg.key_cnt.keys())
        import contextlib
        with contextlib.ExitStack() as st:
            eng_sems = {e: [st.enter_context(nc.semaphore(f"s_{e}{i}")) for i in range(n_ep[e])] for e in pg.ENGS}
            dma_sems = {k: st.enter_context(nc.semaphore(f"d_{k}")) for k in keys}
            block = st.enter_context(nc.Block())

            @block.tensor
            def _(e):
                pg.replay("pe", e, eng_sems, dma_sems)

            @block.scalar
            def _(e):
                pg.replay("act", e, eng_sems, dma_sems)

            @block.vector
            def _(e):
                pg.replay("dve", e, eng_sems, dma_sems)

            @block.gpsimd
            def _(e):
                pg.replay("pool", e, eng_sems, dma_sems)

            @block.sync
            def _(e):
                pg.replay("sp", e, eng_sems, dma_sems)
        return nc

    def dbg_dump(self, ap2d, bufs, col0, ncols, parts=P):
        if not self.dbg:
            return
        self.dma("sp", lambda e: e.dma_start(out=self.d_dbg[0:parts, col0:col0 + ncols], in_=ap2d), bufs, [], "dbg")

    def rmsnorm(self, src3, src_bufs, wname, dst3, dst_bufs, n, sq3, B_sq, ps=None, sc=None):
        ps = ps or self.ps
        sc = sc or self.sc0
        self.act(lambda e: e.activation(out=sq3[:, :, 0:n], in_=src3, func=AF.Square), src_bufs, [B_sq])
        pt = ps.alloc()
        for kc in range(KC):
            self.pe(lambda e, kc=kc, pt=pt: e.matmul(pt.f32(n), lhsT=self.ones, rhs=sq3[:, kc, 0:n],
                                                     start=(kc == 0), stop=(kc == KC - 1)),
                    [B_sq, self.B_cst], [pt.buf])
        self.rstd_from(pt, n, 1.0 / D, sc)
        rstd, B_rstd = sc[2], sc[3]
        if wname is None:
            rstd_bc = rstd[:, 0:n].unsqueeze(1).broadcast_to([P, KC, n])
            self.dve(lambda e: e.tensor_tensor(out=dst3[:, :, 0:n], in0=src3, in1=rstd_bc, op=ALU.mult),
                     src_bufs + [B_rstd], dst_bufs)
            return
        for kc in range(KC):
            self.dve(lambda e, kc=kc: e.scalar_tensor_tensor(out=dst3[:, kc, 0:n], in0=src3[:, kc, :],
                                                             scalar=self.c(wname, kc), in1=rstd[:, 0:n],
                                                             op0=ALU.mult, op1=ALU.mult),
                     src_bufs + [B_rstd, self.B_cst], dst_bufs)

    def rstd_from(self, pt, n, scale, sc=None):
        rt, B_rt, rstd, B_rstd = sc or self.sc0
        self.act(lambda e, pt=pt: e.activation(out=rstd[:, 0:n], in_=pt.f32(n), func=AF.Ln,
                                               scale=scale, bias=self.c("eps")),
                 [pt.buf, self.B_cst], [B_rstd])
        self.act(lambda e: e.activation(out=rstd[:, 0:n], in_=rstd[:, 0:n], func=AF.Exp, scale=-0.5), [B_rstd], [B_rstd])

    def post_norm_residual(self, m_sb3, B_m, sq3, B_sq, wname, t0, n, xbufs):
        pt = self.ps.alloc()
        for kc in range(KC):
            self.pe(lambda e, kc=kc, pt=pt: e.matmul(pt.f32(n), lhsT=self.ones, rhs=sq3[:, kc, 0:n],
                                                     start=(kc == 0), stop=(kc == KC - 1)),
                    [B_sq, self.B_cst], [pt.buf])
        self.rstd_from(pt, n, 1.0 / D)
        for kc in range(KC):
            self.dve(lambda e, kc=kc: e.scalar_tensor_tensor(out=self.tmpx[:, 0:n], in0=m_sb3[:, kc, 0:n],
                                                             scalar=self.c(wname, kc), in1=self.rstd[:, 0:n],
                                                             op0=ALU.mult, op1=ALU.mult),
                     [B_m, self.B_rstd, self.B_cst], [self.B_tmpx])
            self.dve(lambda e, kc=kc: e.tensor_tensor(out=self.xT[:, kc, t0:t0 + n], in0=self.xT[:, kc, t0:t0 + n],
                                                      in1=self.tmpx[:, 0:n], op=ALU.add),
                     xbufs + [self.B_tmpx], xbufs)

    def layer(self, l, base_mark):
        sb = self.sb
        sb.release(base_mark)
        if l == 0:
            self.win = sb.alloc("win", [P, KC, INW], BF16)
            self.wout = sb.alloc("wout", [P, KC, D], BF16)
            self.B_win = [Buf(f"win{k}") for k in range(KC)]
            self.B_winq = [Buf(f"winq{k}") for k in range(KC)]
            self.B_win2 = [Buf(f"win2{k}") for k in range(KC)]
            self.B_winq2 = [Buf(f"winq2{k}") for k in range(KC)]
            self.B_wout = [Buf(f"wout{k}") for k in range(KC)]
            self.alloc_mixer()
        def wdma(kc, c0, c1, bw, key):
            self.dma("pool", lambda e: e.dma_start(out=self.win[:, kc, c0:c1], in_=self.d_win[l, :, kc * INW + c0:kc * INW + c1]),
                     [], [bw], key)
        for kc in range(KC):
            wdma(kc, C_RK, C_RG, self.B_win[kc], f"winA{kc}")
            wdma(kc, C_GK, INW, self.B_win2[kc], f"winC{kc}")
        later = []
        for kc in range(KC):
            later.append(lambda kc=kc: wdma(kc, C_RQ, C_RK, self.B_winq[kc], f"winB{kc}"))
            later.append(lambda kc=kc: wdma(kc, C_RG, C_GK, self.B_winq2[kc], f"winD{kc}"))
        for kc in range(KC):
            later.append(lambda kc=kc: self.dma("pool", lambda e: e.dma_start(out=self.wout[:, kc, :], in_=self.d_wout[l, :, kc * D:(kc + 1) * D]),
                                                [], [self.B_wout[kc]], f"wout{kc}"))
        def fold(kc, c0, c1):
            bw = self.win_buf(kc, c0)
            self.dve(lambda e: e.tensor_scalar(out=self.win[:, kc, c0:c1], in0=self.win[:, kc, c0:c1], scalar1=self.c(f"pmn{l}", kc),
                                               scalar2=None, op0=ALU.mult), [bw, self.B_cst], [bw])
        for kc in range(KC):
            fold(kc, C_RK, C_RG)
            fold(kc, C_GK, INW)
        self.deferred_folds = later + [(lambda kc=kc, c0=c0, c1=c1: fold(kc, c0, c1)) for kc in range(KC) for (c0, c1) in ((C_RQ, C_RK), (C_RG, C_GK))]
        self.dve(lambda e: e.memset(self.pay[:, :], 0.0), [], [self.B_Sr, self.B_Sg, self.B_D])
        self.dve(lambda e: e.memset(self.Dacc, 1.0), [], [self.B_D])
        for S in self.sets:
            self.dve(lambda e, S=S: e.memset(S.gaT[:, :], 1.0), [], [S.B_ga])
        self.dve(lambda e: e.memset(self.kz[:, :, :], 0.0), [], [self.B_kz])
        import os
        STAGE = int(os.environ.get("KSTAGE", "99"))
        if STAGE < 3:
            return
        self.mixer_pass(l, False)
        while self.deferred_folds:
            self.deferred_folds.pop(0)()
        wno, _ = self.lay[f"retnw{l}"]
        wn_bc = self.cst[:, wno:wno + 8].unsqueeze(2).broadcast_to([P, KC, D])
        self.dve(lambda e: e.tensor_tensor(out=self.wout[:, :, :], in0=self.wout[:, :, :], in1=wn_bc, op=ALU.mult),
                 self.B_wout + [self.B_cst], self.B_wout)
        if STAGE < 4:
            return

        def mid():
            self.exchange_state(l)
            self.pg.barrier()
            self.dve(lambda e: e.memset(self.qz[:, :, :], 0.0), [], [self.B_qz])
        self.mixer_pass(l, True, pre=2, mid_hook=mid)
        if STAGE < 6:
            return
        halo_keys = self.halo_send(l)
        self.pg.barrier(exclude=halo_keys)
        sb.release(base_mark)
        self.ffn(l)
        self.dve(lambda e: e.tensor_scalar(out=self.xT[:, :, 0:NPRE], in0=self.xT[:, :, 0:NPRE], scalar1=self.c("flag"),
                                           scalar2=None, op0=ALU.mult), [self.XB[0], self.B_cst], [self.XB[0]])
        self.pg.barrier()

    def mixer_pass(self, l, with_out, pre=0, mid_hook=None):
        import os
        NT = len(self.mt_tiles)
        ORDER = os.environ.get("KORD", "a,b1,b2").split(",")
        K1 = int(os.environ.get("KK1", "6"))
        a_gen, a_idx, a_cnt = None, -1, 0
        b1_gen, b1_idx = None, -1
        b2_gen, b2_idx = None, -1
        a_ready = [False] * NT
        b1_done = [False] * NT
        b2_done = [False] * NT

        def done2(i):
            return i < 0 or b2_done[i]
        for i in range(pre):
            for _ in self.gen_A(l, i, with_out, self.sets[i % 2]):
                pass
            a_ready[i] = True
            a_idx = i
        if mid_hook is not None:
            mid_hook()
        while True:
            if a_gen is None and a_idx + 1 < NT and done2(a_idx + 1 - 2):
                a_idx, a_cnt = a_idx + 1, 0
                a_gen = self.gen_A(l, a_idx, with_out, self.sets[a_idx % 2])
            if b1_gen is None and b1_idx + 1 < NT and a_ready[b1_idx + 1] and done2(b1_idx + 1 - 2):
                b1_idx += 1
                b1_gen = self.gen_B(l, b1_idx, with_out, self.sets[b1_idx % 2])
            if b2_gen is None and b2_idx + 1 < NT and b1_done[b2_idx + 1]:
                b2_idx += 1
                b2_gen = self.gen_B2(l, b2_idx, with_out, self.sets[b2_idx % 2])
            if a_gen is None and b1_gen is None and b2_gen is None:
                if b2_idx + 1 >= NT:
                    break
                raise RuntimeError("mixer pipeline stalled")
            def step_a():
                nonlocal a_gen, a_cnt
                if a_gen is not None:
                    try:
                        next(a_gen)
                        if not with_out and getattr(self, "deferred_folds", None):
                            self.deferred_folds.pop(0)()
                        a_cnt += 1
                        if a_cnt >= K1:
                            a_ready[a_idx] = True
                    except StopIteration:
                        a_ready[a_idx] = True
                        a_gen = None

            def step_b1():
                nonlocal b1_gen
                if b1_gen is not None:
                    try:
                        next(b1_gen)
                    except StopIteration:
                        b1_done[b1_idx] = True
                        b1_gen = None

            def step_b2():
                nonlocal b2_gen
                if b2_gen is not None:
                    try:
                        next(b2_gen)
                    except StopIteration:
                        b2_done[b2_idx] = True
                        b2_gen = None
            for nm in ORDER:
                {"a": step_a, "b1": step_b1, "b2": step_b2}[nm]()

    def alloc_mixer(self):
        sb = self.sb
        A = sb.alloc

        class NS:
            pass
        self.sets = []
        for i in range(2):
            S = NS()
            S.ropeT = A(f"ropeT{i}", [P, 2, TT], F32)
            S.hT = A(f"hT{i}", [P, KC, TT], BF16)
            S.kr = A(f"kr{i}", [P, 4, TT], BF16)
            S.kg = A(f"kg{i}", [P, 2, TT], F32)
            S.gaT = A(f"gaT{i}", [32, TT], F32)
            S.vt = A(f"vt{i}", [P, 1024], BF16)
            for nm in ("rope", "hT", "kr", "qr", "kg", "qg", "gr", "ga", "vt", "mT"):
                setattr(S, "B_" + nm, Buf(f"{nm}{i}"))
            self.sets.append(S)
        self.sqn = A("sqn", [P, KC, TT], BF16)
        self.B_sqn = Buf("sqn")
        self.rp1 = A("rp1", [P, 4, TT], F32)
        self.rp2 = A("rp2", [P, 4, TT], F32)
        self.B_rp1, self.B_rp2 = Buf("rp1"), Buf("rp2")
        rstdA = A("rstdA", [P, TT], F32)
        B_rstdA = Buf("rstdA")
        self.scA = (rstdA, B_rstdA, rstdA, B_rstdA)
        self.ez = A("ez", [P, 256], F32)
        self.lsp = A("lsp", [P, 256], F32)
        self.B_ez, self.B_lsp = Buf("ez"), Buf("lsp")
        self.E1 = A("E1", [P, 2, CH], F32)
        self.E2 = A("E2", [P, 2, CH], F32)
        self.B_E1, self.B_E2 = Buf("E1"), Buf("E2")
        self.kz = A("kz", [P, 4, CH], BF16)
        self.B_qz, self.B_kz = Buf("qz"), Buf("kz")
        self.st = A("st", [P, 64], F32)
        self.B_st = Buf("st")
        self.dprime = A("dprime", [P, 8], F32)
        self.B_dprime = Buf("dprime")
        rstdB = A("rstdB", [P, TT], F32)
        B_rstdB = Buf("rstdB")
        self.scB = (rstdB, B_rstdB, rstdB, B_rstdB)
        for i, S in enumerate(self.sets):
            S.qr = A(f"qr{i}", [P, 4, TT], BF16)
            S.qg = A(f"qg{i}", [P, 2, TT], F32)
            S.gr = A(f"gr{i}", [P, 8, TT], BF16)
        m = sb.mark()
        self.gath = A("gath", [P, 4, 776], F32)
        self.B_gath = Buf("gath")
        self.ubuf = A("ubuf", [P, 768], F32)
        self.B_ubuf = Buf("ubuf")
        e1 = sb.mark()
        sb.release(m)
        self.ktok = A("ktok", [P, 8, 128], BF16)
        self.B_ktok = Buf("ktok")
        self.tmpS = A("tmpS", [P, 4, 128], F32)
        self.B_tmpS = Buf("tmpS")
        for i, S in enumerate(self.sets):
            S.mT = A(f"mT{i}", [P, KC, TT], BF16)
        self.qz = A("qz", [P, 4, CH], BF16)
        self.A_r = A("A_r", [P, 4, CH], BF16)
        self.A_g = A("A_g", [P, 4, CH], BF16)
        self.B_Ar, self.B_Ag = Buf("A_r"), Buf("A_g")
        self.sqo = A("sqo", [P, 8, 128], BF16)
        self.B_sqo = Buf("sqo")
        self.tmp4 = self.sqo[:, :, :].rearrange("p a b -> p (a b)").bitcast(F32).rearrange("p (a b) -> p a b", a=4)
        self.on = A("on", [P, 8, 128], BF16)
        self.B_on = Buf("on")
        self.sqm = A("sqm", [P, KC, TT], BF16)
        self.B_sqm = Buf("sqm")
        sb.release(max(sb.mark(), e1))

    def win_buf(self, kc, col):
        if col < C_RK:
            return self.B_winq[kc]
        if col < C_RG:
            return self.B_win[kc]
        if col < C_GK:
            return self.B_winq2[kc]
        return self.B_win2[kc]

    def fm_proj(self, S, cols, m, n):
        pt = self.psA.alloc()
        v = pt.v3(4, TT)
        for j, col0 in enumerate(cols):
            for kc in range(KC):
                bw = self.win_buf(kc, col0)
                self.pe(lambda e, kc=kc, j=j, col0=col0: e.matmul(v[0:m, j, 0:n], lhsT=self.win[:, kc, col0:col0 + m],
                                                                  rhs=S.hT[:, kc, 0:n], start=(kc == 0), stop=(kc == KC - 1)),
                        [bw, S.B_hT], [pt.buf])
        return pt, v

    def rope_evac(self, S, pt, v, n, dst3, B_dst, dec_name):
        c_bc = S.ropeT[:, 0, 0:n].unsqueeze(1).broadcast_to([P, 4, n])
        self.dve(lambda e: e.tensor_tensor(out=self.rp1[:, :, 0:n], in0=v[:, :, 0:n], in1=c_bc, op=ALU.mult),
                 [pt.buf, S.B_rope], [self.B_rp1])
        for lo, hi in ((0, 64), (64, 0)):
            s_bc = S.ropeT[lo:lo + 64, 1, 0:n].unsqueeze(1).broadcast_to([64, 4, n])
            self.dve(lambda e, lo=lo, hi=hi, s_bc=s_bc: e.tensor_tensor(out=self.rp2[lo:lo + 64, :, 0:n], in0=v[hi:hi + 64, :, 0:n],
                                                                       in1=s_bc, op=ALU.mult),
                     [pt.buf, S.B_rope], [self.B_rp2])
        self.dve(lambda e: e.tensor_tensor(out=self.rp1[:, :, 0:n], in0=self.rp1[:, :, 0:n], in1=self.rp2[:, :, 0:n], op=ALU.add),
                 [self.B_rp1, self.B_rp2], [self.B_rp1])
        o, _ = self.lay[dec_name]
        dec = self.cst[:, o:o + 512].rearrange("p (h t) -> p h t", h=4)[:, :, 0:n]
        self.dve(lambda e: e.tensor_tensor(out=dst3, in0=self.rp1[:, :, 0:n], in1=dec, op=ALU.mult),
                 [self.B_rp1, self.B_cst], [B_dst])

    def gen_A(self, l, ti, with_out, S):
        t0, n = self.mt_tiles[ti]
        XB = [self.XB[ti]]
        self.dma("sp", lambda e: e.dma_start(out=S.ropeT[:, 0, 0:n], in_=self.d_rope[:, t0:t0 + n]),
                 [], [S.B_rope], "rope0")
        self.dma("sp", lambda e: e.dma_start(out=S.ropeT[:, 1, 0:n], in_=self.d_rope[:, NLOC + t0:NLOC + t0 + n]),
                 [], [S.B_rope], "rope1")
        self.rmsnorm(self.xT[:, :, t0:t0 + n], XB, None, S.hT, [S.B_hT], n, self.sqn, self.B_sqn, ps=self.psA, sc=self.scA)
        yield
        pt, v = self.fm_proj(S, [C_GA], 16, n)
        if GATE_IN_ON_DVE:
            self.dve(lambda e, v=v: e.tensor_copy(out=S.gaT[0:16, 0:n], in_=v[0:16, 0, 0:n]), [pt.buf], [S.B_ga])
        else:
            self.act(lambda e, v=v: e.activation(out=S.gaT[0:16, 0:n], in_=v[0:16, 0, 0:n], func=AF.Copy),
                     [pt.buf], [S.B_ga])
        yield
        pt, v = self.fm_proj(S, [C_GK, C_GK + 128], 128, n)
        if GATE_IN_ON_DVE:
            self.dve(lambda e, v=v: e.tensor_copy(out=S.kg[:, :, 0:n], in_=v[:, 0:2, 0:n]), [pt.buf], [S.B_kg])
        else:
            self.act(lambda e, v=v: e.activation(out=S.kg[:, :, 0:n], in_=v[:, 0:2, 0:n], func=AF.Copy), [pt.buf], [S.B_kg])
        yield
        pt, v = self.fm_proj(S, [C_RK + h * 128 for h in range(4)], 128, n)
        self.rope_evac(S, pt, v, n, S.kr[:, :, 0:n], S.B_kr, "kinv")
        yield
        for half, col in ((0, C_RV), (1, C_GV)):
            pt = self.psA.alloc()
            for kc in range(KC):
                self.pe(lambda e, kc=kc, pt=pt, col=col: e.matmul(
                    pt.t[0:n, 0:512], lhsT=S.hT[:, kc, 0:n], rhs=self.win[:, kc, col:col + 512],
                    start=(kc == 0), stop=(kc == KC - 1)), [S.B_hT, self.win_buf(kc, col)], [pt.buf])
            if V_ON_DVE:
                self.dve(lambda e, pt=pt, half=half: e.tensor_copy(out=S.vt[0:n, half * 512:(half + 1) * 512], in_=pt.t[0:n, 0:512]),
                         [pt.buf], [S.B_vt])
            else:
                self.act(lambda e, pt=pt, half=half: e.activation(
                    out=S.vt[0:n, half * 512:(half + 1) * 512], in_=pt.t[0:n, 0:512], func=AF.Copy),
                    [pt.buf], [S.B_vt])
            yield
        if with_out:
            pt, v = self.fm_proj(S, [C_RQ + h * 128 for h in range(4)], 128, n)
            self.rope_evac(S, pt, v, n, S.qr[:, :, 0:n], S.B_qr, "xiq")
            yield
            pt, v = self.fm_proj(S, [C_GQ, C_GQ + 128], 128, n)
            self.act(lambda e, v=v: e.mul(out=S.qg[:, :, 0:n], in_=v[:, 0:2, 0:n], mul=0.125), [pt.buf], [S.B_qg])
            yield
            for g4 in range(2):
                base = C_RG if g4 == 0 else C_GG
                pt, v = self.fm_proj(S, [base + h * 128 for h in range(4)], 128, n)
                self.act(lambda e, v=v, g4=g4: e.activation(out=S.gr[:, 4 * g4:4 * g4 + 4, 0:n], in_=v[:, :, 0:n], func=AF.Silu),
                         [pt.buf], [S.B_gr])
            yield

    def gen_B(self, l, ti, with_out, S):
        import os
        SUB = int(os.environ.get("KM2SUB", "99")) if with_out else 99
        if SUB < 2:
            return
        t0, cn = self.mt_tiles[ti]
        is_pre = (ti == 0)
        ps = self.psB
        Bv = S.B_vt
        pz = ps.alloc()
        self.pe(lambda e: e.matmul(pz.t[0:cn, 0:256], lhsT=S.gaT[0:17, 0:cn], rhs=self.w2b[0:17, l * 256:(l + 1) * 256],
                                   start=True, stop=True), [S.B_ga, self.B_cst], [pz.buf])
        self.act(lambda e: e.activation(out=self.ez[0:cn, :], in_=pz.t[0:cn, 0:256], func=AF.Exp, scale=-1.0),
                 [pz.buf], [self.B_ez])
        self.act(lambda e: e.activation(out=self.lsp[0:cn, :], in_=self.ez[0:cn, :], func=AF.Ln, bias=self.c("one")[0:cn, :]),
                 [self.B_ez, self.B_cst], [self.B_lsp])
        if is_pre:
            self.dve(lambda e: e.tensor_scalar(out=self.lsp[0:cn, :], in0=self.lsp[0:cn, :], scalar1=self.c("flag")[0:cn, :],
                                               scalar2=None, op0=ALU.mult), [self.B_lsp, self.B_cst], [self.B_lsp])
        yield
        mt = self.c("mt", 0, 128)
        pc3 = pz.t[:, 256:512].rearrange("p (a b) -> p a b", a=2)
        for hp in range(2):
            self.pe(lambda e, hp=hp: e.matmul(pc3[:, hp, 0:cn], lhsT=self.lsp[0:cn, hp * 128:(hp + 1) * 128], rhs=mt[0:cn, 0:cn],
                                              start=True, stop=True), [self.B_lsp, self.B_cst], [pz.buf])
        self.act(lambda e: e.activation(out=self.E2[:, :, 0:cn], in_=pc3[:, :, 0:cn], func=AF.Exp, scale=1.0 / GLA_TAU),
                 [pz.buf], [self.B_E2])
        if with_out:
            self.act(lambda e: e.activation(out=self.E1[:, :, 0:cn], in_=pc3[:, :, 0:cn], func=AF.Exp, scale=-1.0 / GLA_TAU),
                     [pz.buf], [self.B_E1])
        else:
            self.dve(lambda e: e.reciprocal(out=self.E1[:, :, cn - 1], in_=self.E2[:, :, cn - 1]), [self.B_E2], [self.B_E1])
        yield
        kz4 = self.kz[:, :, :].rearrange("p (a b) t -> p a b t", b=2)
        for half in range(2):
            lo = 64 * half
            self.dve(lambda e, lo=lo, half=half: e.tensor_tensor(out=kz4[lo:lo + 64, :, half, 0:cn], in0=S.kg[lo:lo + 64, :, 0:cn],
                                                                 in1=self.E2[lo:lo + 64, :, 0:cn], op=ALU.mult),
                     [S.B_kg, self.B_E2], [self.B_kz])
        if with_out:
            qz4 = self.qz[:, :, :].rearrange("p (a b) t -> p a b t", b=2)
            for half in range(2):
                lo = 64 * half
                self.dve(lambda e, lo=lo, half=half: e.tensor_tensor(out=qz4[lo:lo + 64, :, half, 0:cn], in0=S.qg[lo:lo + 64, :, 0:cn],
                                                                     in1=self.E1[lo:lo + 64, :, 0:cn], op=ALU.mult),
                         [S.B_qg, self.B_E1], [self.B_qz])
        yield
        if with_out:
            pa = ps.alloc()
            pa3 = pa.v3(4, CH)
            for h in range(4):
                self.pe(lambda e, h=h: e.matmul(pa3[0:cn, h, 0:cn], lhsT=S.kr[:, h, 0:cn], rhs=S.qr[:, h, 0:cn],
                                                start=True, stop=True), [S.B_kr, S.B_qr], [pa.buf])
            mask = mt[0:cn, 0:cn].unsqueeze(1).broadcast_to([cn, 4, cn])
            self.dve(lambda e: e.tensor_tensor(out=self.A_r[0:cn, :, 0:cn], in0=pa3[0:cn, :, 0:cn], in1=mask, op=ALU.mult),
                     [pa.buf, self.B_cst], [self.B_Ar])
            yield
            pg_ = ps.alloc()
            pg3 = pg_.v3(4, CH)
            for h in range(4):
                self.pe(lambda e, h=h: e.matmul(pg3[0:cn, h, 0:cn], lhsT=self.kz[:, h, 0:cn], rhs=self.qz[:, h, 0:cn],
                                                start=True, stop=True), [self.B_kz, self.B_qz], [pg_.buf])
            self.dve(lambda e: e.tensor_tensor(out=self.A_g[0:cn, :, 0:cn], in0=pg3[0:cn, :, 0:cn], in1=mask, op=ALU.mult),
                     [pg_.buf, self.B_cst], [self.B_Ag])
            yield
            po_r = self.psO[ti % 2].alloc()
            por3 = po_r.v3(4, 128)
            for h in range(4):
                self.pe(lambda e, h=h: e.matmul(por3[0:cn, h, :], lhsT=self.A_r[0:cn, h, 0:cn], rhs=S.vt[0:cn, h * 128:(h + 1) * 128],
                                                start=True, stop=False), [self.B_Ar, Bv], [po_r.buf])
                self.pe(lambda e, h=h: e.matmul(por3[0:cn, h, :], lhsT=S.qr[:, h, 0:cn], rhs=self.Sb_r[:, h, :],
                                                start=False, stop=True), [S.B_qr, self.B_Sbr], [po_r.buf])
            yield
            po_g = self.psO[ti % 2].alloc()
            pog3 = po_g.v3(4, 128)
            S.po_r, S.po_g = po_r, po_g
            for h in range(4):
                self.pe(lambda e, h=h: e.matmul(pog3[0:cn, h, :], lhsT=self.A_g[0:cn, h, 0:cn],
                                                rhs=S.vt[0:cn, 512 + h * 128:512 + (h + 1) * 128],
                                                start=True, stop=False), [self.B_Ag, Bv], [po_g.buf])
                self.pe(lambda e, h=h: e.matmul(pog3[0:cn, h, :], lhsT=self.qz[:, h, 0:cn],
                                                rhs=self.Sb_g[:, h // 2, :], start=False, stop=True),
                        [self.B_qz, self.B_Sbg], [po_g.buf])
            yield
        pk = ps.alloc()
        pk3 = pk.bf3(8, 128)
        for h in range(4):
            self.pe(lambda e, h=h: e.transpose(pk3[0:cn, h, :], S.kr[:, h, 0:cn], self.ident), [S.B_kr, self.B_cst], [pk.buf])
        for h in range(4):
            self.pe(lambda e, h=h: e.transpose(pk3[0:cn, 4 + h, :], self.kz[:, h, 0:cn], self.ident), [self.B_kz, self.B_cst], [pk.buf])
        if CHAIN_ON_DVE:
            self.dve(lambda e: e.tensor_copy(out=self.ktok[0:cn, :, :], in_=pk3[0:cn, :, :]), [pk.buf], [self.B_ktok])
        else:
            self.act(lambda e: e.activation(out=self.ktok[0:cn, :, :], in_=pk3[0:cn, :, :], func=AF.Copy), [pk.buf], [self.B_ktok])
        yield
        pkv = ps.alloc()
        pkv3 = pkv.v3(4, 128)
        for h in range(4):
            self.pe(lambda e, h=h: e.matmul(pkv3[:, h, :], lhsT=self.ktok[0:cn, h, :], rhs=S.vt[0:cn, h * 128:(h + 1) * 128],
                                            start=True, stop=True), [self.B_ktok, Bv], [pkv.buf])
        gname = "gr16" if is_pre else "gr128"
        go, _ = self.lay[gname]
        gbc = self.cst[:, go:go + 4].unsqueeze(2).broadcast_to([P, 4, 128])
        self.dve(lambda e: e.tensor_tensor(out=self.tmpS[:, 0:4, :], in0=self.S_r, in1=pkv3[:, :, :], op=ALU.add),
                 [self.B_Sr, pkv.buf], [self.B_tmpS])
        self.dve(lambda e: e.tensor_tensor(out=self.S_r, in0=self.tmpS[:, 0:4, :], in1=gbc, op=ALU.mult),
                 [self.B_tmpS, self.B_cst], [self.B_Sr])
        if with_out:
            if CHAIN_ON_DVE:
                self.dve(lambda e: e.tensor_copy(out=self.Sb_r[:, :, :], in_=self.S_r), [self.B_Sr], [self.B_Sbr])
            else:
                self.act(lambda e: e.activation(out=self.Sb_r[:, :, :], in_=self.S_r, func=AF.Copy), [self.B_Sr], [self.B_Sbr])
        yield
        pkg = ps.alloc()
        pkg3 = pkg.v3(2, 128)
        for h in range(4):
            self.pe(lambda e, h=h: e.matmul(pkg3[:, h // 2, :], lhsT=self.ktok[0:cn, 4 + h, :],
                                            rhs=S.vt[0:cn, 512 + h * 128:512 + (h + 1) * 128],
                                            start=(h % 2 == 0), stop=(h % 2 == 1)), [self.B_ktok, Bv], [pkg.buf])
        self.dve(lambda e: e.tensor_tensor(out=self.tmpS[:, 0:2, :], in0=self.S_g, in1=pkg3[:, :, :], op=ALU.add),
                 [self.B_Sg, pkg.buf], [self.B_tmpS])
        for hp in range(2):
            self.dve(lambda e, hp=hp: e.tensor_scalar(out=self.S_g[:, hp, :], in0=self.tmpS[:, hp, :],
                                                      scalar1=self.E1[:, hp, cn - 1:cn], scalar2=None, op0=ALU.mult),
                     [self.B_tmpS, self.B_E1], [self.B_Sg])
        if with_out:
            if CHAIN_ON_DVE:
                self.dve(lambda e: e.tensor_copy(out=self.Sb_g[:, :, :], in_=self.S_g), [self.B_Sg], [self.B_Sbg])
            else:
                self.act(lambda e: e.activation(out=self.Sb_g[:, :, :], in_=self.S_g, func=AF.Copy), [self.B_Sg], [self.B_Sbg])
        else:
            self.dve(lambda e: e.tensor_tensor(out=self.Dacc, in0=self.Dacc, in1=self.E1[:, :, cn - 1], op=ALU.mult),
                     [self.B_D, self.B_E1], [self.B_D])
        yield
        return

    def gen_B2(self, l, ti, with_out, S):
        if not with_out:
            return
        SUB = 99
        t0, cn = self.mt_tiles[ti]
        ps = self.psA
        po_r, po_g = S.po_r, S.po_g
        por3, pog3 = po_r.v3(4, 128), po_g.v3(4, 128)
        st = self.st
        s1 = st[0:cn, 0:4]
        s2 = st[0:cn, 4:12]
        mean = st[0:cn, 12:16]
        msq = st[0:cn, 16:20]
        var = st[0:cn, 20:28]
        rtv = st[0:cn, 28:36]
        rsd = st[0:cn, 36:44]
        nmr = st[0:cn, 44:48]
        self.dve(lambda e: e.reduce_sum(out=s1, in_=por3[0:cn, :, :], axis=AX.X), [po_r.buf], [self.B_st])
        self.act(lambda e: e.activation(out=self.sqo[0:cn, 0:4, :], in_=por3[0:cn, :, :], func=AF.Square), [po_r.buf], [self.B_sqo])
        self.act(lambda e: e.activation(out=self.sqo[0:cn, 4:8, :], in_=pog3[0:cn, :, :], func=AF.Square), [po_g.buf], [self.B_sqo])
        yield
        self.dve(lambda e: e.reduce_sum(out=s2, in_=self.sqo[0:cn, :, :], axis=AX.X), [self.B_sqo], [self.B_st])
        self.dve(lambda e: e.tensor_tensor(out=msq, in0=s1, in1=s1, op=ALU.mult), [self.B_st], [self.B_st])
        self.dve(lambda e: e.scalar_tensor_tensor(out=s2[:, 0:4], in0=msq, scalar=-1.0 / 128, in1=s2[:, 0:4], op0=ALU.mult, op1=ALU.add),
                 [self.B_st], [self.B_st])
        yield
        self.act(lambda e: e.activation(out=rsd, in_=s2, func=AF.Ln, scale=1.0 / 128, bias=self.c("eps")[0:cn, :]), [self.B_st, self.B_cst], [self.B_st])
        self.act(lambda e: e.activation(out=rsd, in_=rsd, func=AF.Exp, scale=-0.5), [self.B_st], [self.B_st])
        self.dve(lambda e: e.tensor_scalar(out=mean, in0=s1, scalar1=1.0 / 128, scalar2=None, op0=ALU.mult), [self.B_st], [self.B_st])
        yield
        mean_bc = mean.unsqueeze(2).broadcast_to([cn, 4, 128])
        rsdr_bc = rsd[:, 0:4].unsqueeze(2).broadcast_to([cn, 4, 128])
        rsdg_bc = rsd[:, 4:8].unsqueeze(2).broadcast_to([cn, 4, 128])
        self.dve(lambda e: e.tensor_tensor(out=self.tmp4[0:cn, :, :], in0=por3[0:cn, :, :], in1=mean_bc, op=ALU.subtract),
                 [po_r.buf, self.B_st], [self.B_sqo])
        self.dve(lambda e: e.tensor_tensor(out=self.on[0:cn, 0:4, :], in0=self.tmp4[0:cn, :, :], in1=rsdr_bc, op=ALU.mult),
                 [self.B_sqo, self.B_st], [self.B_on])
        self.dve(lambda e: e.tensor_tensor(out=self.on[0:cn, 4:8, :], in0=pog3[0:cn, :, :], in1=rsdg_bc, op=ALU.mult),
                 [po_g.buf, self.B_st], [self.B_on])
        yield
        if SUB < 4:
            return
        pT = ps.alloc()
        pT3 = pT.bf3(8, CH)
        for h in range(8):
            self.pe(lambda e, h=h: e.transpose(pT3[:, h, 0:cn], self.on[0:cn, h, :], self.ident[0:cn, 0:cn]),
                    [self.B_on, self.B_cst], [pT.buf])
        self.dve(lambda e: e.tensor_tensor(out=S.mT[:, :, 0:cn], in0=pT3[:, :, 0:cn], in1=S.gr[:, :, 0:cn], op=ALU.mult),
                 [pT.buf, S.B_gr], [S.B_mT])
        yield
        if SUB < 5:
            return
        n = cn
        pts = [po_r, po_g]
        for oc in range(KC):
            pt = pts[oc // 4]
            v = pt.v3(4, TT)[:, oc % 4, 0:n]
            for kc in range(KC):
                self.pe(lambda e, kc=kc, v=v, oc=oc: e.matmul(v, lhsT=self.wout[:, kc, oc * 128:(oc + 1) * 128],
                                                              rhs=S.mT[:, kc, 0:n], start=(kc == 0), stop=(kc == KC - 1)),
                        [self.B_wout[kc], S.B_mT], [pt.buf])
            if oc % 4 == 3:
                self.act(lambda e, pt=pt, oc=oc: e.activation(out=self.sqm[:, oc - 3:oc + 1, 0:n], in_=pt.v3(4, TT)[:, :, 0:n], func=AF.Square),
                         [pt.buf], [self.B_sqm])
                yield
        pss = ps.alloc()
        for kc in range(KC):
            self.pe(lambda e, kc=kc: e.matmul(pss.f32(n), lhsT=self.ones, rhs=self.sqm[:, kc, 0:n],
                                              start=(kc == 0), stop=(kc == KC - 1)), [self.B_sqm, self.B_cst], [pss.buf])
        self.rstd_from(pss, n, 1.0 / D, self.scB)
        rstd, B_rstd = self.scB[2], self.scB[3]
        yield
        if SUB < 6:
            return
        xb = [self.XB[ti]]
        pno, _ = self.lay[f"pon{l}"]
        rstd_bc = rstd[:, 0:n].unsqueeze(1).broadcast_to([P, 4, n])
        for b4 in range(2):
            pt = pts[b4]
            pw_bc = self.cst[:, pno + 4 * b4:pno + 4 * b4 + 4].unsqueeze(2).broadcast_to([P, 4, n])
            self.dve(lambda e, pt=pt: e.tensor_tensor(out=self.tmp4[:, :, 0:n], in0=pt.v3(4, TT)[:, :, 0:n], in1=rstd_bc, op=ALU.mult),
                     [pt.buf, B_rstd], [self.B_sqo])
            self.dve(lambda e, pw_bc=pw_bc: e.tensor_tensor(out=self.tmp4[:, :, 0:n], in0=self.tmp4[:, :, 0:n], in1=pw_bc, op=ALU.mult),
                     [self.B_sqo, self.B_cst], [self.B_sqo])
            self.dve(lambda e, b4=b4: e.tensor_tensor(out=self.xT[:, 4 * b4:4 * b4 + 4, t0:t0 + n], in0=self.xT[:, 4 * b4:4 * b4 + 4, t0:t0 + n],
                                                      in1=self.tmp4[:, :, 0:n], op=ALU.add), xb + [self.B_sqo], xb)
            yield

    def exchange_state(self, l):
        nc = self.nc
        src, dst = self.cc1_src[l], self.cc1_dst[l]
        B_src, B_dst = Buf("cc1src"), Buf("cc1dst")
        self.dma("pool", lambda e: e.dma_start(out=src.ap()[:, :], in_=self.pay[:, :]), [self.B_Sr, self.B_Sg, self.B_D], [B_src], f"cc1a{l}")
        self.dma("pool", lambda e: e.collective_compute("AllGather", ALU.bypass, replica_groups=[[0, 1, 2, 3], [4, 5, 6, 7]],
                                                        ins=[src.ap().opt()], outs=[dst.ap().opt()]),
                 [B_src], [B_dst], f"cc1b{l}", inc=1)
        self.dma("pool", lambda e: e.dma_start(out=self.gath[:, :, :], in_=dst.ap().rearrange("(r p) f -> p r f", p=P)),
                 [B_dst], [self.B_gath], f"cc1c{l}")
        self.dve(lambda e: e.memset(self.pay[:, :], 0.0), [], [self.B_Sr, self.B_Sg, self.B_D])
        dro, _ = self.lay["drt"]
        for i in range(3):
            a_i = self.c("acoef", i)
            self.dve(lambda e, i=i, a_i=a_i: e.tensor_scalar(out=self.dprime[:, 0:4], in0=self.cst[:, dro + 4 * i:dro + 4 * i + 4],
                                                              scalar1=-1.0, scalar2=a_i, op0=ALU.add, op1=ALU.mult),
                     [self.B_cst], [self.B_dprime])
            self.dve(lambda e, i=i, a_i=a_i: e.tensor_scalar(out=self.dprime[:, 4:6], in0=self.gath[:, i, 768:770],
                                                              scalar1=-1.0, scalar2=a_i, op0=ALU.add, op1=ALU.mult),
                     [self.B_gath, self.B_cst], [self.B_dprime])
            self.dve(lambda e: e.tensor_scalar(out=self.dprime[:, 0:6], in0=self.dprime[:, 0:6], scalar1=1.0, scalar2=None, op0=ALU.add),
                     [self.B_dprime], [self.B_dprime])
            self.dve(lambda e, i=i, a_i=a_i: e.tensor_scalar(out=self.ubuf[:, :], in0=self.gath[:, i, 0:768], scalar1=a_i, scalar2=None,
                                                              op0=ALU.mult), [self.B_gath, self.B_cst], [self.B_ubuf])
            dr_bc = self.dprime[:, 0:4].unsqueeze(2).broadcast_to([P, 4, 128])
            dg_bc = self.dprime[:, 4:6].unsqueeze(2).broadcast_to([P, 2, 128])
            self.dve(lambda e, dr_bc=dr_bc: e.tensor_tensor(out=self.S_r, in0=self.S_r, in1=dr_bc, op=ALU.mult),
                     [self.B_Sr, self.B_dprime], [self.B_Sr])
            self.dve(lambda e, dg_bc=dg_bc: e.tensor_tensor(out=self.S_g, in0=self.S_g, in1=dg_bc, op=ALU.mult),
                     [self.B_Sg, self.B_dprime], [self.B_Sg])
            self.dve(lambda e: e.tensor_tensor(out=self.pay[:, 0:768], in0=self.pay[:, 0:768], in1=self.ubuf[:, :], op=ALU.add),
                     [self.B_Sr, self.B_Sg, self.B_ubuf], [self.B_Sr, self.B_Sg])
        self.act(lambda e: e.activation(out=self.Sb_r[:, :, :], in_=self.S_r, func=AF.Copy), [self.B_Sr], [self.B_Sbr])
        self.act(lambda e: e.activation(out=self.Sb_g[:, :, :], in_=self.S_g, func=AF.Copy), [self.B_Sg], [self.B_Sbg])

    def halo_send(self, l):
        src, dst = self.cc2_src[l], self.cc2_dst[l]
        B_src, B_dst = Buf("cc2src"), Buf("cc2dst")
        last = self.XB[-1]
        self.dve(lambda e: e.tensor_copy(out=self.hal[:, :].rearrange("p (k t) -> p k t", k=KC), in_=self.xT[:, :, NLOC - 2:NLOC]),
                 [last], [self.B_hal])
        self.dma("sp", lambda e: e.dma_start(out=src.ap()[:, :], in_=self.hal[:, :]), [self.B_hal], [B_src], f"cc2a{l}")
        self.dma("pool", lambda e: e.collective_compute("AllGather", ALU.bypass, replica_groups=[[0, 1, 2, 3], [4, 5, 6, 7]],
                                                        ins=[src.ap().opt()], outs=[dst.ap().opt()]),
                 [B_src], [B_dst], f"cc2b{l}", inc=1)
        self.dma("sp", lambda e: e.dma_start(out=self.gh[:, :, :], in_=dst.ap().rearrange("(r p) f -> p r f", p=P)),
                 [B_dst], [self.B_gh], f"cc2c{l}")
        return {f"cc2a{l}", f"cc2b{l}", f"cc2c{l}"}

    def halo_recv(self, l):
        self.dve(lambda e: e.tensor_scalar(out=self.hal[:, :], in0=self.gh[:, 0, :], scalar1=self.c("bcoef", 0), scalar2=None, op0=ALU.mult),
                 [self.B_gh, self.B_cst], [self.B_hal])
        for i in range(1, 4):
            self.dve(lambda e, i=i: e.scalar_tensor_tensor(out=self.hal[:, :], in0=self.gh[:, i, :], scalar=self.c("bcoef", i),
                                                           in1=self.hal[:, :], op0=ALU.mult, op1=ALU.add),
                     [self.B_gh, self.B_cst, self.B_hal], [self.B_hal])
        self.dve(lambda e: e.tensor_tensor(out=self.xT[:, :, NPRE - 2:NPRE], in0=self.xT[:, :, NPRE - 2:NPRE],
                                           in1=self.hal[:, :].rearrange("p (k t) -> p k t", k=KC), op=ALU.add),
                 [self.XB[0], self.B_hal], [self.XB[0]])

    def ffn(self, l):
        sb = self.sb
        A = sb.alloc
        NH = 1042
        act_ = A("act", [P, NFC, 1040], BF16)
        B_act = [Buf(f"act{i}") for i in range(NFC)]
        wsl = [A(f"wsl{i}", [P, 4096], BF16) for i in range(3)]
        B_wsl = [Buf(f"wsl{i}") for i in range(3)]
        uhalo = A("uhalo", [P, 44, 2], F32)
        B_uhalo = Buf("uhalo")
        m_c = sb.mark()
        gl0_ = A("gl0", [P, 512], F32)
        gl = [gl0_, gl0_]
        B_gl0_ = Buf("gl0")
        B_gl = [B_gl0_, B_gl0_]
        ca = [A(f"ca{i}", [P, 512], F32) for i in range(2)]
        cg = [A(f"cg{i}", [P, 512], F32) for i in range(2)]
        B_ca, B_cg = [Buf("ca0"), Buf("ca1")], [Buf("cg0"), Buf("cg1")]
        m_c2 = sb.mark()
        sb.release(m_c)
        cB = A("cB", [P, 44, 16], F32)
        tB = A("tB", [P, 44, 16], F32)
        glB = A("glB", [P, NFC, 16], F32)
        assert sb.mark() <= m_c2
        sb.release(m_c2)
        G_c = [B_gl0_, B_ca[0], B_ca[1], B_cg[0], B_cg[1]]
        m_u = sb.mark()
        sqf = A("sqf", [P, 2, 512], BF16)
        B_sqf = [Buf("sqf0"), Buf("sqf1")]
        tmp2 = [self.tmpx, A("tmpx2", [P, 512], F32)]
        B_tmp2 = [self.B_tmpx, Buf("tmpx2")]
        m_u2 = sb.mark()
        sb.release(m_u)
        upre = A("upre", [P, 44, 16], F32)
        assert sb.mark() <= m_u2
        sb.release(m_u2)
        G_u = [B_sqf[0], B_sqf[1], B_tmp2[1]]
        m1 = sb.mark()
        h2T = A("h2T", [P, KC, NH], BF16)
        sqn = A("sqn2", [P, KC, 512], BF16)
        ua = [A(f"ua{i}", [P, NH], F32) for i in range(2)]
        ug = [A(f"ug{i}", [P, NH], F32) for i in range(2)]
        sb.release(m1)
        f_sb = A("f_sb", [P, KC, 1040], F32)
        wo, _ = self.lay[f"convw{l}"]
        bo, _ = self.lay[f"convb{l}"]

        halves = [
            [(2, 16, [0], 0), (18, 512, [1, 2, 3, 4], 16), (530, 512, [5, 6, 7, 8], 528)],
            [(2, 512, [9, 10, 11, 12], 1040), (514, 512, [13, 14, 15, 16], 1552)],
        ]
        wcnt = [0]

        def next_slot():
            i = wcnt[0] % 3
            wcnt[0] += 1
            return i

        B_h2T, B_sqn = Buf("h2T"), Buf("sqn2")
        B_ua, B_ug = [Buf("ua0"), Buf("ua1")], [Buf("ug0"), Buf("ug1")]
        G_fsb = [B_h2T, B_sqn] + B_ua + B_ug
        last_layer = (l == self.nl - 1) and self.final_layer
        for hi, tiles in enumerate(halves):
            order = [t for t in tiles if t[2] != [0]] + [t for t in tiles if t[2] == [0]]
            for (co, n, xt, t0) in order:
                if xt == [0]:
                    self.halo_recv(l)
                self.rmsnorm(self.xT[:, :, t0:t0 + n], [self.XB[i] for i in xt], f"pfn{l}", h2T[:, :, co:co + n], [B_h2T], n, sqn, B_sqn)
            if hi == 0:
                for k in range(2):
                    self.dve(lambda e, k=k: e.memset(ua[k][:, 0:2], 0.0), [], [B_ua[k]])
                    self.dve(lambda e, k=k: e.memset(ug[k][:, 0:2], 0.0), [], [B_ug[k]])
            for j in range(11):
                si = next_slot()
                w = wsl[si]
                self.dma("pool", lambda e, j=j, w=w: e.dma_start(out=w[:, :], in_=self.d_wup[l * 11 + j, :, :]), [], [B_wsl[si]], f"wsl{si}")
                w3 = w[:, :].rearrange("p (k c) -> p k c", k=KC)
                for q in range(2):
                    i = 2 * j + q
                    k = i % 2
                    uab, ugb = ua[k], ug[k]
                    if hi == 1:
                        self.dve(lambda e, i=i, uab=uab: e.tensor_copy(out=uab[:, 0:2], in_=uhalo[:, i, :]), [B_uhalo], [B_ua[k]])
                        self.dve(lambda e, i=i, ugb=ugb: e.tensor_copy(out=ugb[:, 0:2], in_=uhalo[:, NFC + i, :]), [B_uhalo], [B_ug[k]])
                    for ti_, (co, n, xt, t0) in enumerate(tiles):
                        pa = self.ps.alloc()
                        pgt = self.ps.alloc()
                        for kc in range(KC):
                            self.pe(lambda e, kc=kc, pa=pa, co=co, n=n, q=q, w3=w3: e.matmul(
                                pa.f32(n), lhsT=w3[:, kc, q * 128:(q + 1) * 128], rhs=h2T[:, kc, co:co + n],
                                start=(kc == 0), stop=(kc == KC - 1)), [B_wsl[si], B_h2T], [pa.buf])
                        for kc in range(KC):
                            self.pe(lambda e, kc=kc, pgt=pgt, co=co, n=n, q=q, w3=w3: e.matmul(
                                pgt.f32(n), lhsT=w3[:, kc, 256 + q * 128:256 + (q + 1) * 128], rhs=h2T[:, kc, co:co + n],
                                start=(kc == 0), stop=(kc == KC - 1)), [B_wsl[si], B_h2T], [pgt.buf])
                        if xt == [0] and not last_layer:
                            self.act(lambda e, pa=pa, co=co, n=n, uab=uab: e.activation(out=uab[:, co:co + n], in_=pa.f32(n), func=AF.Copy), [pa.buf], [B_ua[k]])
                            self.act(lambda e, pgt=pgt, co=co, n=n, ugb=ugb: e.activation(out=ugb[:, co:co + n], in_=pgt.f32(n), func=AF.Copy), [pgt.buf], [B_ug[k]])
                            self.act(lambda e, pa=pa, i=i, n=n: e.activation(out=upre[:, i, 0:n], in_=pa.f32(n), func=AF.Copy), [pa.buf], G_u)
                            self.act(lambda e, pgt=pgt, i=i, n=n: e.activation(out=upre[:, NFC + i, 0:n], in_=pgt.f32(n), func=AF.Copy), [pgt.buf], G_u)
                            continue
                        if last_layer and xt == [0]:
                            self.act(lambda e, pa=pa, co=co, n=n, uab=uab: e.activation(out=uab[:, co:co + n], in_=pa.f32(n), func=AF.Copy), [pa.buf], [B_ua[k]])
                            self.act(lambda e, pgt=pgt, co=co, n=n, ugb=ugb: e.activation(out=ugb[:, co:co + n], in_=pgt.f32(n), func=AF.Copy), [pgt.buf], [B_ug[k]])
                            continue
                        gi = (i * len(tiles) + ti_) % 2
                        cab, cgb = ca[gi], cg[gi]
                        for (pt_, u, B_u, cdst, B_c, ch, second) in ((pa, uab, B_ua[k], cab, B_ca[gi], i, "dve"),
                                                                     (pgt, ugb, B_ug[k], cgb, B_cg[gi], NFC + i, "dve")):
                            w0 = self.cst[:, wo + ch * 3 + 0:wo + ch * 3 + 1]
                            w1 = self.cst[:, wo + ch * 3 + 1:wo + ch * 3 + 2]
                            w2 = self.cst[:, wo + ch * 3 + 2:wo + ch * 3 + 3]
                            bb = self.cst[:, bo + ch:bo + ch + 1]
                            self.act(lambda e, pt_=pt_, u=u, co=co, n=n: e.activation(out=u[:, co:co + n], in_=pt_.f32(n), func=AF.Copy),
                                     [pt_.buf], [B_u])
                            self.act(lambda e, pt_=pt_, cdst=cdst, n=n, w2=w2, bb=bb: e.activation(
                                out=cdst[:, 0:n], in_=pt_.f32(n), func=AF.Identity, scale=w2, bias=bb), [pt_.buf, self.B_cst], [B_c])
                            self.dve(lambda e, u=u, cdst=cdst, co=co, n=n, w1=w1: e.scalar_tensor_tensor(
                                out=cdst[:, 0:n], in0=u[:, co - 1:co - 1 + n], scalar=w1, in1=cdst[:, 0:n], op0=ALU.mult, op1=ALU.add),
                                [B_u, self.B_cst, B_c], [B_c])
                            self.pg.emit(second, lambda e, u=u, cdst=cdst, co=co, n=n, w0=w0: e.scalar_tensor_tensor(
                                out=cdst[:, 0:n], in0=u[:, co - 2:co - 2 + n], scalar=w0, in1=cdst[:, 0:n], op0=ALU.mult, op1=ALU.add),
                                [B_u, self.B_cst, B_c], [B_c])
                        self.act(lambda e, n=n, gi=gi, cab=cab: e.activation(out=gl[gi][:, 0:n], in_=cab[:, 0:n], func=AF.Gelu_apprx_tanh),
                                 [B_ca[gi]], [B_gl[gi]])
                        ac0 = co - 2
                        import os
                        self.pg.emit(os.environ.get("KGATE", "dve"), lambda e, i=i, ac0=ac0, n=n, gi=gi, cgb=cgb: e.tensor_tensor(
                            out=act_[:, i, ac0:ac0 + n], in0=gl[gi][:, 0:n], in1=cgb[:, 0:n], op=ALU.mult),
                            [B_gl[gi], B_cg[gi]], [B_act[i]])
                    if hi == 0:
                        self.dve(lambda e, i=i, uab=uab: e.tensor_copy(out=uhalo[:, i, :], in_=uab[:, NH - 2:NH]), [B_ua[k]], [B_uhalo])
                        self.dve(lambda e, i=i, ugb=ugb: e.tensor_copy(out=uhalo[:, NFC + i, :], in_=ugb[:, NH - 2:NH]), [B_ug[k]], [B_uhalo])
            if hi == 0 and not last_layer:
                wv = self.cst[:, wo:wo + 132].rearrange("p (c k) -> p c k", k=3)

                def wbc(tap, ncol):
                    return wv[:, :, tap].unsqueeze(2).broadcast_to([P, 44, ncol])
                b_bc = self.cst[:, bo:bo + 44].unsqueeze(2).broadcast_to([P, 44, 16])
                self.dve(lambda e: e.tensor_tensor(out=cB[:, :, :], in0=upre[:, :, :], in1=wbc(2, 16), op=ALU.mult), G_u + [self.B_cst], G_c)
                self.dve(lambda e: e.tensor_tensor(out=tB[:, :, 1:16], in0=upre[:, :, 0:15], in1=wbc(1, 15), op=ALU.mult), G_u + [self.B_cst], G_c)
                self.dve(lambda e: e.tensor_tensor(out=cB[:, :, 1:16], in0=cB[:, :, 1:16], in1=tB[:, :, 1:16], op=ALU.add), G_c, G_c)
                self.dve(lambda e: e.tensor_tensor(out=tB[:, :, 2:16], in0=upre[:, :, 0:14], in1=wbc(0, 14), op=ALU.mult), G_u + [self.B_cst], G_c)
                self.dve(lambda e: e.tensor_tensor(out=cB[:, :, 2:16], in0=cB[:, :, 2:16], in1=tB[:, :, 2:16], op=ALU.add), G_c, G_c)
                self.dve(lambda e: e.tensor_tensor(out=cB[:, :, :], in0=cB[:, :, :], in1=b_bc, op=ALU.add), G_c + [self.B_cst], G_c)
                self.act(lambda e: e.activation(out=glB[:, :, :], in_=cB[:, 0:NFC, :], func=AF.Gelu_apprx_tanh), G_c, G_c)
                self.dve(lambda e: e.tensor_tensor(out=act_[:, :, 0:16], in0=glB[:, :, :], in1=cB[:, NFC:2 * NFC, :], op=ALU.mult), G_c, B_act)
            if last_layer:
                tiles = [t for t in tiles if t[2] != [0]]
            sst = [self.ps.reserve() for _ in tiles]
            pend = None
            for oc in range(KC):
                si = next_slot()
                w = wsl[si]
                self.dma("pool", lambda e, oc=oc, w=w: e.dma_start(out=w[:, 0:NFC * 128], in_=self.d_wdn[l * 8 + oc, :, :]), [], [B_wsl[si]], f"wsl{si}")
                w3 = w[:, 0:NFC * 128].rearrange("p (k c) -> p k c", k=NFC)
                for ri, (co, n, xt, t0) in enumerate(tiles):
                    ac0 = co - 2
                    pt = self.ps.alloc()
                    for kc in range(NFC):
                        self.pe(lambda e, kc=kc, pt=pt, ac0=ac0, n=n, w3=w3: e.matmul(
                            pt.f32(n), lhsT=w3[:, kc, :], rhs=act_[:, kc, ac0:ac0 + n], start=(kc == 0), stop=(kc == NFC - 1)),
                            [B_wsl[si], B_act[kc]], [pt.buf])
                    sq_i = (oc * len(tiles) + ri) % 2
                    self.act(lambda e, pt=pt, oc=oc, ac0=ac0, n=n: e.mul(out=f_sb[:, oc, ac0:ac0 + n], in_=pt.f32(n), mul=self.c(f"pofn{l}", oc)),
                             [pt.buf, self.B_cst], G_fsb)
                    self.act(lambda e, pt=pt, sq_i=sq_i, n=n: e.activation(out=sqf[:, sq_i, 0:n], in_=pt.f32(n), func=AF.Square),
                             [pt.buf], [B_sqf[sq_i]])
                    if pend is not None:
                        self.pe(*pend)
                    pend = (lambda e, st_=sst[ri], sq_i=sq_i, n=n, oc=oc: e.matmul(st_.f32(n), lhsT=self.ones, rhs=sqf[:, sq_i, 0:n],
                                                                                    start=(oc == 0), stop=(oc == KC - 1)),
                            [B_sqf[sq_i], self.B_cst], [sst[ri].buf])
            if pend is not None:
                self.pe(*pend)
            for ri, (co, n, xt, t0) in enumerate(tiles):
                ac0 = co - 2
                self.rstd_from(sst[ri], n, 1.0 / D)
                xb = [self.XB[i] for i in xt]
                for kc in range(KC):
                    tk = kc % 2
                    self.dve(lambda e, kc=kc, ac0=ac0, n=n, tk=tk: e.tensor_tensor(
                        out=tmp2[tk][:, 0:n], in0=f_sb[:, kc, ac0:ac0 + n], in1=self.rstd[:, 0:n], op=ALU.mult),
                        G_fsb + [self.B_rstd], [B_tmp2[tk]])
                    self.pg.emit("pool" if kc % 2 == 0 else "dve",
                                 lambda e, kc=kc, t0=t0, n=n, tk=tk: e.tensor_tensor(out=self.xT[:, kc, t0:t0 + n], in0=self.xT[:, kc, t0:t0 + n],
                                                                                     in1=tmp2[tk][:, 0:n], op=ALU.add), xb + [B_tmp2[tk]], xb)
            for t in sst:
                self.ps.unreserve(t)


def _img(w):
    K, N = w.shape
    return np.ascontiguousarray(w.reshape(K // P, P, N).transpose(1, 0, 2).reshape(P, (K // P) * N))


def _host_consts(r, nl, layers, prm):
    lay, ncst = cst_layout(nl)
    cst = np.zeros((P, ncst), np.float32)

    def put(name, arr):
        o, n = lay[name]
        arr = np.asarray(arr, np.float32)
        cst[:, o:o + n] = arr.reshape(-1, n) if arr.ndim > 1 else arr[None, :]

    put("eps", [EPS])
    put("one", [1.0])
    put("flag", [1.0 if r == 0 else 0.0])
    put("acoef", [1.0 if i < r else 0.0 for i in range(4)])
    put("bcoef", [1.0 if i == r - 1 else 0.0 for i in range(4)])
    hh = np.arange(4, dtype=np.float64)
    log_g = np.log(1.0 - 2.0 ** (-5.0 - hh))
    put("gr128", np.exp(log_g * 128))
    put("gr16", np.exp(log_g * 16) if r == 0 else np.ones(4))
    drt = np.zeros((4, 4))
    for i in range(4):
        drt[i] = np.exp(log_g * (NLOC if i == 0 else NSEQ))
    put("drt", drt.reshape(-1))
    tt = np.arange(128, dtype=np.float64)
    put("xiq", np.exp(log_g[:, None] * (tt[None, :] + 1.0)).reshape(-1))
    put("kinv", (np.exp(-log_g[:, None] * (tt[None, :] + 1.0)) * (128.0 ** -0.5)).reshape(-1))
    mt = (np.arange(128)[None, :] >= np.arange(128)[:, None]).astype(np.float32)
    o, n = lay["mt"]
    cst[:, o:o + n] = mt
    for li, l in enumerate(layers):
        def fm(v):
            return np.asarray(v, np.float32).reshape(KC, P).T
        for nm, key in (("pmn", "pre_mix_norm"), ("pon", "post_mix_norm"), ("pfn", "pre_ffn_norm"), ("pofn", "post_ffn_norm")):
            o, n = lay[f"{nm}{li}"]
            cst[:, o:o + n] = fm(prm[key][l])
        o, n = lay[f"retnw{li}"]
        cst[:, o:o + n] = np.asarray(prm["ret_norm_w"][l], np.float32).reshape(4, P).T
        o, n = lay[f"glanw{li}"]
        cst[:, o:o + n] = np.asarray(prm["gla_norm_w"][l], np.float32).reshape(4, P).T
        cw = np.asarray(prm["ffn_conv_w"][l], np.float32)
        o, n = lay[f"convw{li}"]
        cst[:, o:o + n] = cw.reshape(3, 44, P).transpose(2, 1, 0).reshape(P, 44 * 3)
        cbv = np.asarray(prm["ffn_conv_b"][l], np.float32)
        o, n = lay[f"convb{li}"]
        cst[:, o:o + n] = cbv.reshape(44, P).T
    return cst


def _host_weights(layers, prm):
    nl = len(layers)
    win = np.stack([_img(np.asarray(prm["w_in"][l], np.float32)) for l in layers])
    wout = np.stack([_img(np.asarray(prm["w_out"][l], np.float32)) for l in layers])
    wup = np.zeros((nl * 11, P, KC * 512), np.float32)
    wdn = np.zeros((nl * 8, P, NFC * 128), np.float32)
    for li, l in enumerate(layers):
        up = np.asarray(prm["ffn_up"][l], np.float32)
        dn = np.asarray(prm["ffn_down"][l], np.float32)
        for j in range(11):
            cols = np.concatenate([np.arange(128 * (2 * j), 128 * (2 * j + 2)), DFF + np.arange(128 * (2 * j), 128 * (2 * j + 2))])
            wup[li * 11 + j] = _img(up[:, cols])
        for oc in range(8):
            wdn[li * 8 + oc] = _img(dn[:, oc * 128:(oc + 1) * 128])
    w2b = np.zeros((17, nl * 256), np.float32)
    for li, l in enumerate(layers):
        w2b[0:16, li * 256:(li + 1) * 256] = np.asarray(prm["gla_gate_w2"][l], np.float32)
        w2b[16, li * 256:(li + 1) * 256] = np.asarray(prm["gla_gate_b"][l], np.float32)
    return win, wout, wup, wdn, w2b


def _host_rope(r):
    half = 64
    inv = 10000.0 ** (-np.arange(half, dtype=np.float64) / half)
    pos = float(NSEQ * r) + np.arange(NLOC, dtype=np.float64)
    ang = pos[:, None] * inv[None, :]
    c = np.cos(ang).astype(np.float32).T
    s = np.sin(ang).astype(np.float32).T
    cosT = np.concatenate([c, c], axis=0)
    sinT = np.concatenate([-s, s], axis=0)
    return np.ascontiguousarray(np.concatenate([cosT, sinT], axis=1))


_CACHE = {}


def _get_program(nl, dbg=None, final=True):
    key = (nl, dbg, final)
    if key not in _CACHE:
        b = Builder(nl, dbg, final_layer=final)
        _CACHE[key] = (b.build(), b)
    return _CACHE[key]


def _run(xT_imgs, layers, prm, dbg=None):
    nl = len(layers)
    nc, b = _get_program(nl, dbg, final=(layers[-1] == prm["w_in"].shape[0] - 1))
    win, wout, wup, wdn, w2b = _host_weights(layers, prm)
    cbm = np.concatenate([np.eye(P, dtype=np.float32), np.ones((P, P), np.float32)], axis=1)
    in_maps = []
    for j in range(8):
        r = j % 4
        in_maps.append({
            "xT": xT_imgs[j], "cst": _host_consts(r, nl, layers, prm), "w2b": w2b, "cb": cbm, "rope": _host_rope(r),
            "win": win, "wout": wout, "wup": wup, "wdn": wdn,
        })
    res = run_bass_kernel_spmd(nc, in_maps, core_ids=list(range(8)))
    return res


LAUNCH_SPLIT = False


def kernel(x, meta_tokens, pre_mix_norm, w_in, gla_gate_w2, gla_gate_b, ret_norm_w, gla_norm_w,
           w_out, post_mix_norm, pre_ffn_norm, ffn_up, ffn_conv_w, ffn_conv_b, ffn_down, post_ffn_norm):
    prm = dict(pre_mix_norm=pre_mix_norm, w_in=w_in, gla_gate_w2=gla_gate_w2, gla_gate_b=gla_gate_b,
               ret_norm_w=ret_norm_w, gla_norm_w=gla_norm_w, w_out=w_out, post_mix_norm=post_mix_norm,
               pre_ffn_norm=pre_ffn_norm, ffn_up=ffn_up, ffn_conv_w=ffn_conv_w, ffn_conv_b=ffn_conv_b,
               ffn_down=ffn_down, post_ffn_norm=post_ffn_norm)
    prm = {k: np.asarray(v) for k, v in prm.items()}
    x = np.asarray(x, np.float32)
    meta = np.asarray(meta_tokens, np.float32)
    imgs = []
    for j in range(8):
        b, r = j // 4, j % 4
        xin = np.zeros((NLOC, D), np.float32)
        if r == 0:
            xin[0:NPRE] = meta
        xin[NPRE:] = x[b, NSEQ * r:NSEQ * (r + 1)]
        imgs.append(np.ascontiguousarray(xin.T.reshape(KC, P, NLOC).transpose(1, 0, 2).reshape(P, KC * NLOC)))
    nl_total = prm["w_in"].shape[0]
    if LAUNCH_SPLIT:
        for l in range(nl_total):
            res = _run(imgs, [l], prm)
            imgs = [np.ascontiguousarray(res.results[j]["y"]) for j in range(8)]
        outs = imgs
    else:
        res = _run(imgs, list(range(nl_total)), prm)
        outs = [res.results[j]["y"] for j in range(8)]
    out = np.zeros((2, 4 * NSEQ, D), np.float32)
    for j in range(8):
        b, r = j // 4, j % 4
        y = np.asarray(outs[j]).reshape(P, KC, NLOC)[:, :, NPRE:]
        out[b, NSEQ * r:NSEQ * (r + 1)] = y.transpose(2, 1, 0).reshape(NSEQ, D)
    return out
```

```python
import numpy as np
import concourse.bass as bass
import concourse.mybir as mybir
from concourse.bass_utils import run_bass_kernel_spmd

F32 = mybir.dt.float32
BF16 = mybir.dt.bfloat16
AF = mybir.ActivationFunctionType
ALU = mybir.AluOpType
AX = mybir.AxisListType

P = 128
D = 1024
KC = 8
NPRE = 16
NSEQ = 2048
NLOC = NPRE + NSEQ
TT = 128
CH = 128
INW = 3600
DFF = 2816
NFC = 22
EPS = 1e-6
GLA_TAU = 16.0
EPOCH = 6000
import os as _os
GATE_IN_ON_DVE = _os.environ.get("KGA", "dve") == "dve"
V_ON_DVE = _os.environ.get("KV", "act") == "dve"
CHAIN_ON_DVE = _os.environ.get("KCHAIN", "dve") == "dve"

C_RQ, C_RK, C_RV, C_RG, C_GQ, C_GK, C_GV, C_GG, C_GA = 0, 512, 1024, 1536, 2048, 2304, 2560, 3072, 3584


class Buf:
    __slots__ = ("name", "lw", "rd", "rd_dma")

    def __init__(self, name):
        self.name = name
        self.lw = None
        self.rd = {}
        self.rd_dma = []


class Ins:
    __slots__ = ("eng", "fn", "deps", "sig", "cnt", "is_dma", "key", "val", "inc")

    def __init__(self, eng, fn):
        self.eng = eng
        self.fn = fn
        self.deps = []
        self.sig = False
        self.cnt = 0
        self.is_dma = False
        self.key = None
        self.val = 0
        self.inc = 16


class Prog:
    ENGS = ("pe", "act", "dve", "pool", "sp")

    def __init__(self):
        self.streams = {e: [] for e in self.ENGS}
        self.key_cnt = {}
        self.key_last = {}
        self.bar = {}

    def barrier(self, exclude=()):
        deps = []
        for e in self.ENGS:
            for ins in reversed(self.streams[e]):
                if not ins.is_dma:
                    deps.append(ins)
                    break
        deps += [v for k, v in self.key_last.items() if k not in exclude]
        for e in self.ENGS:
            self.bar[e] = list(deps) + self.bar.get(e, [])

    def emit(self, eng, fn, reads=(), writes=(), dma_key=None, serialize=True, inc=16):
        ins = Ins(eng, fn)
        raw = set()
        oth = set()
        for b in reads:
            if b.lw is not None:
                raw.add(b.lw)
        for b in writes:
            if b.lw is not None:
                oth.add(b.lw)
            for r in b.rd.values():
                oth.add(r)
            for r in b.rd_dma:
                oth.add(r)
        if dma_key is not None:
            ins.is_dma = True
            ins.key = dma_key
            ins.inc = inc
            self.key_cnt[dma_key] = self.key_cnt.get(dma_key, 0) + inc
            ins.val = self.key_cnt[dma_key]
            if serialize and dma_key in self.key_last:
                oth.add(self.key_last[dma_key])
            self.key_last[dma_key] = ins
        deps = []
        for d in self.bar.pop(eng, []):
            if d.is_dma or d.eng != eng:
                if d not in raw and d not in oth:
                    deps.append(d)
        for d in raw | oth:
            if d is ins:
                continue
            if d.is_dma or ins.is_dma:
                deps.append(d)
            elif d.eng != eng:
                deps.append(d)
            else:
                if eng != "pe":
                    deps.append(d)
        for d in deps:
            if not d.is_dma:
                d.sig = True
        ins.deps = deps
        for b in reads:
            if ins.is_dma:
                b.rd_dma.append(ins)
            else:
                b.rd[eng] = ins
        for b in writes:
            b.lw = ins
            b.rd = {}
            b.rd_dma = []
        self.streams[eng].append(ins)
        return ins

    def finalize(self):
        for e in self.ENGS:
            c = 0
            for ins in self.streams[e]:
                if ins.is_dma:
                    continue
                if ins.sig:
                    c += 1
                    ins.cnt = c
        return {e: sum(1 for i in self.streams[e] if i.sig and not i.is_dma) for e in self.ENGS}

    def replay(self, eng, eobj, eng_sems, dma_sems):
        seen_cnt = {e: 0 for e in self.ENGS}
        seen_dma = {}
        for ins in self.streams[eng]:
            for d in ins.deps:
                if d.is_dma:
                    if seen_dma.get(d.key, 0) >= d.val:
                        continue
                    eobj.wait_ge(dma_sems[d.key], d.val)
                    seen_dma[d.key] = d.val
                else:
                    if seen_cnt[d.eng] >= d.cnt:
                        continue
                    ep = (d.cnt - 1) // EPOCH
                    eobj.wait_ge(eng_sems[d.eng][ep], d.cnt - ep * EPOCH)
                    seen_cnt[d.eng] = d.cnt
            bi = ins.fn(eobj)
            if ins.is_dma:
                bi.then_inc(dma_sems[ins.key], ins.inc)
            elif ins.sig:
                ep = (ins.cnt - 1) // EPOCH
                bi.then_inc(eng_sems[eng][ep], 1)


class SBAlloc:
    BASE = 16512
    LIMIT = 229376 - 64

    def __init__(self, nc):
        self.nc = nc
        self.off = self.BASE
        self.peak = self.off

    def alloc(self, name, shape, dtype):
        isz = 2 if dtype == BF16 else 4
        size = int(np.prod(shape[1:])) * isz
        off = (self.off + 63) // 64 * 64
        assert off + size <= self.LIMIT, f"SBUF overflow allocating {name}: {off + size}"
        t = self.nc.alloc_sbuf_tensor_at(name, list(shape), dtype, offset=off)
        self.off = off + size
        self.peak = max(self.peak, self.off)
        return t

    def mark(self):
        return self.off

    def release(self, m):
        self.off = m


class PsTile:
    def __init__(self, mgr, bank, gen):
        self.mgr = mgr
        self.bank = bank
        self.gen = gen

    @property
    def buf(self):
        assert self.mgr.gen[self.bank] == self.gen, "PSUM tile used after its bank was re-allocated"
        return self.mgr.bufs[self.bank]

    @property
    def t(self):
        return self.mgr.tens[self.bank]

    def f32(self, cols=512):
        return self.t[:, 0:cols]

    def v3(self, a, b):
        return self.t[:, 0:a * b].rearrange("p (a b) -> p a b", a=a)

    def bf(self):
        return self.t[:, :].bitcast(BF16)

    def bf3(self, a, b):
        return self.bf()[:, 0:a * b].rearrange("p (a b) -> p a b", a=a)


class PsMgr:
    def __init__(self, nc=None, parent=None, banks=None):
        if parent is None:
            self.tens = [nc.alloc_psum_tensor(f"psb{i}", [P, 512], F32) for i in range(8)]
            self.bufs = [Buf(f"psb{i}") for i in range(8)]
            self.gen = [0] * 8
        else:
            self.tens, self.bufs, self.gen = parent.tens, parent.bufs, parent.gen
        self.banks = list(banks) if banks is not None else list(range(8))
        self.nxt = 0
        self.reserved = set()

    def alloc(self):
        for _ in range(len(self.banks)):
            b = self.banks[self.nxt]
            self.nxt = (self.nxt + 1) % len(self.banks)
            if b not in self.reserved:
                self.gen[b] += 1
                return PsTile(self, b, self.gen[b])
        raise RuntimeError("no PSUM bank")

    def reserve(self):
        t = self.alloc()
        self.reserved.add(t.bank)
        return t

    def unreserve(self, t):
        self.reserved.discard(t.bank)


def cst_layout(nl):
    lay = {}
    off = 0

    def add(name, n):
        nonlocal off
        lay[name] = (off, n)
        off += n

    add("eps", 1)
    add("one", 1)
    add("flag", 1)
    add("acoef", 4)
    add("bcoef", 4)
    add("gr128", 4)
    add("gr16", 4)
    add("drt", 16)
    for l in range(nl):
        add(f"pmn{l}", 8)
        add(f"pon{l}", 8)
        add(f"pfn{l}", 8)
        add(f"pofn{l}", 8)
        add(f"retnw{l}", 4)
        add(f"glanw{l}", 4)
        add(f"convw{l}", 44 * 3)
        add(f"convb{l}", 44)
    add("xiq", 512)
    add("kinv", 512)
    add("mt", 128)
    return lay, off


class Builder:
    def __init__(self, nl, dbg=None, final_layer=True):
        self.nl = nl
        self.final_layer = final_layer
        self.dbg = dbg
        self.nc = nc = bass.Bass("TRN2", target_bir_lowering=False)
        self.pg = Prog()
        self.sb = SBAlloc(nc)
        self.ps = PsMgr(nc)
        self.psA = PsMgr(parent=self.ps, banks=[0, 1])
        self.psB = PsMgr(parent=self.ps, banks=[2, 3])
        self.psO = [PsMgr(parent=self.ps, banks=[4, 5]), PsMgr(parent=self.ps, banks=[6, 7])]
        self.lay, self.ncst = cst_layout(nl)
        self.d_x = nc.dram_tensor("xT", [P, KC * NLOC], F32, kind="ExternalInput").ap()
        self.d_cst = nc.dram_tensor("cst", [P, self.ncst], F32, kind="ExternalInput").ap()
        self.d_w2b = nc.dram_tensor("w2b", [17, nl * 256], F32, kind="ExternalInput").ap()
        self.d_cb = nc.dram_tensor("cb", [P, 256], F32, kind="ExternalInput").ap()
        self.d_rope = nc.dram_tensor("rope", [P, 2 * NLOC], F32, kind="ExternalInput").ap()
        self.d_win = nc.dram_tensor("win", [nl, P, KC * INW], F32, kind="ExternalInput").ap()
        self.d_wout = nc.dram_tensor("wout", [nl, P, KC * D], F32, kind="ExternalInput").ap()
        self.d_wup = nc.dram_tensor("wup", [nl * 11, P, KC * 512], F32, kind="ExternalInput").ap()
        self.d_wdn = nc.dram_tensor("wdn", [nl * 8, P, NFC * 128], F32, kind="ExternalInput").ap()
        self.d_y = nc.dram_tensor("y", [P, KC * NLOC], F32, kind="ExternalOutput").ap()
        self.cc1_src = [nc.dram_tensor(f"cc1s{l}", [P, 776], F32) for l in range(nl)]
        self.cc1_dst = [nc.dram_tensor(f"cc1d{l}", [4 * P, 776], F32) for l in range(nl)]
        self.cc2_src = [nc.dram_tensor(f"cc2s{l}", [P, 16], F32) for l in range(nl)]
        self.cc2_dst = [nc.dram_tensor(f"cc2d{l}", [4 * P, 16], F32) for l in range(nl)]
        if dbg:
            self.d_dbg = nc.dram_tensor("dbg", [P, dbg], F32, kind="ExternalOutput").ap()

    def pe(self, fn, R, W):
        return self.pg.emit("pe", fn, R, W)

    def act(self, fn, R, W):
        return self.pg.emit("act", fn, R, W)

    def dve(self, fn, R, W):
        return self.pg.emit("dve", fn, R, W)

    def pool(self, fn, R, W):
        return self.pg.emit("pool", fn, R, W)

    def dma(self, q, fn, R, W, key, serialize=True, inc=16):
        return self.pg.emit(q, fn, R, W, dma_key=key, serialize=serialize, inc=inc)

    def c(self, name, i=0, n=1):
        o, _ = self.lay[name]
        return self.cst[:, o + i:o + i + n]

    def build(self):
        nc, sb = self.nc, self.sb
        nl = self.nl
        self.xT = sb.alloc("xT", [P, KC, NLOC], F32)
        self.mt_tiles = [(0, NPRE)] + [(NPRE + TT * i, TT) for i in range(NSEQ // TT)]
        self.XB = [Buf(f"X{i}") for i in range(len(self.mt_tiles))]
        self.cst = sb.alloc("cst", [P, self.ncst], F32)
        self.B_cst = Buf("cst")
        self.w2b = sb.alloc("w2b", [32, nl * 256], F32)
        self.cb = sb.alloc("cb", [P, 256], BF16)
        self.ident = self.cb[:, 0:128]
        self.ones = self.cb[:, 128:256]
        self.pay = sb.alloc("pay", [P, 776], F32)
        self.S_r = self.pay[:, 0:512].rearrange("p (h v) -> p h v", h=4)
        self.S_g = self.pay[:, 512:768].rearrange("p (h v) -> p h v", h=2)
        self.Dacc = self.pay[:, 768:770]
        self.B_Sr, self.B_Sg, self.B_D = Buf("S_r"), Buf("S_g"), Buf("Dacc")
        self.Sb_r = sb.alloc("Sb_r", [P, 4, 128], BF16)
        self.Sb_g = sb.alloc("Sb_g", [P, 2, 128], BF16)
        self.B_Sbr, self.B_Sbg = Buf("Sb_r"), Buf("Sb_g")
        self.hal = sb.alloc("hal", [P, 16], F32)
        self.B_hal = Buf("hal")
        self.gh = sb.alloc("gh", [P, 4, 16], F32)
        self.B_gh = Buf("gh")
        self.rt = sb.alloc("rt", [P, 512], F32)
        self.rstd = sb.alloc("rstd", [P, 512], F32)
        self.B_rt, self.B_rstd = Buf("rt"), Buf("rstd")
        self.tmpx, self.B_tmpx = self.rt, self.B_rt
        self.sc0 = (self.rt, self.B_rt, self.rstd, self.B_rstd)
        base_mark = sb.mark()

        self.dma("sp", lambda e: e.dma_start(out=self.cst[:, :], in_=self.d_cst[:, :]), [], [self.B_cst], "c0")
        self.dma("sp", lambda e: e.dma_start(out=self.w2b[0:17, :], in_=self.d_w2b[:, :]), [], [self.B_cst], "c1")
        self.dma("pool", lambda e: e.dma_start(out=self.cb[:, :], in_=self.d_cb[:, :]), [], [self.B_cst], "c2")
        xflat = self.xT[:, :, :].rearrange("p k t -> p (k t)")
        self.dma("sp", lambda e: e.dma_start(out=xflat, in_=self.d_x[:, :]), [], self.XB, "x")

        import os
        for l in range(nl):
            if int(os.environ.get("KSTAGE", "99")) >= 1:
                self.layer(l, base_mark)

        self.dma("sp", lambda e: e.dma_start(out=self.d_y[:, :], in_=xflat), self.XB, [], "y")
        self.B_fin = Buf("fin")
        fin_reads = []
        b = Buf("finy")
        b.lw = self.pg.key_last["y"]
        fin_reads.append(b)
        if self.dbg:
            b = Buf("findbg")
            if "dbg" in self.pg.key_last:
                b.lw = self.pg.key_last["dbg"]
                fin_reads.append(b)
        self.pg.emit("sp", lambda e: e.nop(), fin_reads, [self.B_fin])
        return self.finish()

    def finish(self):
        nc, pg = self.nc, self.pg
        counts = pg.finalize()
        self.counts = counts
        n_ep = {e: max(1, (counts[e] + EPOCH - 1) // EPOCH) for e in pg.ENGS}
        keys = sorted(pg.key_cnt.keys())
        import contextlib
        with contextlib.ExitStack() as st:
            eng_sems = {e: [st.enter_context(nc.semaphore(f"s_{e}{i}")) for i in range(n_ep[e])] for e in pg.ENGS}
            dma_sems = {k: st.enter_context(nc.semaphore(f"d_{k}")) for k in keys}
            block = st.enter_context(nc.Block())

            @block.tensor
            def _(e):
                pg.replay("pe", e, eng_sems, dma_sems)

            @block.scalar
            def _(e):
                pg.replay("act", e, eng_sems, dma_sems)

            @block.vector
            def _(e):
                pg.replay("dve", e, eng_sems, dma_sems)

            @block.gpsimd
            def _(e):
                pg.replay("pool", e, eng_sems, dma_sems)

            @block.sync
            def _(e):
                pg.replay("sp", e, eng_sems, dma_sems)
        return nc

    def dbg_dump(self, ap2d, bufs, col0, ncols, parts=P):
        if not self.dbg:
            return
        self.dma("sp", lambda e: e.dma_start(out=self.d_dbg[0:parts, col0:col0 + ncols], in_=ap2d), bufs, [], "dbg")

    def rmsnorm(self, src3, src_bufs, wname, dst3, dst_bufs, n, sq3, B_sq, ps=None, sc=None):
        ps = ps or self.ps
        sc = sc or self.sc0
        self.act(lambda e: e.activation(out=sq3[:, :, 0:n], in_=src3, func=AF.Square), src_bufs, [B_sq])
        pt = ps.alloc()
        for kc in range(KC):
            self.pe(lambda e, kc=kc, pt=pt: e.matmul(pt.f32(n), lhsT=self.ones, rhs=sq3[:, kc, 0:n],
                                                     start=(kc == 0), stop=(kc == KC - 1)),
                    [B_sq, self.B_cst], [pt.buf])
        self.rstd_from(pt, n, 1.0 / D, sc)
        rstd, B_rstd = sc[2], sc[3]
        if wname is None:
            rstd_bc = rstd[:, 0:n].unsqueeze(1).broadcast_to([P, KC, n])
            self.dve(lambda e: e.tensor_tensor(out=dst3[:, :, 0:n], in0=src3, in1=rstd_bc, op=ALU.mult),
                     src_bufs + [B_rstd], dst_bufs)
            return
        for kc in range(KC):
            self.dve(lambda e, kc=kc: e.scalar_tensor_tensor(out=dst3[:, kc, 0:n], in0=src3[:, kc, :],
                                                             scalar=self.c(wname, kc), in1=rstd[:, 0:n],
                                                             op0=ALU.mult, op1=ALU.mult),
                     src_bufs + [B_rstd, self.B_cst], dst_bufs)

    def rstd_from(self, pt, n, scale, sc=None):
        rt, B_rt, rstd, B_rstd = sc or self.sc0
        self.act(lambda e, pt=pt: e.activation(out=rstd[:, 0:n], in_=pt.f32(n), func=AF.Ln,
                                               scale=scale, bias=self.c("eps")),
                 [pt.buf, self.B_cst], [B_rstd])
        self.act(lambda e: e.activation(out=rstd[:, 0:n], in_=rstd[:, 0:n], func=AF.Exp, scale=-0.5), [B_rstd], [B_rstd])

    def post_norm_residual(self, m_sb3, B_m, sq3, B_sq, wname, t0, n, xbufs):
        pt = self.ps.alloc()
        for kc in range(KC):
            self.pe(lambda e, kc=kc, pt=pt: e.matmul(pt.f32(n), lhsT=self.ones, rhs=sq3[:, kc, 0:n],
                                                     start=(kc == 0), stop=(kc == KC - 1)),
                    [B_sq, self.B_cst], [pt.buf])
        self.rstd_from(pt, n, 1.0 / D)
        for kc in range(KC):
            self.dve(lambda e, kc=kc: e.scalar_tensor_tensor(out=self.tmpx[:, 0:n], in0=m_sb3[:, kc, 0:n],
                                                             scalar=self.c(wname, kc), in1=self.rstd[:, 0:n],
                                                             op0=ALU.mult, op1=ALU.mult),
                     [B_m, self.B_rstd, self.B_cst], [self.B_tmpx])
            self.dve(lambda e, kc=kc: e.tensor_tensor(out=self.xT[:, kc, t0:t0 + n], in0=self.xT[:, kc, t0:t0 + n],
                                                      in1=self.tmpx[:, 0:n], op=ALU.add),
                     xbufs + [self.B_tmpx], xbufs)

    def layer(self, l, base_mark):
        sb = self.sb
        sb.release(base_mark)
        if l == 0:
            self.win = sb.alloc("win", [P, KC, INW], BF16)
            self.wout = sb.alloc("wout", [P, KC, D], BF16)
            self.B_win = [Buf(f"win{k}") for k in range(KC)]
            self.B_winq = [Buf(f"winq{k}") for k in range(KC)]
            self.B_win2 = [Buf(f"win2{k}") for k in range(KC)]
            self.B_winq2 = [Buf(f"winq2{k}") for k in range(KC)]
            self.B_wout = [Buf(f"wout{k}") for k in range(KC)]
            self.alloc_mixer()
        def wdma(kc, c0, c1, bw, key):
            self.dma("pool", lambda e: e.dma_start(out=self.win[:, kc, c0:c1], in_=self.d_win[l, :, kc * INW + c0:kc * INW + c1]),
                     [], [bw], key)
        for kc in range(KC):
            wdma(kc, C_RK, C_RG, self.B_win[kc], f"winA{kc}")
            wdma(kc, C_GK, INW, self.B_win2[kc], f"winC{kc}")
        later = []
        for kc in range(KC):
            later.append(lambda kc=kc: wdma(kc, C_RQ, C_RK, self.B_winq[kc], f"winB{kc}"))
            later.append(lambda kc=kc: wdma(kc, C_RG, C_GK, self.B_winq2[kc], f"winD{kc}"))
        for kc in range(KC):
            later.append(lambda kc=kc: self.dma("pool", lambda e: e.dma_start(out=self.wout[:, kc, :], in_=self.d_wout[l, :, kc * D:(kc + 1) * D]),
                                                [], [self.B_wout[kc]], f"wout{kc}"))
        def fold(kc, c0, c1):
            bw = self.win_buf(kc, c0)
            self.dve(lambda e: e.tensor_scalar(out=self.win[:, kc, c0:c1], in0=self.win[:, kc, c0:c1], scalar1=self.c(f"pmn{l}", kc),
                                               scalar2=None, op0=ALU.mult), [bw, self.B_cst], [bw])
        for kc in range(KC):
            fold(kc, C_RK, C_RG)
            fold(kc, C_GK, INW)
        self.deferred_folds = later + [(lambda kc=kc, c0=c0, c1=c1: fold(kc, c0, c1)) for kc in range(KC) for (c0, c1) in ((C_RQ, C_RK), (C_RG, C_GK))]
        self.dve(lambda e: e.memset(self.pay[:, :], 0.0), [], [self.B_Sr, self.B_Sg, self.B_D])
        self.dve(lambda e: e.memset(self.Dacc, 1.0), [], [self.B_D])
        for S in self.sets:
            self.dve(lambda e, S=S: e.memset(S.gaT[:, :], 1.0), [], [S.B_ga])
        self.dve(lambda e: e.memset(self.kz[:, :, :], 0.0), [], [self.B_kz])
        import os
        STAGE = int(os.environ.get("KSTAGE", "99"))
        if STAGE < 3:
            return
        self.mixer_pass(l, False)
        while self.deferred_folds:
            self.deferred_folds.pop(0)()
        wno, _ = self.lay[f"retnw{l}"]
        wn_bc = self.cst[:, wno:wno + 8].unsqueeze(2).broadcast_to([P, KC, D])
        self.dve(lambda e: e.tensor_tensor(out=self.wout[:, :, :], in0=self.wout[:, :, :], in1=wn_bc, op=ALU.mult),
                 self.B_wout + [self.B_cst], self.B_wout)
        if STAGE < 4:
            return

        def mid():
            self.exchange_state(l)
            self.pg.barrier()
            self.dve(lambda e: e.memset(self.qz[:, :, :], 0.0), [], [self.B_qz])
        self.mixer_pass(l, True, pre=2, mid_hook=mid)
        if STAGE < 6:
            return
        halo_keys = self.halo_send(l)
        self.pg.barrier(exclude=halo_keys)
        sb.release(base_mark)
        self.ffn(l)
        self.dve(lambda e: e.tensor_scalar(out=self.xT[:, :, 0:NPRE], in0=self.xT[:, :, 0:NPRE], scalar1=self.c("flag"),
                                           scalar2=None, op0=ALU.mult), [self.XB[0], self.B_cst], [self.XB[0]])
        self.pg.barrier()

    def mixer_pass(self, l, with_out, pre=0, mid_hook=None):
        import os
        NT = len(self.mt_tiles)
        ORDER = os.environ.get("KORD", "a,b1,b2").split(",")
        K1 = int(os.environ.get("KK1", "6"))
        a_gen, a_idx, a_cnt = None, -1, 0
        b1_gen, b1_idx = None, -1
        b2_gen, b2_idx = None, -1
        a_ready = [False] * NT
        b1_done = [False] * NT
        b2_done = [False] * NT

        def done2(i):
            return i < 0 or b2_done[i]
        for i in range(pre):
            for _ in self.gen_A(l, i, with_out, self.sets[i % 2]):
                pass
            a_ready[i] = True
            a_idx = i
        if mid_hook is not None:
            mid_hook()
        while True:
            if a_gen is None and a_idx + 1 < NT and done2(a_idx + 1 - 2):
                a_idx, a_cnt = a_idx + 1, 0
                a_gen = self.gen_A(l, a_idx, with_out, self.sets[a_idx % 2])
            if b1_gen is None and b1_idx + 1 < NT and a_ready[b1_idx + 1] and done2(b1_idx + 1 - 2):
                b1_idx += 1
                b1_gen = self.gen_B(l, b1_idx, with_out, self.sets[b1_idx % 2])
            if b2_gen is None and b2_idx + 1 < NT and b1_done[b2_idx + 1]:
                b2_idx += 1
                b2_gen = self.gen_B2(l, b2_idx, with_out, self.sets[b2_idx % 2])
            if a_gen is None and b1_gen is None and b2_gen is None:
                if b2_idx + 1 >= NT:
                    break
                raise RuntimeError("mixer pipeline stalled")
            def step_a():
                nonlocal a_gen, a_cnt
                if a_gen is not None:
                    try:
                        next(a_gen)
                        if not with_out and getattr(self, "deferred_folds", None):
                            self.deferred_folds.pop(0)()
                        a_cnt += 1
                        if a_cnt >= K1:
                            a_ready[a_idx] = True
                    except StopIteration:
                        a_ready[a_idx] = True
                        a_gen = None

            def step_b1():
                nonlocal b1_gen
                if b1_gen is not None:
                    try:
                        next(b1_gen)
                    except StopIteration:
                        b1_done[b1_idx] = True
                        b1_gen = None

            def step_b2():
                nonlocal b2_gen
                if b2_gen is not None:
                    try:
                        next(b2_gen)
                    except StopIteration:
                        b2_done[b2_idx] = True
                        b2_gen = None
            for nm in ORDER:
                {"a": step_a, "b1": step_b1, "b2": step_b2}[nm]()

    def alloc_mixer(self):
        sb = self.sb
        A = sb.alloc

        class NS:
            pass
        self.sets = []
        for i in range(2):
            S = NS()
            S.ropeT = A(f"ropeT{i}", [P, 2, TT], F32)
            S.hT = A(f"hT{i}", [P, KC, TT], BF16)
            S.kr = A(f"kr{i}", [P, 4, TT], BF16)
            S.kg = A(f"kg{i}", [P, 2, TT], F32)
            S.gaT = A(f"gaT{i}", [32, TT], F32)
            S.vt = A(f"vt{i}", [P, 1024], BF16)
            for nm in ("rope", "hT", "kr", "qr", "kg", "qg", "gr", "ga", "vt", "mT"):
                setattr(S, "B_" + nm, Buf(f"{nm}{i}"))
            self.sets.append(S)
        self.sqn = A("sqn", [P, KC, TT], BF16)
        self.B_sqn = Buf("sqn")
        self.rp1 = A("rp1", [P, 4, TT], F32)
        self.rp2 = A("rp2", [P, 4, TT], F32)
        self.B_rp1, self.B_rp2 = Buf("rp1"), Buf("rp2")
        rstdA = A("rstdA", [P, TT], F32)
        B_rstdA = Buf("rstdA")
        self.scA = (rstdA, B_rstdA, rstdA, B_rstdA)
        self.ez = A("ez", [P, 256], F32)
        self.lsp = A("lsp", [P, 256], F32)
        self.B_ez, self.B_lsp = Buf("ez"), Buf("lsp")
        self.E1 = A("E1", [P, 2, CH], F32)
        self.E2 = A("E2", [P, 2, CH], F32)
        self.B_E1, self.B_E2 = Buf("E1"), Buf("E2")
        self.kz = A("kz", [P, 4, CH], BF16)
        self.B_qz, self.B_kz = Buf("qz"), Buf("kz")
        self.st = A("st", [P, 64], F32)
        self.B_st = Buf("st")
        self.dprime = A("dprime", [P, 8], F32)
        self.B_dprime = Buf("dprime")
        rstdB = A("rstdB", [P, TT], F32)
        B_rstdB = Buf("rstdB")
        self.scB = (rstdB, B_rstdB, rstdB, B_rstdB)
        for i, S in enumerate(self.sets):
            S.qr = A(f"qr{i}", [P, 4, TT], BF16)
            S.qg = A(f"qg{i}", [P, 2, TT], F32)
            S.gr = A(f"gr{i}", [P, 8, TT], BF16)
        m = sb.mark()
        self.gath = A("gath", [P, 4, 776], F32)
        self.B_gath = Buf("gath")
        self.ubuf = A("ubuf", [P, 768], F32)
        self.B_ubuf = Buf("ubuf")
        e1 = sb.mark()
        sb.release(m)
        self.ktok = A("ktok", [P, 8, 128], BF16)
        self.B_ktok = Buf("ktok")
        self.tmpS = A("tmpS", [P, 4, 128], F32)
        self.B_tmpS = Buf("tmpS")
        for i, S in enumerate(self.sets):
            S.mT = A(f"mT{i}", [P, KC, TT], BF16)
        self.qz = A("qz", [P, 4, CH], BF16)
        self.A_r = A("A_r", [P, 4, CH], BF16)
        self.A_g = A("A_g", [P, 4, CH], BF16)
        self.B_Ar, self.B_Ag = Buf("A_r"), Buf("A_g")
        self.sqo = A("sqo", [P, 8, 128], BF16)
        self.B_sqo = Buf("sqo")
        self.tmp4 = self.sqo[:, :, :].rearrange("p a b -> p (a b)").bitcast(F32).rearrange("p (a b) -> p a b", a=4)
        self.on = A("on", [P, 8, 128], BF16)
        self.B_on = Buf("on")
        self.sqm = A("sqm", [P, KC, TT], BF16)
        self.B_sqm = Buf("sqm")
        sb.release(max(sb.mark(), e1))

    def win_buf(self, kc, col):
        if col < C_RK:
            return self.B_winq[kc]
        if col < C_RG:
            return self.B_win[kc]
        if col < C_GK:
            return self.B_winq2[kc]
        return self.B_win2[kc]

    def fm_proj(self, S, cols, m, n):
        pt = self.psA.alloc()
        v = pt.v3(4, TT)
        for j, col0 in enumerate(cols):
            for kc in range(KC):
                bw = self.win_buf(kc, col0)
                self.pe(lambda e, kc=kc, j=j, col0=col0: e.matmul(v[0:m, j, 0:n], lhsT=self.win[:, kc, col0:col0 + m],
                                                                  rhs=S.hT[:, kc, 0:n], start=(kc == 0), stop=(kc == KC - 1)),
                        [bw, S.B_hT], [pt.buf])
        return pt, v

    def rope_evac(self, S, pt, v, n, dst3, B_dst, dec_name):
        c_bc = S.ropeT[:, 0, 0:n].unsqueeze(1).broadcast_to([P, 4, n])
        self.dve(lambda e: e.tensor_tensor(out=self.rp1[:, :, 0:n], in0=v[:, :, 0:n], in1=c_bc, op=ALU.mult),
                 [pt.buf, S.B_rope], [self.B_rp1])
        for lo, hi in ((0, 64), (64, 0)):
            s_bc = S.ropeT[lo:lo + 64, 1, 0:n].unsqueeze(1).broadcast_to([64, 4, n])
            self.dve(lambda e, lo=lo, hi=hi, s_bc=s_bc: e.tensor_tensor(out=self.rp2[lo:lo + 64, :, 0:n], in0=v[hi:hi + 64, :, 0:n],
                                                                       in1=s_bc, op=ALU.mult),
                     [pt.buf, S.B_rope], [self.B_rp2])
        self.dve(lambda e: e.tensor_tensor(out=self.rp1[:, :, 0:n], in0=self.rp1[:, :, 0:n], in1=self.rp2[:, :, 0:n], op=ALU.add),
                 [self.B_rp1, self.B_rp2], [self.B_rp1])
        o, _ = self.lay[dec_name]
        dec = self.cst[:, o:o + 512].rearrange("p (h t) -> p h t", h=4)[:, :, 0:n]
        self.dve(lambda e: e.tensor_tensor(out=dst3, in0=self.rp1[:, :, 0:n], in1=dec, op=ALU.mult),
                 [self.B_rp1, self.B_cst], [B_dst])

    def gen_A(self, l, ti, with_out, S):
        t0, n = self.mt_tiles[ti]
        XB = [self.XB[ti]]
        self.dma("sp", lambda e: e.dma_start(out=S.ropeT[:, 0, 0:n], in_=self.d_rope[:, t0:t0 + n]),
                 [], [S.B_rope], "rope0")
        self.dma("sp", lambda e: e.dma_start(out=S.ropeT[:, 1, 0:n], in_=self.d_rope[:, NLOC + t0:NLOC + t0 + n]),
                 [], [S.B_rope], "rope1")
        self.rmsnorm(self.xT[:, :, t0:t0 + n], XB, None, S.hT, [S.B_hT], n, self.sqn, self.B_sqn, ps=self.psA, sc=self.scA)
        yield
        pt, v = self.fm_proj(S, [C_GA], 16, n)
        if GATE_IN_ON_DVE:
            self.dve(lambda e, v=v: e.tensor_copy(out=S.gaT[0:16, 0:n], in_=v[0:16, 0, 0:n]), [pt.buf], [S.B_ga])
        else:
            self.act(lambda e, v=v: e.activation(out=S.gaT[0:16, 0:n], in_=v[0:16, 0, 0:n], func=AF.Copy),
                     [pt.buf], [S.B_ga])
        yield
        pt, v = self.fm_proj(S, [C_GK, C_GK + 128], 128, n)
        if GATE_IN_ON_DVE:
            self.dve(lambda e, v=v: e.tensor_copy(out=S.kg[:, :, 0:n], in_=v[:, 0:2, 0:n]), [pt.buf], [S.B_kg])
        else:
            self.act(lambda e, v=v: e.activation(out=S.kg[:, :, 0:n], in_=v[:, 0:2, 0:n], func=AF.Copy), [pt.buf], [S.B_kg])
        yield
        pt, v = self.fm_proj(S, [C_RK + h * 128 for h in range(4)], 128, n)
        self.rope_evac(S, pt, v, n, S.kr[:, :, 0:n], S.B_kr, "kinv")
        yield
        for half, col in ((0, C_RV), (1, C_GV)):
            pt = self.psA.alloc()
            for kc in range(KC):
                self.pe(lambda e, kc=kc, pt=pt, col=col: e.matmul(
                    pt.t[0:n, 0:512], lhsT=S.hT[:, kc, 0:n], rhs=self.win[:, kc, col:col + 512],
                    start=(kc == 0), stop=(kc == KC - 1)), [S.B_hT, self.win_buf(kc, col)], [pt.buf])
            if V_ON_DVE:
                self.dve(lambda e, pt=pt, half=half: e.tensor_copy(out=S.vt[0:n, half * 512:(half + 1) * 512], in_=pt.t[0:n, 0:512]),
                         [pt.buf], [S.B_vt])
            else:
                self.act(lambda e, pt=pt, half=half: e.activation(
                    out=S.vt[0:n, half * 512:(half + 1) * 512], in_=pt.t[0:n, 0:512], func=AF.Copy),
                    [pt.buf], [S.B_vt])
            yield
        if with_out:
            pt, v = self.fm_proj(S, [C_RQ + h * 128 for h in range(4)], 128, n)
            self.rope_evac(S, pt, v, n, S.qr[:, :, 0:n], S.B_qr, "xiq")
            yield
            pt, v = self.fm_proj(S, [C_GQ, C_GQ + 128], 128, n)
            self.act(lambda e, v=v: e.mul(out=S.qg[:, :, 0:n], in_=v[:, 0:2, 0:n], mul=0.125), [pt.buf], [S.B_qg])
            yield
            for g4 in range(2):
                base = C_RG if g4 == 0 else C_GG
                pt, v = self.fm_proj(S, [base + h * 128 for h in range(4)], 128, n)
                self.act(lambda e, v=v, g4=g4: e.activation(out=S.gr[:, 4 * g4:4 * g4 + 4, 0:n], in_=v[:, :, 0:n], func=AF.Silu),
                         [pt.buf], [S.B_gr])
            yield

    def gen_B(self, l, ti, with_out, S):
        import os
        SUB = int(os.environ.get("KM2SUB", "99")) if with_out else 99
        if SUB < 2:
            return
        t0, cn = self.mt_tiles[ti]
        is_pre = (ti == 0)
        ps = self.psB
        Bv = S.B_vt
        pz = ps.alloc()
        self.pe(lambda e: e.matmul(pz.t[0:cn, 0:256], lhsT=S.gaT[0:17, 0:cn], rhs=self.w2b[0:17, l * 256:(l + 1) * 256],
                                   start=True, stop=True), [S.B_ga, self.B_cst], [pz.buf])
        self.act(lambda e: e.activation(out=self.ez[0:cn, :], in_=pz.t[0:cn, 0:256], func=AF.Exp, scale=-1.0),
                 [pz.buf], [self.B_ez])
        self.act(lambda e: e.activation(out=self.lsp[0:cn, :], in_=self.ez[0:cn, :], func=AF.Ln, bias=self.c("one")[0:cn, :]),
                 [self.B_ez, self.B_cst], [self.B_lsp])
        if is_pre:
            self.dve(lambda e: e.tensor_scalar(out=self.lsp[0:cn, :], in0=self.lsp[0:cn, :], scalar1=self.c("flag")[0:cn, :],
                                               scalar2=None, op0=ALU.mult), [self.B_lsp, self.B_cst], [self.B_lsp])
        yield
        mt = self.c("mt", 0, 128)
        pc3 = pz.t[:, 256:512].rearrange("p (a b) -> p a b", a=2)
        for hp in range(2):
            self.pe(lambda e, hp=hp: e.matmul(pc3[:, hp, 0:cn], lhsT=self.lsp[0:cn, hp * 128:(hp + 1) * 128], rhs=mt[0:cn, 0:cn],
                                              start=True, stop=True), [self.B_lsp, self.B_cst], [pz.buf])
        self.act(lambda e: e.activation(out=self.E2[:, :, 0:cn], in_=pc3[:, :, 0:cn], func=AF.Exp, scale=1.0 / GLA_TAU),
                 [pz.buf], [self.B_E2])
        self.act(lambda e: e.activation(out=self.E1[:, :, 0:cn], in_=pc3[:, :, 0:cn], func=AF.Exp, scale=-1.0 / GLA_TAU),
                 [pz.buf], [self.B_E1])
        yield
        kz4 = self.kz[:, :, :].rearrange("p (a b) t -> p a b t", b=2)
        for half in range(2):
            lo = 64 * half
            self.dve(lambda e, lo=lo, half=half: e.tensor_tensor(out=kz4[lo:lo + 64, :, half, 0:cn], in0=S.kg[lo:lo + 64, :, 0:cn],
                                                                 in1=self.E2[lo:lo + 64, :, 0:cn], op=ALU.mult),
                     [S.B_kg, self.B_E2], [self.B_kz])
        if with_out:
            qz4 = self.qz[:, :, :].rearrange("p (a b) t -> p a b t", b=2)
            for half in range(2):
                lo = 64 * half
                self.dve(lambda e, lo=lo, half=half: e.tensor_tensor(out=qz4[lo:lo + 64, :, half, 0:cn], in0=S.qg[lo:lo + 64, :, 0:cn],
                                                                     in1=self.E1[lo:lo + 64, :, 0:cn], op=ALU.mult),
                         [S.B_qg, self.B_E1], [self.B_qz])
        yield
        if with_out:
            pa = ps.alloc()
            pa3 = pa.v3(4, CH)
            for h in range(4):
                self.pe(lambda e, h=h: e.matmul(pa3[0:cn, h, 0:cn], lhsT=S.kr[:, h, 0:cn], rhs=S.qr[:, h, 0:cn],
                                                start=True, stop=True), [S.B_kr, S.B_qr], [pa.buf])
            mask = mt[0:cn, 0:cn].unsqueeze(1).broadcast_to([cn, 4, cn])
            self.dve(lambda e: e.tensor_tensor(out=self.A_r[0:cn, :, 0:cn], in0=pa3[0:cn, :, 0:cn], in1=mask, op=ALU.mult),
                     [pa.buf, self.B_cst], [self.B_Ar])
            yield
            pg_ = ps.alloc()
            pg3 = pg_.v3(4, CH)
            for h in range(4):
                self.pe(lambda e, h=h: e.matmul(pg3[0:cn, h, 0:cn], lhsT=self.kz[:, h, 0:cn], rhs=self.qz[:, h, 0:cn],
                                                start=True, stop=True), [self.B_kz, self.B_qz], [pg_.buf])
            self.dve(lambda e: e.tensor_tensor(out=self.A_g[0:cn, :, 0:cn], in0=pg3[0:cn, :, 0:cn], in1=mask, op=ALU.mult),
                     [pg_.buf, self.B_cst], [self.B_Ag])
            yield
            po_r = self.psO[ti % 2].alloc()
            por3 = po_r.v3(4, 128)
            for h in range(4):
                self.pe(lambda e, h=h: e.matmul(por3[0:cn, h, :], lhsT=self.A_r[0:cn, h, 0:cn], rhs=S.vt[0:cn, h * 128:(h + 1) * 128],
                                                start=True, stop=False), [self.B_Ar, Bv], [po_r.buf])
                self.pe(lambda e, h=h: e.matmul(por3[0:cn, h, :], lhsT=S.qr[:, h, 0:cn], rhs=self.Sb_r[:, h, :],
                                                start=False, stop=True), [S.B_qr, self.B_Sbr], [po_r.buf])
            yield
            po_g = self.psO[ti % 2].alloc()
            pog3 = po_g.v3(4, 128)
            S.po_r, S.po_g = po_r, po_g
            for h in range(4):
                self.pe(lambda e, h=h: e.matmul(pog3[0:cn, h, :], lhsT=self.A_g[0:cn, h, 0:cn],
                                                rhs=S.vt[0:cn, 512 + h * 128:512 + (h + 1) * 128],
                                                start=True, stop=False), [self.B_Ag, Bv], [po_g.buf])
                self.pe(lambda e, h=h: e.matmul(pog3[0:cn, h, :], lhsT=self.qz[:, h, 0:cn],
                                                rhs=self.Sb_g[:, h // 2, :], start=False, stop=True),
                        [self.B_qz, self.B_Sbg], [po_g.buf])
            yield
        pk = ps.alloc()
        pk3 = pk.bf3(8, 128)
        for h in range(4):
            self.pe(lambda e, h=h: e.transpose(pk3[0:cn, h, :], S.kr[:, h, 0:cn], self.ident), [S.B_kr, self.B_cst], [pk.buf])
        for h in range(4):
            self.pe(lambda e, h=h: e.transpose(pk3[0:cn, 4 + h, :], self.kz[:, h, 0:cn], self.ident), [self.B_kz, self.B_cst], [pk.buf])
        if CHAIN_ON_DVE:
            self.dve(lambda e: e.tensor_copy(out=self.ktok[0:cn, :, :], in_=pk3[0:cn, :, :]), [pk.buf], [self.B_ktok])
        else:
            self.act(lambda e: e.activation(out=self.ktok[0:cn, :, :], in_=pk3[0:cn, :, :], func=AF.Copy), [pk.buf], [self.B_ktok])
        yield
        pkv = ps.alloc()
        pkv3 = pkv.v3(4, 128)
        for h in range(4):
            self.pe(lambda e, h=h: e.matmul(pkv3[:, h, :], lhsT=self.ktok[0:cn, h, :], rhs=S.vt[0:cn, h * 128:(h + 1) * 128],
                                            start=True, stop=True), [self.B_ktok, Bv], [pkv.buf])
        gname = "gr16" if is_pre else "gr128"
        go, _ = self.lay[gname]
        gbc = self.cst[:, go:go + 4].unsqueeze(2).broadcast_to([P, 4, 128])
        self.dve(lambda e: e.tensor_tensor(out=self.tmpS[:, 0:4, :], in0=self.S_r, in1=pkv3[:, :, :], op=ALU.add),
                 [self.B_Sr, pkv.buf], [self.B_tmpS])
        self.dve(lambda e: e.tensor_tensor(out=self.S_r, in0=self.tmpS[:, 0:4, :], in1=gbc, op=ALU.mult),
                 [self.B_tmpS, self.B_cst], [self.B_Sr])
        if with_out:
            if CHAIN_ON_DVE:
                self.dve(lambda e: e.tensor_copy(out=self.Sb_r[:, :, :], in_=self.S_r), [self.B_Sr], [self.B_Sbr])
            else:
                self.act(lambda e: e.activation(out=self.Sb_r[:, :, :], in_=self.S_r, func=AF.Copy), [self.B_Sr], [self.B_Sbr])
        yield
        pkg = ps.alloc()
        pkg3 = pkg.v3(2, 128)
        for h in range(4):
            self.pe(lambda e, h=h: e.matmul(pkg3[:, h // 2, :], lhsT=self.ktok[0:cn, 4 + h, :],
                                            rhs=S.vt[0:cn, 512 + h * 128:512 + (h + 1) * 128],
                                            start=(h % 2 == 0), stop=(h % 2 == 1)), [self.B_ktok, Bv], [pkg.buf])
        self.dve(lambda e: e.tensor_tensor(out=self.tmpS[:, 0:2, :], in0=self.S_g, in1=pkg3[:, :, :], op=ALU.add),
                 [self.B_Sg, pkg.buf], [self.B_tmpS])
        for hp in range(2):
            self.dve(lambda e, hp=hp: e.tensor_scalar(out=self.S_g[:, hp, :], in0=self.tmpS[:, hp, :],
                                                      scalar1=self.E1[:, hp, cn - 1:cn], scalar2=None, op0=ALU.mult),
                     [self.B_tmpS, self.B_E1], [self.B_Sg])
        if with_out:
            if CHAIN_ON_DVE:
                self.dve(lambda e: e.tensor_copy(out=self.Sb_g[:, :, :], in_=self.S_g), [self.B_Sg], [self.B_Sbg])
            else:
                self.act(lambda e: e.activation(out=self.Sb_g[:, :, :], in_=self.S_g, func=AF.Copy), [self.B_Sg], [self.B_Sbg])
        else:
            self.dve(lambda e: e.tensor_tensor(out=self.Dacc, in0=self.Dacc, in1=self.E1[:, :, cn - 1], op=ALU.mult),
                     [self.B_D, self.B_E1], [self.B_D])
        yield
        return

    def gen_B2(self, l, ti, with_out, S):
        if not with_out:
            return
        SUB = 99
        t0, cn = self.mt_tiles[ti]
        ps = self.psA
        po_r, po_g = S.po_r, S.po_g
        por3, pog3 = po_r.v3(4, 128), po_g.v3(4, 128)
        st = self.st
        s1 = st[0:cn, 0:4]
        s2 = st[0:cn, 4:12]
        mean = st[0:cn, 12:16]
        msq = st[0:cn, 16:20]
        var = st[0:cn, 20:28]
        rtv = st[0:cn, 28:36]
        rsd = st[0:cn, 36:44]
        nmr = st[0:cn, 44:48]
        self.dve(lambda e: e.reduce_sum(out=s1, in_=por3[0:cn, :, :], axis=AX.X), [po_r.buf], [self.B_st])
        self.act(lambda e: e.activation(out=self.sqo[0:cn, 0:4, :], in_=por3[0:cn, :, :], func=AF.Square), [po_r.buf], [self.B_sqo])
        self.act(lambda e: e.activation(out=self.sqo[0:cn, 4:8, :], in_=pog3[0:cn, :, :], func=AF.Square), [po_g.buf], [self.B_sqo])
        yield
        self.dve(lambda e: e.reduce_sum(out=s2, in_=self.sqo[0:cn, :, :], axis=AX.X), [self.B_sqo], [self.B_st])
        self.dve(lambda e: e.tensor_tensor(out=msq, in0=s1, in1=s1, op=ALU.mult), [self.B_st], [self.B_st])
        self.dve(lambda e: e.scalar_tensor_tensor(out=s2[:, 0:4], in0=msq, scalar=-1.0 / 128, in1=s2[:, 0:4], op0=ALU.mult, op1=ALU.add),
                 [self.B_st], [self.B_st])
        yield
        self.act(lambda e: e.activation(out=rsd, in_=s2, func=AF.Ln, scale=1.0 / 128, bias=self.c("eps")[0:cn, :]), [self.B_st, self.B_cst], [self.B_st])
        self.act(lambda e: e.activation(out=rsd, in_=rsd, func=AF.Exp, scale=-0.5), [self.B_st], [self.B_st])
        self.dve(lambda e: e.tensor_scalar(out=mean, in0=s1, scalar1=1.0 / 128, scalar2=None, op0=ALU.mult), [self.B_st], [self.B_st])
        yield
        mean_bc = mean.unsqueeze(2).broadcast_to([cn, 4, 128])
        rsdr_bc = rsd[:, 0:4].unsqueeze(2).broadcast_to([cn, 4, 128])
        rsdg_bc = rsd[:, 4:8].unsqueeze(2).broadcast_to([cn, 4, 128])
        self.dve(lambda e: e.tensor_tensor(out=self.tmp4[0:cn, :, :], in0=por3[0:cn, :, :], in1=mean_bc, op=ALU.subtract),
                 [po_r.buf, self.B_st], [self.B_sqo])
        self.dve(lambda e: e.tensor_tensor(out=self.on[0:cn, 0:4, :], in0=self.tmp4[0:cn, :, :], in1=rsdr_bc, op=ALU.mult),
                 [self.B_sqo, self.B_st], [self.B_on])
        self.dve(lambda e: e.tensor_tensor(out=self.on[0:cn, 4:8, :], in0=pog3[0:cn, :, :], in1=rsdg_bc, op=ALU.mult),
                 [po_g.buf, self.B_st], [self.B_on])
        yield
        if SUB < 4:
            return
        pT = ps.alloc()
        pT3 = pT.bf3(8, CH)
        for h in range(8):
            self.pe(lambda e, h=h: e.transpose(pT3[:, h, 0:cn], self.on[0:cn, h, :], self.ident[0:cn, 0:cn]),
                    [self.B_on, self.B_cst], [pT.buf])
        self.dve(lambda e: e.tensor_tensor(out=S.mT[:, :, 0:cn], in0=pT3[:, :, 0:cn], in1=S.gr[:, :, 0:cn], op=ALU.mult),
                 [pT.buf, S.B_gr], [S.B_mT])
        yield
        if SUB < 5:
            return
        n = cn
        pts = [po_r, po_g]
        for oc in range(KC):
            pt = pts[oc // 4]
            v = pt.v3(4, TT)[:, oc % 4, 0:n]
            for kc in range(KC):
                self.pe(lambda e, kc=kc, v=v, oc=oc: e.matmul(v, lhsT=self.wout[:, kc, oc * 128:(oc + 1) * 128],
                                                              rhs=S.mT[:, kc, 0:n], start=(kc == 0), stop=(kc == KC - 1)),
                        [self.B_wout[kc], S.B_mT], [pt.buf])
            if oc % 4 == 3:
                self.act(lambda e, pt=pt, oc=oc: e.activation(out=self.sqm[:, oc - 3:oc + 1, 0:n], in_=pt.v3(4, TT)[:, :, 0:n], func=AF.Square),
                         [pt.buf], [self.B_sqm])
                yield
        pss = ps.alloc()
        for kc in range(KC):
            self.pe(lambda e, kc=kc: e.matmul(pss.f32(n), lhsT=self.ones, rhs=self.sqm[:, kc, 0:n],
                                              start=(kc == 0), stop=(kc == KC - 1)), [self.B_sqm, self.B_cst], [pss.buf])
        self.rstd_from(pss, n, 1.0 / D, self.scB)
        rstd, B_rstd = self.scB[2], self.scB[3]
        yield
        if SUB < 6:
            return
        xb = [self.XB[ti]]
        pno, _ = self.lay[f"pon{l}"]
        rstd_bc = rstd[:, 0:n].unsqueeze(1).broadcast_to([P, 4, n])
        for b4 in range(2):
            pt = pts[b4]
            pw_bc = self.cst[:, pno + 4 * b4:pno + 4 * b4 + 4].unsqueeze(2).broadcast_to([P, 4, n])
            self.dve(lambda e, pt=pt: e.tensor_tensor(out=self.tmp4[:, :, 0:n], in0=pt.v3(4, TT)[:, :, 0:n], in1=rstd_bc, op=ALU.mult),
                     [pt.buf, B_rstd], [self.B_sqo])
            self.dve(lambda e, pw_bc=pw_bc: e.tensor_tensor(out=self.tmp4[:, :, 0:n], in0=self.tmp4[:, :, 0:n], in1=pw_bc, op=ALU.mult),
                     [self.B_sqo, self.B_cst], [self.B_sqo])
            self.dve(lambda e, b4=b4: e.tensor_tensor(out=self.xT[:, 4 * b4:4 * b4 + 4, t0:t0 + n], in0=self.xT[:, 4 * b4:4 * b4 + 4, t0:t0 + n],
                                                      in1=self.tmp4[:, :, 0:n], op=ALU.add), xb + [self.B_sqo], xb)
            yield

    def exchange_state(self, l):
        nc = self.nc
        src, dst = self.cc1_src[l], self.cc1_dst[l]
        B_src, B_dst = Buf("cc1src"), Buf("cc1dst")
        self.dma("pool", lambda e: e.dma_start(out=src.ap()[:, :], in_=self.pay[:, :]), [self.B_Sr, self.B_Sg, self.B_D], [B_src], f"cc1a{l}")
        self.dma("pool", lambda e: e.collective_compute("AllGather", ALU.bypass, replica_groups=[[0, 1, 2, 3], [4, 5, 6, 7]],
                                                        ins=[src.ap().opt()], outs=[dst.ap().opt()]),
                 [B_src], [B_dst], f"cc1b{l}", inc=1)
        self.dma("pool", lambda e: e.dma_start(out=self.gath[:, :, :], in_=dst.ap().rearrange("(r p) f -> p r f", p=P)),
                 [B_dst], [self.B_gath], f"cc1c{l}")
        self.dve(lambda e: e.memset(self.pay[:, :], 0.0), [], [self.B_Sr, self.B_Sg, self.B_D])
        dro, _ = self.lay["drt"]
        for i in range(3):
            a_i = self.c("acoef", i)
            self.dve(lambda e, i=i, a_i=a_i: e.tensor_scalar(out=self.dprime[:, 0:4], in0=self.cst[:, dro + 4 * i:dro + 4 * i + 4],
                                                              scalar1=-1.0, scalar2=a_i, op0=ALU.add, op1=ALU.mult),
                     [self.B_cst], [self.B_dprime])
            self.dve(lambda e, i=i, a_i=a_i: e.tensor_scalar(out=self.dprime[:, 4:6], in0=self.gath[:, i, 768:770],
                                                              scalar1=-1.0, scalar2=a_i, op0=ALU.add, op1=ALU.mult),
                     [self.B_gath, self.B_cst], [self.B_dprime])
            self.dve(lambda e: e.tensor_scalar(out=self.dprime[:, 0:6], in0=self.dprime[:, 0:6], scalar1=1.0, scalar2=None, op0=ALU.add),
                     [self.B_dprime], [self.B_dprime])
            self.dve(lambda e, i=i, a_i=a_i: e.tensor_scalar(out=self.ubuf[:, :], in0=self.gath[:, i, 0:768], scalar1=a_i, scalar2=None,
                                                              op0=ALU.mult), [self.B_gath, self.B_cst], [self.B_ubuf])
            dr_bc = self.dprime[:, 0:4].unsqueeze(2).broadcast_to([P, 4, 128])
            dg_bc = self.dprime[:, 4:6].unsqueeze(2).broadcast_to([P, 2, 128])
            self.dve(lambda e, dr_bc=dr_bc: e.tensor_tensor(out=self.S_r, in0=self.S_r, in1=dr_bc, op=ALU.mult),
                     [self.B_Sr, self.B_dprime], [self.B_Sr])
            self.dve(lambda e, dg_bc=dg_bc: e.tensor_tensor(out=self.S_g, in0=self.S_g, in1=dg_bc, op=ALU.mult),
                     [self.B_Sg, self.B_dprime], [self.B_Sg])
            self.dve(lambda e: e.tensor_tensor(out=self.pay[:, 0:768], in0=self.pay[:, 0:768], in1=self.ubuf[:, :], op=ALU.add),
                     [self.B_Sr, self.B_Sg, self.B_ubuf], [self.B_Sr, self.B_Sg])
        self.act(lambda e: e.activation(out=self.Sb_r[:, :, :], in_=self.S_r, func=AF.Copy), [self.B_Sr], [self.B_Sbr])
        self.act(lambda e: e.activation(out=self.Sb_g[:, :, :], in_=self.S_g, func=AF.Copy), [self.B_Sg], [self.B_Sbg])

    def halo_send(self, l):
        src, dst = self.cc2_src[l], self.cc2_dst[l]
        B_src, B_dst = Buf("cc2src"), Buf("cc2dst")
        last = self.XB[-1]
        self.dve(lambda e: e.tensor_copy(out=self.hal[:, :].rearrange("p (k t) -> p k t", k=KC), in_=self.xT[:, :, NLOC - 2:NLOC]),
                 [last], [self.B_hal])
        self.dma("sp", lambda e: e.dma_start(out=src.ap()[:, :], in_=self.hal[:, :]), [self.B_hal], [B_src], f"cc2a{l}")
        self.dma("pool", lambda e: e.collective_compute("AllGather", ALU.bypass, replica_groups=[[0, 1, 2, 3], [4, 5, 6, 7]],
                                                        ins=[src.ap().opt()], outs=[dst.ap().opt()]),
                 [B_src], [B_dst], f"cc2b{l}", inc=1)
        self.dma("sp", lambda e: e.dma_start(out=self.gh[:, :, :], in_=dst.ap().rearrange("(r p) f -> p r f", p=P)),
                 [B_dst], [self.B_gh], f"cc2c{l}")
        return {f"cc2a{l}", f"cc2b{l}", f"cc2c{l}"}

    def halo_recv(self, l):
        self.dve(lambda e: e.tensor_scalar(out=self.hal[:, :], in0=self.gh[:, 0, :], scalar1=self.c("bcoef", 0), scalar2=None, op0=ALU.mult),
                 [self.B_gh, self.B_cst], [self.B_hal])
        for i in range(1, 4):
            self.dve(lambda e, i=i: e.scalar_tensor_tensor(out=self.hal[:, :], in0=self.gh[:, i, :], scalar=self.c("bcoef", i),
                                                           in1=self.hal[:, :], op0=ALU.mult, op1=ALU.add),
                     [self.B_gh, self.B_cst, self.B_hal], [self.B_hal])
        self.dve(lambda e: e.tensor_tensor(out=self.xT[:, :, NPRE - 2:NPRE], in0=self.xT[:, :, NPRE - 2:NPRE],
                                           in1=self.hal[:, :].rearrange("p (k t) -> p k t", k=KC), op=ALU.add),
                 [self.XB[0], self.B_hal], [self.XB[0]])

    def ffn(self, l):
        sb = self.sb
        A = sb.alloc
        NH = 1042
        act_ = A("act", [P, NFC, 1040], BF16)
        B_act = [Buf(f"act{i}") for i in range(NFC)]
        wsl = [A(f"wsl{i}", [P, 4096], BF16) for i in range(3)]
        B_wsl = [Buf(f"wsl{i}") for i in range(3)]
        uhalo = A("uhalo", [P, 44, 2], F32)
        B_uhalo = Buf("uhalo")
        m_c = sb.mark()
        gl0_ = A("gl0", [P, 512], F32)
        gl = [gl0_, gl0_]
        B_gl0_ = Buf("gl0")
        B_gl = [B_gl0_, B_gl0_]
        ca = [A(f"ca{i}", [P, 512], F32) for i in range(2)]
        cg = [A(f"cg{i}", [P, 512], F32) for i in range(2)]
        B_ca, B_cg = [Buf("ca0"), Buf("ca1")], [Buf("cg0"), Buf("cg1")]
        m_c2 = sb.mark()
        sb.release(m_c)
        cB = A("cB", [P, 44, 16], F32)
        tB = A("tB", [P, 44, 16], F32)
        glB = A("glB", [P, NFC, 16], F32)
        assert sb.mark() <= m_c2
        sb.release(m_c2)
        G_c = [B_gl0_, B_ca[0], B_ca[1], B_cg[0], B_cg[1]]
        m_u = sb.mark()
        sqf = A("sqf", [P, 2, 512], BF16)
        B_sqf = [Buf("sqf0"), Buf("sqf1")]
        tmp2 = [self.tmpx, A("tmpx2", [P, 512], F32)]
        B_tmp2 = [self.B_tmpx, Buf("tmpx2")]
        m_u2 = sb.mark()
        sb.release(m_u)
        upre = A("upre", [P, 44, 16], F32)
        assert sb.mark() <= m_u2
        sb.release(m_u2)
        G_u = [B_sqf[0], B_sqf[1], B_tmp2[1]]
        m1 = sb.mark()
        h2T = A("h2T", [P, KC, NH], BF16)
        ua = [A(f"ua{i}", [P, NH], F32) for i in range(2)]
        ug = [A(f"ug{i}", [P, NH], F32) for i in range(2)]
        m_sq = sb.mark()
        sqn = A("sqn2", [P, KC, 512], BF16)
        m_end = sb.mark()
        sb.release(m1)
        f_sb = A("f_sb", [P, KC, 1040], F32)
        assert sb.mark() <= m_sq
        sb.release(m_end)
        wo, _ = self.lay[f"convw{l}"]
        bo, _ = self.lay[f"convb{l}"]

        halves = [
            [(2, 16, [0], 0), (18, 512, [1, 2, 3, 4], 16), (530, 512, [5, 6, 7, 8], 528)],
            [(2, 512, [9, 10, 11, 12], 1040), (514, 512, [13, 14, 15, 16], 1552)],
        ]
        wcnt = [0]

        def next_slot():
            i = wcnt[0] % 3
            wcnt[0] += 1
            return i

        B_h2T, B_sqn = Buf("h2T"), Buf("sqn2")
        B_ua, B_ug = [Buf("ua0"), Buf("ua1")], [Buf("ug0"), Buf("ug1")]
        G_fsb = [B_h2T] + B_ua + B_ug
        last_layer = (l == self.nl - 1) and self.final_layer
        for hi, tiles in enumerate(halves):
            order = [t for t in tiles if t[2] != [0]] + [t for t in tiles if t[2] == [0]]
            for (co, n, xt, t0) in order:
                if xt == [0]:
                    self.halo_recv(l)
                self.rmsnorm(self.xT[:, :, t0:t0 + n], [self.XB[i] for i in xt], f"pfn{l}", h2T[:, :, co:co + n], [B_h2T], n, sqn, B_sqn)
            if hi == 0:
                for k in range(2):
                    self.dve(lambda e, k=k: e.memset(ua[k][:, 0:2], 0.0), [], [B_ua[k]])
                    self.dve(lambda e, k=k: e.memset(ug[k][:, 0:2], 0.0), [], [B_ug[k]])
            for j in range(11):
                si = next_slot()
                w = wsl[si]
                self.dma("pool", lambda e, j=j, w=w: e.dma_start(out=w[:, :], in_=self.d_wup[l * 11 + j, :, :]), [], [B_wsl[si]], f"wsl{si}")
                w3 = w[:, :].rearrange("p (k c) -> p k c", k=KC)
                for q in range(2):
                    i = 2 * j + q
                    k = i % 2
                    uab, ugb = ua[k], ug[k]
                    if hi == 1:
                        self.dve(lambda e, i=i, uab=uab: e.tensor_copy(out=uab[:, 0:2], in_=uhalo[:, i, :]), [B_uhalo], [B_ua[k]])
                        self.dve(lambda e, i=i, ugb=ugb: e.tensor_copy(out=ugb[:, 0:2], in_=uhalo[:, NFC + i, :]), [B_uhalo], [B_ug[k]])
                    for ti_, (co, n, xt, t0) in enumerate(tiles):
                        pa = self.ps.alloc()
                        pgt = self.ps.alloc()
                        for kc in range(KC):
                            self.pe(lambda e, kc=kc, pa=pa, co=co, n=n, q=q, w3=w3: e.matmul(
                                pa.f32(n), lhsT=w3[:, kc, q * 128:(q + 1) * 128], rhs=h2T[:, kc, co:co + n],
                                start=(kc == 0), stop=(kc == KC - 1)), [B_wsl[si], B_h2T], [pa.buf])
                        for kc in range(KC):
                            self.pe(lambda e, kc=kc, pgt=pgt, co=co, n=n, q=q, w3=w3: e.matmul(
                                pgt.f32(n), lhsT=w3[:, kc, 256 + q * 128:256 + (q + 1) * 128], rhs=h2T[:, kc, co:co + n],
                                start=(kc == 0), stop=(kc == KC - 1)), [B_wsl[si], B_h2T], [pgt.buf])
                        if xt == [0] and not last_layer:
                            self.act(lambda e, pa=pa, co=co, n=n, uab=uab: e.activation(out=uab[:, co:co + n], in_=pa.f32(n), func=AF.Copy), [pa.buf], [B_ua[k]])
                            self.act(lambda e, pgt=pgt, co=co, n=n, ugb=ugb: e.activation(out=ugb[:, co:co + n], in_=pgt.f32(n), func=AF.Copy), [pgt.buf], [B_ug[k]])
                            self.act(lambda e, pa=pa, i=i, n=n: e.activation(out=upre[:, i, 0:n], in_=pa.f32(n), func=AF.Copy), [pa.buf], G_u)
                            self.act(lambda e, pgt=pgt, i=i, n=n: e.activation(out=upre[:, NFC + i, 0:n], in_=pgt.f32(n), func=AF.Copy), [pgt.buf], G_u)
                            continue
                        if last_layer and xt == [0]:
                            self.act(lambda e, pa=pa, co=co, n=n, uab=uab: e.activation(out=uab[:, co:co + n], in_=pa.f32(n), func=AF.Copy), [pa.buf], [B_ua[k]])
                            self.act(lambda e, pgt=pgt, co=co, n=n, ugb=ugb: e.activation(out=ugb[:, co:co + n], in_=pgt.f32(n), func=AF.Copy), [pgt.buf], [B_ug[k]])
                            continue
                        gi = (i * len(tiles) + ti_) % 2
                        cab, cgb = ca[gi], cg[gi]
                        for (pt_, u, B_u, cdst, B_c, ch, second) in ((pa, uab, B_ua[k], cab, B_ca[gi], i, "dve"),
                                                                     (pgt, ugb, B_ug[k], cgb, B_cg[gi], NFC + i, "dve")):
                            w0 = self.cst[:, wo + ch * 3 + 0:wo + ch * 3 + 1]
                            w1 = self.cst[:, wo + ch * 3 + 1:wo + ch * 3 + 2]
                            w2 = self.cst[:, wo + ch * 3 + 2:wo + ch * 3 + 3]
                            bb = self.cst[:, bo + ch:bo + ch + 1]
                            self.act(lambda e, pt_=pt_, u=u, co=co, n=n: e.activation(out=u[:, co:co + n], in_=pt_.f32(n), func=AF.Copy),
                                     [pt_.buf], [B_u])
                            self.act(lambda e, pt_=pt_, cdst=cdst, n=n, w2=w2, bb=bb: e.activation(
                                out=cdst[:, 0:n], in_=pt_.f32(n), func=AF.Identity, scale=w2, bias=bb), [pt_.buf, self.B_cst], [B_c])
                            self.dve(lambda e, u=u, cdst=cdst, co=co, n=n, w1=w1: e.scalar_tensor_tensor(
                                out=cdst[:, 0:n], in0=u[:, co - 1:co - 1 + n], scalar=w1, in1=cdst[:, 0:n], op0=ALU.mult, op1=ALU.add),
                                [B_u, self.B_cst, B_c], [B_c])
                            self.pg.emit(second, lambda e, u=u, cdst=cdst, co=co, n=n, w0=w0: e.scalar_tensor_tensor(
                                out=cdst[:, 0:n], in0=u[:, co - 2:co - 2 + n], scalar=w0, in1=cdst[:, 0:n], op0=ALU.mult, op1=ALU.add),
                                [B_u, self.B_cst, B_c], [B_c])
                        self.act(lambda e, n=n, gi=gi, cab=cab: e.activation(out=gl[gi][:, 0:n], in_=cab[:, 0:n], func=AF.Gelu_apprx_tanh),
                                 [B_ca[gi]], [B_gl[gi]])
                        ac0 = co - 2
                        import os
                        self.pg.emit(os.environ.get("KGATE", "dve"), lambda e, i=i, ac0=ac0, n=n, gi=gi, cgb=cgb: e.tensor_tensor(
                            out=act_[:, i, ac0:ac0 + n], in0=gl[gi][:, 0:n], in1=cgb[:, 0:n], op=ALU.mult),
                            [B_gl[gi], B_cg[gi]], [B_act[i]])
                    if hi == 0:
                        self.dve(lambda e, i=i, uab=uab: e.tensor_copy(out=uhalo[:, i, :], in_=uab[:, NH - 2:NH]), [B_ua[k]], [B_uhalo])
                        self.dve(lambda e, i=i, ugb=ugb: e.tensor_copy(out=uhalo[:, NFC + i, :], in_=ugb[:, NH - 2:NH]), [B_ug[k]], [B_uhalo])
            if hi == 0 and not last_layer:
                wv = self.cst[:, wo:wo + 132].rearrange("p (c k) -> p c k", k=3)

                def wbc(tap, ncol):
                    return wv[:, :, tap].unsqueeze(2).broadcast_to([P, 44, ncol])
                b_bc = self.cst[:, bo:bo + 44].unsqueeze(2).broadcast_to([P, 44, 16])
                self.dve(lambda e: e.tensor_tensor(out=cB[:, :, :], in0=upre[:, :, :], in1=wbc(2, 16), op=ALU.mult), G_u + [self.B_cst], G_c)
                self.dve(lambda e: e.tensor_tensor(out=tB[:, :, 1:16], in0=upre[:, :, 0:15], in1=wbc(1, 15), op=ALU.mult), G_u + [self.B_cst], G_c)
                self.dve(lambda e: e.tensor_tensor(out=cB[:, :, 1:16], in0=cB[:, :, 1:16], in1=tB[:, :, 1:16], op=ALU.add), G_c, G_c)
                self.dve(lambda e: e.tensor_tensor(out=tB[:, :, 2:16], in0=upre[:, :, 0:14], in1=wbc(0, 14), op=ALU.mult), G_u + [self.B_cst], G_c)
                self.dve(lambda e: e.tensor_tensor(out=cB[:, :, 2:16], in0=cB[:, :, 2:16], in1=tB[:, :, 2:16], op=ALU.add), G_c, G_c)
                self.dve(lambda e: e.tensor_tensor(out=cB[:, :, :], in0=cB[:, :, :], in1=b_bc, op=ALU.add), G_c + [self.B_cst], G_c)
                self.act(lambda e: e.activation(out=glB[:, :, :], in_=cB[:, 0:NFC, :], func=AF.Gelu_apprx_tanh), G_c, G_c)
                self.dve(lambda e: e.tensor_tensor(out=act_[:, :, 0:16], in0=glB[:, :, :], in1=cB[:, NFC:2 * NFC, :], op=ALU.mult), G_c, B_act)
            if last_layer:
                tiles = [t for t in tiles if t[2] != [0]]
            sst = [self.ps.reserve() for _ in tiles]
            pend = None
            for oc in range(KC):
                si = next_slot()
                w = wsl[si]
                self.dma("pool", lambda e, oc=oc, w=w: e.dma_start(out=w[:, 0:NFC * 128], in_=self.d_wdn[l * 8 + oc, :, :]), [], [B_wsl[si]], f"wsl{si}")
                w3 = w[:, 0:NFC * 128].rearrange("p (k c) -> p k c", k=NFC)
                for ri, (co, n, xt, t0) in enumerate(tiles):
                    ac0 = co - 2
                    pt = self.ps.alloc()
                    for kc in range(NFC):
                        self.pe(lambda e, kc=kc, pt=pt, ac0=ac0, n=n, w3=w3: e.matmul(
                            pt.f32(n), lhsT=w3[:, kc, :], rhs=act_[:, kc, ac0:ac0 + n], start=(kc == 0), stop=(kc == NFC - 1)),
                            [B_wsl[si], B_act[kc]], [pt.buf])
                    sq_i = (oc * len(tiles) + ri) % 2
                    self.act(lambda e, pt=pt, oc=oc, ac0=ac0, n=n: e.mul(out=f_sb[:, oc, ac0:ac0 + n], in_=pt.f32(n), mul=self.c(f"pofn{l}", oc)),
                             [pt.buf, self.B_cst], G_fsb)
                    self.act(lambda e, pt=pt, sq_i=sq_i, n=n: e.activation(out=sqf[:, sq_i, 0:n], in_=pt.f32(n), func=AF.Square),
                             [pt.buf], [B_sqf[sq_i]])
                    if pend is not None:
                        self.pe(*pend)
                    pend = (lambda e, st_=sst[ri], sq_i=sq_i, n=n, oc=oc: e.matmul(st_.f32(n), lhsT=self.ones, rhs=sqf[:, sq_i, 0:n],
                                                                                    start=(oc == 0), stop=(oc == KC - 1)),
                            [B_sqf[sq_i], self.B_cst], [sst[ri].buf])
            if pend is not None:
                self.pe(*pend)
            for ri, (co, n, xt, t0) in enumerate(tiles):
                ac0 = co - 2
                self.rstd_from(sst[ri], n, 1.0 / D)
                xb = [self.XB[i] for i in xt]
                for kc in range(KC):
                    tk = kc % 2
                    self.dve(lambda e, kc=kc, ac0=ac0, n=n, tk=tk: e.tensor_tensor(
                        out=tmp2[tk][:, 0:n], in0=f_sb[:, kc, ac0:ac0 + n], in1=self.rstd[:, 0:n], op=ALU.mult),
                        G_fsb + [self.B_rstd], [B_tmp2[tk]])
                    self.pg.emit("pool" if kc % 2 == 0 else "dve",
                                 lambda e, kc=kc, t0=t0, n=n, tk=tk: e.tensor_tensor(out=self.xT[:, kc, t0:t0 + n], in0=self.xT[:, kc, t0:t0 + n],
                                                                                     in1=tmp2[tk][:, 0:n], op=ALU.add), xb + [B_tmp2[tk]], xb)
            for t in sst:
                self.ps.unreserve(t)


def _img(w):
    K, N = w.shape
    return np.ascontiguousarray(w.reshape(K // P, P, N).transpose(1, 0, 2).reshape(P, (K // P) * N))


def _host_consts(r, nl, layers, prm):
    lay, ncst = cst_layout(nl)
    cst = np.zeros((P, ncst), np.float32)

    def put(name, arr):
        o, n = lay[name]
        arr = np.asarray(arr, np.float32)
        cst[:, o:o + n] = arr.reshape(-1, n) if arr.ndim > 1 else arr[None, :]

    put("eps", [EPS])
    put("one", [1.0])
    put("flag", [1.0 if r == 0 else 0.0])
    put("acoef", [1.0 if i < r else 0.0 for i in range(4)])
    put("bcoef", [1.0 if i == r - 1 else 0.0 for i in range(4)])
    hh = np.arange(4, dtype=np.float64)
    log_g = np.log(1.0 - 2.0 ** (-5.0 - hh))
    put("gr128", np.exp(log_g * 128))
    put("gr16", np.exp(log_g * 16) if r == 0 else np.ones(4))
    drt = np.zeros((4, 4))
    for i in range(4):
        drt[i] = np.exp(log_g * (NLOC if i == 0 else NSEQ))
    put("drt", drt.reshape(-1))
    tt = np.arange(128, dtype=np.float64)
    put("xiq", np.exp(log_g[:, None] * (tt[None, :] + 1.0)).reshape(-1))
    put("kinv", (np.exp(-log_g[:, None] * (tt[None, :] + 1.0)) * (128.0 ** -0.5)).reshape(-1))
    mt = (np.arange(128)[None, :] >= np.arange(128)[:, None]).astype(np.float32)
    o, n = lay["mt"]
    cst[:, o:o + n] = mt
    for li, l in enumerate(layers):
        def fm(v):
            return np.asarray(v, np.float32).reshape(KC, P).T
        for nm, key in (("pmn", "pre_mix_norm"), ("pon", "post_mix_norm"), ("pfn", "pre_ffn_norm"), ("pofn", "post_ffn_norm")):
            o, n = lay[f"{nm}{li}"]
            cst[:, o:o + n] = fm(prm[key][l])
        o, n = lay[f"retnw{li}"]
        cst[:, o:o + n] = np.asarray(prm["ret_norm_w"][l], np.float32).reshape(4, P).T
        o, n = lay[f"glanw{li}"]
        cst[:, o:o + n] = np.asarray(prm["gla_norm_w"][l], np.float32).reshape(4, P).T
        cw = np.asarray(prm["ffn_conv_w"][l], np.float32)
        o, n = lay[f"convw{li}"]
        cst[:, o:o + n] = cw.reshape(3, 44, P).transpose(2, 1, 0).reshape(P, 44 * 3)
        cbv = np.asarray(prm["ffn_conv_b"][l], np.float32)
        o, n = lay[f"convb{li}"]
        cst[:, o:o + n] = cbv.reshape(44, P).T
    return cst


def _host_weights(layers, prm):
    nl = len(layers)
    win = np.stack([_img(np.asarray(prm["w_in"][l], np.float32)) for l in layers])
    wout = np.stack([_img(np.asarray(prm["w_out"][l], np.float32)) for l in layers])
    wup = np.zeros((nl * 11, P, KC * 512), np.float32)
    wdn = np.zeros((nl * 8, P, NFC * 128), np.float32)
    for li, l in enumerate(layers):
        up = np.asarray(prm["ffn_up"][l], np.float32)
        dn = np.asarray(prm["ffn_down"][l], np.float32)
        for j in range(11):
            cols = np.concatenate([np.arange(128 * (2 * j), 128 * (2 * j + 2)), DFF + np.arange(128 * (2 * j), 128 * (2 * j + 2))])
            wup[li * 11 + j] = _img(up[:, cols])
        for oc in range(8):
            wdn[li * 8 + oc] = _img(dn[:, oc * 128:(oc + 1) * 128])
    w2b = np.zeros((17, nl * 256), np.float32)
    for li, l in enumerate(layers):
        w2b[0:16, li * 256:(li + 1) * 256] = np.asarray(prm["gla_gate_w2"][l], np.float32)
        w2b[16, li * 256:(li + 1) * 256] = np.asarray(prm["gla_gate_b"][l], np.float32)
    return win, wout, wup, wdn, w2b


def _host_rope(r):
    half = 64
    inv = 10000.0 ** (-np.arange(half, dtype=np.float64) / half)
    pos = float(NSEQ * r) + np.arange(NLOC, dtype=np.float64)
    ang = pos[:, None] * inv[None, :]
    c = np.cos(ang).astype(np.float32).T
    s = np.sin(ang).astype(np.float32).T
    cosT = np.concatenate([c, c], axis=0)
    sinT = np.concatenate([-s, s], axis=0)
    return np.ascontiguousarray(np.concatenate([cosT, sinT], axis=1))


_CACHE = {}


def _get_program(nl, dbg=None, final=True):
    key = (nl, dbg, final)
    if key not in _CACHE:
        b = Builder(nl, dbg, final_layer=final)
        _CACHE[key] = (b.build(), b)
    return _CACHE[key]


def _run(xT_imgs, layers, prm, dbg=None):
    nl = len(layers)
    nc, b = _get_program(nl, dbg, final=(layers[-1] == prm["w_in"].shape[0] - 1))
    win, wout, wup, wdn, w2b = _host_weights(layers, prm)
    cbm = np.concatenate([np.eye(P, dtype=np.float32), np.ones((P, P), np.float32)], axis=1)
    in_maps = []
    for j in range(8):
        r = j % 4
        in_maps.append({
            "xT": xT_imgs[j], "cst": _host_consts(r, nl, layers, prm), "w2b": w2b, "cb": cbm, "rope": _host_rope(r),
            "win": win, "wout": wout, "wup": wup, "wdn": wdn,
        })
    res = run_bass_kernel_spmd(nc, in_maps, core_ids=list(range(8)))
    return res


LAUNCH_SPLIT = False


def kernel(x, meta_tokens, pre_mix_norm, w_in, gla_gate_w2, gla_gate_b, ret_norm_w, gla_norm_w,
           w_out, post_mix_norm, pre_ffn_norm, ffn_up, ffn_conv_w, ffn_conv_b, ffn_down, post_ffn_norm):
    prm = dict(pre_mix_norm=pre_mix_norm, w_in=w_in, gla_gate_w2=gla_gate_w2, gla_gate_b=gla_gate_b,
               ret_norm_w=ret_norm_w, gla_norm_w=gla_norm_w, w_out=w_out, post_mix_norm=post_mix_norm,
               pre_ffn_norm=pre_ffn_norm, ffn_up=ffn_up, ffn_conv_w=ffn_conv_w, ffn_conv_b=ffn_conv_b,
               ffn_down=ffn_down, post_ffn_norm=post_ffn_norm)
    prm = {k: np.asarray(v) for k, v in prm.items()}
    x = np.asarray(x, np.float32)
    meta = np.asarray(meta_tokens, np.float32)
    imgs = []
    for j in range(8):
        b, r = j // 4, j % 4
        xin = np.zeros((NLOC, D), np.float32)
        if r == 0:
            xin[0:NPRE] = meta
        xin[NPRE:] = x[b, NSEQ * r:NSEQ * (r + 1)]
        imgs.append(np.ascontiguousarray(xin.T.reshape(KC, P, NLOC).transpose(1, 0, 2).reshape(P, KC * NLOC)))
    nl_total = prm["w_in"].shape[0]
    if LAUNCH_SPLIT:
        for l in range(nl_total):
            res = _run(imgs, [l], prm)
            imgs = [np.ascontiguousarray(res.results[j]["y"]) for j in range(8)]
        outs = imgs
    else:
        res = _run(imgs, list(range(nl_total)), prm)
        outs = [res.results[j]["y"] for j in range(8)]
    out = np.zeros((2, 4 * NSEQ, D), np.float32)
    for j in range(8):
        b, r = j // 4, j % 4
        y = np.asarray(outs[j]).reshape(P, KC, NLOC)[:, :, NPRE:]
        out[b, NSEQ * r:NSEQ * (r + 1)] = y.transpose(2, 1, 0).reshape(NSEQ, D)
    return out
```
